# Optimizing a Trainium2 kernel written in Bass

```python
import math
import jax, jax.numpy as jnp
from jax import lax
import numpy as np

D_MODEL = 1024
BATCH = 8
SEQ = 2048
DEPTH = 2
DEC_BATCH = 16
DEC_SEQ = 2048
PAST_LEN = 128

HEAD_DIM = 64
A_HEADS = D_MODEL // 128
A_WIDTH = A_HEADS * HEAD_DIM
DILATED_PATTERNS = ((128, 1), (512, 4), (2048, 16))
R_HEADS = D_MODEL // 256
R_KEY_DIM = HEAD_DIM
R_VAL_DIM = 2 * HEAD_DIM
R_QK_WIDTH = R_HEADS * R_KEY_DIM
R_WIDTH = R_HEADS * R_VAL_DIM
MIX_WIDTH = A_WIDTH + R_WIDTH
IN_SPLITS = (A_WIDTH, A_WIDTH, A_WIDTH, A_WIDTH, R_QK_WIDTH, R_QK_WIDTH, R_WIDTH, R_WIDTH)
IN_WIDTH = sum(IN_SPLITS)
CHUNK = 128
ROPE_THETA = 10000.0
EPS = 1e-6
NEG = -1e30

kernel_name = "hybrid_dilated_attn_retention_encoder"


def rmsnorm(x, g):
    xf = x.astype(jnp.float32)
    y = xf * lax.rsqrt(jnp.mean(xf * xf, axis=-1, keepdims=True) + EPS)
    return (y * g.astype(jnp.float32)).astype(x.dtype)


def rope(x):
    S, dh = x.shape[1], x.shape[-1]
    inv = ROPE_THETA ** (-jnp.arange(0, dh, 2, dtype=jnp.float32) / dh)
    ang = jnp.arange(S, dtype=jnp.float32)[:, None] * inv[None, :]
    cos = jnp.cos(ang)[None, :, None, :].astype(x.dtype)
    sin = jnp.sin(ang)[None, :, None, :].astype(x.dtype)
    x1, x2 = x[..., : dh // 2], x[..., dh // 2:]
    return jnp.concatenate([x1 * cos - x2 * sin, x2 * cos + x1 * sin], axis=-1)


def dilated_attention(q, k, v, window, dilation):
    B, S, H, dh = q.shape
    R = window // (2 * dilation)
    blk = R
    L = S // dilation
    nb = -(-L // blk)
    Lp = nb * blk

    def to_classes(t):
        return t.reshape(B, L, dilation, H, dh).transpose(0, 2, 1, 3, 4)

    qc = jnp.pad(to_classes(q), ((0, 0), (0, 0), (0, Lp - L), (0, 0), (0, 0)))
    qc = qc.reshape(B, dilation, nb, blk, H, dh)

    def windows(t):
        tp = jnp.pad(to_classes(t), ((0, 0), (0, 0), (blk, Lp - L + blk), (0, 0), (0, 0)))
        tp = tp.reshape(B, dilation, nb + 2, blk, H, dh)
        return jnp.concatenate([tp[:, :, :-2], tp[:, :, 1:-1], tp[:, :, 2:]], axis=3)

    kw, vw = windows(k), windows(v)
    s = jnp.einsum('bdnqhe,bdnkhe->bdnhqk', qc, kw).astype(jnp.float32)
    l_q = jnp.arange(nb)[:, None, None] * blk + jnp.arange(blk)[None, :, None]
    l_k = (jnp.arange(nb)[:, None, None] - 1) * blk + jnp.arange(3 * blk)[None, None, :]
    valid = (l_k >= 0) & (l_k < L) & (jnp.abs(l_q - l_k) <= R)
    s = jnp.where(valid[None, None, :, None], s, NEG)
    lse = jax.nn.logsumexp(s, axis=-1)
    p = jnp.exp(s - lse[..., None]).astype(v.dtype)
    o = jnp.einsum('bdnhqk,bdnkhe->bdnqhe', p, vw)
    o = o.reshape(B, dilation, Lp, H, dh)[:, :, :L].transpose(0, 2, 1, 3, 4).reshape(B, S, H, dh)
    lse = lse.transpose(0, 1, 2, 4, 3).reshape(B, dilation, Lp, H)[:, :, :L]
    lse = lse.transpose(0, 2, 1, 3).reshape(B, S, H)
    return o, lse


def retention_direction(q, k, v, log_g, strict):
    B, S, H, dk = q.shape
    dv = v.shape[-1]
    nC = S // CHUNK
    q = q.reshape(B, nC, CHUNK, H, dk)
    k = k.reshape(B, nC, CHUNK, H, dk)
    v = v.reshape(B, nC, CHUNK, H, dv)
    t = jnp.arange(CHUNK, dtype=jnp.float32)
    diff = t[:, None] - t[None, :]
    mask = (diff > 0) if strict else (diff >= 0)
    D = jnp.where(mask[None], jnp.exp(jnp.where(mask, diff, 0.0)[None] * log_g[:, None, None]), 0.0)
    inner = jnp.einsum('bnthe,bnshe->bnhts', q, k) * D[None, None]
    y = jnp.einsum('bnhts,bnshf->bnthf', inner, v)
    kdec = k * jnp.exp((CHUNK - 1 - t)[:, None] * log_g[None, :])[None, None, :, :, None]
    chunk_kv = jnp.einsum('bnshe,bnshf->nbhef', kdec, v)
    g_chunk = jnp.exp(CHUNK * log_g)[None, :, None, None]

    def step(state, kv):
        return g_chunk * state + kv, state

    _, state_prev = lax.scan(step, jnp.zeros((B, H, dk, dv), jnp.float32), chunk_kv)
    qdec = q * jnp.exp((t + 1.0)[:, None] * log_g[None, :])[None, None, :, :, None]
    y = y + jnp.einsum('bnthe,nbhef->bnthf', qdec, state_prev)
    return y.reshape(B, S, H, dv)


def encoder_layer(x, c, g_norm, w_ada, b_ada, w_in, w_out, decay_fwd, decay_bwd):
    B, S, _ = x.shape
    mod = jax.nn.silu(c) @ w_ada + b_ada
    shift, scale, gate = jnp.split(mod, 3, axis=-1)
    h = rmsnorm(x, g_norm) * (1.0 + scale[:, None, :]) + shift[:, None, :]
    proj = h @ w_in
    idx = list(np.cumsum(IN_SPLITS)[:-1])
    qa, ka, va, ga, qb, kb, vb, gb = jnp.split(proj, idx, axis=-1)

    qa = rope(qa.reshape(B, S, A_HEADS, HEAD_DIM)) * (HEAD_DIM ** -0.5)
    ka = rope(ka.reshape(B, S, A_HEADS, HEAD_DIM))
    va = va.reshape(B, S, A_HEADS, HEAD_DIM)
    outs, lses = [], []
    for window, dilation in DILATED_PATTERNS:
        o_p, lse_p = dilated_attention(qa, ka, va, window, dilation)
        outs.append(o_p)
        lses.append(lse_p)
    wts = jax.nn.softmax(jnp.stack(lses, axis=0), axis=0).astype(x.dtype)
    oa = jnp.einsum('pbsh,pbshe->bshe', wts, jnp.stack(outs, axis=0))
    ya = oa.reshape(B, S, A_WIDTH) * jax.nn.silu(ga)

    qf = rope(qb.reshape(B, S, R_HEADS, R_KEY_DIM)).astype(jnp.float32)
    kf = rope(kb.reshape(B, S, R_HEADS, R_KEY_DIM)).astype(jnp.float32) * (R_KEY_DIM ** -0.5)
    vf = vb.reshape(B, S, R_HEADS, R_VAL_DIM).astype(jnp.float32)
    lg_f = jax.nn.log_sigmoid(decay_fwd.astype(jnp.float32))
    lg_b = jax.nn.log_sigmoid(decay_bwd.astype(jnp.float32))
    yf = retention_direction(qf, kf, vf, lg_f, False)
    yb = retention_direction(qf[:, ::-1], kf[:, ::-1], vf[:, ::-1], lg_b, True)[:, ::-1]
    r = yf + yb
    r = r * lax.rsqrt(jnp.mean(r * r, axis=-1, keepdims=True) + EPS)
    yr = r.astype(x.dtype).reshape(B, S, R_WIDTH) * jax.nn.silu(gb)

    out = jnp.concatenate([ya, yr], axis=-1) @ w_out
    return x + gate[:, None, :] * out


def trunk(x, c, g_norm, w_ada, b_ada, w_in, w_out, decay_fwd, decay_bwd, g_final):
    for l in range(DEPTH):
        x = encoder_layer(x, c, g_norm[l], w_ada[l], b_ada[l], w_in[l], w_out[l],
                          decay_fwd[l], decay_bwd[l])
    return rmsnorm(x, g_final)


def setup_inputs(seed: int = 0) -> dict:
    key = jax.random.key(seed)
    ks = jax.random.split(key, 14)
    f32 = jnp.float32
    base = 1.0 - 2.0 ** (-5.0 - jnp.arange(R_HEADS, dtype=f32))
    base_logit = jnp.log(base / (1.0 - base))
    return {
        "x_prompt": jax.random.normal(ks[0], (BATCH, SEQ, D_MODEL), f32),
        "x_sample": jax.random.normal(ks[1], (DEC_BATCH, DEC_SEQ, D_MODEL), f32),
        "c_prompt": jax.random.normal(ks[2], (BATCH, D_MODEL), f32),
        "c_sample": jax.random.normal(ks[3], (DEC_BATCH, D_MODEL), f32),
        "g_norm": 1.0 + 0.02 * jax.random.normal(ks[4], (DEPTH, D_MODEL), f32),
        "w_ada": 0.5 * D_MODEL ** -0.5 * jax.random.normal(ks[5], (DEPTH, D_MODEL, 3 * D_MODEL), f32),
        "b_ada": 0.02 * jax.random.normal(ks[6], (DEPTH, 3 * D_MODEL), f32),
        "w_in": D_MODEL ** -0.5 * jax.random.normal(ks[7], (DEPTH, D_MODEL, IN_WIDTH), f32),
        "w_out": MIX_WIDTH ** -0.5 * jax.random.normal(ks[8], (DEPTH, MIX_WIDTH, D_MODEL), f32),
        "decay_fwd": base_logit[None] + 0.1 * jax.random.normal(ks[9], (DEPTH, R_HEADS), f32),
        "decay_bwd": base_logit[None] + 0.1 * jax.random.normal(ks[10], (DEPTH, R_HEADS), f32),
        "g_final": 1.0 + 0.02 * jax.random.normal(ks[11], (D_MODEL,), f32),
    }


def reference(x_prompt, x_sample, c_prompt, c_sample, g_norm, w_ada, b_ada, w_in, w_out,
              decay_fwd, decay_bwd, g_final):
    y_prompt = trunk(x_prompt, c_prompt, g_norm, w_ada, b_ada, w_in, w_out,
                     decay_fwd, decay_bwd, g_final)
    y_sample = trunk(x_sample, c_sample, g_norm, w_ada, b_ada, w_in, w_out,
                     decay_fwd, decay_bwd, g_final)
    return (y_prompt, y_sample)
```

```python
import math
import os
from contextlib import ExitStack

import numpy as np
import concourse.bass as bass
import concourse.mybir as mybir
from concourse.bass_utils import run_bass_kernel_spmd

F32 = mybir.dt.float32
BF16 = mybir.dt.bfloat16
AF = mybir.ActivationFunctionType
ALU = mybir.AluOpType

D = 1024
S = 2048
NT = 16
DEPTH = 2
NCORES = 8
NSEQ = 3
INW = 3584
EPS = 1e-6


class Tok:
    __slots__ = ("eng", "key", "val")

    def __init__(self, eng, key, val):
        self.eng, self.key, self.val = eng, key, val


class Res:
    __slots__ = ("name", "w", "r")

    def __init__(self, name=""):
        self.name, self.w, self.r = name, None, []


class Ctx:
    def __init__(self, nc, es):
        self.nc, self.es = nc, es
        self.engs = {"pe": nc.tensor, "act": nc.scalar, "dve": nc.vector, "pool": nc.gpsimd, "sp": nc.sync}
        self.sems, self.cnt = {}, {}
        self.seen = {e: {} for e in self.engs}
        self.epoch = 0
        self.dma_keys = set()
        self.new_epoch()

    def _mksem(self, key):
        if key not in self.sems:
            self.sems[key] = self.es.enter_context(self.nc.semaphore(key))
            self.cnt[key] = 0

    def new_epoch(self):
        self.epoch += 1
        self.ekey = {e: "%s%d" % (e, self.epoch) for e in self.engs if e != "sp"}
        for k in self.ekey.values():
            self._mksem(k)

    def _waits(self, eng, reads, writes, skipkey=None):
        deps = {}

        def need(tok, kind):
            if tok is None or tok.key == skipkey:
                return
            if tok.eng == eng and (eng == "pe" or kind != "RAW"):
                return
            if deps.get(tok.key, 0) < tok.val:
                deps[tok.key] = tok.val

        for r in reads:
            need(r.w, "RAW")
        for w in writes:
            need(w.w, "WAW")
            for t in w.r:
                need(t, "WAR")
        E, seen = self.engs[eng], self.seen[eng]
        for key, val in deps.items():
            if seen.get(key, 0) < val:
                E.wait_ge(self.sems[key], val)
                seen[key] = val

    def _commit(self, tok, reads, writes):
        for r in reads:
            r.r = [t for t in r.r if t.key != tok.key] + [tok]
        for w in writes:
            w.w, w.r = tok, []

    def op(self, eng, fns, reads=(), writes=()):
        self._waits(eng, reads, writes)
        if not isinstance(fns, (list, tuple)):
            fns = [fns]
        ins = None
        for f in fns:
            ins = f(self.engs[eng])
        key = self.ekey[eng]
        self.cnt[key] += 1
        ins.then_inc(self.sems[key], 1)
        tok = Tok(eng, key, self.cnt[key])
        self._commit(tok, reads, writes)
        return tok

    def dma(self, q, semname, fn, reads=(), writes=()):
        key = "d%d_%s" % (self.epoch, semname)
        self._mksem(key)
        self.dma_keys.add(key)
        self._waits(q, reads, writes, skipkey=key)
        ins = fn(self.engs[q])
        self.cnt[key] += 16
        ins.then_inc(self.sems[key], 16)
        tok = Tok("dma", key, self.cnt[key])
        self._commit(tok, reads, writes)
        return tok

    def barrier(self, with_dma=True):
        keys = list(self.ekey.values())
        if with_dma:
            keys += sorted(self.dma_keys)
        for e, E in self.engs.items():
            seen = self.seen[e]
            for key in keys:
                val = self.cnt[key]
                if val > 0 and seen.get(key, 0) < val:
                    E.wait_ge(self.sems[key], val)
                    seen[key] = val


def _pipeline(n_iter, stages):
    mx = max(sk for sk, _ in stages)
    for i in range(n_iter + mx):
        for sk, fn in stages:
            n = i - sk
            if 0 <= n < n_iter:
                fn(n)


def _ap(base, dims):
    return bass.AP(base.tensor, base.offset, [list(base.ap[0])] + [list(d) for d in dims])


class _Stop(Exception):
    pass


def build_nc(nseq=NSEQ, depth=DEPTH, dbg=(), stop=None):
    nc = bass.Bass("TRN2", target_bir_lowering=False)
    dt = nc.dram_tensor
    x_d = dt("x", [nseq, S, D], F32, kind="ExternalInput").ap()
    c_d = dt("c", [nseq, D], F32, kind="ExternalInput").ap()
    gn_d = dt("g_norm", [DEPTH, D], F32, kind="ExternalInput").ap()
    wada_d = dt("w_ada", [DEPTH, D, 3 * D], F32, kind="ExternalInput").ap()
    bada_d = dt("b_ada", [DEPTH, 3 * D], F32, kind="ExternalInput").ap()
    win_d = dt("w_in", [DEPTH, D, INW], F32, kind="ExternalInput").ap()
    wout_d = dt("w_out", [DEPTH, D, D], F32, kind="ExternalInput").ap()
    dfw_d = dt("decay_fwd", [DEPTH, 4], F32, kind="ExternalInput").ap()
    dbw_d = dt("decay_bwd", [DEPTH, 4], F32, kind="ExternalInput").ap()
    gfin_d = dt("g_final", [D], F32, kind="ExternalInput").ap()
    crope_d = dt("cst_rope", [128, 2, NT, 64], F32, kind="ExternalInput").ap()
    cmask_d = dt("cst_mask", [128, 4, 512], F32, kind="ExternalInput").ap()
    cret_d = dt("cst_ret", [128, 4, 128], F32, kind="ExternalInput").ap()
    ctau_d = dt("cst_tau", [128, 4], F32, kind="ExternalInput").ap()
    cid_d = dt("cst_ident", [128, 128], F32, kind="ExternalInput").ap()
    y_d = dt("y", [nseq, S, D], F32, kind="ExternalOutput").ap()
    dbg_d = {}
    for name, shape in dbg:
        dbg_d[name] = dt("dbg_" + name, list(shape), F32, kind="ExternalOutput").ap()

    with ExitStack() as es:
        E = es.enter_context
        cx = Ctx(nc, es)
        sb = lambda name, shape, dtype: E(nc.sbuf_tensor(name, list(shape), dtype))
        uid = [0]

        def PT(ph, name, shape, dtype):
            uid[0] += 1
            return ph.enter_context(nc.sbuf_tensor("%s_u%d" % (name, uid[0]), list(shape), dtype))

        hT = sb("hT", [128, 8, S], BF16)
        yT = sb("yT", [128, 8, S], BF16)
        wsl = [sb("wsl%d" % i, [128, 8, 512], BF16) for i in range(3)]
        kT1 = sb("kT1", [128, S + 128], BF16)
        kT2 = sb("kT2", [128, 4, 640], BF16)
        VT1 = sb("VT1", [128, S + 128], BF16)
        VT2 = sb("VT2", [128, 4, 640], BF16)
        ident = sb("ident", [128, 128], BF16)
        identf = sb("identf", [128, 128], F32)
        onesf = sb("onesf", [128, 128], F32)
        rope_t = sb("rope_t", [128, 2, NT, 64], F32)
        maskA = sb("maskA", [128, 4, 512], BF16)
        cret = sb("cret", [128, 4, 128], F32)
        ctau = sb("ctau", [128, 4], F32)
        DTt_all = sb("DTt", [128, DEPTH, 4, 128], F32)
        TAB_all = sb("TAB", [128, DEPTH, 4, 4, 64], F32)
        Gfb_all = sb("Gfb", [128, DEPTH, 2, 2], F32)
        gsA = sb("gsA", [128, DEPTH, nseq, 8], F32)
        shA = sb("shA", [128, DEPTH, nseq, 8], F32)
        gtA = sb("gtA", [128, DEPTH, nseq, 8], F32)
        mhalf = sb("mhalf", [128, 16], F32)

        mm = [E(nc.psum_tensor("mm%d" % i, [128, 512], F32)) for i in range(2)]
        tp = [E(nc.psum_tensor("tp%d" % i, [128, 1024], BF16)) for i in range(2)]
        st = [E(nc.psum_tensor("st%d" % i, [128, 512], F32)) for i in range(2)]
        ov = [E(nc.psum_tensor("ov%d" % i, [128, 512], F32)) for i in range(2)]
        R_mm = [Res("mm0"), Res("mm1")]
        R_tp = [Res("tp0"), Res("tp1")]
        R_st = [Res("st0"), Res("st1")]
        R_ov = [Res("ov0"), Res("ov1")]
        ctr = {"mm": 0, "tp": 0, "st": 0, "ov": 0, "w": 0}

        def nxt(kind):
            i = ctr[kind] % 2
            ctr[kind] += 1
            return i

        R_const = Res("const")
        R_hT = [Res("hT%d" % g) for g in range(4)]
        R_yT = [Res("yT%d" % k) for k in range(8)]
        R_w = [Res("w%d" % i) for i in range(3)]
        R_kT1, R_kT2, R_VT1, R_VT2 = Res("kT1"), Res("kT2"), Res("VT1"), Res("VT2")
        R_tab, R_gate, R_mod = Res("tab"), Res("gate"), Res("mod")
        R_yd = [[Res("yd%d_%d" % (s, t)) for t in range(NT)] for s in range(nseq)]
        dump_list = []

        def dump(name, src_ap, reads):
            if name in dbg_d:
                dump_list.append(cx.dma("pool", "dbg", lambda e: e.dma_start(out=dbg_d[name], in_=src_ap), reads=reads))

        cx.dma("sp", "cst", lambda e: e.dma_start(out=rope_t[:], in_=crope_d), writes=[R_const])
        cx.dma("pool", "cstp", lambda e: e.dma_start(out=maskA[:], in_=cmask_d), writes=[R_const])
        cx.dma("sp", "cst", lambda e: e.dma_start(out=cret[:], in_=cret_d), writes=[R_const])
        cx.dma("sp", "cst", lambda e: e.dma_start(out=ctau[:], in_=ctau_d), writes=[R_const])
        cx.dma("pool", "cstp", lambda e: e.dma_start(out=ident[:], in_=cid_d), writes=[R_const])
        cx.dma("sp", "cst", lambda e: e.dma_start(out=identf[:], in_=cid_d), writes=[R_const])
        cx.op("pool", lambda e: e.memset(onesf[:], 1.0), writes=[R_const])
        cx.op("pool", lambda e: e.memset(mhalf[:], -0.5), writes=[R_const])
        cx.op("pool", lambda e: e.memset(kT1[:], 0.0), writes=[R_kT1])
        cx.op("pool", lambda e: e.memset(kT2[:], 0.0), writes=[R_kT2])
        cx.op("pool", lambda e: e.memset(VT1[:], 0.0), writes=[R_VT1])
        cx.op("pool", lambda e: e.memset(VT2[:], 0.0), writes=[R_VT2])

        with ExitStack() as ph:
            P = ph.enter_context
            wada = PT(ph, "wada", [128, 8, 3 * D], BF16)
            cT = PT(ph, "cT", [128, nseq, 8], F32)
            silc = PT(ph, "silc", [128, 8, nseq], BF16)
            badaT = PT(ph, "badaT", [128, 24], F32)
            gnT = PT(ph, "gnT", [128, 8], F32)
            modT = PT(ph, "modT", [128, 24, nseq], F32)
            R_wada, R_cT, R_silc, R_bada, R_gn, R_modT = (Res() for _ in range(6))
            for s in range(nseq):
                cx.dma("sp", "cst", lambda e, s=s: e.dma_start(
                    out=cT[:, s, :], in_=c_d[s].rearrange("(k p) -> p k", p=128), allow_slow_non_contiguous=True),
                    writes=[R_cT])
            cx.op("act", lambda e: e.activation(out=silc[:].rearrange("p k s -> p s k"), in_=cT[:], func=AF.Silu),
                  reads=[R_cT], writes=[R_silc])
            for l in range(depth):
                for i in range(6):
                    cx.dma("pool", "wada", lambda e, i=i, l=l: e.dma_start(
                        out=wada[:, :, i * 512:(i + 1) * 512],
                        in_=wada_d[l].rearrange("(k p) n -> p k n", p=128)[:, :, i * 512:(i + 1) * 512]),
                        writes=[R_wada])
                cx.dma("sp", "cst", lambda e, l=l: e.dma_start(
                    out=badaT[:], in_=bada_d[l].rearrange("(o p) -> p o", p=128), allow_slow_non_contiguous=True),
                    writes=[R_bada])
                cx.dma("sp", "cst", lambda e, l=l: e.dma_start(
                    out=gnT[:], in_=gn_d[l].rearrange("(k p) -> p k", p=128), allow_slow_non_contiguous=True),
                    writes=[R_gn])
                fns = []
                for oc in range(24):
                    for kc in range(8):
                        fns.append(lambda e, oc=oc, kc=kc: e.matmul(
                            mm[0][:, oc * nseq:(oc + 1) * nseq], lhsT=wada[:, kc, oc * 128:(oc + 1) * 128],
                            rhs=silc[:, kc, :], start=(kc == 0), stop=(kc == 7)))
                cx.op("pe", fns, reads=[R_wada, R_silc], writes=[R_mm[0]])
                mmv = mm[0][:, 0:24 * nseq].rearrange("p (o s) -> p o s", s=nseq)
                for s in range(nseq):
                    cx.op("dve", lambda e, s=s: e.tensor_tensor(out=modT[:, :, s], in0=mmv[:, :, s], in1=badaT[:], op=ALU.add),
                          reads=[R_mm[0], R_bada], writes=[R_modT])
                for s in range(nseq):
                    cx.op("dve", lambda e, s=s, l=l: e.scalar_tensor_tensor(
                        out=gsA[:, l, s, :], in0=modT[:, 8:16, s], scalar=1.0, in1=gnT[:], op0=ALU.add, op1=ALU.mult),
                        reads=[R_modT, R_gn], writes=[R_mod])
                    cx.op("dve", lambda e, s=s, l=l: e.tensor_copy(out=shA[:, l, s, :], in_=modT[:, 0:8, s]),
                          reads=[R_modT], writes=[R_mod])
                    cx.op("dve", lambda e, s=s, l=l: e.tensor_copy(out=gtA[:, l, s, :], in_=modT[:, 16:24, s]),
                          reads=[R_modT], writes=[R_mod])
            cx.barrier()

        for l in range(depth):
            DTt, TAB, Gfb = DTt_all[:, l], TAB_all[:, l], Gfb_all[:, l]
            with ExitStack() as ph:
                P = ph.enter_context
                dfb = PT(ph, "dfb", [128, 8], F32)
                lg = PT(ph, "lg", [128, 8], F32)
                dsc = PT(ph, "dsc", [128, 4, 4], F32)
                e1 = PT(ph, "e1", [128, 128], F32)
                e2 = PT(ph, "e2", [128, 128], F32)
                R_dfb, R_lg, R_dsc, R_e1, R_e2 = (Res() for _ in range(5))
                cx.dma("sp", "cst", lambda e: e.dma_start(out=dfb[:, 0:4], in_=dfw_d[l].partition_broadcast(128)), writes=[R_dfb])
                cx.dma("sp", "cst", lambda e: e.dma_start(out=dfb[:, 4:8], in_=dbw_d[l].partition_broadcast(128)), writes=[R_dfb])
                cx.op("act", lambda e: e.activation(out=lg[:], in_=dfb[:], func=AF.Exp, scale=-1.0), reads=[R_dfb], writes=[R_lg])
                cx.op("act", lambda e: e.activation(out=lg[:], in_=lg[:], func=AF.Ln, bias=1.0), reads=[R_lg], writes=[R_lg])
                cx.op("dve", lambda e: e.tensor_scalar(out=lg[:], in0=lg[:], scalar1=-1.0, scalar2=None, op0=ALU.mult),
                      reads=[R_lg], writes=[R_lg])
                for kind in range(4):
                    lo = 0 if kind in (0, 2) else 4
                    cx.op("act", lambda e, kind=kind, lo=lo: e.activation(
                        out=dsc[:, kind, :], in_=lg[:, lo:lo + 4], func=AF.Exp, scale=ctau[:, kind:kind + 1]),
                        reads=[R_lg, R_const], writes=[R_dsc])
                for kind in range(4):
                    for h in range(4):
                        cx.op("dve", lambda e, kind=kind, h=h: e.tensor_scalar(
                            out=TAB[:, kind, h, :], in0=onesf[:, 0:64], scalar1=dsc[:, kind, h:h + 1],
                            scalar2=(0.125 if kind < 2 else 1.0), op0=ALU.mult, op1=ALU.mult),
                            reads=[R_dsc, R_const], writes=[R_tab])
                for h in range(4):
                    cx.op("act", lambda e, h=h: e.activation(out=e1[:], in_=cret[:, 0, :], func=AF.Exp, scale=lg[:, h:h + 1]),
                          reads=[R_lg, R_const], writes=[R_e1])
                    cx.op("dve", lambda e: e.tensor_tensor(out=e1[:], in0=e1[:], in1=cret[:, 1, :], op=ALU.mult),
                          reads=[R_e1, R_const], writes=[R_e1])
                    cx.op("act", lambda e, h=h: e.activation(out=e2[:], in_=cret[:, 2, :], func=AF.Exp, scale=lg[:, 4 + h:5 + h]),
                          reads=[R_lg, R_const], writes=[R_e2])
                    cx.op("dve", lambda e: e.tensor_tensor(out=e2[:], in0=e2[:], in1=cret[:, 3, :], op=ALU.mult),
                          reads=[R_e2, R_const], writes=[R_e2])
                    cx.op("dve", lambda e, h=h: e.tensor_tensor(out=DTt[:, h, :], in0=e1[:], in1=e2[:], op=ALU.add),
                          reads=[R_e1, R_e2], writes=[R_tab])
                for d_ in range(2):
                    for p in range(2):
                        for hh in range(2):
                            rows = slice(hh * 64, hh * 64 + 64)
                            col = d_ * 4 + 2 * p + hh
                            cx.op("act", lambda e, d_=d_, p=p, rows=rows, col=col: e.activation(
                                out=Gfb[rows, d_, p:p + 1], in_=lg[rows, col:col + 1], func=AF.Exp, scale=128.0),
                                reads=[R_lg], writes=[R_tab])
                cx.barrier()

        R_wser = Res("wser")

        def load_w(slot, src_ap, ncols):
            if len(src_ap.shape) == 3:
                view = wsl[slot][:, :, 0:ncols]
                cx.dma("pool", "w%d" % slot, lambda e: e.dma_start(out=view, in_=src_ap), writes=[R_w[slot], R_wser])
            else:
                nr = src_ap.shape[2]
                w_ = src_ap.shape[3]
                for r in range(nr):
                    view = wsl[slot][:, :, r * w_:(r + 1) * w_]
                    cx.dma("pool", "w%d" % slot, lambda e, view=view, r=r: e.dma_start(out=view, in_=src_ap[:, :, r, :]),
                           writes=[R_w[slot], R_wser])

        def rope(src_psum, R_src, t, out_ap, R_out, tmp1, tmp2, R_t1, R_t2):
            xv = src_psum.rearrange("p (h d) -> p h d", d=64)
            cb = _ap(rope_t[:, 0, t, :], [[0, 4], [1, 64]])
            s_lo = _ap(rope_t[:, 1, t, 0:32], [[0, 4], [1, 32]])
            s_hi = _ap(rope_t[:, 1, t, 32:64], [[0, 4], [1, 32]])
            t1v = tmp1.rearrange("p (h d) -> p h d", d=64)
            t2v = tmp2.rearrange("p (h d) -> p h d", d=64)
            cx.op("dve", lambda e: e.tensor_tensor(out=t1v, in0=xv, in1=cb, op=ALU.mult),
                  reads=[R_src, R_const], writes=[R_t1])
            cx.op("dve", [lambda e: e.tensor_tensor(out=t2v[:, :, 0:32], in0=xv[:, :, 32:64], in1=s_lo, op=ALU.mult),
                          lambda e: e.tensor_tensor(out=t2v[:, :, 32:64], in0=xv[:, :, 0:32], in1=s_hi, op=ALU.mult)],
                  reads=[R_src, R_const], writes=[R_t2])
            cx.op("dve", lambda e: e.tensor_tensor(out=out_ap, in0=tmp1, in1=tmp2, op=ALU.add),
                  reads=[R_t1, R_t2], writes=[R_out])

        def proj_tok(t, wslot, c0, ncols):
            a = nxt("mm")
            fns = [lambda e, kc=kc: e.matmul(mm[a][:, 0:ncols], lhsT=hT[:, kc, t * 128:(t + 1) * 128],
                                              rhs=wsl[wslot][:, kc, c0:c0 + ncols], start=(kc == 0), stop=(kc == 7))
                   for kc in range(8)]
            cx.op("pe", fns, reads=[R_hT[t // 4], R_w[wslot]], writes=[R_mm[a]])
            return a

        def proj_feat(g4, wslot, c0):
            a = nxt("mm")
            fns = [lambda e, kc=kc: e.matmul(mm[a][:, :], lhsT=wsl[wslot][:, kc, c0:c0 + 128],
                                              rhs=hT[:, kc, g4 * 512:(g4 + 1) * 512], start=(kc == 0), stop=(kc == 7))
                   for kc in range(8)]
            cx.op("pe", fns, reads=[R_hT[g4], R_w[wslot]], writes=[R_mm[a]])
            return a

        win_v = [win_d[l].rearrange("(k p) n -> p k n", p=128) for l in range(DEPTH)]
        win_a = [win_d[l].rearrange("(k p) (r c) -> p k r c", p=128, r=7) for l in range(DEPTH)]
        wout_v = [wout_d[l].rearrange("(k p) n -> p k n", p=128) for l in range(DEPTH)]

        def chk(name):
            if stop == name:
                raise _Stop()

        try:
          chk("M")
          for s in range(nseq):
            for l in range(depth):
                xsrc = x_d if l == 0 else y_d
                DTt, TAB, Gfb = DTt_all[:, l], TAB_all[:, l], Gfb_all[:, l]
                last = (l == depth - 1)
                if not (s == 0 and l == 0):
                    cx.barrier()
                    cx.new_epoch()
                load_w(0, win_v[l][:, :, 2048:2560], 512)
                if s == 0 and l == 0:
                    load_w(1, win_v[l][:, :, 2560:3072], 512)
                load_w(2, win_v[l][:, :, 3072:3584], 512)

                chk("T")

                with ExitStack() as ph:
                  if l == 0:
                      P = ph.enter_context
                      xin = [PT(ph, "xin%d" % i, [128, D], F32) for i in range(4)]
                      xn = [PT(ph, "xn%d" % i, [128, 4, D], BF16) for i in range(2)]
                      junk = PT(ph, "junk", [128, D], BF16)
                      stat = PT(ph, "stat", [128, 3, NT], F32)
                      R_xin = [Res(), Res(), Res(), Res()]
                      R_xn = [Res(), Res()]
                      R_junk = Res()
                      R_stat = [Res() for _ in range(NT)]
                      def st_X(g4):
                          xb = xn[g4 % 2]
                          for j in range(4):
                              t = 4 * g4 + j
                              xi = xin[t % 4]
                              cx.dma("sp", "xin%d" % (t % 4), lambda e, t=t, xi=xi: e.dma_start(out=xi[:], in_=xsrc[s, t * 128:(t + 1) * 128, :]),
                                     reads=[R_yd[s][t]], writes=[R_xin[t % 4]])
                              cx.op("pool", lambda e, t=t: e.memset(stat[:, 0, t:t + 1], 0.0), writes=[R_stat[t]])
                              cx.op("act", lambda e, t=t, xi=xi: e.activation(out=junk[:], in_=xi[:], func=AF.Square,
                                                                             accum_out=stat[:, 0, t:t + 1]),
                                    reads=[R_xin[t % 4]], writes=[R_junk, R_stat[t]])
                              cx.op("act", lambda e, t=t: e.activation(out=stat[:, 1, t:t + 1], in_=stat[:, 0, t:t + 1], func=AF.Sqrt,
                                                                       scale=1.0 / D, bias=EPS),
                                    reads=[R_stat[t]], writes=[R_stat[t]])
                              cx.op("dve", lambda e, t=t: e.reciprocal(out=stat[:, 2, t:t + 1], in_=stat[:, 1, t:t + 1]),
                                    reads=[R_stat[t]], writes=[R_stat[t]])
                              cx.op("dve", lambda e, t=t, xi=xi, j=j, xb=xb: e.tensor_scalar(
                                  out=xb[:, j, :], in0=xi[:], scalar1=stat[:, 2, t:t + 1], scalar2=None, op0=ALU.mult),
                                  reads=[R_xin[t % 4], R_stat[t]], writes=[R_xn[g4 % 2]])

                      def st_H(g4):
                          xb = xn[g4 % 2]
                          for k in range(8):
                              b = nxt("tp")
                              fns = [lambda e, j=j, k=k, b=b, xb=xb: e.transpose(tp[b][:, j * 128:(j + 1) * 128],
                                                                                xb[:, j, k * 128:(k + 1) * 128], ident[:])
                                     for j in range(4)]
                              cx.op("pe", fns, reads=[R_xn[g4 % 2], R_const], writes=[R_tp[b]])
                              cx.op("act", lambda e, k=k, b=b, g4=g4: e.activation(
                                  out=hT[:, k, g4 * 512:(g4 + 1) * 512], in_=tp[b][:, 0:512], func=AF.Identity,
                                  scale=gsA[:, l, s, k:k + 1], bias=shA[:, l, s, k:k + 1]),
                                  reads=[R_tp[b], R_mod], writes=[R_hT[g4]])

                      _pipeline(4, [(0, st_X), (1, st_H)])
                      cx.barrier()
                if s == 0 and l == 0:
                    dump("hT", hT[:, 0, :], [R_hT[0], R_hT[1], R_hT[2], R_hT[3]])
                chk("N")

                with ExitStack() as ph:
                    P = ph.enter_context
                    kbT = PT(ph, "kbT", [128, 2, S], BF16)
                    vb = PT(ph, "vb", [128, NT, 512], BF16)
                    kdb = PT(ph, "kdb", [128, NT, 256], BF16)
                    SfB = PT(ph, "SfB", [128, NT, 2, 128], BF16)
                    SbB = PT(ph, "SbB", [128, NT, 2, 128], BF16)
                    Sst = PT(ph, "Sst", [128, 2, 2, 128], F32)
                    rt1 = PT(ph, "rt1", [128, 256], F32)
                    rt2 = PT(ph, "rt2", [128, 256], F32)
                    kr = [PT(ph, "kr%d" % i, [128, 256], F32) for i in range(2)]
                    kbf = [PT(ph, "kbf%d" % i, [128, 256], BF16) for i in range(2)]
                    kdf = [PT(ph, "kdf%d" % i, [128, 256], BF16) for i in range(2)]
                    q3 = [PT(ph, "q3_%d" % i, [128, 3, 256], BF16) for i in range(2)]
                    qT3 = [PT(ph, "qT3_%d" % i, [128, 6, 128], BF16) for i in range(2)]
                    gsl = [PT(ph, "gsl%d" % i, [128, 512], BF16) for i in range(2)]
                    inT = [PT(ph, "inT%d" % i, [128, 512], BF16) for i in range(2)]
                    yr = [PT(ph, "yr%d" % i, [128, 512], BF16) for i in range(2)]
                    junk2 = PT(ph, "junk2", [128, 128], BF16)
                    gst = [PT(ph, "gst%d" % i, [128, 3, 4], F32) for i in range(2)]
                    R_kbT = [Res() for _ in range(NT)]
                    R_vb = [Res() for _ in range(NT)]
                    R_kdb = [Res() for _ in range(NT)]
                    R_SfB = [Res() for _ in range(NT)]
                    R_SbB = [Res() for _ in range(NT)]
                    R_Sst = [Res(), Res()]
                    R_rt1, R_rt2, R_junk2 = Res(), Res(), Res()
                    R_kr, R_kbf, R_kdf, R_q3, R_qT3, R_gsl, R_inT, R_yr, R_gst = (
                        [Res(), Res()] for _ in range(9))
                    TABv = lambda kind: TAB[:, kind, :, :].rearrange("p h d -> p (h d)")
                    R_q3a, R_q3b, R_q3c = ([Res(), Res()] for _ in range(3))
                    cx.op("pool", lambda e: e.memset(Sst[:], 0.0), writes=R_Sst)

                    def scan_update(d_, a):
                        fns = []
                        for p in range(2):
                            for hh in range(2):
                                rows = slice(hh * 64, hh * 64 + 64)
                                fns.append(lambda e, p=p, hh=hh, rows=rows: e.scalar_tensor_tensor(
                                    out=Sst[rows, d_, p, :], in0=Sst[rows, d_, p, :], scalar=Gfb[rows, d_, p:p + 1],
                                    in1=ov[a][rows, p * 256 + hh * 128:p * 256 + hh * 128 + 128], op0=ALU.mult, op1=ALU.add))
                        return fns

                    def b1_P(n):
                        i2 = n % 2
                        a = proj_tok(n, 0, 256, 256)
                        rope(mm[a][:, 0:256], R_mm[a], n, kr[i2][:], R_kr[i2], rt1[:], rt2[:], R_rt1, R_rt2)
                        cx.op("act", lambda e, i2=i2: e.activation(out=kbf[i2][:], in_=kr[i2][:], func=AF.Copy, scale=0.125),
                              reads=[R_kr[i2]], writes=[R_kbf[i2]])
                        cx.op("pool", lambda e, i2=i2: e.tensor_tensor(out=kdf[i2][:], in0=kr[i2][:], in1=TABv(0), op=ALU.mult),
                              reads=[R_kr[i2], R_tab], writes=[R_kdf[i2]])
                        cx.op("dve", lambda e, i2=i2, n=n: e.tensor_tensor(out=kdb[:, n, :], in0=kr[i2][:], in1=TABv(1), op=ALU.mult),
                              reads=[R_kr[i2], R_tab], writes=[R_kdb[n]])
                        a2 = proj_tok(n, 1, 0, 512)
                        cx.op("act", lambda e, a2=a2, n=n: e.activation(out=vb[:, n, :], in_=mm[a2][:, :], func=AF.Copy),
                              reads=[R_mm[a2]], writes=[R_vb[n]])

                    def b1_T(n):
                        i2 = n % 2
                        b = nxt("tp")
                        cx.op("pe", [lambda e, p=p, b=b, i2=i2: e.transpose(tp[b][:, p * 128:(p + 1) * 128], kbf[i2][:, p * 128:(p + 1) * 128], ident[:])
                                     for p in range(2)], reads=[R_kbf[i2], R_const], writes=[R_tp[b]])
                        cx.op("act", lambda e, b=b, n=n: e.activation(
                            out=kbT[:, :, n * 128:(n + 1) * 128], in_=tp[b][:, 0:256].rearrange("p (a c) -> p a c", a=2), func=AF.Copy),
                            reads=[R_tp[b]], writes=[R_kbT[n]])
                        cx.op("dve", lambda e, n=n: e.tensor_copy(out=SfB[:, n, :, :], in_=Sst[:, 0, :, :]),
                              reads=[R_Sst[0]], writes=[R_SfB[n]])
                        if n < NT - 1:
                            o = nxt("ov")
                            cx.op("pe", [lambda e, p=p, o=o, i2=i2, n=n: e.matmul(
                                ov[o][:, p * 256:(p + 1) * 256], lhsT=kdf[i2][:, p * 128:(p + 1) * 128],
                                rhs=vb[:, n, p * 256:(p + 1) * 256], start=True, stop=True) for p in range(2)],
                                reads=[R_kdf[i2], R_vb[n]], writes=[R_ov[o]])
                            cx.op("dve", scan_update(0, o), reads=[R_ov[o], R_tab, R_Sst[0]], writes=[R_Sst[0]])

                    _pipeline(NT, [(0, b1_P), (1, b1_T)])
                    b_stage = {"B1": 1, "Bb": 2}.get(stop, 3)
                    for n in (range(NT - 1, -1, -1) if b_stage >= 2 else []):
                        cx.op("dve", lambda e, n=n: e.tensor_copy(out=SbB[:, n, :, :], in_=Sst[:, 1, :, :]),
                              reads=[R_Sst[1]], writes=[R_SbB[n]])
                        if n > 0:
                            o = nxt("ov")
                            cx.op("pe", [lambda e, p=p, o=o, n=n: e.matmul(
                                ov[o][:, p * 256:(p + 1) * 256], lhsT=kdb[:, n, p * 128:(p + 1) * 128],
                                rhs=vb[:, n, p * 256:(p + 1) * 256], start=True, stop=True) for p in range(2)],
                                reads=[R_kdb[n], R_vb[n]], writes=[R_ov[o]])
                            cx.op("dve", scan_update(1, o), reads=[R_ov[o], R_tab, R_Sst[1]], writes=[R_Sst[1]])
                    def b2_P(n):
                        i2 = n % 2
                        a = proj_tok(n, 0, 0, 256)
                        rope(mm[a][:, 0:256], R_mm[a], n, kr[i2][:], R_kr[i2], rt1[:], rt2[:], R_rt1, R_rt2)
                        cx.op("act", lambda e, i2=i2: e.activation(out=q3[i2][:, 0, :], in_=kr[i2][:], func=AF.Copy),
                              reads=[R_kr[i2]], writes=[R_q3a[i2]])
                        cx.op("pool", lambda e, i2=i2: e.tensor_tensor(out=q3[i2][:, 1, :], in0=kr[i2][:], in1=TABv(2), op=ALU.mult),
                              reads=[R_kr[i2], R_tab], writes=[R_q3b[i2]])
                        cx.op("dve", lambda e, i2=i2: e.tensor_tensor(out=q3[i2][:, 2, :], in0=kr[i2][:], in1=TABv(3), op=ALU.mult),
                              reads=[R_kr[i2], R_tab], writes=[R_q3c[i2]])

                    def b2_T(n):
                        i2 = n % 2
                        b = nxt("tp")
                        cx.op("pe", [lambda e, v=v, p=p, b=b, i2=i2: e.transpose(
                            tp[b][:, (v * 2 + p) * 128:(v * 2 + p + 1) * 128], q3[i2][:, v, p * 128:(p + 1) * 128], ident[:])
                            for v in range(3) for p in range(2)], reads=[R_q3a[i2], R_q3b[i2], R_q3c[i2], R_const], writes=[R_tp[b]])
                        cx.op("act", lambda e, b=b, i2=i2: e.activation(
                            out=qT3[i2][:], in_=tp[b][:, 0:768].rearrange("p (a c) -> p a c", a=6), func=AF.Copy),
                            reads=[R_tp[b]], writes=[R_qT3[i2]])
                        fns = []
                        for h in range(4):
                            p, hh = h // 2, h % 2
                            rows = slice(hh * 64, hh * 64 + 64)
                            fns.append(lambda e, p=p, hh=hh, rows=rows, i2=i2, n=n: e.matmul(
                                st[hh][:, p * 128:(p + 1) * 128], lhsT=kbT[rows, p, n * 128:(n + 1) * 128],
                                rhs=qT3[i2][rows, p, :], start=True, stop=True))
                        cx.op("pe", fns, reads=[R_kbT[n], R_qT3[i2]], writes=[R_st[0], R_st[1]])
                        inv = inT[i2][:].rearrange("p (a b t) -> p a b t", a=2, b=2)
                        for hh in range(2):
                            cx.op("dve", lambda e, hh=hh, inv=inv: e.tensor_tensor(
                                out=inv[:, :, hh, :], in0=st[hh][:, 0:256].rearrange("p (a t) -> p a t", a=2),
                                in1=DTt[:].rearrange("p (a b) t -> p a b t", b=2)[:, :, hh, :], op=ALU.mult),
                                reads=[R_st[hh], R_tab], writes=[R_inT[i2]])
                        a2 = proj_tok(n, 2, 0, 512)
                        cx.op("act", lambda e, a2=a2, i2=i2: e.activation(out=gsl[i2][:], in_=mm[a2][:, :], func=AF.Silu),
                              reads=[R_mm[a2]], writes=[R_gsl[i2]])

                    def b2_Y(n):
                        i2 = n % 2
                        o = nxt("ov")
                        fns = []
                        for h in range(4):
                            p, hh = h // 2, h % 2
                            rows = slice(hh * 64, hh * 64 + 64)
                            oc = slice(h * 128, (h + 1) * 128)
                            fns.append(lambda e, oc=oc, o=o, i2=i2, n=n: e.matmul(
                                ov[o][:, oc], lhsT=inT[i2][:, oc], rhs=vb[:, n, oc], start=True, stop=False))
                            fns.append(lambda e, oc=oc, o=o, i2=i2, n=n, p=p, rows=rows: e.matmul(
                                ov[o][:, oc], lhsT=qT3[i2][rows, 2 + p, :], rhs=SfB[rows, n, p, :], start=False, stop=False))
                            fns.append(lambda e, oc=oc, o=o, i2=i2, n=n, p=p, rows=rows: e.matmul(
                                ov[o][:, oc], lhsT=qT3[i2][rows, 4 + p, :], rhs=SbB[rows, n, p, :], start=False, stop=True))
                        cx.op("pe", fns, reads=[R_inT[i2], R_vb[n], R_qT3[i2], R_SfB[n], R_SbB[n]], writes=[R_ov[o]])
                        cx.op("pool", lambda e, i2=i2: e.memset(gst[i2][:, 0, :], 0.0), writes=[R_gst[i2]])
                        cx.op("act", [lambda e, h=h, o=o, i2=i2: e.activation(
                            out=junk2[:], in_=ov[o][:, h * 128:(h + 1) * 128], func=AF.Square, accum_out=gst[i2][:, 0, h:h + 1])
                            for h in range(4)], reads=[R_ov[o]], writes=[R_junk2, R_gst[i2]])
                        cx.op("dve", lambda e, i2=i2: e.tensor_scalar(out=gst[i2][:, 1, :], in0=gst[i2][:, 0, :], scalar1=1.0 / 128,
                                                                      scalar2=EPS, op0=ALU.mult, op1=ALU.add),
                              reads=[R_gst[i2]], writes=[R_gst[i2]])
                        cx.op("pool", lambda e, i2=i2: e.tensor_tensor(out=gst[i2][:, 2, :], in0=gst[i2][:, 1, :], in1=mhalf[:, 0:4], op=ALU.pow),
                              reads=[R_gst[i2], R_const], writes=[R_gst[i2]])
                        cx.op("dve", [lambda e, h=h, o=o, i2=i2: e.scalar_tensor_tensor(
                            out=yr[i2][:, h * 128:(h + 1) * 128], in0=ov[o][:, h * 128:(h + 1) * 128],
                            scalar=gst[i2][:, 2, h:h + 1], in1=gsl[i2][:, h * 128:(h + 1) * 128], op0=ALU.mult, op1=ALU.mult)
                            for h in range(4)], reads=[R_ov[o], R_gst[i2], R_gsl[i2]], writes=[R_yr[i2]])

                    def b2_Z(n):
                        i2 = n % 2
                        b = nxt("tp")
                        cx.op("pe", [lambda e, h=h, b=b, i2=i2: e.transpose(tp[b][:, h * 128:(h + 1) * 128], yr[i2][:, h * 128:(h + 1) * 128], ident[:])
                                     for h in range(4)], reads=[R_yr[i2], R_const], writes=[R_tp[b]])
                        cx.op("act", lambda e, b=b, n=n: e.activation(
                            out=yT[:, 4:8, n * 128:(n + 1) * 128], in_=tp[b][:, 0:512].rearrange("p (a c) -> p a c", a=4), func=AF.Copy),
                            reads=[R_tp[b]], writes=R_yT[4:8])

                    if b_stage >= 3:
                        load_w(1, win_a[l][:, :, 0:4, 0:128], 512)
                        _pipeline(NT, [(0, b2_P), (1, b2_T), (2, b2_Y), (3, b2_Z)])
                    cx.barrier()
                if s == 0 and l == 0:
                    dump("yrT", yT[:, 4, :], R_yT[4:8])
                if stop in ("B1", "Bb", "B2a", "B2b", "B2c", "B2d"):
                    raise _Stop()
                chk("B")

                with ExitStack() as ph:
                    P = ph.enter_context
                    qT = PT(ph, "qT", [128, S], BF16)
                    gaT2 = [PT(ph, "gaT%d" % i, [128, S], BF16) for i in range(2)]
                    Vp = [PT(ph, "Vp%d" % i, [128, 20, 2, 128], BF16) for i in range(2)]
                    ACC = [PT(ph, "ACC%d" % i, [128, S], F32) for i in range(2)]
                    rt1 = PT(ph, "art1", [128, 256], F32)
                    rt2 = PT(ph, "art2", [128, 256], F32)
                    qkr = [PT(ph, "qkr%d" % i, [128, 256], BF16) for i in range(2)]
                    pt = [PT(ph, "pt%d" % i, [128, 512], BF16) for i in range(4)]
                    Rr = [PT(ph, "Rr%d" % i, [128, 512], F32) for i in range(2)]
                    Tm = [PT(ph, "Tm%d" % i, [128, 512], F32) for i in range(2)]
                    R_qT = [Res() for _ in range(NT)]
                    R_gaT2 = [[Res() for _ in range(4)] for _ in range(2)]
                    R_Vp = [Res(), Res()]
                    R_ACC = [Res(), Res()]
                    R_rt1, R_rt2 = Res(), Res()
                    R_qkr, R_pt, R_Rr, R_Tm = ([Res(), Res(), Res(), Res()] for _ in range(4))
                    mone = PT(ph, "mone", [128, 512], F32)
                    R_mone = Res()
                    cx.op("pool", lambda e: e.memset(mone[:], -1.0), writes=[R_mone])
                    sbank = [st[0], st[1], mm[0], mm[1]]
                    R_sbank = [R_st[0], R_st[1], R_mm[0], R_mm[1]]
                    def vp_init():
                        for i in range(2):
                            vflat = Vp[i][:].rearrange("p a b c -> p (a b) c")
                            for q_ in range(5):
                                cx.op("act", lambda e, i=i, q_=q_, vflat=vflat: e.activation(
                                    out=vflat[:, q_ * 8:(q_ + 1) * 8, :], in_=_ap(onesf[:, 0:128], [[0, 8], [1, 128]]), func=AF.Copy),
                                    reads=[R_const], writes=[R_Vp[i]])
                    if os.environ.get("DUMMY_INIT"):
                        for q_ in range(10):
                            cx.op("act", lambda e, q_=q_: e.activation(out=ACC[0][:, q_ * 128:(q_ + 1) * 128], in_=onesf[:, 0:128], func=AF.Copy),
                                  reads=[R_const], writes=[R_ACC[0]])
                    elif not os.environ.get("VPM_LATE") and not os.environ.get("SKIP_VPM"):
                        vp_init()
                    vpc = [0]
                    fin_pend = []
                    for j in range(4):
                        wslot = [1, 0, 2, 1][j]
                        gaT, R_gaT = gaT2[j % 2], R_gaT2[j % 2]
                        def a1_P(t, wslot=wslot, gaT=gaT, R_gaT=R_gaT):
                            i2 = t % 2
                            a = proj_tok(t, wslot, 0, 256)
                            rope(mm[a][:, 0:256], R_mm[a], t, qkr[i2][:], R_qkr[i2], rt1[:], rt2[:], R_rt1, R_rt2)
                            if t % 4 == 3:
                                g4 = t // 4
                                a = proj_feat(g4, wslot, 256)
                                cx.op("act", lambda e, a=a, g4=g4: e.activation(out=VT1[:, 64 + g4 * 512:64 + (g4 + 1) * 512], in_=mm[a][:, :], func=AF.Copy),
                                      reads=[R_mm[a]], writes=[R_VT1])
                                cx.op("act", lambda e, a=a, g4=g4: e.activation(
                                    out=VT2[:, :, 64 + g4 * 128:64 + (g4 + 1) * 128],
                                    in_=mm[a][:, :].rearrange("p (l r) -> p r l", r=4), func=AF.Copy),
                                    reads=[R_mm[a]], writes=[R_VT2])
                                a = proj_feat(g4, wslot, 384)
                                cx.op("act", lambda e, a=a, g4=g4: e.activation(out=gaT[:, g4 * 512:(g4 + 1) * 512], in_=mm[a][:, :], func=AF.Silu),
                                      reads=[R_mm[a]], writes=[R_gaT[g4]])

                        def a1_T(t):
                            i2 = t % 2
                            b = nxt("tp")
                            cx.op("pe", [lambda e, p=p, b=b, i2=i2: e.transpose(tp[b][:, p * 128:(p + 1) * 128], qkr[i2][:, p * 128:(p + 1) * 128], ident[:])
                                         for p in range(2)], reads=[R_qkr[i2], R_const], writes=[R_tp[b]])
                            cx.op("act", lambda e, b=b, t=t: e.activation(out=qT[:, t * 128:(t + 1) * 128], in_=tp[b][:, 0:128], func=AF.Copy),
                                  reads=[R_tp[b]], writes=[R_qT[t]])
                            cx.op("act", lambda e, b=b, t=t: e.activation(out=kT1[:, 64 + t * 128:64 + (t + 1) * 128], in_=tp[b][:, 128:256], func=AF.Copy),
                                  reads=[R_tp[b]], writes=[R_kT1])
                            cx.op("act", lambda e, b=b, t=t: e.activation(
                                out=kT2[:, :, 64 + t * 32:64 + (t + 1) * 32],
                                in_=tp[b][:, 128:256].rearrange("p (l r) -> p r l", r=4), func=AF.Copy),
                                reads=[R_tp[b]], writes=[R_kT2])

                        _pipeline(NT, [(0, a1_P), (1, a1_T)])
                        while fin_pend:
                            fin_pend.pop(0)()
                        if j == 0:
                            load_w(0, win_a[l][:, :, 0:4, 128:256], 512)
                            load_w(2, win_a[l][:, :, 0:4, 256:384], 512)
                        elif j == 1:
                            load_w(1, win_a[l][:, :, 0:4, 384:512], 512)
                        elif j == 2:
                            load_w(0, wout_v[l][:, :, 0:512], 512)
                        else:
                            load_w(2, wout_v[l][:, :, 512:1024], 512)
                            nl, ns = (l + 1, s) if l + 1 < depth else (0, s + 1)
                            if ns < nseq:
                                load_w(1, win_v[nl][:, :, 2560:3072], 512)

                        if os.environ.get("VPM_LATE") and j == 0:
                            vp_init()
                        def build_vp(vi, srcs, R_src):
                            for g0 in range(0, len(srcs), 4):
                                grp = srcs[g0:g0 + 4]
                                b = nxt("tp")
                                cx.op("pe", [lambda e, ii=ii, sa_=sa_, b=b: e.transpose(tp[b][:, ii * 128:(ii + 1) * 128], sa_, ident[:])
                                             for ii, (_, sa_) in enumerate(grp)], reads=[R_src, R_const], writes=[R_tp[b]])
                                i0 = grp[0][0]
                                ng = len(grp)
                                for hh in range(2):
                                    eng = "act"
                                    src = tp[b][:, 0:ng * 128].rearrange("p (a c) -> p a c", c=128)[:, :, hh * 64:hh * 64 + 64]
                                    dst = Vp[vi][:, i0:i0 + ng, hh, hh * 64:hh * 64 + 64]
                                    if eng == "act":
                                        cx.op("act", lambda e, src=src, dst=dst: e.activation(out=dst, in_=src, func=AF.Copy),
                                              reads=[R_tp[b]], writes=[R_Vp[vi]])
                                    else:
                                        cx.op("dve", lambda e, src=src, dst=dst: e.tensor_copy(out=dst, in_=src),
                                              reads=[R_tp[b]], writes=[R_Vp[vi]])

                        pend = []

                        def flush_pend():
                            while pend:
                                pend.pop(0)()

                        tpf = [tp[0][:].bitcast(F32), tp[1][:].bitcast(F32)]
                        obank = [[ov[0][:, :], tpf[0]], [ov[1][:, :], tpf[1]]]
                        R_obank = [[R_ov[0], R_tp[0]], [R_ov[1], R_tp[1]]]

                        def attend2(vi, items2, mask_variant, R_k, acc2):
                            fc = ctr["ov"]
                            ctr["ov"] += 1
                            nk = len(items2[0][0][1])
                            per = 512 // (128 * nk)
                            ngrp = 4 // per
                            for gi, g0 in enumerate(range(0, 4, per)):
                                base = (ctr["w"] % 2) * 2
                                ctr["w"] += 1
                                mv = mask_variant(g0)
                                fns = [lambda e, sa=base + hh, mv=mv: e.matmul(sbank[sa][:, :], lhsT=ident[:], rhs=maskA[:, mv, :],
                                                                               start=True, stop=False) for hh in range(2)]
                                order = [(ii, kk, hh) for ii in range(per) for kk in range(nk) for hh in range(2)]
                                if os.environ.get("NO_ILV"):
                                    order = [(ii, kk, hh) for hh in range(2) for ii in range(per) for kk in range(nk)]
                                for (ii, kk, hh) in order:
                                    if True:
                                        c0 = (ii * nk + kk) * 128
                                        if True:
                                            q_ap, ks = items2[hh][g0 + ii]
                                            k_ap = ks[kk][0]
                                            fns.append(lambda e, c0=c0, sa=base + hh, k_ap=k_ap, q_ap=q_ap: e.matmul(
                                                sbank[sa][:, c0:c0 + 128], lhsT=k_ap, rhs=q_ap, start=False, stop=True))
                                cx.op("pe", fns, reads=[R_k, R_const] + [R_qT[t] for t in range(NT)], writes=[R_sbank[base], R_sbank[base + 1]])
                                for hh in range(2):
                                    pi = base + hh
                                    cx.op("act", lambda e, pi=pi: e.activation(out=pt[pi][:], in_=sbank[pi][:, :], func=AF.Exp, scale=0.125),
                                          reads=[R_sbank[pi]], writes=[R_pt[pi]])

                                def stage2(g0=g0, gi=gi, base=base):
                                    for hh in range(2):
                                        pi = base + hh
                                        fsel = 0
                                        o_ap, R_o = obank[hh][fsel], R_obank[hh][fsel]
                                        fns = []
                                        for ii in range(per):
                                            _, ks = items2[hh][g0 + ii]
                                            qi = g0 + ii
                                            for kk, (_, vidx) in enumerate(ks):
                                                c0 = (ii * nk + kk) * 128
                                                fns.append(lambda e, c0=c0, qi=qi, vidx=vidx, kk=kk, hh=hh, pi=pi, o_ap=o_ap: e.matmul(
                                                    o_ap[:, qi * 128:(qi + 1) * 128], lhsT=Vp[vi][:, vidx, hh, :], rhs=pt[pi][:, c0:c0 + 128],
                                                    start=(kk == 0), stop=(kk == nk - 1)))
                                        cx.op("pe", fns, reads=[R_pt[pi], R_Vp[vi]], writes=[R_o])
                                        if gi == ngrp - 1:
                                            acc2(hh, o_ap, R_o)

                                pend.append(stage2)
                                while len(pend) > 1:
                                    pend.pop(0)()

                        a_st = {"A1": 0, "Ap0": 1, "Ap1": 2, "Ap2": 3}.get(stop, 9)
                        if a_st < 9 and j > 0:
                            continue
                        hrows = [slice(0, 64), slice(64, 128)]
                        for pat in range(min(3, a_st)):
                            vi = vpc[0] % 2
                            vpc[0] += 1
                            if pat == 0:
                                build_vp(vi, [(jt, VT1[:, 128 * jt:128 * jt + 128]) for jt in range(17)], R_VT1)
                                for u in range(4):
                                    items2 = [[(qT[rows, 128 * i_:128 * i_ + 128],
                                                [(kT1[rows, 128 * i_:128 * i_ + 128], i_),
                                                 (kT1[rows, 128 * (i_ + 1):128 * (i_ + 1) + 128], i_ + 1)])
                                               for i_ in range(4 * u, 4 * u + 4)] for rows in hrows]
                                    mvf = lambda g0, u=u: (0 if (u == 0 and g0 == 0) else (2 if (u == 3 and g0 == 2) else 1))
                                    acc2 = lambda hh, o_ap, R_o, u=u: cx.op("act", lambda e: e.activation(
                                        out=ACC[hh][:, 512 * u:512 * (u + 1)], in_=o_ap, func=AF.Copy),
                                        reads=[R_o], writes=[R_ACC[hh]])
                                    attend2(vi, items2, mvf, R_kT1, acc2)
                            elif pat == 1:
                                build_vp(vi, [(r * 5 + jt, VT2[:, r, 128 * jt:128 * jt + 128]) for r in range(4) for jt in range(5)], R_VT2)
                                for r in range(4):
                                    items2 = [[(qT[rows, 512 * i_ + r:512 * (i_ + 1):4],
                                                [(kT2[rows, r, 128 * i_:128 * i_ + 128], r * 5 + i_),
                                                 (kT2[rows, r, 128 * (i_ + 1):128 * (i_ + 1) + 128], r * 5 + i_ + 1)])
                                               for i_ in range(4)] for rows in hrows]
                                    mvf = lambda g0: (0 if g0 == 0 else 2)
                                    acc2 = lambda hh, o_ap, R_o, r=r: cx.op("dve", lambda e: e.tensor_tensor(
                                        out=ACC[hh][:, r:S:4], in0=o_ap, in1=ACC[hh][:, r:S:4], op=ALU.add),
                                        reads=[R_o, R_ACC[hh]], writes=[R_ACC[hh]])
                                    attend2(vi, items2, mvf, R_kT2, acc2)
                            else:
                                build_vp(vi, [(r, VT1[:, 64 + r:64 + S:16]) for r in range(16)], R_VT1)
                                for r0 in range(0, 16, 4):
                                    items2 = [[(qT[rows, r:S:16], [(kT1[rows, 64 + r:64 + S:16], r)])
                                               for r in range(r0, r0 + 4)] for rows in hrows]
                                    mvf = lambda g0: 3

                                    def acc2(hh, o_ap, R_o, r0=r0):
                                        accv = ACC[hh][:].rearrange("p (l r) -> p r l", r=16)[:, r0:r0 + 4, :]
                                        cx.op("dve", lambda e: e.tensor_tensor(
                                            out=accv, in0=o_ap.rearrange("p (r l) -> p r l", r=4), in1=accv, op=ALU.add),
                                            reads=[R_o, R_ACC[hh]], writes=[R_ACC[hh]])
                                    attend2(vi, items2, mvf, R_kT1, acc2)
                        flush_pend()

                        def finalize(j=j, gaT=gaT, R_gaT=R_gaT):
                            for hh in range(2):
                                nr = slice(hh * 64, hh * 64 + 64)
                                dr = slice((1 - hh) * 64, (1 - hh) * 64 + 64)
                                for u in range(4):
                                    cs = slice(512 * u, 512 * (u + 1))
                                    i2 = u % 2
                                    cx.op("pool", lambda e, nr=nr, dr=dr, cs=cs, i2=i2, hh=hh: e.tensor_tensor(
                                        out=Rr[i2][nr, :], in0=ACC[hh][dr, cs], in1=mone[dr, :], op=ALU.pow),
                                        reads=[R_ACC[hh], R_mone], writes=[R_Rr[i2]])
                                    cx.op("dve", lambda e, nr=nr, cs=cs, i2=i2, hh=hh: e.tensor_tensor(
                                        out=Tm[i2][nr, :], in0=ACC[hh][nr, cs], in1=Rr[i2][nr, :], op=ALU.mult),
                                        reads=[R_ACC[hh], R_Rr[i2]], writes=[R_Tm[i2]])
                                    cx.op("pool", lambda e, nr=nr, cs=cs, i2=i2, j=j: e.tensor_tensor(
                                        out=yT[nr, j, cs], in0=Tm[i2][nr, :], in1=gaT[nr, cs], op=ALU.mult),
                                        reads=[R_Tm[i2], R_gaT[u]], writes=[R_yT[j]])

                        if a_st >= 9:
                            fin_pend.append(finalize)
                    while fin_pend:
                        fin_pend.pop(0)()
                    cx.barrier()
                if s == 0 and l == 0:
                    dump("yaT", yT[:, 0, :], R_yT[0:4])
                if stop in ("A1", "Ap0", "Ap1", "Ap2"):
                    raise _Stop()
                chk("A")

                with ExitStack() as ph:
                    P = ph.enter_context
                    gate_b = PT(ph, "gate_b", [128, D], F32)
                    gfin_b = PT(ph, "gfin_b", [128, D], F32)
                    R_gfin = Res()
                    if last:
                        cx.dma("sp", "cst", lambda e: e.dma_start(out=gfin_b[:], in_=gfin_d.partition_broadcast(128)), writes=[R_gfin])
                    xin = [PT(ph, "oxin%d" % i, [128, D], F32) for i in range(2)]
                    xo = [PT(ph, "xo%d" % i, [128, D], F32) for i in range(2)]
                    Dk = [PT(ph, "Dk%d" % i, [128, 128], F32) for i in range(2)]
                    junk = PT(ph, "ojunk", [128, D], BF16)
                    fst = [PT(ph, "fst%d" % i, [128, 3], F32) for i in range(2)]
                    R_xin, R_xo, R_Dk, R_fst = ([Res(), Res()] for _ in range(4))
                    R_junk = Res()
                    if not last:
                        oxn = [PT(ph, "oxn%d" % i, [128, 4, D], BF16) for i in range(2)]
                        ostat = PT(ph, "ostat", [128, 3, NT], F32)
                        R_oxn = [Res(), Res()]
                        R_ostat = [Res() for _ in range(NT)]
                    h_pend = []

                    def o_H(g4):
                        xb = oxn[g4 % 2]
                        for k in range(8):
                            b = nxt("tp")
                            cx.op("pe", [lambda e, jj=jj, k=k, b=b, xb=xb: e.transpose(tp[b][:, jj * 128:(jj + 1) * 128],
                                                                                   xb[:, jj, k * 128:(k + 1) * 128], ident[:])
                                         for jj in range(4)], reads=[R_oxn[g4 % 2], R_const], writes=[R_tp[b]])
                            cx.op("act", lambda e, k=k, b=b, g4=g4: e.activation(
                                out=hT[:, k, g4 * 512:(g4 + 1) * 512], in_=tp[b][:, 0:512], func=AF.Identity,
                                scale=gsA[:, l + 1, s, k:k + 1], bias=shA[:, l + 1, s, k:k + 1]),
                                reads=[R_tp[b], R_mod], writes=[R_hT[g4]])
                    for k in range(8):
                        i2 = k % 2
                        cx.op("dve", lambda e, k=k, i2=i2: e.tensor_scalar(out=Dk[i2][:], in0=identf[:], scalar1=gtA[:, l, s, k:k + 1],
                                                                         scalar2=None, op0=ALU.mult),
                              reads=[R_const, R_mod], writes=[R_Dk[i2]])
                        a = nxt("mm")
                        cx.op("pe", lambda e, a=a, i2=i2: e.matmul(mm[a][:, 0:128], lhsT=onesf[:], rhs=Dk[i2][:], start=True, stop=True),
                              reads=[R_Dk[i2], R_const], writes=[R_mm[a]])
                        cx.op("act", lambda e, a=a, k=k: e.activation(out=gate_b[:, k * 128:(k + 1) * 128], in_=mm[a][:, 0:128], func=AF.Copy),
                              reads=[R_mm[a]], writes=[R_gate])
                    for t in range(NT):
                        i2 = t % 2
                        cx.dma("sp", "xin%d" % i2, lambda e, t=t, i2=i2: e.dma_start(out=xin[i2][:], in_=xsrc[s, t * 128:(t + 1) * 128, :]),
                               reads=[R_yd[s][t]], writes=[R_xin[i2]])
                        for half in range(2):
                            a = nxt("mm")
                            wslot = 0 if half == 0 else 2
                            hs = slice(half * 512, (half + 1) * 512)
                            cx.op("pe", [lambda e, kc=kc, a=a, wslot=wslot, t=t: e.matmul(
                                mm[a][:, :], lhsT=yT[:, kc, t * 128:(t + 1) * 128], rhs=wsl[wslot][:, kc, :],
                                start=(kc == 0), stop=(kc == 7)) for kc in range(8)],
                                reads=R_yT + [R_w[wslot]], writes=[R_mm[a]])
                            cx.op("dve", lambda e, a=a, hs=hs, i2=i2: e.tensor_tensor(out=xo[i2][:, hs], in0=mm[a][:, :], in1=gate_b[:, hs], op=ALU.mult),
                                  reads=[R_mm[a], R_gate], writes=[R_xo[i2]])
                        cx.op("pool", lambda e, i2=i2: e.tensor_tensor(out=xo[i2][:], in0=xo[i2][:], in1=xin[i2][:], op=ALU.add),
                              reads=[R_xo[i2], R_xin[i2]], writes=[R_xo[i2]])
                        if last:
                            cx.op("pool", lambda e, i2=i2: e.memset(fst[i2][:, 0:1], 0.0), writes=[R_fst[i2]])
                            cx.op("act", lambda e, i2=i2: e.activation(out=junk[:], in_=xo[i2][:], func=AF.Square, accum_out=fst[i2][:, 0:1]),
                                  reads=[R_xo[i2]], writes=[R_junk, R_fst[i2]])
                            cx.op("act", lambda e, i2=i2: e.activation(out=fst[i2][:, 1:2], in_=fst[i2][:, 0:1], func=AF.Sqrt, scale=1.0 / D, bias=EPS),
                                  reads=[R_fst[i2]], writes=[R_fst[i2]])
                            cx.op("dve", lambda e, i2=i2: e.reciprocal(out=fst[i2][:, 2:3], in_=fst[i2][:, 1:2]),
                                  reads=[R_fst[i2]], writes=[R_fst[i2]])
                            cx.op("dve", lambda e, i2=i2: e.scalar_tensor_tensor(
                                out=xo[i2][:], in0=xo[i2][:], scalar=fst[i2][:, 2:3], in1=gfin_b[:], op0=ALU.mult, op1=ALU.mult),
                                reads=[R_xo[i2], R_fst[i2], R_gfin], writes=[R_xo[i2]])
                        cx.dma("sp", "xout%d" % i2, lambda e, t=t, i2=i2: e.dma_start(out=y_d[s, t * 128:(t + 1) * 128, :], in_=xo[i2][:]),
                               reads=[R_xo[i2]], writes=[R_yd[s][t]])
                        if not last:
                            g4, jj = t // 4, t % 4
                            cx.op("pool", lambda e, t=t: e.memset(ostat[:, 0, t:t + 1], 0.0), writes=[R_ostat[t]])
                            cx.op("act", lambda e, t=t, i2=i2: e.activation(out=junk[:], in_=xo[i2][:], func=AF.Square,
                                                                           accum_out=ostat[:, 0, t:t + 1]),
                                  reads=[R_xo[i2]], writes=[R_junk, R_ostat[t]])
                            cx.op("act", lambda e, t=t: e.activation(out=ostat[:, 1, t:t + 1], in_=ostat[:, 0, t:t + 1], func=AF.Sqrt,
                                                                     scale=1.0 / D, bias=EPS),
                                  reads=[R_ostat[t]], writes=[R_ostat[t]])
                            cx.op("dve", lambda e, t=t: e.reciprocal(out=ostat[:, 2, t:t + 1], in_=ostat[:, 1, t:t + 1]),
                                  reads=[R_ostat[t]], writes=[R_ostat[t]])
                            cx.op("dve", lambda e, t=t, i2=i2, jj=jj, g4=g4: e.tensor_scalar(
                                out=oxn[g4 % 2][:, jj, :], in0=xo[i2][:], scalar1=ostat[:, 2, t:t + 1], scalar2=None, op0=ALU.mult),
                                reads=[R_xo[i2], R_ostat[t]], writes=[R_oxn[g4 % 2]])
                            if h_pend and h_pend[0][0] <= t:
                                o_H(h_pend.pop(0)[1])
                            if jj == 3:
                                h_pend.append((t + 2, g4))
                    while h_pend:
                        o_H(h_pend.pop(0)[1])
                    cx.barrier()
        except _Stop:
            pass
        cx.barrier()
    return nc


def _consts():
    f32 = np.float32
    pos = np.arange(S, dtype=np.float32)
    inv = (10000.0 ** (-np.arange(0, 64, 2, dtype=np.float32) / 64)).astype(f32)
    ang = (pos[:, None] * inv[None, :]).astype(f32)
    cos, sin = np.cos(ang).astype(f32), np.sin(ang).astype(f32)
    C2 = np.concatenate([cos, cos], axis=1)
    S2 = np.concatenate([-sin, sin], axis=1)
    rope = np.stack([C2.reshape(NT, 128, 64).transpose(1, 0, 2), S2.reshape(NT, 128, 64).transpose(1, 0, 2)], axis=1)
    p = np.arange(128)[:, None]
    c = np.arange(128)[None, :]
    A = (c <= p).astype(f32)
    B = (p <= c).astype(f32)
    A_first = A * (p >= 64)
    B_last = B * (p < 64)
    m_norm = np.concatenate([A, B], axis=1)
    m_first = np.concatenate([A_first, B], axis=1)
    m_last = np.concatenate([A, B_last], axis=1)
    band = (np.abs(p - c) <= 64).astype(f32)
    mask = np.stack([np.concatenate([m_first, m_norm], 1), np.concatenate([m_norm, m_norm], 1),
                     np.concatenate([m_norm, m_last], 1), np.concatenate([band] * 4, 1)], axis=1)
    diff = (c - p).astype(f32)
    ret = np.stack([np.maximum(diff, 0), (diff >= 0).astype(f32), np.maximum(-diff, 0), (diff < 0).astype(f32)], axis=1)
    tau = np.arange(128, dtype=f32)
    taus = np.stack([127 - tau, tau, tau + 1, 128 - tau], axis=1)
    mask = (mask - 1.0) * 30000.0
    return dict(cst_rope=np.ascontiguousarray(rope, f32), cst_mask=np.ascontiguousarray(mask, f32),
                cst_ret=np.ascontiguousarray(ret, f32), cst_tau=np.ascontiguousarray(taus, f32),
                cst_ident=np.eye(128, dtype=f32))


def kernel(x_prompt, x_sample, c_prompt, c_sample, g_norm, w_ada, b_ada, w_in, w_out,
           decay_fwd, decay_bwd, g_final):
    f = lambda a: np.ascontiguousarray(np.asarray(a), dtype=np.float32)
    xs = np.concatenate([f(x_prompt), f(x_sample)], axis=0)
    cs = np.concatenate([f(c_prompt), f(c_sample)], axis=0)
    shared = dict(g_norm=f(g_norm), w_ada=f(w_ada), b_ada=f(b_ada), w_in=f(w_in), w_out=f(w_out),
                  decay_fwd=f(decay_fwd), decay_bwd=f(decay_bwd), g_final=f(g_final))
    shared.update(_consts())
    nc = build_nc()
    in_maps = []
    for i in range(NCORES):
        m = dict(shared)
        m["x"] = np.ascontiguousarray(xs[i * NSEQ:(i + 1) * NSEQ])
        m["c"] = np.ascontiguousarray(cs[i * NSEQ:(i + 1) * NSEQ])
        in_maps.append(m)
    res = run_bass_kernel_spmd(nc, in_maps, core_ids=list(range(NCORES)))
    ys = np.concatenate([np.asarray(r["y"], dtype=np.float32) for r in res.results], axis=0)
    nb = np.asarray(x_prompt).shape[0]
    return (np.ascontiguousarray(ys[:nb]), np.ascontiguousarray(ys[nb:]))
```

```python
import math
import os
from contextlib import ExitStack

import numpy as np
import concourse.bass as bass
import concourse.mybir as mybir
from concourse.bass_utils import run_bass_kernel_spmd

F32 = mybir.dt.float32
BF16 = mybir.dt.bfloat16
AF = mybir.ActivationFunctionType
ALU = mybir.AluOpType

D = 1024
S = 2048
NT = 16
DEPTH = 2
NCORES = 8
NSEQ = 3
INW = 3584
EPS = 1e-6


class Tok:
    __slots__ = ("eng", "key", "val")

    def __init__(self, eng, key, val):
        self.eng, self.key, self.val = eng, key, val


class Res:
    __slots__ = ("name", "w", "r")

    def __init__(self, name=""):
        self.name, self.w, self.r = name, None, []


class Ctx:
    def __init__(self, nc, es):
        self.nc, self.es = nc, es
        self.engs = {"pe": nc.tensor, "act": nc.scalar, "dve": nc.vector, "pool": nc.gpsimd, "sp": nc.sync}
        self.sems, self.cnt = {}, {}
        self.seen = {e: {} for e in self.engs}
        self.epoch = 0
        self.dma_keys = set()
        self.new_epoch()

    def _mksem(self, key):
        if key not in self.sems:
            self.sems[key] = self.es.enter_context(self.nc.semaphore(key))
            self.cnt[key] = 0

    def new_epoch(self):
        self.epoch += 1
        self.ekey = {e: "%s%d" % (e, self.epoch) for e in self.engs if e != "sp"}
        for k in self.ekey.values():
            self._mksem(k)

    def _waits(self, eng, reads, writes, skipkey=None):
        deps = {}

        def need(tok, kind):
            if tok is None or tok.key == skipkey:
                return
            if tok.eng == eng and (eng == "pe" or kind != "RAW"):
                return
            if deps.get(tok.key, 0) < tok.val:
                deps[tok.key] = tok.val

        for r in reads:
            need(r.w, "RAW")
        for w in writes:
            need(w.w, "WAW")
            for t in w.r:
                need(t, "WAR")
        E, seen = self.engs[eng], self.seen[eng]
        for key, val in deps.items():
            if seen.get(key, 0) < val:
                E.wait_ge(self.sems[key], val)
                seen[key] = val

    def _commit(self, tok, reads, writes):
        for r in reads:
            r.r = [t for t in r.r if t.key != tok.key] + [tok]
        for w in writes:
            w.w, w.r = tok, []

    def op(self, eng, fns, reads=(), writes=()):
        self._waits(eng, reads, writes)
        if not isinstance(fns, (list, tuple)):
            fns = [fns]
        ins = None
        for f in fns:
            ins = f(self.engs[eng])
        key = self.ekey[eng]
        self.cnt[key] += 1
        ins.then_inc(self.sems[key], 1)
        tok = Tok(eng, key, self.cnt[key])
        self._commit(tok, reads, writes)
        return tok

    def dma(self, q, semname, fn, reads=(), writes=()):
        key = "d%d_%s" % (self.epoch, semname)
        self._mksem(key)
        self.dma_keys.add(key)
        self._waits(q, reads, writes, skipkey=key)
        ins = fn(self.engs[q])
        self.cnt[key] += 16
        ins.then_inc(self.sems[key], 16)
        tok = Tok("dma", key, self.cnt[key])
        self._commit(tok, reads, writes)
        return tok

    def barrier(self, with_dma=True):
        keys = list(self.ekey.values())
        if with_dma:
            keys += sorted(self.dma_keys)
        for e, E in self.engs.items():
            seen = self.seen[e]
            for key in keys:
                val = self.cnt[key]
                if val > 0 and seen.get(key, 0) < val:
                    E.wait_ge(self.sems[key], val)
                    seen[key] = val


def _pipeline(n_iter, stages):
    mx = max(sk for sk, _ in stages)
    for i in range(n_iter + mx):
        for sk, fn in stages:
            n = i - sk
            if 0 <= n < n_iter:
                fn(n)


def _ap(base, dims):
    return bass.AP(base.tensor, base.offset, [list(base.ap[0])] + [list(d) for d in dims])


class _Stop(Exception):
    pass


def build_nc(nseq=NSEQ, depth=DEPTH, dbg=(), stop=None):
    nc = bass.Bass("TRN2", target_bir_lowering=False)
    dt = nc.dram_tensor
    x_d = dt("x", [nseq, S, D], F32, kind="ExternalInput").ap()
    c_d = dt("c", [nseq, D], F32, kind="ExternalInput").ap()
    gn_d = dt("g_norm", [DEPTH, D], F32, kind="ExternalInput").ap()
    wada_d = dt("w_ada", [DEPTH, D, 3 * D], F32, kind="ExternalInput").ap()
    bada_d = dt("b_ada", [DEPTH, 3 * D], F32, kind="ExternalInput").ap()
    win_d = dt("w_in", [DEPTH, D, INW], F32, kind="ExternalInput").ap()
    wout_d = dt("w_out", [DEPTH, D, D], F32, kind="ExternalInput").ap()
    dfw_d = dt("decay_fwd", [DEPTH, 4], F32, kind="ExternalInput").ap()
    dbw_d = dt("decay_bwd", [DEPTH, 4], F32, kind="ExternalInput").ap()
    gfin_d = dt("g_final", [D], F32, kind="ExternalInput").ap()
    crope_d = dt("cst_rope", [128, 2, NT, 64], F32, kind="ExternalInput").ap()
    cmask_d = dt("cst_mask", [128, 4, 512], F32, kind="ExternalInput").ap()
    cret_d = dt("cst_ret", [128, 4, 128], F32, kind="ExternalInput").ap()
    ctau_d = dt("cst_tau", [128, 4], F32, kind="ExternalInput").ap()
    cid_d = dt("cst_ident", [128, 128], F32, kind="ExternalInput").ap()
    y_d = dt("y", [nseq, S, D], F32, kind="ExternalOutput").ap()
    dbg_d = {}
    for name, shape in dbg:
        dbg_d[name] = dt("dbg_" + name, list(shape), F32, kind="ExternalOutput").ap()

    with ExitStack() as es:
        E = es.enter_context
        cx = Ctx(nc, es)
        sb = lambda name, shape, dtype: E(nc.sbuf_tensor(name, list(shape), dtype))
        uid = [0]

        def PT(ph, name, shape, dtype):
            uid[0] += 1
            return ph.enter_context(nc.sbuf_tensor("%s_u%d" % (name, uid[0]), list(shape), dtype))

        hT = sb("hT", [128, 8, S], BF16)
        yT = sb("yT", [128, 8, S], BF16)
        wsl = [sb("wsl%d" % i, [128, 8, 512], BF16) for i in range(3)]
        kT1 = sb("kT1", [128, S + 128], BF16)
        kT2 = sb("kT2", [128, 4, 640], BF16)
        VT1 = sb("VT1", [128, S + 128], BF16)
        VT2 = sb("VT2", [128, 4, 640], BF16)
        ident = sb("ident", [128, 128], BF16)
        identf = sb("identf", [128, 128], F32)
        onesf = sb("onesf", [128, 128], F32)
        rope_t = sb("rope_t", [128, 2, NT, 64], F32)
        maskA = sb("maskA", [128, 4, 512], BF16)
        cret = sb("cret", [128, 4, 128], F32)
        ctau = sb("ctau", [128, 4], F32)
        DTt_all = sb("DTt", [128, DEPTH, 4, 128], F32)
        TAB_all = sb("TAB", [128, DEPTH, 4, 4, 64], F32)
        Gfb_all = sb("Gfb", [128, DEPTH, 2, 2], F32)
        gsA = sb("gsA", [128, DEPTH, nseq, 8], F32)
        shA = sb("shA", [128, DEPTH, nseq, 8], F32)
        gtA = sb("gtA", [128, DEPTH, nseq, 8], F32)
        mhalf = sb("mhalf", [128, 16], F32)

        mm = [E(nc.psum_tensor("mm%d" % i, [128, 512], F32)) for i in range(2)]
        tp = [E(nc.psum_tensor("tp%d" % i, [128, 1024], BF16)) for i in range(2)]
        st = [E(nc.psum_tensor("st%d" % i, [128, 512], F32)) for i in range(2)]
        ov = [E(nc.psum_tensor("ov%d" % i, [128, 512], F32)) for i in range(2)]
        R_mm = [Res("mm0"), Res("mm1")]
        R_tp = [Res("tp0"), Res("tp1")]
        R_st = [Res("st0"), Res("st1")]
        R_ov = [Res("ov0"), Res("ov1")]
        ctr = {"mm": 0, "tp": 0, "st": 0, "ov": 0, "w": 0, "fin": 0}

        def nxt(kind):
            i = ctr[kind] % 2
            ctr[kind] += 1
            return i

        R_const = Res("const")
        R_hT = [Res("hT%d" % g) for g in range(4)]
        R_yT = [Res("yT%d" % k) for k in range(8)]
        R_w = [Res("w%d" % i) for i in range(3)]
        R_kT1, R_kT2, R_VT1, R_VT2 = Res("kT1"), Res("kT2"), Res("VT1"), Res("VT2")
        R_tab, R_gate, R_mod = Res("tab"), Res("gate"), Res("mod")
        R_yd = [[Res("yd%d_%d" % (s, t)) for t in range(NT)] for s in range(nseq)]
        dump_list = []

        def dump(name, src_ap, reads):
            if name in dbg_d:
                dump_list.append(cx.dma("pool", "dbg", lambda e: e.dma_start(out=dbg_d[name], in_=src_ap), reads=reads))

        cx.dma("sp", "cst", lambda e: e.dma_start(out=rope_t[:], in_=crope_d), writes=[R_const])
        cx.dma("pool", "cstp", lambda e: e.dma_start(out=maskA[:], in_=cmask_d), writes=[R_const])
        cx.dma("sp", "cst", lambda e: e.dma_start(out=cret[:], in_=cret_d), writes=[R_const])
        cx.dma("sp", "cst", lambda e: e.dma_start(out=ctau[:], in_=ctau_d), writes=[R_const])
        cx.dma("pool", "cstp", lambda e: e.dma_start(out=ident[:], in_=cid_d), writes=[R_const])
        cx.dma("sp", "cst", lambda e: e.dma_start(out=identf[:], in_=cid_d), writes=[R_const])
        cx.op("pool", lambda e: e.memset(onesf[:], 1.0), writes=[R_const])
        cx.op("pool", lambda e: e.memset(mhalf[:], -0.5), writes=[R_const])
        cx.op("pool", lambda e: e.memset(kT1[:], 0.0), writes=[R_kT1])
        cx.op("pool", lambda e: e.memset(kT2[:], 0.0), writes=[R_kT2])
        cx.op("pool", lambda e: e.memset(VT1[:], 0.0), writes=[R_VT1])
        cx.op("pool", lambda e: e.memset(VT2[:], 0.0), writes=[R_VT2])

        with ExitStack() as ph:
            P = ph.enter_context
            wada = PT(ph, "wada", [128, 8, 3 * D], BF16)
            cT = PT(ph, "cT", [128, nseq, 8], F32)
            silc = PT(ph, "silc", [128, 8, nseq], BF16)
            badaT = PT(ph, "badaT", [128, 24], F32)
            gnT = PT(ph, "gnT", [128, 8], F32)
            modT = PT(ph, "modT", [128, 24, nseq], F32)
            R_wada, R_cT, R_silc, R_bada, R_gn, R_modT = (Res() for _ in range(6))
            for s in range(nseq):
                cx.dma("sp", "cst", lambda e, s=s: e.dma_start(
                    out=cT[:, s, :], in_=c_d[s].rearrange("(k p) -> p k", p=128), allow_slow_non_contiguous=True),
                    writes=[R_cT])
            cx.op("act", lambda e: e.activation(out=silc[:].rearrange("p k s -> p s k"), in_=cT[:], func=AF.Silu),
                  reads=[R_cT], writes=[R_silc])
            for l in range(depth):
                for i in range(6):
                    cx.dma("pool", "wada", lambda e, i=i, l=l: e.dma_start(
                        out=wada[:, :, i * 512:(i + 1) * 512],
                        in_=wada_d[l].rearrange("(k p) n -> p k n", p=128)[:, :, i * 512:(i + 1) * 512]),
                        writes=[R_wada])
                cx.dma("sp", "cst", lambda e, l=l: e.dma_start(
                    out=badaT[:], in_=bada_d[l].rearrange("(o p) -> p o", p=128), allow_slow_non_contiguous=True),
                    writes=[R_bada])
                cx.dma("sp", "cst", lambda e, l=l: e.dma_start(
                    out=gnT[:], in_=gn_d[l].rearrange("(k p) -> p k", p=128), allow_slow_non_contiguous=True),
                    writes=[R_gn])
                fns = []
                for oc in range(24):
                    for kc in range(8):
                        fns.append(lambda e, oc=oc, kc=kc: e.matmul(
                            mm[0][:, oc * nseq:(oc + 1) * nseq], lhsT=wada[:, kc, oc * 128:(oc + 1) * 128],
                            rhs=silc[:, kc, :], start=(kc == 0), stop=(kc == 7)))
                cx.op("pe", fns, reads=[R_wada, R_silc], writes=[R_mm[0]])
                mmv = mm[0][:, 0:24 * nseq].rearrange("p (o s) -> p o s", s=nseq)
                for s in range(nseq):
                    cx.op("dve", lambda e, s=s: e.tensor_tensor(out=modT[:, :, s], in0=mmv[:, :, s], in1=badaT[:], op=ALU.add),
                          reads=[R_mm[0], R_bada], writes=[R_modT])
                for s in range(nseq):
                    cx.op("dve", lambda e, s=s, l=l: e.scalar_tensor_tensor(
                        out=gsA[:, l, s, :], in0=modT[:, 8:16, s], scalar=1.0, in1=gnT[:], op0=ALU.add, op1=ALU.mult),
                        reads=[R_modT, R_gn], writes=[R_mod])
                    cx.op("dve", lambda e, s=s, l=l: e.tensor_copy(out=shA[:, l, s, :], in_=modT[:, 0:8, s]),
                          reads=[R_modT], writes=[R_mod])
                    cx.op("dve", lambda e, s=s, l=l: e.tensor_copy(out=gtA[:, l, s, :], in_=modT[:, 16:24, s]),
                          reads=[R_modT], writes=[R_mod])
            cx.barrier()

        for l in range(depth):
            DTt, TAB, Gfb = DTt_all[:, l], TAB_all[:, l], Gfb_all[:, l]
            with ExitStack() as ph:
                P = ph.enter_context
                dfb = PT(ph, "dfb", [128, 8], F32)
                lg = PT(ph, "lg", [128, 8], F32)
                dsc = PT(ph, "dsc", [128, 4, 4], F32)
                e1 = PT(ph, "e1", [128, 128], F32)
                e2 = PT(ph, "e2", [128, 128], F32)
                R_dfb, R_lg, R_dsc, R_e1, R_e2 = (Res() for _ in range(5))
                cx.dma("sp", "cst", lambda e: e.dma_start(out=dfb[:, 0:4], in_=dfw_d[l].partition_broadcast(128)), writes=[R_dfb])
                cx.dma("sp", "cst", lambda e: e.dma_start(out=dfb[:, 4:8], in_=dbw_d[l].partition_broadcast(128)), writes=[R_dfb])
                cx.op("act", lambda e: e.activation(out=lg[:], in_=dfb[:], func=AF.Exp, scale=-1.0), reads=[R_dfb], writes=[R_lg])
                cx.op("act", lambda e: e.activation(out=lg[:], in_=lg[:], func=AF.Ln, bias=1.0), reads=[R_lg], writes=[R_lg])
                cx.op("dve", lambda e: e.tensor_scalar(out=lg[:], in0=lg[:], scalar1=-1.0, scalar2=None, op0=ALU.mult),
                      reads=[R_lg], writes=[R_lg])
                for kind in range(4):
                    lo = 0 if kind in (0, 2) else 4
                    cx.op("act", lambda e, kind=kind, lo=lo: e.activation(
                        out=dsc[:, kind, :], in_=lg[:, lo:lo + 4], func=AF.Exp, scale=ctau[:, kind:kind + 1]),
                        reads=[R_lg, R_const], writes=[R_dsc])
                for kind in range(4):
                    for h in range(4):
                        cx.op("dve", lambda e, kind=kind, h=h: e.tensor_scalar(
                            out=TAB[:, kind, h, :], in0=onesf[:, 0:64], scalar1=dsc[:, kind, h:h + 1],
                            scalar2=(0.125 if kind < 2 else 1.0), op0=ALU.mult, op1=ALU.mult),
                            reads=[R_dsc, R_const], writes=[R_tab])
                for h in range(4):
                    cx.op("act", lambda e, h=h: e.activation(out=e1[:], in_=cret[:, 0, :], func=AF.Exp, scale=lg[:, h:h + 1]),
                          reads=[R_lg, R_const], writes=[R_e1])
                    cx.op("dve", lambda e: e.tensor_tensor(out=e1[:], in0=e1[:], in1=cret[:, 1, :], op=ALU.mult),
                          reads=[R_e1, R_const], writes=[R_e1])
                    cx.op("act", lambda e, h=h: e.activation(out=e2[:], in_=cret[:, 2, :], func=AF.Exp, scale=lg[:, 4 + h:5 + h]),
                          reads=[R_lg, R_const], writes=[R_e2])
                    cx.op("dve", lambda e: e.tensor_tensor(out=e2[:], in0=e2[:], in1=cret[:, 3, :], op=ALU.mult),
                          reads=[R_e2, R_const], writes=[R_e2])
                    cx.op("dve", lambda e, h=h: e.tensor_tensor(out=DTt[:, h, :], in0=e1[:], in1=e2[:], op=ALU.add),
                          reads=[R_e1, R_e2], writes=[R_tab])
                for d_ in range(2):
                    for p in range(2):
                        for hh in range(2):
                            rows = slice(hh * 64, hh * 64 + 64)
                            col = d_ * 4 + 2 * p + hh
                            cx.op("act", lambda e, d_=d_, p=p, rows=rows, col=col: e.activation(
                                out=Gfb[rows, d_, p:p + 1], in_=lg[rows, col:col + 1], func=AF.Exp, scale=128.0),
                                reads=[R_lg], writes=[R_tab])
                cx.barrier()

        R_wser = Res("wser")

        def load_w(slot, src_ap, ncols):
            if len(src_ap.shape) == 3:
                view = wsl[slot][:, :, 0:ncols]
                cx.dma("pool", "w%d" % slot, lambda e: e.dma_start(out=view, in_=src_ap), writes=[R_w[slot], R_wser])
            else:
                nr = src_ap.shape[2]
                w_ = src_ap.shape[3]
                for r in range(nr):
                    view = wsl[slot][:, :, r * w_:(r + 1) * w_]
                    cx.dma("pool", "w%d" % slot, lambda e, view=view, r=r: e.dma_start(out=view, in_=src_ap[:, :, r, :]),
                           writes=[R_w[slot], R_wser])

        def rope(src_psum, R_src, t, out_ap, R_out, tmp1, tmp2, R_t1, R_t2):
            xv = src_psum.rearrange("p (h d) -> p h d", d=64)
            cb = _ap(rope_t[:, 0, t, :], [[0, 4], [1, 64]])
            s_lo = _ap(rope_t[:, 1, t, 0:32], [[0, 4], [1, 32]])
            s_hi = _ap(rope_t[:, 1, t, 32:64], [[0, 4], [1, 32]])
            t1v = tmp1.rearrange("p (h d) -> p h d", d=64)
            t2v = tmp2.rearrange("p (h d) -> p h d", d=64)
            cx.op("dve", lambda e: e.tensor_tensor(out=t1v, in0=xv, in1=cb, op=ALU.mult),
                  reads=[R_src, R_const], writes=[R_t1])
            cx.op("dve", [lambda e: e.tensor_tensor(out=t2v[:, :, 0:32], in0=xv[:, :, 32:64], in1=s_lo, op=ALU.mult),
                          lambda e: e.tensor_tensor(out=t2v[:, :, 32:64], in0=xv[:, :, 0:32], in1=s_hi, op=ALU.mult)],
                  reads=[R_src, R_const], writes=[R_t2])
            cx.op("dve", lambda e: e.tensor_tensor(out=out_ap, in0=tmp1, in1=tmp2, op=ALU.add),
                  reads=[R_t1, R_t2], writes=[R_out])

        def proj_tok(t, wslot, c0, ncols):
            a = nxt("mm")
            fns = [lambda e, kc=kc: e.matmul(mm[a][:, 0:ncols], lhsT=hT[:, kc, t * 128:(t + 1) * 128],
                                              rhs=wsl[wslot][:, kc, c0:c0 + ncols], start=(kc == 0), stop=(kc == 7))
                   for kc in range(8)]
            cx.op("pe", fns, reads=[R_hT[t // 4], R_w[wslot]], writes=[R_mm[a]])
            return a

        def proj_feat(g4, wslot, c0):
            a = nxt("mm")
            fns = [lambda e, kc=kc: e.matmul(mm[a][:, :], lhsT=wsl[wslot][:, kc, c0:c0 + 128],
                                              rhs=hT[:, kc, g4 * 512:(g4 + 1) * 512], start=(kc == 0), stop=(kc == 7))
                   for kc in range(8)]
            cx.op("pe", fns, reads=[R_hT[g4], R_w[wslot]], writes=[R_mm[a]])
            return a

        win_v = [win_d[l].rearrange("(k p) n -> p k n", p=128) for l in range(DEPTH)]
        win_a = [win_d[l].rearrange("(k p) (r c) -> p k r c", p=128, r=7) for l in range(DEPTH)]
        wout_v = [wout_d[l].rearrange("(k p) n -> p k n", p=128) for l in range(DEPTH)]

        def chk(name):
            if stop == name:
                raise _Stop()

        try:
          chk("M")
          for s in range(nseq):
            for l in range(depth):
                xsrc = x_d if l == 0 else y_d
                DTt, TAB, Gfb = DTt_all[:, l], TAB_all[:, l], Gfb_all[:, l]
                last = (l == depth - 1)
                if not (s == 0 and l == 0):
                    cx.barrier()
                    cx.new_epoch()
                load_w(0, win_v[l][:, :, 2048:2560], 512)
                if s == 0 and l == 0:
                    load_w(1, win_v[l][:, :, 2560:3072], 512)
                load_w(2, win_v[l][:, :, 3072:3584], 512)

                chk("T")

                with ExitStack() as ph:
                  if l == 0:
                      P = ph.enter_context
                      xin = [PT(ph, "xin%d" % i, [128, D], F32) for i in range(4)]
                      xn = [PT(ph, "xn%d" % i, [128, 4, D], BF16) for i in range(2)]
                      junk = PT(ph, "junk", [128, D], BF16)
                      stat = PT(ph, "stat", [128, 3, NT], F32)
                      R_xin = [Res(), Res(), Res(), Res()]
                      R_xn = [Res(), Res()]
                      R_junk = Res()
                      R_stat = [Res() for _ in range(NT)]
                      def st_X(g4):
                          xb = xn[g4 % 2]
                          for j in range(4):
                              t = 4 * g4 + j
                              xi = xin[t % 4]
                              cx.dma("sp", "xin%d" % (t % 4), lambda e, t=t, xi=xi: e.dma_start(out=xi[:], in_=xsrc[s, t * 128:(t + 1) * 128, :]),
                                     reads=[R_yd[s][t]], writes=[R_xin[t % 4]])
                              cx.op("pool", lambda e, t=t: e.memset(stat[:, 0, t:t + 1], 0.0), writes=[R_stat[t]])
                              cx.op("act", lambda e, t=t, xi=xi: e.activation(out=junk[:], in_=xi[:], func=AF.Square,
                                                                             accum_out=stat[:, 0, t:t + 1]),
                                    reads=[R_xin[t % 4]], writes=[R_junk, R_stat[t]])
                              cx.op("act", lambda e, t=t: e.activation(out=stat[:, 1, t:t + 1], in_=stat[:, 0, t:t + 1], func=AF.Sqrt,
                                                                       scale=1.0 / D, bias=EPS),
                                    reads=[R_stat[t]], writes=[R_stat[t]])
                              cx.op("dve", lambda e, t=t: e.reciprocal(out=stat[:, 2, t:t + 1], in_=stat[:, 1, t:t + 1]),
                                    reads=[R_stat[t]], writes=[R_stat[t]])
                              cx.op("dve", lambda e, t=t, xi=xi, j=j, xb=xb: e.tensor_scalar(
                                  out=xb[:, j, :], in0=xi[:], scalar1=stat[:, 2, t:t + 1], scalar2=None, op0=ALU.mult),
                                  reads=[R_xin[t % 4], R_stat[t]], writes=[R_xn[g4 % 2]])

                      def st_H(g4):
                          xb = xn[g4 % 2]
                          for k in range(8):
                              b = nxt("tp")
                              fns = [lambda e, j=j, k=k, b=b, xb=xb: e.transpose(tp[b][:, j * 128:(j + 1) * 128],
                                                                                xb[:, j, k * 128:(k + 1) * 128], ident[:])
                                     for j in range(4)]
                              cx.op("pe", fns, reads=[R_xn[g4 % 2], R_const], writes=[R_tp[b]])
                              cx.op("act", lambda e, k=k, b=b, g4=g4: e.activation(
                                  out=hT[:, k, g4 * 512:(g4 + 1) * 512], in_=tp[b][:, 0:512], func=AF.Identity,
                                  scale=gsA[:, l, s, k:k + 1], bias=shA[:, l, s, k:k + 1]),
                                  reads=[R_tp[b], R_mod], writes=[R_hT[g4]])

                      _pipeline(4, [(0, st_X), (1, st_H)])
                      cx.barrier()
                if s == 0 and l == 0:
                    dump("hT", hT[:, 0, :], [R_hT[0], R_hT[1], R_hT[2], R_hT[3]])
                chk("N")

                with ExitStack() as ph:
                    P = ph.enter_context
                    kbT = PT(ph, "kbT", [128, 2, S], BF16)
                    vb = PT(ph, "vb", [128, NT, 512], BF16)
                    kdb = PT(ph, "kdb", [128, NT, 256], BF16)
                    SfB = PT(ph, "SfB", [128, NT, 2, 128], BF16)
                    SbB = PT(ph, "SbB", [128, NT, 2, 128], BF16)
                    Sst = PT(ph, "Sst", [128, 2, 2, 128], F32)
                    rt1 = PT(ph, "rt1", [128, 256], F32)
                    rt2 = PT(ph, "rt2", [128, 256], F32)
                    kr = [PT(ph, "kr%d" % i, [128, 256], F32) for i in range(2)]
                    kbf = [PT(ph, "kbf%d" % i, [128, 256], BF16) for i in range(2)]
                    kdf = [PT(ph, "kdf%d" % i, [128, 256], BF16) for i in range(2)]
                    q3 = [PT(ph, "q3_%d" % i, [128, 3, 256], BF16) for i in range(2)]
                    qT3 = [PT(ph, "qT3_%d" % i, [128, 6, 128], BF16) for i in range(2)]
                    gsl = [PT(ph, "gsl%d" % i, [128, 512], BF16) for i in range(2)]
                    inT = [PT(ph, "inT%d" % i, [128, 512], BF16) for i in range(2)]
                    yr = [PT(ph, "yr%d" % i, [128, 512], BF16) for i in range(2)]
                    junk2 = PT(ph, "junk2", [128, 128], BF16)
                    gst = [PT(ph, "gst%d" % i, [128, 3, 4], F32) for i in range(2)]
                    R_kbT = [Res() for _ in range(NT)]
                    R_vb = [Res() for _ in range(NT)]
                    R_kdb = [Res() for _ in range(NT)]
                    R_SfB = [Res() for _ in range(NT)]
                    R_SbB = [Res() for _ in range(NT)]
                    R_Sst = [Res(), Res()]
                    R_rt1, R_rt2, R_junk2 = Res(), Res(), Res()
                    R_kr, R_kbf, R_kdf, R_q3, R_qT3, R_gsl, R_inT, R_yr, R_gst = (
                        [Res(), Res()] for _ in range(9))
                    TABv = lambda kind: TAB[:, kind, :, :].rearrange("p h d -> p (h d)")
                    R_q3a, R_q3b, R_q3c = ([Res(), Res()] for _ in range(3))
                    cx.op("pool", lambda e: e.memset(Sst[:], 0.0), writes=R_Sst)

                    def scan_update(d_, a):
                        fns = []
                        for p in range(2):
                            for hh in range(2):
                                rows = slice(hh * 64, hh * 64 + 64)
                                fns.append(lambda e, p=p, hh=hh, rows=rows: e.scalar_tensor_tensor(
                                    out=Sst[rows, d_, p, :], in0=Sst[rows, d_, p, :], scalar=Gfb[rows, d_, p:p + 1],
                                    in1=ov[a][rows, p * 256 + hh * 128:p * 256 + hh * 128 + 128], op0=ALU.mult, op1=ALU.add))
                        return fns

                    def b1_P(n):
                        i2 = n % 2
                        a = proj_tok(n, 0, 256, 256)
                        rope(mm[a][:, 0:256], R_mm[a], n, kr[i2][:], R_kr[i2], rt1[:], rt2[:], R_rt1, R_rt2)
                        cx.op("act", lambda e, i2=i2: e.activation(out=kbf[i2][:], in_=kr[i2][:], func=AF.Copy, scale=0.125),
                              reads=[R_kr[i2]], writes=[R_kbf[i2]])
                        cx.op("pool", lambda e, i2=i2: e.tensor_tensor(out=kdf[i2][:], in0=kr[i2][:], in1=TABv(0), op=ALU.mult),
                              reads=[R_kr[i2], R_tab], writes=[R_kdf[i2]])
                        cx.op("dve", lambda e, i2=i2, n=n: e.tensor_tensor(out=kdb[:, n, :], in0=kr[i2][:], in1=TABv(1), op=ALU.mult),
                              reads=[R_kr[i2], R_tab], writes=[R_kdb[n]])
                        a2 = proj_tok(n, 1, 0, 512)
                        cx.op("act", lambda e, a2=a2, n=n: e.activation(out=vb[:, n, :], in_=mm[a2][:, :], func=AF.Copy),
                              reads=[R_mm[a2]], writes=[R_vb[n]])

                    def b1_T(n):
                        i2 = n % 2
                        b = nxt("tp")
                        cx.op("pe", [lambda e, p=p, b=b, i2=i2: e.transpose(tp[b][:, p * 128:(p + 1) * 128], kbf[i2][:, p * 128:(p + 1) * 128], ident[:])
                                     for p in range(2)], reads=[R_kbf[i2], R_const], writes=[R_tp[b]])
                        cx.op("act", lambda e, b=b, n=n: e.activation(
                            out=kbT[:, :, n * 128:(n + 1) * 128], in_=tp[b][:, 0:256].rearrange("p (a c) -> p a c", a=2), func=AF.Copy),
                            reads=[R_tp[b]], writes=[R_kbT[n]])
                        cx.op("dve", lambda e, n=n: e.tensor_copy(out=SfB[:, n, :, :], in_=Sst[:, 0, :, :]),
                              reads=[R_Sst[0]], writes=[R_SfB[n]])
                        if n < NT - 1:
                            o = nxt("ov")
                            cx.op("pe", [lambda e, p=p, o=o, i2=i2, n=n: e.matmul(
                                ov[o][:, p * 256:(p + 1) * 256], lhsT=kdf[i2][:, p * 128:(p + 1) * 128],
                                rhs=vb[:, n, p * 256:(p + 1) * 256], start=True, stop=True) for p in range(2)],
                                reads=[R_kdf[i2], R_vb[n]], writes=[R_ov[o]])
                            cx.op("dve", scan_update(0, o), reads=[R_ov[o], R_tab, R_Sst[0]], writes=[R_Sst[0]])

                    _pipeline(NT, [(0, b1_P), (1, b1_T)])
                    b_stage = {"B1": 1, "Bb": 2}.get(stop, 3)
                    for n in (range(NT - 1, -1, -1) if b_stage >= 2 else []):
                        cx.op("dve", lambda e, n=n: e.tensor_copy(out=SbB[:, n, :, :], in_=Sst[:, 1, :, :]),
                              reads=[R_Sst[1]], writes=[R_SbB[n]])
                        if n > 0:
                            o = nxt("ov")
                            cx.op("pe", [lambda e, p=p, o=o, n=n: e.matmul(
                                ov[o][:, p * 256:(p + 1) * 256], lhsT=kdb[:, n, p * 128:(p + 1) * 128],
                                rhs=vb[:, n, p * 256:(p + 1) * 256], start=True, stop=True) for p in range(2)],
                                reads=[R_kdb[n], R_vb[n]], writes=[R_ov[o]])
                            cx.op("dve", scan_update(1, o), reads=[R_ov[o], R_tab, R_Sst[1]], writes=[R_Sst[1]])
                    def b2_P(n):
                        i2 = n % 2
                        a = proj_tok(n, 0, 0, 256)
                        rope(mm[a][:, 0:256], R_mm[a], n, kr[i2][:], R_kr[i2], rt1[:], rt2[:], R_rt1, R_rt2)
                        cx.op("act", lambda e, i2=i2: e.activation(out=q3[i2][:, 0, :], in_=kr[i2][:], func=AF.Copy),
                              reads=[R_kr[i2]], writes=[R_q3a[i2]])
                        cx.op("pool", lambda e, i2=i2: e.tensor_tensor(out=q3[i2][:, 1, :], in0=kr[i2][:], in1=TABv(2), op=ALU.mult),
                              reads=[R_kr[i2], R_tab], writes=[R_q3b[i2]])
                        cx.op("dve", lambda e, i2=i2: e.tensor_tensor(out=q3[i2][:, 2, :], in0=kr[i2][:], in1=TABv(3), op=ALU.mult),
                              reads=[R_kr[i2], R_tab], writes=[R_q3c[i2]])

                    def b2_T(n):
                        i2 = n % 2
                        b = nxt("tp")
                        cx.op("pe", [lambda e, v=v, p=p, b=b, i2=i2: e.transpose(
                            tp[b][:, (v * 2 + p) * 128:(v * 2 + p + 1) * 128], q3[i2][:, v, p * 128:(p + 1) * 128], ident[:])
                            for v in range(3) for p in range(2)], reads=[R_q3a[i2], R_q3b[i2], R_q3c[i2], R_const], writes=[R_tp[b]])
                        cx.op("act", lambda e, b=b, i2=i2: e.activation(
                            out=qT3[i2][:], in_=tp[b][:, 0:768].rearrange("p (a c) -> p a c", a=6), func=AF.Copy),
                            reads=[R_tp[b]], writes=[R_qT3[i2]])
                        fns = []
                        for h in range(4):
                            p, hh = h // 2, h % 2
                            rows = slice(hh * 64, hh * 64 + 64)
                            fns.append(lambda e, p=p, hh=hh, rows=rows, i2=i2, n=n: e.matmul(
                                st[hh][:, p * 128:(p + 1) * 128], lhsT=kbT[rows, p, n * 128:(n + 1) * 128],
                                rhs=qT3[i2][rows, p, :], start=True, stop=True))
                        cx.op("pe", fns, reads=[R_kbT[n], R_qT3[i2]], writes=[R_st[0], R_st[1]])
                        inv = inT[i2][:].rearrange("p (a b t) -> p a b t", a=2, b=2)
                        for hh in range(2):
                            cx.op("dve", lambda e, hh=hh, inv=inv: e.tensor_tensor(
                                out=inv[:, :, hh, :], in0=st[hh][:, 0:256].rearrange("p (a t) -> p a t", a=2),
                                in1=DTt[:].rearrange("p (a b) t -> p a b t", b=2)[:, :, hh, :], op=ALU.mult),
                                reads=[R_st[hh], R_tab], writes=[R_inT[i2]])
                        a2 = proj_tok(n, 2, 0, 512)
                        cx.op("act", lambda e, a2=a2, i2=i2: e.activation(out=gsl[i2][:], in_=mm[a2][:, :], func=AF.Silu),
                              reads=[R_mm[a2]], writes=[R_gsl[i2]])

                    def b2_Y(n):
                        i2 = n % 2
                        o = nxt("ov")
                        fns = []
                        for h in range(4):
                            p, hh = h // 2, h % 2
                            rows = slice(hh * 64, hh * 64 + 64)
                            oc = slice(h * 128, (h + 1) * 128)
                            fns.append(lambda e, oc=oc, o=o, i2=i2, n=n: e.matmul(
                                ov[o][:, oc], lhsT=inT[i2][:, oc], rhs=vb[:, n, oc], start=True, stop=False))
                            fns.append(lambda e, oc=oc, o=o, i2=i2, n=n, p=p, rows=rows: e.matmul(
                                ov[o][:, oc], lhsT=qT3[i2][rows, 2 + p, :], rhs=SfB[rows, n, p, :], start=False, stop=False))
                            fns.append(lambda e, oc=oc, o=o, i2=i2, n=n, p=p, rows=rows: e.matmul(
                                ov[o][:, oc], lhsT=qT3[i2][rows, 4 + p, :], rhs=SbB[rows, n, p, :], start=False, stop=True))
                        cx.op("pe", fns, reads=[R_inT[i2], R_vb[n], R_qT3[i2], R_SfB[n], R_SbB[n]], writes=[R_ov[o]])
                        cx.op("pool", lambda e, i2=i2: e.memset(gst[i2][:, 0, :], 0.0), writes=[R_gst[i2]])
                        cx.op("act", [lambda e, h=h, o=o, i2=i2: e.activation(
                            out=junk2[:], in_=ov[o][:, h * 128:(h + 1) * 128], func=AF.Square, accum_out=gst[i2][:, 0, h:h + 1])
                            for h in range(4)], reads=[R_ov[o]], writes=[R_junk2, R_gst[i2]])
                        cx.op("dve", lambda e, i2=i2: e.tensor_scalar(out=gst[i2][:, 1, :], in0=gst[i2][:, 0, :], scalar1=1.0 / 128,
                                                                      scalar2=EPS, op0=ALU.mult, op1=ALU.add),
                              reads=[R_gst[i2]], writes=[R_gst[i2]])
                        cx.op("pool", lambda e, i2=i2: e.tensor_tensor(out=gst[i2][:, 2, :], in0=gst[i2][:, 1, :], in1=mhalf[:, 0:4], op=ALU.pow),
                              reads=[R_gst[i2], R_const], writes=[R_gst[i2]])
                        cx.op("dve", [lambda e, h=h, o=o, i2=i2: e.scalar_tensor_tensor(
                            out=yr[i2][:, h * 128:(h + 1) * 128], in0=ov[o][:, h * 128:(h + 1) * 128],
                            scalar=gst[i2][:, 2, h:h + 1], in1=gsl[i2][:, h * 128:(h + 1) * 128], op0=ALU.mult, op1=ALU.mult)
                            for h in range(4)], reads=[R_ov[o], R_gst[i2], R_gsl[i2]], writes=[R_yr[i2]])

                    def b2_Z(n):
                        i2 = n % 2
                        b = nxt("tp")
                        cx.op("pe", [lambda e, h=h, b=b, i2=i2: e.transpose(tp[b][:, h * 128:(h + 1) * 128], yr[i2][:, h * 128:(h + 1) * 128], ident[:])
                                     for h in range(4)], reads=[R_yr[i2], R_const], writes=[R_tp[b]])
                        cx.op("act", lambda e, b=b, n=n: e.activation(
                            out=yT[:, 4:8, n * 128:(n + 1) * 128], in_=tp[b][:, 0:512].rearrange("p (a c) -> p a c", a=4), func=AF.Copy),
                            reads=[R_tp[b]], writes=R_yT[4:8])

                    if b_stage >= 3:
                        load_w(1, win_a[l][:, :, 0:4, 0:128], 512)
                        _pipeline(NT, [(0, b2_P), (1, b2_T), (2, b2_Y), (3, b2_Z)])
                    cx.barrier()
                if s == 0 and l == 0:
                    dump("yrT", yT[:, 4, :], R_yT[4:8])
                if stop in ("B1", "Bb", "B2a", "B2b", "B2c", "B2d"):
                    raise _Stop()
                chk("B")

                with ExitStack() as ph:
                    P = ph.enter_context
                    qT = PT(ph, "qT", [128, S], BF16)
                    gaT2 = [PT(ph, "gaT%d" % i, [128, S], BF16) for i in range(2)]
                    Vp = [PT(ph, "Vp%d" % i, [128, 20, 2, 128], BF16) for i in range(2)]
                    ACC = [PT(ph, "ACC%d" % i, [128, S], F32) for i in range(2)]
                    rt1 = PT(ph, "art1", [128, 256], F32)
                    rt2 = PT(ph, "art2", [128, 256], F32)
                    qkr = [PT(ph, "qkr%d" % i, [128, 256], BF16) for i in range(2)]
                    pt = [PT(ph, "pt%d" % i, [128, 512], BF16) for i in range(4)]
                    Rr = [PT(ph, "Rr%d" % i, [128, 512], F32) for i in range(2)]
                    Tm = [PT(ph, "Tm%d" % i, [128, 512], F32) for i in range(2)]
                    R_qT = [Res() for _ in range(NT)]
                    R_gaT2 = [[Res() for _ in range(4)] for _ in range(2)]
                    R_Vp = [Res(), Res()]
                    R_ACC = [[Res() for _ in range(4)] for _ in range(2)]
                    R_rt1, R_rt2 = Res(), Res()
                    R_qkr, R_pt, R_Rr, R_Tm = ([Res(), Res(), Res(), Res()] for _ in range(4))
                    sbank = [st[0], st[1], mm[0], mm[1]]
                    R_sbank = [R_st[0], R_st[1], R_mm[0], R_mm[1]]
                    def vp_init():
                        for i in range(2):
                            vflat = Vp[i][:].rearrange("p a b c -> p (a b) c")
                            for q_ in range(5):
                                cx.op("act", lambda e, i=i, q_=q_, vflat=vflat: e.activation(
                                    out=vflat[:, q_ * 8:(q_ + 1) * 8, :], in_=_ap(onesf[:, 0:128], [[0, 8], [1, 128]]), func=AF.Copy),
                                    reads=[R_const], writes=[R_Vp[i]])
                    if os.environ.get("DUMMY_INIT"):
                        for q_ in range(10):
                            cx.op("act", lambda e, q_=q_: e.activation(out=ACC[0][:, q_ * 128:(q_ + 1) * 128], in_=onesf[:, 0:128], func=AF.Copy),
                                  reads=[R_const], writes=[R_ACC[0]])
                    elif not os.environ.get("VPM_LATE") and not os.environ.get("SKIP_VPM"):
                        vp_init()
                    vpc = [0]
                    fin_pend = []
                    for j in range(4):
                        wslot = [1, 0, 2, 1][j]
                        gaT, R_gaT = gaT2[j % 2], R_gaT2[j % 2]
                        def a1_P(t, wslot=wslot, gaT=gaT, R_gaT=R_gaT):
                            i2 = t % 2
                            a = proj_tok(t, wslot, 0, 256)
                            rope(mm[a][:, 0:256], R_mm[a], t, qkr[i2][:], R_qkr[i2], rt1[:], rt2[:], R_rt1, R_rt2)
                            if t % 4 == 3:
                                g4 = t // 4
                                a = proj_feat(g4, wslot, 256)
                                cx.op("act", lambda e, a=a, g4=g4: e.activation(out=VT1[:, 64 + g4 * 512:64 + (g4 + 1) * 512], in_=mm[a][:, :], func=AF.Copy),
                                      reads=[R_mm[a]], writes=[R_VT1])
                                cx.op("act", lambda e, a=a, g4=g4: e.activation(
                                    out=VT2[:, :, 64 + g4 * 128:64 + (g4 + 1) * 128],
                                    in_=mm[a][:, :].rearrange("p (l r) -> p r l", r=4), func=AF.Copy),
                                    reads=[R_mm[a]], writes=[R_VT2])
                                a = proj_feat(g4, wslot, 384)
                                cx.op("act", lambda e, a=a, g4=g4: e.activation(out=gaT[:, g4 * 512:(g4 + 1) * 512], in_=mm[a][:, :], func=AF.Silu),
                                      reads=[R_mm[a]], writes=[R_gaT[g4]])

                        def a1_T(t):
                            i2 = t % 2
                            b = nxt("tp")
                            cx.op("pe", [lambda e, p=p, b=b, i2=i2: e.transpose(tp[b][:, p * 128:(p + 1) * 128], qkr[i2][:, p * 128:(p + 1) * 128], ident[:])
                                         for p in range(2)], reads=[R_qkr[i2], R_const], writes=[R_tp[b]])
                            cx.op("act", lambda e, b=b, t=t: e.activation(out=qT[:, t * 128:(t + 1) * 128], in_=tp[b][:, 0:128], func=AF.Copy),
                                  reads=[R_tp[b]], writes=[R_qT[t]])
                            cx.op("act", lambda e, b=b, t=t: e.activation(out=kT1[:, 64 + t * 128:64 + (t + 1) * 128], in_=tp[b][:, 128:256], func=AF.Copy),
                                  reads=[R_tp[b]], writes=[R_kT1])
                            cx.op("act", lambda e, b=b, t=t: e.activation(
                                out=kT2[:, :, 64 + t * 32:64 + (t + 1) * 32],
                                in_=tp[b][:, 128:256].rearrange("p (l r) -> p r l", r=4), func=AF.Copy),
                                reads=[R_tp[b]], writes=[R_kT2])

                        _pipeline(NT, [(0, a1_P), (1, a1_T)])
                        if j == 0:
                            load_w(0, win_a[l][:, :, 0:4, 128:256], 512)
                            load_w(2, win_a[l][:, :, 0:4, 256:384], 512)
                        elif j == 1:
                            load_w(1, win_a[l][:, :, 0:4, 384:512], 512)
                        elif j == 2:
                            load_w(0, wout_v[l][:, :, 0:512], 512)
                        else:
                            load_w(2, wout_v[l][:, :, 512:1024], 512)
                            nl, ns = (l + 1, s) if l + 1 < depth else (0, s + 1)
                            if ns < nseq:
                                load_w(1, win_v[nl][:, :, 2560:3072], 512)

                        if os.environ.get("VPM_LATE") and j == 0:
                            vp_init()
                        def build_vp(vi, srcs, R_src):
                            for g0 in range(0, len(srcs), 4):
                                grp = srcs[g0:g0 + 4]
                                b = nxt("tp")
                                cx.op("pe", [lambda e, ii=ii, sa_=sa_, b=b: e.transpose(tp[b][:, ii * 128:(ii + 1) * 128], sa_, ident[:])
                                             for ii, (_, sa_) in enumerate(grp)], reads=[R_src, R_const], writes=[R_tp[b]])
                                i0 = grp[0][0]
                                ng = len(grp)
                                for hh in range(2):
                                    eng = "act"
                                    src = tp[b][:, 0:ng * 128].rearrange("p (a c) -> p a c", c=128)[:, :, hh * 64:hh * 64 + 64]
                                    dst = Vp[vi][:, i0:i0 + ng, hh, hh * 64:hh * 64 + 64]
                                    if eng == "act":
                                        cx.op("act", lambda e, src=src, dst=dst: e.activation(out=dst, in_=src, func=AF.Copy),
                                              reads=[R_tp[b]], writes=[R_Vp[vi]])
                                    else:
                                        cx.op("dve", lambda e, src=src, dst=dst: e.tensor_copy(out=dst, in_=src),
                                              reads=[R_tp[b]], writes=[R_Vp[vi]])

                        pend = []

                        def flush_pend():
                            while pend:
                                pend.pop(0)()

                        tpf = [tp[0][:].bitcast(F32), tp[1][:].bitcast(F32)]
                        obank = [[ov[0][:, :], tpf[0]], [ov[1][:, :], tpf[1]]]
                        R_obank = [[R_ov[0], R_tp[0]], [R_ov[1], R_tp[1]]]

                        def attend2(vi, items2, mask_variant, R_k, acc2):
                            fc = ctr["ov"]
                            ctr["ov"] += 1
                            nk = len(items2[0][0][1])
                            per = 512 // (128 * nk)
                            ngrp = 4 // per
                            for gi, g0 in enumerate(range(0, 4, per)):
                                base = (ctr["w"] % 2) * 2
                                ctr["w"] += 1
                                mv = mask_variant(g0)
                                fns = [lambda e, sa=base + hh, mv=mv: e.matmul(sbank[sa][:, :], lhsT=ident[:], rhs=maskA[:, mv, :],
                                                                               start=True, stop=False) for hh in range(2)]
                                order = [(ii, kk, hh) for ii in range(per) for kk in range(nk) for hh in range(2)]
                                if os.environ.get("NO_ILV"):
                                    order = [(ii, kk, hh) for hh in range(2) for ii in range(per) for kk in range(nk)]
                                for (ii, kk, hh) in order:
                                    if True:
                                        c0 = (ii * nk + kk) * 128
                                        if True:
                                            q_ap, ks = items2[hh][g0 + ii]
                                            k_ap = ks[kk][0]
                                            fns.append(lambda e, c0=c0, sa=base + hh, k_ap=k_ap, q_ap=q_ap: e.matmul(
                                                sbank[sa][:, c0:c0 + 128], lhsT=k_ap, rhs=q_ap, start=False, stop=True))
                                cx.op("pe", fns, reads=[R_k, R_const] + [R_qT[t] for t in range(NT)], writes=[R_sbank[base], R_sbank[base + 1]])
                                for hh in range(2):
                                    pi = base + hh
                                    cx.op("act", lambda e, pi=pi: e.activation(out=pt[pi][:], in_=sbank[pi][:, :], func=AF.Exp, scale=0.125),
                                          reads=[R_sbank[pi]], writes=[R_pt[pi]])

                                def stage2(g0=g0, gi=gi, base=base):
                                    for hh in range(2):
                                        pi = base + hh
                                        fsel = 0
                                        o_ap, R_o = obank[hh][fsel], R_obank[hh][fsel]
                                        fns = []
                                        for ii in range(per):
                                            _, ks = items2[hh][g0 + ii]
                                            qi = g0 + ii
                                            for kk, (_, vidx) in enumerate(ks):
                                                c0 = (ii * nk + kk) * 128
                                                fns.append(lambda e, c0=c0, qi=qi, vidx=vidx, kk=kk, hh=hh, pi=pi, o_ap=o_ap: e.matmul(
                                                    o_ap[:, qi * 128:(qi + 1) * 128], lhsT=Vp[vi][:, vidx, hh, :], rhs=pt[pi][:, c0:c0 + 128],
                                                    start=(kk == 0), stop=(kk == nk - 1)))
                                        cx.op("pe", fns, reads=[R_pt[pi], R_Vp[vi]], writes=[R_o])
                                        if gi == ngrp - 1:
                                            acc2(hh, o_ap, R_o)

                                pend.append(stage2)
                                while len(pend) > 1:
                                    pend.pop(0)()

                        a_st = {"A1": 0, "Ap0": 1, "Ap1": 2, "Ap2": 3}.get(stop, 9)
                        if a_st < 9 and j > 0:
                            continue
                        hrows = [slice(0, 64), slice(64, 128)]
                        for pat in range(min(3, a_st)):
                            vi = vpc[0] % 2
                            vpc[0] += 1
                            if pat == 0:
                                build_vp(vi, [(jt, VT1[:, 128 * jt:128 * jt + 128]) for jt in range(17)], R_VT1)
                                for u in range(4):
                                    items2 = [[(qT[rows, 128 * i_:128 * i_ + 128],
                                                [(kT1[rows, 128 * i_:128 * i_ + 128], i_),
                                                 (kT1[rows, 128 * (i_ + 1):128 * (i_ + 1) + 128], i_ + 1)])
                                               for i_ in range(4 * u, 4 * u + 4)] for rows in hrows]
                                    mvf = lambda g0, u=u: (0 if (u == 0 and g0 == 0) else (2 if (u == 3 and g0 == 2) else 1))
                                    acc2 = lambda hh, o_ap, R_o, u=u: cx.op("dve", lambda e: e.tensor_copy(
                                        out=ACC[hh][:, 512 * u:512 * (u + 1)], in_=o_ap),
                                        reads=[R_o], writes=[R_ACC[hh][u]])
                                    for _ in range(2):
                                        if fin_pend:
                                            fin_pend.pop(0)()
                                    attend2(vi, items2, mvf, R_kT1, acc2)
                            elif pat == 1:
                                build_vp(vi, [(r * 5 + jt, VT2[:, r, 128 * jt:128 * jt + 128]) for r in range(4) for jt in range(5)], R_VT2)
                                for r in range(4):
                                    items2 = [[(qT[rows, 512 * i_ + r:512 * (i_ + 1):4],
                                                [(kT2[rows, r, 128 * i_:128 * i_ + 128], r * 5 + i_),
                                                 (kT2[rows, r, 128 * (i_ + 1):128 * (i_ + 1) + 128], r * 5 + i_ + 1)])
                                               for i_ in range(4)] for rows in hrows]
                                    mvf = lambda g0: (0 if g0 == 0 else 2)
                                    acc2 = lambda hh, o_ap, R_o, r=r: cx.op("dve", lambda e: e.tensor_tensor(
                                        out=ACC[hh][:, r:S:4], in0=o_ap, in1=ACC[hh][:, r:S:4], op=ALU.add),
                                        reads=[R_o] + R_ACC[hh], writes=R_ACC[hh])
                                    attend2(vi, items2, mvf, R_kT2, acc2)
                            else:
                                build_vp(vi, [(r, VT1[:, 64 + r:64 + S:16]) for r in range(16)], R_VT1)
                                for r0 in range(0, 16, 4):
                                    items2 = [[(qT[rows, r:S:16], [(kT1[rows, 64 + r:64 + S:16], r)])
                                               for r in range(r0, r0 + 4)] for rows in hrows]
                                    mvf = lambda g0: 3

                                    def acc2(hh, o_ap, R_o, r0=r0):
                                        accv = ACC[hh][:].rearrange("p (l r) -> p r l", r=16)[:, r0:r0 + 4, :]
                                        cx.op("dve", lambda e: e.tensor_tensor(
                                            out=accv, in0=o_ap.rearrange("p (r l) -> p r l", r=4), in1=accv, op=ALU.add),
                                            reads=[R_o] + R_ACC[hh], writes=R_ACC[hh])
                                    attend2(vi, items2, mvf, R_kT1, acc2)
                        flush_pend()

                        def fin_step(hh, u, j=j, gaT=gaT, R_gaT=R_gaT):
                            nr = slice(hh * 64, hh * 64 + 64)
                            dr = slice((1 - hh) * 64, (1 - hh) * 64 + 64)
                            cs = slice(512 * u, 512 * (u + 1))
                            i2 = ctr["fin"] % 2
                            ctr["fin"] += 1
                            cx.op("act", lambda e: e.activation(out=Rr[i2][nr, :], in_=ACC[hh][dr, cs], func=AF.Ln),
                                  reads=[R_ACC[hh][u]], writes=[R_Rr[i2]])
                            cx.op("act", lambda e: e.activation(out=Rr[i2][nr, :], in_=Rr[i2][nr, :], func=AF.Exp, scale=-1.0),
                                  reads=[R_Rr[i2]], writes=[R_Rr[i2]])
                            cx.op("dve", lambda e: e.tensor_tensor(out=Tm[i2][nr, :], in0=ACC[hh][nr, cs], in1=Rr[i2][nr, :], op=ALU.mult),
                                  reads=[R_ACC[hh][u], R_Rr[i2]], writes=[R_Tm[i2]])
                            cx.op("pool", lambda e: e.tensor_tensor(out=yT[nr, j, cs], in0=Tm[i2][nr, :], in1=gaT[nr, cs], op=ALU.mult),
                                  reads=[R_Tm[i2], R_gaT[u]], writes=[R_yT[j]])

                        if a_st >= 9:
                            for u in range(4):
                                for hh in range(2):
                                    fin_pend.append(lambda hh=hh, u=u, f=fin_step: f(hh, u))
                    while fin_pend:
                        fin_pend.pop(0)()
                    cx.barrier()
                if s == 0 and l == 0:
                    dump("yaT", yT[:, 0, :], R_yT[0:4])
                if stop in ("A1", "Ap0", "Ap1", "Ap2"):
                    raise _Stop()
                chk("A")

                with ExitStack() as ph:
                    P = ph.enter_context
                    gate_b = PT(ph, "gate_b", [128, D], F32)
                    gfin_b = PT(ph, "gfin_b", [128, D], F32)
                    R_gfin = Res()
                    if last:
                        cx.dma("sp", "cst", lambda e: e.dma_start(out=gfin_b[:], in_=gfin_d.partition_broadcast(128)), writes=[R_gfin])
                    xin = [PT(ph, "oxin%d" % i, [128, D], F32) for i in range(2)]
                    xo = [PT(ph, "xo%d" % i, [128, D], F32) for i in range(2)]
                    Dk = [PT(ph, "Dk%d" % i, [128, 128], F32) for i in range(2)]
                    junk = PT(ph, "ojunk", [128, D], BF16)
                    fst = [PT(ph, "fst%d" % i, [128, 3], F32) for i in range(2)]
                    R_xin, R_xo, R_Dk, R_fst = ([Res(), Res()] for _ in range(4))
                    R_junk = Res()
                    if not last:
                        oxn = [PT(ph, "oxn%d" % i, [128, 4, D], BF16) for i in range(2)]
                        ostat = PT(ph, "ostat", [128, 3, NT], F32)
                        R_oxn = [Res(), Res()]
                        R_ostat = [Res() for _ in range(NT)]
                    h_pend = []

                    def o_H(g4):
                        xb = oxn[g4 % 2]
                        for k in range(8):
                            b = nxt("tp")
                            cx.op("pe", [lambda e, jj=jj, k=k, b=b, xb=xb: e.transpose(tp[b][:, jj * 128:(jj + 1) * 128],
                                                                                   xb[:, jj, k * 128:(k + 1) * 128], ident[:])
                                         for jj in range(4)], reads=[R_oxn[g4 % 2], R_const], writes=[R_tp[b]])
                            cx.op("act", lambda e, k=k, b=b, g4=g4: e.activation(
                                out=hT[:, k, g4 * 512:(g4 + 1) * 512], in_=tp[b][:, 0:512], func=AF.Identity,
                                scale=gsA[:, l + 1, s, k:k + 1], bias=shA[:, l + 1, s, k:k + 1]),
                                reads=[R_tp[b], R_mod], writes=[R_hT[g4]])
                    for k in range(8):
                        i2 = k % 2
                        cx.op("dve", lambda e, k=k, i2=i2: e.tensor_scalar(out=Dk[i2][:], in0=identf[:], scalar1=gtA[:, l, s, k:k + 1],
                                                                         scalar2=None, op0=ALU.mult),
                              reads=[R_const, R_mod], writes=[R_Dk[i2]])
                        a = nxt("mm")
                        cx.op("pe", lambda e, a=a, i2=i2: e.matmul(mm[a][:, 0:128], lhsT=onesf[:], rhs=Dk[i2][:], start=True, stop=True),
                              reads=[R_Dk[i2], R_const], writes=[R_mm[a]])
                        cx.op("act", lambda e, a=a, k=k: e.activation(out=gate_b[:, k * 128:(k + 1) * 128], in_=mm[a][:, 0:128], func=AF.Copy),
                              reads=[R_mm[a]], writes=[R_gate])
                    for t in range(NT):
                        i2 = t % 2
                        cx.dma("sp", "xin%d" % i2, lambda e, t=t, i2=i2: e.dma_start(out=xin[i2][:], in_=xsrc[s, t * 128:(t + 1) * 128, :]),
                               reads=[R_yd[s][t]], writes=[R_xin[i2]])
                        for half in range(2):
                            a = nxt("mm")
                            wslot = 0 if half == 0 else 2
                            hs = slice(half * 512, (half + 1) * 512)
                            cx.op("pe", [lambda e, kc=kc, a=a, wslot=wslot, t=t: e.matmul(
                                mm[a][:, :], lhsT=yT[:, kc, t * 128:(t + 1) * 128], rhs=wsl[wslot][:, kc, :],
                                start=(kc == 0), stop=(kc == 7)) for kc in range(8)],
                                reads=R_yT + [R_w[wslot]], writes=[R_mm[a]])
                            cx.op("dve", lambda e, a=a, hs=hs, i2=i2: e.tensor_tensor(out=xo[i2][:, hs], in0=mm[a][:, :], in1=gate_b[:, hs], op=ALU.mult),
                                  reads=[R_mm[a], R_gate], writes=[R_xo[i2]])
                        cx.op("pool", lambda e, i2=i2: e.tensor_tensor(out=xo[i2][:], in0=xo[i2][:], in1=xin[i2][:], op=ALU.add),
                              reads=[R_xo[i2], R_xin[i2]], writes=[R_xo[i2]])
                        if last:
                            cx.op("pool", lambda e, i2=i2: e.memset(fst[i2][:, 0:1], 0.0), writes=[R_fst[i2]])
                            cx.op("act", lambda e, i2=i2: e.activation(out=junk[:], in_=xo[i2][:], func=AF.Square, accum_out=fst[i2][:, 0:1]),
                                  reads=[R_xo[i2]], writes=[R_junk, R_fst[i2]])
                            cx.op("act", lambda e, i2=i2: e.activation(out=fst[i2][:, 1:2], in_=fst[i2][:, 0:1], func=AF.Sqrt, scale=1.0 / D, bias=EPS),
                                  reads=[R_fst[i2]], writes=[R_fst[i2]])
                            cx.op("dve", lambda e, i2=i2: e.reciprocal(out=fst[i2][:, 2:3], in_=fst[i2][:, 1:2]),
                                  reads=[R_fst[i2]], writes=[R_fst[i2]])
                            cx.op("dve", lambda e, i2=i2: e.scalar_tensor_tensor(
                                out=xo[i2][:], in0=xo[i2][:], scalar=fst[i2][:, 2:3], in1=gfin_b[:], op0=ALU.mult, op1=ALU.mult),
                                reads=[R_xo[i2], R_fst[i2], R_gfin], writes=[R_xo[i2]])
                        cx.dma("sp", "xout%d" % i2, lambda e, t=t, i2=i2: e.dma_start(out=y_d[s, t * 128:(t + 1) * 128, :], in_=xo[i2][:]),
                               reads=[R_xo[i2]], writes=[R_yd[s][t]])
                        if not last:
                            g4, jj = t // 4, t % 4
                            cx.op("pool", lambda e, t=t: e.memset(ostat[:, 0, t:t + 1], 0.0), writes=[R_ostat[t]])
                            cx.op("act", lambda e, t=t, i2=i2: e.activation(out=junk[:], in_=xo[i2][:], func=AF.Square,
                                                                           accum_out=ostat[:, 0, t:t + 1]),
                                  reads=[R_xo[i2]], writes=[R_junk, R_ostat[t]])
                            cx.op("act", lambda e, t=t: e.activation(out=ostat[:, 1, t:t + 1], in_=ostat[:, 0, t:t + 1], func=AF.Sqrt,
                                                                     scale=1.0 / D, bias=EPS),
                                  reads=[R_ostat[t]], writes=[R_ostat[t]])
                            cx.op("dve", lambda e, t=t: e.reciprocal(out=ostat[:, 2, t:t + 1], in_=ostat[:, 1, t:t + 1]),
                                  reads=[R_ostat[t]], writes=[R_ostat[t]])
                            cx.op("dve", lambda e, t=t, i2=i2, jj=jj, g4=g4: e.tensor_scalar(
                                out=oxn[g4 % 2][:, jj, :], in0=xo[i2][:], scalar1=ostat[:, 2, t:t + 1], scalar2=None, op0=ALU.mult),
                                reads=[R_xo[i2], R_ostat[t]], writes=[R_oxn[g4 % 2]])
                            if h_pend and h_pend[0][0] <= t:
                                o_H(h_pend.pop(0)[1])
                            if jj == 3:
                                h_pend.append((t + 2, g4))
                    while h_pend:
                        o_H(h_pend.pop(0)[1])
                    cx.barrier()
        except _Stop:
            pass
        cx.barrier()
    return nc


def _consts():
    f32 = np.float32
    pos = np.arange(S, dtype=np.float32)
    inv = (10000.0 ** (-np.arange(0, 64, 2, dtype=np.float32) / 64)).astype(f32)
    ang = (pos[:, None] * inv[None, :]).astype(f32)
    cos, sin = np.cos(ang).astype(f32), np.sin(ang).astype(f32)
    C2 = np.concatenate([cos, cos], axis=1)
    S2 = np.concatenate([-sin, sin], axis=1)
    rope = np.stack([C2.reshape(NT, 128, 64).transpose(1, 0, 2), S2.reshape(NT, 128, 64).transpose(1, 0, 2)], axis=1)
    p = np.arange(128)[:, None]
    c = np.arange(128)[None, :]
    A = (c <= p).astype(f32)
    B = (p <= c).astype(f32)
    A_first = A * (p >= 64)
    B_last = B * (p < 64)
    m_norm = np.concatenate([A, B], axis=1)
    m_first = np.concatenate([A_first, B], axis=1)
    m_last = np.concatenate([A, B_last], axis=1)
    band = (np.abs(p - c) <= 64).astype(f32)
    mask = np.stack([np.concatenate([m_first, m_norm], 1), np.concatenate([m_norm, m_norm], 1),
                     np.concatenate([m_norm, m_last], 1), np.concatenate([band] * 4, 1)], axis=1)
    diff = (c - p).astype(f32)
    ret = np.stack([np.maximum(diff, 0), (diff >= 0).astype(f32), np.maximum(-diff, 0), (diff < 0).astype(f32)], axis=1)
    tau = np.arange(128, dtype=f32)
    taus = np.stack([127 - tau, tau, tau + 1, 128 - tau], axis=1)
    mask = (mask - 1.0) * 30000.0
    return dict(cst_rope=np.ascontiguousarray(rope, f32), cst_mask=np.ascontiguousarray(mask, f32),
                cst_ret=np.ascontiguousarray(ret, f32), cst_tau=np.ascontiguousarray(taus, f32),
                cst_ident=np.eye(128, dtype=f32))


def kernel(x_prompt, x_sample, c_prompt, c_sample, g_norm, w_ada, b_ada, w_in, w_out,
           decay_fwd, decay_bwd, g_final):
    f = lambda a: np.ascontiguousarray(np.asarray(a), dtype=np.float32)
    xs = np.concatenate([f(x_prompt), f(x_sample)], axis=0)
    cs = np.concatenate([f(c_prompt), f(c_sample)], axis=0)
    shared = dict(g_norm=f(g_norm), w_ada=f(w_ada), b_ada=f(b_ada), w_in=f(w_in), w_out=f(w_out),
                  decay_fwd=f(decay_fwd), decay_bwd=f(decay_bwd), g_final=f(g_final))
    shared.update(_consts())
    nc = build_nc()
    in_maps = []
    for i in range(NCORES):
        m = dict(shared)
        m["x"] = np.ascontiguousarray(xs[i * NSEQ:(i + 1) * NSEQ])
        m["c"] = np.ascontiguousarray(cs[i * NSEQ:(i + 1) * NSEQ])
        in_maps.append(m)
    res = run_bass_kernel_spmd(nc, in_maps, core_ids=list(range(NCORES)))
    ys = np.concatenate([np.asarray(r["y"], dtype=np.float32) for r in res.results], axis=0)
    nb = np.asarray(x_prompt).shape[0]
    return (np.ascontiguousarray(ys[:nb]), np.ascontiguousarray(ys[nb:]))
```

```python
import math
import os
from contextlib import ExitStack

import numpy as np
import concourse.bass as bass
import concourse.mybir as mybir
from concourse.bass_utils import run_bass_kernel_spmd

F32 = mybir.dt.float32
BF16 = mybir.dt.bfloat16
AF = mybir.ActivationFunctionType
ALU = mybir.AluOpType

D = 1024
S = 2048
NT = 16
DEPTH = 2
NCORES = 8
NSEQ = 3
INW = 3584
EPS = 1e-6


class Tok:
    __slots__ = ("eng", "key", "val")

    def __init__(self, eng, key, val):
        self.eng, self.key, self.val = eng, key, val


class Res:
    __slots__ = ("name", "w", "r")

    def __init__(self, name=""):
        self.name, self.w, self.r = name, None, []


class Ctx:
    def __init__(self, nc, es):
        self.nc, self.es = nc, es
        self.engs = {"pe": nc.tensor, "act": nc.scalar, "dve": nc.vector, "pool": nc.gpsimd, "sp": nc.sync}
        self.sems, self.cnt = {}, {}
        self.seen = {e: {} for e in self.engs}
        self.epoch = 0
        self.dma_keys = set()
        self.new_epoch()

    def _mksem(self, key):
        if key not in self.sems:
            self.sems[key] = self.es.enter_context(self.nc.semaphore(key))
            self.cnt[key] = 0

    def new_epoch(self):
        self.epoch += 1
        self.ekey = {e: "%s%d" % (e, self.epoch) for e in self.engs if e != "sp"}
        for k in self.ekey.values():
            self._mksem(k)

    def _waits(self, eng, reads, writes, skipkey=None):
        deps = {}

        def need(tok, kind):
            if tok is None or tok.key == skipkey:
                return
            if tok.eng == eng and (eng == "pe" or kind != "RAW"):
                return
            if deps.get(tok.key, 0) < tok.val:
                deps[tok.key] = tok.val

        for r in reads:
            need(r.w, "RAW")
        for w in writes:
            need(w.w, "WAW")
            for t in w.r:
                need(t, "WAR")
        E, seen = self.engs[eng], self.seen[eng]
        for key, val in deps.items():
            if seen.get(key, 0) < val:
                E.wait_ge(self.sems[key], val)
                seen[key] = val

    def _commit(self, tok, reads, writes):
        for r in reads:
            r.r = [t for t in r.r if t.key != tok.key] + [tok]
        for w in writes:
            w.w, w.r = tok, []

    def op(self, eng, fns, reads=(), writes=()):
        self._waits(eng, reads, writes)
        if not isinstance(fns, (list, tuple)):
            fns = [fns]
        ins = None
        for f in fns:
            ins = f(self.engs[eng])
        key = self.ekey[eng]
        self.cnt[key] += 1
        ins.then_inc(self.sems[key], 1)
        tok = Tok(eng, key, self.cnt[key])
        self._commit(tok, reads, writes)
        return tok

    def dma(self, q, semname, fn, reads=(), writes=()):
        key = "d%d_%s" % (self.epoch, semname)
        self._mksem(key)
        self.dma_keys.add(key)
        self._waits(q, reads, writes, skipkey=key)
        ins = fn(self.engs[q])
        self.cnt[key] += 16
        ins.then_inc(self.sems[key], 16)
        tok = Tok("dma", key, self.cnt[key])
        self._commit(tok, reads, writes)
        return tok

    def barrier(self, with_dma=True):
        keys = list(self.ekey.values())
        if with_dma:
            keys += sorted(self.dma_keys)
        for e, E in self.engs.items():
            seen = self.seen[e]
            for key in keys:
                val = self.cnt[key]
                if val > 0 and seen.get(key, 0) < val:
                    E.wait_ge(self.sems[key], val)
                    seen[key] = val


def _pipeline(n_iter, stages):
    mx = max(sk for sk, _ in stages)
    for i in range(n_iter + mx):
        for sk, fn in stages:
            n = i - sk
            if 0 <= n < n_iter:
                fn(n)


def _ap(base, dims):
    return bass.AP(base.tensor, base.offset, [list(base.ap[0])] + [list(d) for d in dims])


class _Stop(Exception):
    pass


def build_nc(nseq=NSEQ, depth=DEPTH, dbg=(), stop=None):
    nc = bass.Bass("TRN2", target_bir_lowering=False)
    dt = nc.dram_tensor
    x_d = dt("x", [nseq, S, D], F32, kind="ExternalInput").ap()
    c_d = dt("c", [nseq, D], F32, kind="ExternalInput").ap()
    gn_d = dt("g_norm", [DEPTH, D], F32, kind="ExternalInput").ap()
    wada_d = dt("w_ada", [DEPTH, D, 3 * D], F32, kind="ExternalInput").ap()
    bada_d = dt("b_ada", [DEPTH, 3 * D], F32, kind="ExternalInput").ap()
    win_d = dt("w_in", [DEPTH, D, INW], F32, kind="ExternalInput").ap()
    wout_d = dt("w_out", [DEPTH, D, D], F32, kind="ExternalInput").ap()
    dfw_d = dt("decay_fwd", [DEPTH, 4], F32, kind="ExternalInput").ap()
    dbw_d = dt("decay_bwd", [DEPTH, 4], F32, kind="ExternalInput").ap()
    gfin_d = dt("g_final", [D], F32, kind="ExternalInput").ap()
    crope_d = dt("cst_rope", [128, 2, NT, 64], F32, kind="ExternalInput").ap()
    cmask_d = dt("cst_mask", [128, 4, 512], F32, kind="ExternalInput").ap()
    cret_d = dt("cst_ret", [128, 4, 128], F32, kind="ExternalInput").ap()
    ctau_d = dt("cst_tau", [128, 4], F32, kind="ExternalInput").ap()
    cid_d = dt("cst_ident", [128, 128], F32, kind="ExternalInput").ap()
    y_d = dt("y", [nseq, S, D], F32, kind="ExternalOutput").ap()
    dbg_d = {}
    for name, shape in dbg:
        dbg_d[name] = dt("dbg_" + name, list(shape), F32, kind="ExternalOutput").ap()

    with ExitStack() as es:
        E = es.enter_context
        cx = Ctx(nc, es)
        sb = lambda name, shape, dtype: E(nc.sbuf_tensor(name, list(shape), dtype))
        uid = [0]

        def PT(ph, name, shape, dtype):
            uid[0] += 1
            return ph.enter_context(nc.sbuf_tensor("%s_u%d" % (name, uid[0]), list(shape), dtype))

        hT = sb("hT", [128, 8, S], BF16)
        yT = sb("yT", [128, 8, S], BF16)
        wsl = [sb("wsl%d" % i, [128, 8, 512], BF16) for i in range(3)]
        kT1 = sb("kT1", [128, S + 128], BF16)
        kT2 = sb("kT2", [128, 4, 640], BF16)
        VT1 = sb("VT1", [128, S + 128], BF16)
        VT2 = sb("VT2", [128, 4, 640], BF16)
        ident = sb("ident", [128, 128], BF16)
        identf = sb("identf", [128, 128], F32)
        onesf = sb("onesf", [128, 128], F32)
        rope_t = sb("rope_t", [128, 2, NT, 64], F32)
        maskA = sb("maskA", [128, 4, 512], BF16)
        cret = sb("cret", [128, 4, 128], F32)
        ctau = sb("ctau", [128, 4], F32)
        DTt_all = sb("DTt", [128, DEPTH, 4, 128], F32)
        TAB_all = sb("TAB", [128, DEPTH, 4, 4, 64], F32)
        Gfb_all = sb("Gfb", [128, DEPTH, 2, 2], F32)
        gsA = sb("gsA", [128, DEPTH, nseq, 8], F32)
        shA = sb("shA", [128, DEPTH, nseq, 8], F32)
        gtA = sb("gtA", [128, DEPTH, nseq, 8], F32)
        mhalf = sb("mhalf", [128, 16], F32)

        mm = [E(nc.psum_tensor("mm%d" % i, [128, 512], F32)) for i in range(2)]
        tp = [E(nc.psum_tensor("tp%d" % i, [128, 1024], BF16)) for i in range(2)]
        st = [E(nc.psum_tensor("st%d" % i, [128, 512], F32)) for i in range(2)]
        ov = [E(nc.psum_tensor("ov%d" % i, [128, 512], F32)) for i in range(2)]
        R_mm = [Res("mm0"), Res("mm1")]
        R_tp = [Res("tp0"), Res("tp1")]
        R_st = [Res("st0"), Res("st1")]
        R_ov = [Res("ov0"), Res("ov1")]
        ctr = {"mm": 0, "tp": 0, "st": 0, "ov": 0, "w": 0, "fin": 0}

        def nxt(kind):
            i = ctr[kind] % 2
            ctr[kind] += 1
            return i

        R_const = Res("const")
        R_hT = [Res("hT%d" % g) for g in range(4)]
        R_yT = [Res("yT%d" % k) for k in range(8)]
        R_w = [Res("w%d" % i) for i in range(3)]
        R_kT1, R_kT2, R_VT1, R_VT2 = Res("kT1"), Res("kT2"), Res("VT1"), Res("VT2")
        R_tab, R_gate, R_mod = Res("tab"), Res("gate"), Res("mod")
        R_yd = [[Res("yd%d_%d" % (s, t)) for t in range(NT)] for s in range(nseq)]
        dump_list = []

        def dump(name, src_ap, reads):
            if name in dbg_d:
                dump_list.append(cx.dma("pool", "dbg", lambda e: e.dma_start(out=dbg_d[name], in_=src_ap), reads=reads))

        cx.dma("sp", "cst", lambda e: e.dma_start(out=rope_t[:], in_=crope_d), writes=[R_const])
        cx.dma("pool", "cstp", lambda e: e.dma_start(out=maskA[:], in_=cmask_d), writes=[R_const])
        cx.dma("sp", "cst", lambda e: e.dma_start(out=cret[:], in_=cret_d), writes=[R_const])
        cx.dma("sp", "cst", lambda e: e.dma_start(out=ctau[:], in_=ctau_d), writes=[R_const])
        cx.dma("pool", "cstp", lambda e: e.dma_start(out=ident[:], in_=cid_d), writes=[R_const])
        cx.dma("sp", "cst", lambda e: e.dma_start(out=identf[:], in_=cid_d), writes=[R_const])
        cx.op("pool", lambda e: e.memset(onesf[:], 1.0), writes=[R_const])
        cx.op("pool", lambda e: e.memset(mhalf[:], -0.5), writes=[R_const])
        cx.op("pool", lambda e: e.memset(kT1[:], 0.0), writes=[R_kT1])
        cx.op("pool", lambda e: e.memset(kT2[:], 0.0), writes=[R_kT2])
        cx.op("pool", lambda e: e.memset(VT1[:], 0.0), writes=[R_VT1])
        cx.op("pool", lambda e: e.memset(VT2[:], 0.0), writes=[R_VT2])

        with ExitStack() as ph:
            P = ph.enter_context
            wada = PT(ph, "wada", [128, 8, 3 * D], BF16)
            cT = PT(ph, "cT", [128, nseq, 8], F32)
            silc = PT(ph, "silc", [128, 8, nseq], BF16)
            badaT = PT(ph, "badaT", [128, 24], F32)
            gnT = PT(ph, "gnT", [128, 8], F32)
            modT = PT(ph, "modT", [128, 24, nseq], F32)
            R_wada, R_cT, R_silc, R_bada, R_gn, R_modT = (Res() for _ in range(6))
            for s in range(nseq):
                cx.dma("sp", "cst", lambda e, s=s: e.dma_start(
                    out=cT[:, s, :], in_=c_d[s].rearrange("(k p) -> p k", p=128), allow_slow_non_contiguous=True),
                    writes=[R_cT])
            cx.op("act", lambda e: e.activation(out=silc[:].rearrange("p k s -> p s k"), in_=cT[:], func=AF.Silu),
                  reads=[R_cT], writes=[R_silc])
            for l in range(depth):
                for i in range(6):
                    cx.dma("pool", "wada", lambda e, i=i, l=l: e.dma_start(
                        out=wada[:, :, i * 512:(i + 1) * 512],
                        in_=wada_d[l].rearrange("(k p) n -> p k n", p=128)[:, :, i * 512:(i + 1) * 512]),
                        writes=[R_wada])
                cx.dma("sp", "cst", lambda e, l=l: e.dma_start(
                    out=badaT[:], in_=bada_d[l].rearrange("(o p) -> p o", p=128), allow_slow_non_contiguous=True),
                    writes=[R_bada])
                cx.dma("sp", "cst", lambda e, l=l: e.dma_start(
                    out=gnT[:], in_=gn_d[l].rearrange("(k p) -> p k", p=128), allow_slow_non_contiguous=True),
                    writes=[R_gn])
                fns = []
                for oc in range(24):
                    for kc in range(8):
                        fns.append(lambda e, oc=oc, kc=kc: e.matmul(
                            mm[0][:, oc * nseq:(oc + 1) * nseq], lhsT=wada[:, kc, oc * 128:(oc + 1) * 128],
                            rhs=silc[:, kc, :], start=(kc == 0), stop=(kc == 7)))
                cx.op("pe", fns, reads=[R_wada, R_silc], writes=[R_mm[0]])
                mmv = mm[0][:, 0:24 * nseq].rearrange("p (o s) -> p o s", s=nseq)
                for s in range(nseq):
                    cx.op("dve", lambda e, s=s: e.tensor_tensor(out=modT[:, :, s], in0=mmv[:, :, s], in1=badaT[:], op=ALU.add),
                          reads=[R_mm[0], R_bada], writes=[R_modT])
                for s in range(nseq):
                    cx.op("dve", lambda e, s=s, l=l: e.scalar_tensor_tensor(
                        out=gsA[:, l, s, :], in0=modT[:, 8:16, s], scalar=1.0, in1=gnT[:], op0=ALU.add, op1=ALU.mult),
                        reads=[R_modT, R_gn], writes=[R_mod])
                    cx.op("dve", lambda e, s=s, l=l: e.tensor_copy(out=shA[:, l, s, :], in_=modT[:, 0:8, s]),
                          reads=[R_modT], writes=[R_mod])
                    cx.op("dve", lambda e, s=s, l=l: e.tensor_copy(out=gtA[:, l, s, :], in_=modT[:, 16:24, s]),
                          reads=[R_modT], writes=[R_mod])
            cx.barrier()

        for l in range(depth):
            DTt, TAB, Gfb = DTt_all[:, l], TAB_all[:, l], Gfb_all[:, l]
            with ExitStack() as ph:
                P = ph.enter_context
                dfb = PT(ph, "dfb", [128, 8], F32)
                lg = PT(ph, "lg", [128, 8], F32)
                dsc = PT(ph, "dsc", [128, 4, 4], F32)
                e1 = PT(ph, "e1", [128, 128], F32)
                e2 = PT(ph, "e2", [128, 128], F32)
                R_dfb, R_lg, R_dsc, R_e1, R_e2 = (Res() for _ in range(5))
                cx.dma("sp", "cst", lambda e: e.dma_start(out=dfb[:, 0:4], in_=dfw_d[l].partition_broadcast(128)), writes=[R_dfb])
                cx.dma("sp", "cst", lambda e: e.dma_start(out=dfb[:, 4:8], in_=dbw_d[l].partition_broadcast(128)), writes=[R_dfb])
                cx.op("act", lambda e: e.activation(out=lg[:], in_=dfb[:], func=AF.Exp, scale=-1.0), reads=[R_dfb], writes=[R_lg])
                cx.op("act", lambda e: e.activation(out=lg[:], in_=lg[:], func=AF.Ln, bias=1.0), reads=[R_lg], writes=[R_lg])
                cx.op("dve", lambda e: e.tensor_scalar(out=lg[:], in0=lg[:], scalar1=-1.0, scalar2=None, op0=ALU.mult),
                      reads=[R_lg], writes=[R_lg])
                for kind in range(4):
                    lo = 0 if kind in (0, 2) else 4
                    cx.op("act", lambda e, kind=kind, lo=lo: e.activation(
                        out=dsc[:, kind, :], in_=lg[:, lo:lo + 4], func=AF.Exp, scale=ctau[:, kind:kind + 1]),
                        reads=[R_lg, R_const], writes=[R_dsc])
                for kind in range(4):
                    for h in range(4):
                        cx.op("dve", lambda e, kind=kind, h=h: e.tensor_scalar(
                            out=TAB[:, kind, h, :], in0=onesf[:, 0:64], scalar1=dsc[:, kind, h:h + 1],
                            scalar2=(0.125 if kind < 2 else 1.0), op0=ALU.mult, op1=ALU.mult),
                            reads=[R_dsc, R_const], writes=[R_tab])
                for h in range(4):
                    cx.op("act", lambda e, h=h: e.activation(out=e1[:], in_=cret[:, 0, :], func=AF.Exp, scale=lg[:, h:h + 1]),
                          reads=[R_lg, R_const], writes=[R_e1])
                    cx.op("dve", lambda e: e.tensor_tensor(out=e1[:], in0=e1[:], in1=cret[:, 1, :], op=ALU.mult),
                          reads=[R_e1, R_const], writes=[R_e1])
                    cx.op("act", lambda e, h=h: e.activation(out=e2[:], in_=cret[:, 2, :], func=AF.Exp, scale=lg[:, 4 + h:5 + h]),
                          reads=[R_lg, R_const], writes=[R_e2])
                    cx.op("dve", lambda e: e.tensor_tensor(out=e2[:], in0=e2[:], in1=cret[:, 3, :], op=ALU.mult),
                          reads=[R_e2, R_const], writes=[R_e2])
                    cx.op("dve", lambda e, h=h: e.tensor_tensor(out=DTt[:, h, :], in0=e1[:], in1=e2[:], op=ALU.add),
                          reads=[R_e1, R_e2], writes=[R_tab])
                for d_ in range(2):
                    for p in range(2):
                        for hh in range(2):
                            rows = slice(hh * 64, hh * 64 + 64)
                            col = d_ * 4 + 2 * p + hh
                            cx.op("act", lambda e, d_=d_, p=p, rows=rows, col=col: e.activation(
                                out=Gfb[rows, d_, p:p + 1], in_=lg[rows, col:col + 1], func=AF.Exp, scale=128.0),
                                reads=[R_lg], writes=[R_tab])
                cx.barrier()

        R_wser = Res("wser")

        def load_w(slot, src_ap, ncols):
            if len(src_ap.shape) == 3:
                view = wsl[slot][:, :, 0:ncols]
                cx.dma("pool", "w%d" % slot, lambda e: e.dma_start(out=view, in_=src_ap), writes=[R_w[slot], R_wser])
            else:
                nr = src_ap.shape[2]
                w_ = src_ap.shape[3]
                for r in range(nr):
                    view = wsl[slot][:, :, r * w_:(r + 1) * w_]
                    cx.dma("pool", "w%d" % slot, lambda e, view=view, r=r: e.dma_start(out=view, in_=src_ap[:, :, r, :]),
                           writes=[R_w[slot], R_wser])

        def rope(src_psum, R_src, t, out_ap, R_out, tmp1, tmp2, R_t1, R_t2):
            xv = src_psum.rearrange("p (h d) -> p h d", d=64)
            cb = _ap(rope_t[:, 0, t, :], [[0, 4], [1, 64]])
            s_lo = _ap(rope_t[:, 1, t, 0:32], [[0, 4], [1, 32]])
            s_hi = _ap(rope_t[:, 1, t, 32:64], [[0, 4], [1, 32]])
            t1v = tmp1.rearrange("p (h d) -> p h d", d=64)
            t2v = tmp2.rearrange("p (h d) -> p h d", d=64)
            cx.op("dve", lambda e: e.tensor_tensor(out=t1v, in0=xv, in1=cb, op=ALU.mult),
                  reads=[R_src, R_const], writes=[R_t1])
            cx.op("dve", [lambda e: e.tensor_tensor(out=t2v[:, :, 0:32], in0=xv[:, :, 32:64], in1=s_lo, op=ALU.mult),
                          lambda e: e.tensor_tensor(out=t2v[:, :, 32:64], in0=xv[:, :, 0:32], in1=s_hi, op=ALU.mult)],
                  reads=[R_src, R_const], writes=[R_t2])
            cx.op("dve", lambda e: e.tensor_tensor(out=out_ap, in0=tmp1, in1=tmp2, op=ALU.add),
                  reads=[R_t1, R_t2], writes=[R_out])

        def proj_tok(t, wslot, c0, ncols):
            a = nxt("mm")
            fns = [lambda e, kc=kc: e.matmul(mm[a][:, 0:ncols], lhsT=hT[:, kc, t * 128:(t + 1) * 128],
                                              rhs=wsl[wslot][:, kc, c0:c0 + ncols], start=(kc == 0), stop=(kc == 7))
                   for kc in range(8)]
            cx.op("pe", fns, reads=[R_hT[t // 4], R_w[wslot]], writes=[R_mm[a]])
            return a

        def proj_feat(g4, wslot, c0):
            a = nxt("mm")
            fns = [lambda e, kc=kc: e.matmul(mm[a][:, :], lhsT=wsl[wslot][:, kc, c0:c0 + 128],
                                              rhs=hT[:, kc, g4 * 512:(g4 + 1) * 512], start=(kc == 0), stop=(kc == 7))
                   for kc in range(8)]
            cx.op("pe", fns, reads=[R_hT[g4], R_w[wslot]], writes=[R_mm[a]])
            return a

        win_v = [win_d[l].rearrange("(k p) n -> p k n", p=128) for l in range(DEPTH)]
        win_a = [win_d[l].rearrange("(k p) (r c) -> p k r c", p=128, r=7) for l in range(DEPTH)]
        wout_v = [wout_d[l].rearrange("(k p) n -> p k n", p=128) for l in range(DEPTH)]

        def chk(name):
            if stop == name:
                raise _Stop()

        try:
          chk("M")
          for s in range(nseq):
            for l in range(depth):
                xsrc = x_d if l == 0 else y_d
                DTt, TAB, Gfb = DTt_all[:, l], TAB_all[:, l], Gfb_all[:, l]
                last = (l == depth - 1)
                if not (s == 0 and l == 0):
                    cx.barrier()
                    cx.new_epoch()
                load_w(0, win_v[l][:, :, 2048:2560], 512)
                if s == 0 and l == 0:
                    load_w(1, win_v[l][:, :, 2560:3072], 512)
                load_w(2, win_v[l][:, :, 3072:3584], 512)

                chk("T")

                with ExitStack() as ph:
                  if l == 0:
                      P = ph.enter_context
                      xin = [PT(ph, "xin%d" % i, [128, D], F32) for i in range(4)]
                      xn = [PT(ph, "xn%d" % i, [128, 4, D], BF16) for i in range(2)]
                      junk = PT(ph, "junk", [128, D], BF16)
                      stat = PT(ph, "stat", [128, 3, NT], F32)
                      R_xin = [Res(), Res(), Res(), Res()]
                      R_xn = [Res(), Res()]
                      R_junk = Res()
                      R_stat = [Res() for _ in range(NT)]
                      def st_X(g4):
                          xb = xn[g4 % 2]
                          for j in range(4):
                              t = 4 * g4 + j
                              xi = xin[t % 4]
                              cx.dma("sp", "xin%d" % (t % 4), lambda e, t=t, xi=xi: e.dma_start(out=xi[:], in_=xsrc[s, t * 128:(t + 1) * 128, :]),
                                     reads=[R_yd[s][t]], writes=[R_xin[t % 4]])
                              cx.op("pool", lambda e, t=t: e.memset(stat[:, 0, t:t + 1], 0.0), writes=[R_stat[t]])
                              cx.op("act", lambda e, t=t, xi=xi: e.activation(out=junk[:], in_=xi[:], func=AF.Square,
                                                                             accum_out=stat[:, 0, t:t + 1]),
                                    reads=[R_xin[t % 4]], writes=[R_junk, R_stat[t]])
                              cx.op("act", lambda e, t=t: e.activation(out=stat[:, 1, t:t + 1], in_=stat[:, 0, t:t + 1], func=AF.Sqrt,
                                                                       scale=1.0 / D, bias=EPS),
                                    reads=[R_stat[t]], writes=[R_stat[t]])
                              cx.op("dve", lambda e, t=t: e.reciprocal(out=stat[:, 2, t:t + 1], in_=stat[:, 1, t:t + 1]),
                                    reads=[R_stat[t]], writes=[R_stat[t]])
                              cx.op("dve", lambda e, t=t, xi=xi, j=j, xb=xb: e.tensor_scalar(
                                  out=xb[:, j, :], in0=xi[:], scalar1=stat[:, 2, t:t + 1], scalar2=None, op0=ALU.mult),
                                  reads=[R_xin[t % 4], R_stat[t]], writes=[R_xn[g4 % 2]])

                      def st_H(g4):
                          xb = xn[g4 % 2]
                          for k in range(8):
                              b = nxt("tp")
                              fns = [lambda e, j=j, k=k, b=b, xb=xb: e.transpose(tp[b][:, j * 128:(j + 1) * 128],
                                                                                xb[:, j, k * 128:(k + 1) * 128], ident[:])
                                     for j in range(4)]
                              cx.op("pe", fns, reads=[R_xn[g4 % 2], R_const], writes=[R_tp[b]])
                              cx.op("act", lambda e, k=k, b=b, g4=g4: e.activation(
                                  out=hT[:, k, g4 * 512:(g4 + 1) * 512], in_=tp[b][:, 0:512], func=AF.Identity,
                                  scale=gsA[:, l, s, k:k + 1], bias=shA[:, l, s, k:k + 1]),
                                  reads=[R_tp[b], R_mod], writes=[R_hT[g4]])

                      _pipeline(4, [(0, st_X), (1, st_H)])
                      cx.barrier()
                if s == 0 and l == 0:
                    dump("hT", hT[:, 0, :], [R_hT[0], R_hT[1], R_hT[2], R_hT[3]])
                chk("N")

                with ExitStack() as ph:
                    P = ph.enter_context
                    kbT = PT(ph, "kbT", [128, 2, S], BF16)
                    vb = PT(ph, "vb", [128, NT, 512], BF16)
                    kdb = PT(ph, "kdb", [128, NT, 256], BF16)
                    SfB = PT(ph, "SfB", [128, NT, 2, 128], BF16)
                    SbB = PT(ph, "SbB", [128, NT, 2, 128], BF16)
                    Sst = PT(ph, "Sst", [128, 2, 2, 128], F32)
                    rt1 = PT(ph, "rt1", [128, 256], F32)
                    rt2 = PT(ph, "rt2", [128, 256], F32)
                    kr = [PT(ph, "kr%d" % i, [128, 256], F32) for i in range(2)]
                    kbf = [PT(ph, "kbf%d" % i, [128, 256], BF16) for i in range(2)]
                    kdf = [PT(ph, "kdf%d" % i, [128, 256], BF16) for i in range(2)]
                    q3 = [PT(ph, "q3_%d" % i, [128, 3, 256], BF16) for i in range(2)]
                    qT3 = [PT(ph, "qT3_%d" % i, [128, 6, 128], BF16) for i in range(2)]
                    gsl = [PT(ph, "gsl%d" % i, [128, 512], BF16) for i in range(2)]
                    inT = [PT(ph, "inT%d" % i, [128, 512], BF16) for i in range(2)]
                    yr = [PT(ph, "yr%d" % i, [128, 512], BF16) for i in range(2)]
                    junk2 = PT(ph, "junk2", [128, 128], BF16)
                    gst = [PT(ph, "gst%d" % i, [128, 3, 4], F32) for i in range(2)]
                    R_kbT = [Res() for _ in range(NT)]
                    R_vb = [Res() for _ in range(NT)]
                    R_kdb = [Res() for _ in range(NT)]
                    R_SfB = [Res() for _ in range(NT)]
                    R_SbB = [Res() for _ in range(NT)]
                    R_Sst = [Res(), Res()]
                    R_rt1, R_rt2, R_junk2 = Res(), Res(), Res()
                    R_kr, R_kbf, R_kdf, R_q3, R_qT3, R_gsl, R_inT, R_yr, R_gst = (
                        [Res(), Res()] for _ in range(9))
                    TABv = lambda kind: TAB[:, kind, :, :].rearrange("p h d -> p (h d)")
                    R_q3a, R_q3b, R_q3c = ([Res(), Res()] for _ in range(3))
                    cx.op("pool", lambda e: e.memset(Sst[:], 0.0), writes=R_Sst)

                    def scan_update(d_, a):
                        fns = []
                        for p in range(2):
                            for hh in range(2):
                                rows = slice(hh * 64, hh * 64 + 64)
                                fns.append(lambda e, p=p, hh=hh, rows=rows: e.scalar_tensor_tensor(
                                    out=Sst[rows, d_, p, :], in0=Sst[rows, d_, p, :], scalar=Gfb[rows, d_, p:p + 1],
                                    in1=ov[a][rows, p * 256 + hh * 128:p * 256 + hh * 128 + 128], op0=ALU.mult, op1=ALU.add))
                        return fns

                    def b1_P(n):
                        i2 = n % 2
                        a = proj_tok(n, 0, 256, 256)
                        rope(mm[a][:, 0:256], R_mm[a], n, kr[i2][:], R_kr[i2], rt1[:], rt2[:], R_rt1, R_rt2)
                        cx.op("act", lambda e, i2=i2: e.activation(out=kbf[i2][:], in_=kr[i2][:], func=AF.Copy, scale=0.125),
                              reads=[R_kr[i2]], writes=[R_kbf[i2]])
                        cx.op("pool", lambda e, i2=i2: e.tensor_tensor(out=kdf[i2][:], in0=kr[i2][:], in1=TABv(0), op=ALU.mult),
                              reads=[R_kr[i2], R_tab], writes=[R_kdf[i2]])
                        cx.op("dve", lambda e, i2=i2, n=n: e.tensor_tensor(out=kdb[:, n, :], in0=kr[i2][:], in1=TABv(1), op=ALU.mult),
                              reads=[R_kr[i2], R_tab], writes=[R_kdb[n]])
                        a2 = proj_tok(n, 1, 0, 512)
                        cx.op("act", lambda e, a2=a2, n=n: e.activation(out=vb[:, n, :], in_=mm[a2][:, :], func=AF.Copy),
                              reads=[R_mm[a2]], writes=[R_vb[n]])

                    def b1_T(n):
                        i2 = n % 2
                        b = nxt("tp")
                        cx.op("pe", [lambda e, p=p, b=b, i2=i2: e.transpose(tp[b][:, p * 128:(p + 1) * 128], kbf[i2][:, p * 128:(p + 1) * 128], ident[:])
                                     for p in range(2)], reads=[R_kbf[i2], R_const], writes=[R_tp[b]])
                        cx.op("act", lambda e, b=b, n=n: e.activation(
                            out=kbT[:, :, n * 128:(n + 1) * 128], in_=tp[b][:, 0:256].rearrange("p (a c) -> p a c", a=2), func=AF.Copy),
                            reads=[R_tp[b]], writes=[R_kbT[n]])
                        cx.op("dve", lambda e, n=n: e.tensor_copy(out=SfB[:, n, :, :], in_=Sst[:, 0, :, :]),
                              reads=[R_Sst[0]], writes=[R_SfB[n]])
                        if n < NT - 1:
                            o = nxt("ov")
                            cx.op("pe", [lambda e, p=p, o=o, i2=i2, n=n: e.matmul(
                                ov[o][:, p * 256:(p + 1) * 256], lhsT=kdf[i2][:, p * 128:(p + 1) * 128],
                                rhs=vb[:, n, p * 256:(p + 1) * 256], start=True, stop=True) for p in range(2)],
                                reads=[R_kdf[i2], R_vb[n]], writes=[R_ov[o]])
                            cx.op("dve", scan_update(0, o), reads=[R_ov[o], R_tab, R_Sst[0]], writes=[R_Sst[0]])

                    _pipeline(NT, [(0, b1_P), (1, b1_T)])
                    b_stage = {"B1": 1, "Bb": 2}.get(stop, 3)
                    for n in (range(NT - 1, -1, -1) if b_stage >= 2 else []):
                        cx.op("dve", lambda e, n=n: e.tensor_copy(out=SbB[:, n, :, :], in_=Sst[:, 1, :, :]),
                              reads=[R_Sst[1]], writes=[R_SbB[n]])
                        if n > 0:
                            o = nxt("ov")
                            cx.op("pe", [lambda e, p=p, o=o, n=n: e.matmul(
                                ov[o][:, p * 256:(p + 1) * 256], lhsT=kdb[:, n, p * 128:(p + 1) * 128],
                                rhs=vb[:, n, p * 256:(p + 1) * 256], start=True, stop=True) for p in range(2)],
                                reads=[R_kdb[n], R_vb[n]], writes=[R_ov[o]])
                            cx.op("dve", scan_update(1, o), reads=[R_ov[o], R_tab, R_Sst[1]], writes=[R_Sst[1]])
                    def b2_P(n):
                        i2 = n % 2
                        a = proj_tok(n, 0, 0, 256)
                        rope(mm[a][:, 0:256], R_mm[a], n, kr[i2][:], R_kr[i2], rt1[:], rt2[:], R_rt1, R_rt2)
                        cx.op("act", lambda e, i2=i2: e.activation(out=q3[i2][:, 0, :], in_=kr[i2][:], func=AF.Copy),
                              reads=[R_kr[i2]], writes=[R_q3a[i2]])
                        cx.op("pool", lambda e, i2=i2: e.tensor_tensor(out=q3[i2][:, 1, :], in0=kr[i2][:], in1=TABv(2), op=ALU.mult),
                              reads=[R_kr[i2], R_tab], writes=[R_q3b[i2]])
                        cx.op("dve", lambda e, i2=i2: e.tensor_tensor(out=q3[i2][:, 2, :], in0=kr[i2][:], in1=TABv(3), op=ALU.mult),
                              reads=[R_kr[i2], R_tab], writes=[R_q3c[i2]])

                    def b2_T(n):
                        i2 = n % 2
                        b = nxt("tp")
                        cx.op("pe", [lambda e, v=v, p=p, b=b, i2=i2: e.transpose(
                            tp[b][:, (v * 2 + p) * 128:(v * 2 + p + 1) * 128], q3[i2][:, v, p * 128:(p + 1) * 128], ident[:])
                            for v in range(3) for p in range(2)], reads=[R_q3a[i2], R_q3b[i2], R_q3c[i2], R_const], writes=[R_tp[b]])
                        cx.op("act", lambda e, b=b, i2=i2: e.activation(
                            out=qT3[i2][:], in_=tp[b][:, 0:768].rearrange("p (a c) -> p a c", a=6), func=AF.Copy),
                            reads=[R_tp[b]], writes=[R_qT3[i2]])

                    def b2_T2(n):
                        i2 = n % 2
                        fns = []
                        for h in range(4):
                            p, hh = h // 2, h % 2
                            rows = slice(hh * 64, hh * 64 + 64)
                            fns.append(lambda e, p=p, hh=hh, rows=rows, i2=i2, n=n: e.matmul(
                                st[hh][:, p * 128:(p + 1) * 128], lhsT=kbT[rows, p, n * 128:(n + 1) * 128],
                                rhs=qT3[i2][rows, p, :], start=True, stop=True))
                        cx.op("pe", fns, reads=[R_kbT[n], R_qT3[i2]], writes=[R_st[0], R_st[1]])
                        inv = inT[i2][:].rearrange("p (a b t) -> p a b t", a=2, b=2)
                        for hh in range(2):
                            cx.op("dve", lambda e, hh=hh, inv=inv: e.tensor_tensor(
                                out=inv[:, :, hh, :], in0=st[hh][:, 0:256].rearrange("p (a t) -> p a t", a=2),
                                in1=DTt[:].rearrange("p (a b) t -> p a b t", b=2)[:, :, hh, :], op=ALU.mult),
                                reads=[R_st[hh], R_tab], writes=[R_inT[i2]])
                        a2 = proj_tok(n, 2, 0, 512)
                        cx.op("act", lambda e, a2=a2, i2=i2: e.activation(out=gsl[i2][:], in_=mm[a2][:, :], func=AF.Silu),
                              reads=[R_mm[a2]], writes=[R_gsl[i2]])

                    def b2_Y(n):
                        i2 = n % 2
                        o = nxt("ov")
                        fns = []
                        for h in range(4):
                            p, hh = h // 2, h % 2
                            rows = slice(hh * 64, hh * 64 + 64)
                            oc = slice(h * 128, (h + 1) * 128)
                            fns.append(lambda e, oc=oc, o=o, i2=i2, n=n: e.matmul(
                                ov[o][:, oc], lhsT=inT[i2][:, oc], rhs=vb[:, n, oc], start=True, stop=False))
                            fns.append(lambda e, oc=oc, o=o, i2=i2, n=n, p=p, rows=rows: e.matmul(
                                ov[o][:, oc], lhsT=qT3[i2][rows, 2 + p, :], rhs=SfB[rows, n, p, :], start=False, stop=False))
                            fns.append(lambda e, oc=oc, o=o, i2=i2, n=n, p=p, rows=rows: e.matmul(
                                ov[o][:, oc], lhsT=qT3[i2][rows, 4 + p, :], rhs=SbB[rows, n, p, :], start=False, stop=True))
                        cx.op("pe", fns, reads=[R_inT[i2], R_vb[n], R_qT3[i2], R_SfB[n], R_SbB[n]], writes=[R_ov[o]])
                        cx.op("pool", lambda e, i2=i2: e.memset(gst[i2][:, 0, :], 0.0), writes=[R_gst[i2]])
                        cx.op("act", [lambda e, h=h, o=o, i2=i2: e.activation(
                            out=junk2[:], in_=ov[o][:, h * 128:(h + 1) * 128], func=AF.Square, accum_out=gst[i2][:, 0, h:h + 1])
                            for h in range(4)], reads=[R_ov[o]], writes=[R_junk2, R_gst[i2]])
                        cx.op("dve", lambda e, i2=i2: e.tensor_scalar(out=gst[i2][:, 1, :], in0=gst[i2][:, 0, :], scalar1=1.0 / 128,
                                                                      scalar2=EPS, op0=ALU.mult, op1=ALU.add),
                              reads=[R_gst[i2]], writes=[R_gst[i2]])
                        cx.op("pool", lambda e, i2=i2: e.tensor_tensor(out=gst[i2][:, 2, :], in0=gst[i2][:, 1, :], in1=mhalf[:, 0:4], op=ALU.pow),
                              reads=[R_gst[i2], R_const], writes=[R_gst[i2]])
                        cx.op("dve", [lambda e, h=h, o=o, i2=i2: e.scalar_tensor_tensor(
                            out=yr[i2][:, h * 128:(h + 1) * 128], in0=ov[o][:, h * 128:(h + 1) * 128],
                            scalar=gst[i2][:, 2, h:h + 1], in1=gsl[i2][:, h * 128:(h + 1) * 128], op0=ALU.mult, op1=ALU.mult)
                            for h in range(4)], reads=[R_ov[o], R_gst[i2], R_gsl[i2]], writes=[R_yr[i2]])

                    def b2_Z(n):
                        i2 = n % 2
                        b = nxt("tp")
                        cx.op("pe", [lambda e, h=h, b=b, i2=i2: e.transpose(tp[b][:, h * 128:(h + 1) * 128], yr[i2][:, h * 128:(h + 1) * 128], ident[:])
                                     for h in range(4)], reads=[R_yr[i2], R_const], writes=[R_tp[b]])
                        cx.op("act", lambda e, b=b, n=n: e.activation(
                            out=yT[:, 4:8, n * 128:(n + 1) * 128], in_=tp[b][:, 0:512].rearrange("p (a c) -> p a c", a=4), func=AF.Copy),
                            reads=[R_tp[b]], writes=R_yT[4:8])

                    if b_stage >= 3:
                        load_w(1, win_a[l][:, :, 0:4, 0:128], 512)
                        _pipeline(NT, [(0, b2_P), (1, b2_T), (2, b2_Y), (3, b2_Z), (1, b2_T2)])
                    cx.barrier()
                if s == 0 and l == 0:
                    dump("yrT", yT[:, 4, :], R_yT[4:8])
                if stop in ("B1", "Bb", "B2a", "B2b", "B2c", "B2d"):
                    raise _Stop()
                chk("B")

                with ExitStack() as ph:
                    P = ph.enter_context
                    qT = PT(ph, "qT", [128, S], BF16)
                    gaT2 = [PT(ph, "gaT%d" % i, [128, S], BF16) for i in range(2)]
                    Vp = [PT(ph, "Vp%d" % i, [128, 20, 2, 128], BF16) for i in range(2)]
                    ACC = [PT(ph, "ACC%d" % i, [128, S], F32) for i in range(2)]
                    rt1 = PT(ph, "art1", [128, 256], F32)
                    rt2 = PT(ph, "art2", [128, 256], F32)
                    qkr = [PT(ph, "qkr%d" % i, [128, 256], BF16) for i in range(2)]
                    pt = [PT(ph, "pt%d" % i, [128, 512], BF16) for i in range(4)]
                    Rr = [PT(ph, "Rr%d" % i, [128, 512], F32) for i in range(2)]
                    Tm = [PT(ph, "Tm%d" % i, [128, 512], F32) for i in range(2)]
                    R_qT = [Res() for _ in range(NT)]
                    R_gaT2 = [[Res() for _ in range(4)] for _ in range(2)]
                    R_Vp = [Res(), Res()]
                    R_ACC = [[Res() for _ in range(4)] for _ in range(2)]
                    R_rt1, R_rt2 = Res(), Res()
                    R_qkr, R_pt, R_Rr, R_Tm = ([Res(), Res(), Res(), Res()] for _ in range(4))
                    sbank = [st[0], st[1], mm[0], mm[1]]
                    R_sbank = [R_st[0], R_st[1], R_mm[0], R_mm[1]]
                    def vp_init():
                        for i in range(2):
                            vflat = Vp[i][:].rearrange("p a b c -> p (a b) c")
                            for q_ in range(5):
                                cx.op("act", lambda e, i=i, q_=q_, vflat=vflat: e.activation(
                                    out=vflat[:, q_ * 8:(q_ + 1) * 8, :], in_=_ap(onesf[:, 0:128], [[0, 8], [1, 128]]), func=AF.Copy),
                                    reads=[R_const], writes=[R_Vp[i]])
                    if os.environ.get("DUMMY_INIT"):
                        for q_ in range(10):
                            cx.op("act", lambda e, q_=q_: e.activation(out=ACC[0][:, q_ * 128:(q_ + 1) * 128], in_=onesf[:, 0:128], func=AF.Copy),
                                  reads=[R_const], writes=[R_ACC[0]])
                    elif not os.environ.get("VPM_LATE") and not os.environ.get("SKIP_VPM"):
                        vp_init()
                    vpc = [0]
                    fin_pend = []
                    for j in range(4):
                        wslot = [1, 0, 2, 1][j]
                        gaT, R_gaT = gaT2[j % 2], R_gaT2[j % 2]
                        def a1_P(t, wslot=wslot, gaT=gaT, R_gaT=R_gaT):
                            i2 = t % 2
                            a = proj_tok(t, wslot, 0, 256)
                            rope(mm[a][:, 0:256], R_mm[a], t, qkr[i2][:], R_qkr[i2], rt1[:], rt2[:], R_rt1, R_rt2)
                            if t % 4 == 3:
                                g4 = t // 4
                                a = proj_feat(g4, wslot, 256)
                                cx.op("act", lambda e, a=a, g4=g4: e.activation(out=VT1[:, 64 + g4 * 512:64 + (g4 + 1) * 512], in_=mm[a][:, :], func=AF.Copy),
                                      reads=[R_mm[a]], writes=[R_VT1])
                                cx.op("act", lambda e, a=a, g4=g4: e.activation(
                                    out=VT2[:, :, 64 + g4 * 128:64 + (g4 + 1) * 128],
                                    in_=mm[a][:, :].rearrange("p (l r) -> p r l", r=4), func=AF.Copy),
                                    reads=[R_mm[a]], writes=[R_VT2])
                                a = proj_feat(g4, wslot, 384)
                                cx.op("act", lambda e, a=a, g4=g4: e.activation(out=gaT[:, g4 * 512:(g4 + 1) * 512], in_=mm[a][:, :], func=AF.Silu),
                                      reads=[R_mm[a]], writes=[R_gaT[g4]])

                        def a1_T(t):
                            i2 = t % 2
                            b = nxt("tp")
                            cx.op("pe", [lambda e, p=p, b=b, i2=i2: e.transpose(tp[b][:, p * 128:(p + 1) * 128], qkr[i2][:, p * 128:(p + 1) * 128], ident[:])
                                         for p in range(2)], reads=[R_qkr[i2], R_const], writes=[R_tp[b]])
                            def cp(out_ap, in_ap, R_out, t=t, b=b):
                                if t % 2 == 0:
                                    cx.op("act", lambda e: e.activation(out=out_ap, in_=in_ap, func=AF.Copy), reads=[R_tp[b]], writes=[R_out])
                                else:
                                    cx.op("dve", lambda e: e.tensor_copy(out=out_ap, in_=in_ap), reads=[R_tp[b]], writes=[R_out])
                            cp(qT[:, t * 128:(t + 1) * 128], tp[b][:, 0:128], R_qT[t])
                            cp(kT1[:, 64 + t * 128:64 + (t + 1) * 128], tp[b][:, 128:256], R_kT1)
                            cp(kT2[:, :, 64 + t * 32:64 + (t + 1) * 32], tp[b][:, 128:256].rearrange("p (l r) -> p r l", r=4), R_kT2)

                        _pipeline(NT, [(0, a1_P), (1, a1_T)])
                        if j == 0:
                            load_w(0, win_a[l][:, :, 0:4, 128:256], 512)
                            load_w(2, win_a[l][:, :, 0:4, 256:384], 512)
                        elif j == 1:
                            load_w(1, win_a[l][:, :, 0:4, 384:512], 512)
                        elif j == 2:
                            load_w(0, wout_v[l][:, :, 0:512], 512)
                        else:
                            load_w(2, wout_v[l][:, :, 512:1024], 512)
                            nl, ns = (l + 1, s) if l + 1 < depth else (0, s + 1)
                            if ns < nseq:
                                load_w(1, win_v[nl][:, :, 2560:3072], 512)

                        if os.environ.get("VPM_LATE") and j == 0:
                            vp_init()
                        def build_vp(vi, srcs, R_src):
                            for g0 in range(0, len(srcs), 4):
                                grp = srcs[g0:g0 + 4]
                                b = nxt("tp")
                                cx.op("pe", [lambda e, ii=ii, sa_=sa_, b=b: e.transpose(tp[b][:, ii * 128:(ii + 1) * 128], sa_, ident[:])
                                             for ii, (_, sa_) in enumerate(grp)], reads=[R_src, R_const], writes=[R_tp[b]])
                                i0 = grp[0][0]
                                ng = len(grp)
                                for hh in range(2):
                                    eng = "act" if (g0 // 4) % 2 == 0 else "dve"
                                    src = tp[b][:, 0:ng * 128].rearrange("p (a c) -> p a c", c=128)[:, :, hh * 64:hh * 64 + 64]
                                    dst = Vp[vi][:, i0:i0 + ng, hh, hh * 64:hh * 64 + 64]
                                    if eng == "act":
                                        cx.op("act", lambda e, src=src, dst=dst: e.activation(out=dst, in_=src, func=AF.Copy),
                                              reads=[R_tp[b]], writes=[R_Vp[vi]])
                                    else:
                                        cx.op("dve", lambda e, src=src, dst=dst: e.tensor_copy(out=dst, in_=src),
                                              reads=[R_tp[b]], writes=[R_Vp[vi]])

                        pend = []

                        def flush_pend():
                            while pend:
                                pend.pop(0)()

                        tpf = [tp[0][:].bitcast(F32), tp[1][:].bitcast(F32)]
                        obank = [[ov[0][:, :], tpf[0]], [ov[1][:, :], tpf[1]]]
                        R_obank = [[R_ov[0], R_tp[0]], [R_ov[1], R_tp[1]]]

                        def attend2(vi, items2, mask_variant, R_k, acc2):
                            fc = ctr["ov"]
                            ctr["ov"] += 1
                            nk = len(items2[0][0][1])
                            per = 512 // (128 * nk)
                            ngrp = 4 // per
                            for gi, g0 in enumerate(range(0, 4, per)):
                                base = (ctr["w"] % 2) * 2
                                ctr["w"] += 1
                                mv = mask_variant(g0)
                                fns = [lambda e, sa=base + hh, mv=mv: e.matmul(sbank[sa][:, :], lhsT=ident[:], rhs=maskA[:, mv, :],
                                                                               start=True, stop=False) for hh in range(2)]
                                order = [(ii, kk, hh) for ii in range(per) for kk in range(nk) for hh in range(2)]
                                if os.environ.get("NO_ILV"):
                                    order = [(ii, kk, hh) for hh in range(2) for ii in range(per) for kk in range(nk)]
                                for (ii, kk, hh) in order:
                                    if True:
                                        c0 = (ii * nk + kk) * 128
                                        if True:
                                            q_ap, ks = items2[hh][g0 + ii]
                                            k_ap = ks[kk][0]
                                            fns.append(lambda e, c0=c0, sa=base + hh, k_ap=k_ap, q_ap=q_ap: e.matmul(
                                                sbank[sa][:, c0:c0 + 128], lhsT=k_ap, rhs=q_ap, start=False, stop=True))
                                cx.op("pe", fns, reads=[R_k, R_const] + [R_qT[t] for t in range(NT)], writes=[R_sbank[base], R_sbank[base + 1]])
                                for hh in range(2):
                                    pi = base + hh
                                    cx.op("act", lambda e, pi=pi: e.activation(out=pt[pi][:], in_=sbank[pi][:, :], func=AF.Exp, scale=0.125),
                                          reads=[R_sbank[pi]], writes=[R_pt[pi]])

                                def stage2(g0=g0, gi=gi, base=base):
                                    for hh in range(2):
                                        pi = base + hh
                                        fsel = 0
                                        o_ap, R_o = obank[hh][fsel], R_obank[hh][fsel]
                                        fns = []
                                        for ii in range(per):
                                            _, ks = items2[hh][g0 + ii]
                                            qi = g0 + ii
                                            for kk, (_, vidx) in enumerate(ks):
                                                c0 = (ii * nk + kk) * 128
                                                fns.append(lambda e, c0=c0, qi=qi, vidx=vidx, kk=kk, hh=hh, pi=pi, o_ap=o_ap: e.matmul(
                                                    o_ap[:, qi * 128:(qi + 1) * 128], lhsT=Vp[vi][:, vidx, hh, :], rhs=pt[pi][:, c0:c0 + 128],
                                                    start=(kk == 0), stop=(kk == nk - 1)))
                                        cx.op("pe", fns, reads=[R_pt[pi], R_Vp[vi]], writes=[R_o])
                                        if gi == ngrp - 1:
                                            acc2(hh, o_ap, R_o)

                                pend.append(stage2)
                                while len(pend) > 1:
                                    pend.pop(0)()

                        a_st = {"A1": 0, "Ap0": 1, "Ap1": 2, "Ap2": 3}.get(stop, 9)
                        if a_st < 9 and j > 0:
                            continue
                        hrows = [slice(0, 64), slice(64, 128)]
                        for pat in range(min(3, a_st)):
                            vi = vpc[0] % 2
                            vpc[0] += 1
                            if pat == 0:
                                build_vp(vi, [(jt, VT1[:, 128 * jt:128 * jt + 128]) for jt in range(17)], R_VT1)
                                for u in range(4):
                                    items2 = [[(qT[rows, 128 * i_:128 * i_ + 128],
                                                [(kT1[rows, 128 * i_:128 * i_ + 128], i_),
                                                 (kT1[rows, 128 * (i_ + 1):128 * (i_ + 1) + 128], i_ + 1)])
                                               for i_ in range(4 * u, 4 * u + 4)] for rows in hrows]
                                    mvf = lambda g0, u=u: (0 if (u == 0 and g0 == 0) else (2 if (u == 3 and g0 == 2) else 1))
                                    acc2 = lambda hh, o_ap, R_o, u=u: cx.op("dve", lambda e: e.tensor_copy(
                                        out=ACC[hh][:, 512 * u:512 * (u + 1)], in_=o_ap),
                                        reads=[R_o], writes=[R_ACC[hh][u]])
                                    for _ in range(2):
                                        if fin_pend:
                                            fin_pend.pop(0)()
                                    attend2(vi, items2, mvf, R_kT1, acc2)
                            elif pat == 1:
                                build_vp(vi, [(r * 5 + jt, VT2[:, r, 128 * jt:128 * jt + 128]) for r in range(4) for jt in range(5)], R_VT2)
                                for r in range(4):
                                    items2 = [[(qT[rows, 512 * i_ + r:512 * (i_ + 1):4],
                                                [(kT2[rows, r, 128 * i_:128 * i_ + 128], r * 5 + i_),
                                                 (kT2[rows, r, 128 * (i_ + 1):128 * (i_ + 1) + 128], r * 5 + i_ + 1)])
                                               for i_ in range(4)] for rows in hrows]
                                    mvf = lambda g0: (0 if g0 == 0 else 2)
                                    acc2 = lambda hh, o_ap, R_o, r=r: cx.op("dve", lambda e: e.tensor_tensor(
                                        out=ACC[hh][:, r:S:4], in0=o_ap, in1=ACC[hh][:, r:S:4], op=ALU.add),
                                        reads=[R_o] + R_ACC[hh], writes=R_ACC[hh])
                                    attend2(vi, items2, mvf, R_kT2, acc2)
                            else:
                                build_vp(vi, [(r, VT1[:, 64 + r:64 + S:16]) for r in range(16)], R_VT1)
                                for r0 in range(0, 16, 4):
                                    items2 = [[(qT[rows, r:S:16], [(kT1[rows, 64 + r:64 + S:16], r)])
                                               for r in range(r0, r0 + 4)] for rows in hrows]
                                    mvf = lambda g0: 3

                                    def acc2(hh, o_ap, R_o, r0=r0):
                                        accv = ACC[hh][:].rearrange("p (l r) -> p r l", r=16)[:, r0:r0 + 4, :]
                                        cx.op("dve", lambda e: e.tensor_tensor(
                                            out=accv, in0=o_ap.rearrange("p (r l) -> p r l", r=4), in1=accv, op=ALU.add),
                                            reads=[R_o] + R_ACC[hh], writes=R_ACC[hh])
                                    attend2(vi, items2, mvf, R_kT1, acc2)
                        flush_pend()

                        def fin_step(hh, u, j=j, gaT=gaT, R_gaT=R_gaT):
                            nr = slice(hh * 64, hh * 64 + 64)
                            dr = slice((1 - hh) * 64, (1 - hh) * 64 + 64)
                            cs = slice(512 * u, 512 * (u + 1))
                            i2 = ctr["fin"] % 2
                            ctr["fin"] += 1
                            cx.op("act", lambda e: e.activation(out=Rr[i2][nr, :], in_=ACC[hh][dr, cs], func=AF.Ln),
                                  reads=[R_ACC[hh][u]], writes=[R_Rr[i2]])
                            cx.op("act", lambda e: e.activation(out=Rr[i2][nr, :], in_=Rr[i2][nr, :], func=AF.Exp, scale=-1.0),
                                  reads=[R_Rr[i2]], writes=[R_Rr[i2]])
                            cx.op("dve", lambda e: e.tensor_tensor(out=Tm[i2][nr, :], in0=ACC[hh][nr, cs], in1=Rr[i2][nr, :], op=ALU.mult),
                                  reads=[R_ACC[hh][u], R_Rr[i2]], writes=[R_Tm[i2]])
                            cx.op("pool", lambda e: e.tensor_tensor(out=yT[nr, j, cs], in0=Tm[i2][nr, :], in1=gaT[nr, cs], op=ALU.mult),
                                  reads=[R_Tm[i2], R_gaT[u]], writes=[R_yT[j]])

                        if a_st >= 9:
                            for u in range(4):
                                for hh in range(2):
                                    fin_pend.append(lambda hh=hh, u=u, f=fin_step: f(hh, u))
                    while fin_pend:
                        fin_pend.pop(0)()
                    cx.barrier()
                if s == 0 and l == 0:
                    dump("yaT", yT[:, 0, :], R_yT[0:4])
                if stop in ("A1", "Ap0", "Ap1", "Ap2"):
                    raise _Stop()
                chk("A")

                with ExitStack() as ph:
                    P = ph.enter_context
                    gate_b = PT(ph, "gate_b", [128, D], F32)
                    gfin_b = PT(ph, "gfin_b", [128, D], F32)
                    R_gfin = Res()
                    if last:
                        cx.dma("sp", "cst", lambda e: e.dma_start(out=gfin_b[:], in_=gfin_d.partition_broadcast(128)), writes=[R_gfin])
                    xin = [PT(ph, "oxin%d" % i, [128, D], F32) for i in range(2)]
                    xo = [PT(ph, "xo%d" % i, [128, D], F32) for i in range(2)]
                    Dk = [PT(ph, "Dk%d" % i, [128, 128], F32) for i in range(2)]
                    junk = PT(ph, "ojunk", [128, D], BF16)
                    fst = [PT(ph, "fst%d" % i, [128, 3], F32) for i in range(2)]
                    R_xin, R_xo, R_Dk, R_fst = ([Res(), Res()] for _ in range(4))
                    R_junk = Res()
                    if not last:
                        oxn = [PT(ph, "oxn%d" % i, [128, 4, D], BF16) for i in range(2)]
                        ostat = PT(ph, "ostat", [128, 3, NT], F32)
                        R_oxn = [Res(), Res()]
                        R_ostat = [Res() for _ in range(NT)]
                    h_pend = []

                    def o_H(g4):
                        xb = oxn[g4 % 2]
                        for k in range(8):
                            b = nxt("tp")
                            cx.op("pe", [lambda e, jj=jj, k=k, b=b, xb=xb: e.transpose(tp[b][:, jj * 128:(jj + 1) * 128],
                                                                                   xb[:, jj, k * 128:(k + 1) * 128], ident[:])
                                         for jj in range(4)], reads=[R_oxn[g4 % 2], R_const], writes=[R_tp[b]])
                            cx.op("act", lambda e, k=k, b=b, g4=g4: e.activation(
                                out=hT[:, k, g4 * 512:(g4 + 1) * 512], in_=tp[b][:, 0:512], func=AF.Identity,
                                scale=gsA[:, l + 1, s, k:k + 1], bias=shA[:, l + 1, s, k:k + 1]),
                                reads=[R_tp[b], R_mod], writes=[R_hT[g4]])
                    for k in range(8):
                        i2 = k % 2
                        cx.op("dve", lambda e, k=k, i2=i2: e.tensor_scalar(out=Dk[i2][:], in0=identf[:], scalar1=gtA[:, l, s, k:k + 1],
                                                                         scalar2=None, op0=ALU.mult),
                              reads=[R_const, R_mod], writes=[R_Dk[i2]])
                        a = nxt("mm")
                        cx.op("pe", lambda e, a=a, i2=i2: e.matmul(mm[a][:, 0:128], lhsT=onesf[:], rhs=Dk[i2][:], start=True, stop=True),
                              reads=[R_Dk[i2], R_const], writes=[R_mm[a]])
                        cx.op("act", lambda e, a=a, k=k: e.activation(out=gate_b[:, k * 128:(k + 1) * 128], in_=mm[a][:, 0:128], func=AF.Copy),
                              reads=[R_mm[a]], writes=[R_gate])
                    for t in range(NT):
                        i2 = t % 2
                        cx.dma("sp", "xin%d" % i2, lambda e, t=t, i2=i2: e.dma_start(out=xin[i2][:], in_=xsrc[s, t * 128:(t + 1) * 128, :]),
                               reads=[R_yd[s][t]], writes=[R_xin[i2]])
                        for half in range(2):
                            a = nxt("mm")
                            wslot = 0 if half == 0 else 2
                            hs = slice(half * 512, (half + 1) * 512)
                            cx.op("pe", [lambda e, kc=kc, a=a, wslot=wslot, t=t: e.matmul(
                                mm[a][:, :], lhsT=yT[:, kc, t * 128:(t + 1) * 128], rhs=wsl[wslot][:, kc, :],
                                start=(kc == 0), stop=(kc == 7)) for kc in range(8)],
                                reads=R_yT + [R_w[wslot]], writes=[R_mm[a]])
                            cx.op("dve", lambda e, a=a, hs=hs, i2=i2: e.tensor_tensor(out=xo[i2][:, hs], in0=mm[a][:, :], in1=gate_b[:, hs], op=ALU.mult),
                                  reads=[R_mm[a], R_gate], writes=[R_xo[i2]])
                        cx.op("pool", lambda e, i2=i2: e.tensor_tensor(out=xo[i2][:], in0=xo[i2][:], in1=xin[i2][:], op=ALU.add),
                              reads=[R_xo[i2], R_xin[i2]], writes=[R_xo[i2]])
                        if last:
                            cx.op("pool", lambda e, i2=i2: e.memset(fst[i2][:, 0:1], 0.0), writes=[R_fst[i2]])
                            cx.op("act", lambda e, i2=i2: e.activation(out=junk[:], in_=xo[i2][:], func=AF.Square, accum_out=fst[i2][:, 0:1]),
                                  reads=[R_xo[i2]], writes=[R_junk, R_fst[i2]])
                            cx.op("act", lambda e, i2=i2: e.activation(out=fst[i2][:, 1:2], in_=fst[i2][:, 0:1], func=AF.Sqrt, scale=1.0 / D, bias=EPS),
                                  reads=[R_fst[i2]], writes=[R_fst[i2]])
                            cx.op("dve", lambda e, i2=i2: e.reciprocal(out=fst[i2][:, 2:3], in_=fst[i2][:, 1:2]),
                                  reads=[R_fst[i2]], writes=[R_fst[i2]])
                            cx.op("dve", lambda e, i2=i2: e.scalar_tensor_tensor(
                                out=xo[i2][:], in0=xo[i2][:], scalar=fst[i2][:, 2:3], in1=gfin_b[:], op0=ALU.mult, op1=ALU.mult),
                                reads=[R_xo[i2], R_fst[i2], R_gfin], writes=[R_xo[i2]])
                        cx.dma("sp", "xout%d" % i2, lambda e, t=t, i2=i2: e.dma_start(out=y_d[s, t * 128:(t + 1) * 128, :], in_=xo[i2][:]),
                               reads=[R_xo[i2]], writes=[R_yd[s][t]])
                        if not last:
                            g4, jj = t // 4, t % 4
                            cx.op("pool", lambda e, t=t: e.memset(ostat[:, 0, t:t + 1], 0.0), writes=[R_ostat[t]])
                            cx.op("act", lambda e, t=t, i2=i2: e.activation(out=junk[:], in_=xo[i2][:], func=AF.Square,
                                                                           accum_out=ostat[:, 0, t:t + 1]),
                                  reads=[R_xo[i2]], writes=[R_junk, R_ostat[t]])
                            cx.op("act", lambda e, t=t: e.activation(out=ostat[:, 1, t:t + 1], in_=ostat[:, 0, t:t + 1], func=AF.Sqrt,
                                                                     scale=1.0 / D, bias=EPS),
                                  reads=[R_ostat[t]], writes=[R_ostat[t]])
                            cx.op("dve", lambda e, t=t: e.reciprocal(out=ostat[:, 2, t:t + 1], in_=ostat[:, 1, t:t + 1]),
                                  reads=[R_ostat[t]], writes=[R_ostat[t]])
                            cx.op("dve", lambda e, t=t, i2=i2, jj=jj, g4=g4: e.tensor_scalar(
                                out=oxn[g4 % 2][:, jj, :], in0=xo[i2][:], scalar1=ostat[:, 2, t:t + 1], scalar2=None, op0=ALU.mult),
                                reads=[R_xo[i2], R_ostat[t]], writes=[R_oxn[g4 % 2]])
                            if h_pend and h_pend[0][0] <= t:
                                o_H(h_pend.pop(0)[1])
                            if jj == 3:
                                h_pend.append((t + 2, g4))
                    while h_pend:
                        o_H(h_pend.pop(0)[1])
                    cx.barrier()
        except _Stop:
            pass
        cx.barrier()
    return nc


def _consts():
    f32 = np.float32
    pos = np.arange(S, dtype=np.float32)
    inv = (10000.0 ** (-np.arange(0, 64, 2, dtype=np.float32) / 64)).astype(f32)
    ang = (pos[:, None] * inv[None, :]).astype(f32)
    cos, sin = np.cos(ang).astype(f32), np.sin(ang).astype(f32)
    C2 = np.concatenate([cos, cos], axis=1)
    S2 = np.concatenate([-sin, sin], axis=1)
    rope = np.stack([C2.reshape(NT, 128, 64).transpose(1, 0, 2), S2.reshape(NT, 128, 64).transpose(1, 0, 2)], axis=1)
    p = np.arange(128)[:, None]
    c = np.arange(128)[None, :]
    A = (c <= p).astype(f32)
    B = (p <= c).astype(f32)
    A_first = A * (p >= 64)
    B_last = B * (p < 64)
    m_norm = np.concatenate([A, B], axis=1)
    m_first = np.concatenate([A_first, B], axis=1)
    m_last = np.concatenate([A, B_last], axis=1)
    band = (np.abs(p - c) <= 64).astype(f32)
    mask = np.stack([np.concatenate([m_first, m_norm], 1), np.concatenate([m_norm, m_norm], 1),
                     np.concatenate([m_norm, m_last], 1), np.concatenate([band] * 4, 1)], axis=1)
    diff = (c - p).astype(f32)
    ret = np.stack([np.maximum(diff, 0), (diff >= 0).astype(f32), np.maximum(-diff, 0), (diff < 0).astype(f32)], axis=1)
    tau = np.arange(128, dtype=f32)
    taus = np.stack([127 - tau, tau, tau + 1, 128 - tau], axis=1)
    mask = (mask - 1.0) * 30000.0
    return dict(cst_rope=np.ascontiguousarray(rope, f32), cst_mask=np.ascontiguousarray(mask, f32),
                cst_ret=np.ascontiguousarray(ret, f32), cst_tau=np.ascontiguousarray(taus, f32),
                cst_ident=np.eye(128, dtype=f32))


def kernel(x_prompt, x_sample, c_prompt, c_sample, g_norm, w_ada, b_ada, w_in, w_out,
           decay_fwd, decay_bwd, g_final):
    f = lambda a: np.ascontiguousarray(np.asarray(a), dtype=np.float32)
    xs = np.concatenate([f(x_prompt), f(x_sample)], axis=0)
    cs = np.concatenate([f(c_prompt), f(c_sample)], axis=0)
    shared = dict(g_norm=f(g_norm), w_ada=f(w_ada), b_ada=f(b_ada), w_in=f(w_in), w_out=f(w_out),
                  decay_fwd=f(decay_fwd), decay_bwd=f(decay_bwd), g_final=f(g_final))
    shared.update(_consts())
    nc = build_nc()
    in_maps = []
    for i in range(NCORES):
        m = dict(shared)
        m["x"] = np.ascontiguousarray(xs[i * NSEQ:(i + 1) * NSEQ])
        m["c"] = np.ascontiguousarray(cs[i * NSEQ:(i + 1) * NSEQ])
        in_maps.append(m)
    res = run_bass_kernel_spmd(nc, in_maps, core_ids=list(range(NCORES)))
    ys = np.concatenate([np.asarray(r["y"], dtype=np.float32) for r in res.results], axis=0)
    nb = np.asarray(x_prompt).shape[0]
    return (np.ascontiguousarray(ys[:nb]), np.ascontiguousarray(ys[nb:]))
```

```python
import math
import os
from contextlib import ExitStack

import numpy as np
import concourse.bass as bass
import concourse.mybir as mybir
from concourse.bass_utils import run_bass_kernel_spmd

F32 = mybir.dt.float32
BF16 = mybir.dt.bfloat16
AF = mybir.ActivationFunctionType
ALU = mybir.AluOpType

D = 1024
S = 2048
NT = 16
DEPTH = 2
NCORES = 8
NSEQ = 3
INW = 3584
EPS = 1e-6


class Tok:
    __slots__ = ("eng", "key", "val")

    def __init__(self, eng, key, val):
        self.eng, self.key, self.val = eng, key, val


class Res:
    __slots__ = ("name", "w", "r")

    def __init__(self, name=""):
        self.name, self.w, self.r = name, None, []


class Ctx:
    def __init__(self, nc, es):
        self.nc, self.es = nc, es
        self.engs = {"pe": nc.tensor, "act": nc.scalar, "dve": nc.vector, "pool": nc.gpsimd, "sp": nc.sync}
        self.sems, self.cnt = {}, {}
        self.seen = {e: {} for e in self.engs}
        self.epoch = 0
        self.dma_keys = set()
        self.new_epoch()

    def _mksem(self, key):
        if key not in self.sems:
            self.sems[key] = self.es.enter_context(self.nc.semaphore(key))
            self.cnt[key] = 0

    def new_epoch(self):
        self.epoch += 1
        self.ekey = {e: "%s%d" % (e, self.epoch) for e in self.engs if e != "sp"}
        for k in self.ekey.values():
            self._mksem(k)

    def _waits(self, eng, reads, writes, skipkey=None):
        deps = {}

        def need(tok, kind):
            if tok is None or tok.key == skipkey:
                return
            if tok.eng == eng and (eng == "pe" or kind != "RAW"):
                return
            if deps.get(tok.key, 0) < tok.val:
                deps[tok.key] = tok.val

        for r in reads:
            need(r.w, "RAW")
        for w in writes:
            need(w.w, "WAW")
            for t in w.r:
                need(t, "WAR")
        E, seen = self.engs[eng], self.seen[eng]
        for key, val in deps.items():
            if seen.get(key, 0) < val:
                E.wait_ge(self.sems[key], val)
                seen[key] = val

    def _commit(self, tok, reads, writes):
        for r in reads:
            r.r = [t for t in r.r if t.key != tok.key] + [tok]
        for w in writes:
            w.w, w.r = tok, []

    def op(self, eng, fns, reads=(), writes=()):
        self._waits(eng, reads, writes)
        if not isinstance(fns, (list, tuple)):
            fns = [fns]
        ins = None
        for f in fns:
            ins = f(self.engs[eng])
        key = self.ekey[eng]
        self.cnt[key] += 1
        ins.then_inc(self.sems[key], 1)
        tok = Tok(eng, key, self.cnt[key])
        self._commit(tok, reads, writes)
        return tok

    def dma(self, q, semname, fn, reads=(), writes=()):
        key = "d%d_%s" % (self.epoch, semname)
        self._mksem(key)
        self.dma_keys.add(key)
        self._waits(q, reads, writes, skipkey=key)
        ins = fn(self.engs[q])
        self.cnt[key] += 16
        ins.then_inc(self.sems[key], 16)
        tok = Tok("dma", key, self.cnt[key])
        self._commit(tok, reads, writes)
        return tok

    def barrier(self, with_dma=True):
        keys = list(self.ekey.values())
        if with_dma:
            keys += sorted(self.dma_keys)
        for e, E in self.engs.items():
            seen = self.seen[e]
            for key in keys:
                val = self.cnt[key]
                if val > 0 and seen.get(key, 0) < val:
                    E.wait_ge(self.sems[key], val)
                    seen[key] = val


def _pipeline(n_iter, stages):
    mx = max(sk for sk, _ in stages)
    for i in range(n_iter + mx):
        for sk, fn in stages:
            n = i - sk
            if 0 <= n < n_iter:
                fn(n)


def _ap(base, dims):
    return bass.AP(base.tensor, base.offset, [list(base.ap[0])] + [list(d) for d in dims])


class _Stop(Exception):
    pass


def build_nc(nseq=NSEQ, depth=DEPTH, dbg=(), stop=None):
    nc = bass.Bass("TRN2", target_bir_lowering=False)
    dt = nc.dram_tensor
    x_d = dt("x", [nseq, S, D], F32, kind="ExternalInput").ap()
    c_d = dt("c", [nseq, D], F32, kind="ExternalInput").ap()
    gn_d = dt("g_norm", [DEPTH, D], F32, kind="ExternalInput").ap()
    wada_d = dt("w_ada", [DEPTH, D, 3 * D], F32, kind="ExternalInput").ap()
    bada_d = dt("b_ada", [DEPTH, 3 * D], F32, kind="ExternalInput").ap()
    win_d = dt("w_in", [DEPTH, D, INW], F32, kind="ExternalInput").ap()
    wout_d = dt("w_out", [DEPTH, D, D], F32, kind="ExternalInput").ap()
    dfw_d = dt("decay_fwd", [DEPTH, 4], F32, kind="ExternalInput").ap()
    dbw_d = dt("decay_bwd", [DEPTH, 4], F32, kind="ExternalInput").ap()
    gfin_d = dt("g_final", [D], F32, kind="ExternalInput").ap()
    crope_d = dt("cst_rope", [128, 2, NT, 64], F32, kind="ExternalInput").ap()
    cmask_d = dt("cst_mask", [128, 4, 512], F32, kind="ExternalInput").ap()
    cret_d = dt("cst_ret", [128, 4, 128], F32, kind="ExternalInput").ap()
    ctau_d = dt("cst_tau", [128, 4], F32, kind="ExternalInput").ap()
    cid_d = dt("cst_ident", [128, 128], F32, kind="ExternalInput").ap()
    y_d = dt("y", [nseq, S, D], F32, kind="ExternalOutput").ap()
    dbg_d = {}
    for name, shape in dbg:
        dbg_d[name] = dt("dbg_" + name, list(shape), F32, kind="ExternalOutput").ap()

    with ExitStack() as es:
        E = es.enter_context
        cx = Ctx(nc, es)
        sb = lambda name, shape, dtype: E(nc.sbuf_tensor(name, list(shape), dtype))
        uid = [0]

        def PT(ph, name, shape, dtype):
            uid[0] += 1
            return ph.enter_context(nc.sbuf_tensor("%s_u%d" % (name, uid[0]), list(shape), dtype))

        hT = sb("hT", [128, 8, S], BF16)
        yT = sb("yT", [128, 8, S], BF16)
        wsl = [sb("wsl%d" % i, [128, 8, 512], BF16) for i in range(3)]
        kT1 = sb("kT1", [128, S + 128], BF16)
        kT2 = sb("kT2", [128, 4, 640], BF16)
        VT1 = sb("VT1", [128, S + 128], BF16)
        VT2 = sb("VT2", [128, 4, 640], BF16)
        ident = sb("ident", [128, 128], BF16)
        identf = sb("identf", [128, 128], F32)
        onesf = sb("onesf", [128, 128], F32)
        rope_t = sb("rope_t", [128, 2, NT, 64], F32)
        maskA = sb("maskA", [128, 4, 512], BF16)
        cret = sb("cret", [128, 4, 128], F32)
        ctau = sb("ctau", [128, 4], F32)
        DTt_all = sb("DTt", [128, DEPTH, 4, 128], F32)
        TAB_all = sb("TAB", [128, DEPTH, 4, 4, 64], F32)
        Gfb_all = sb("Gfb", [128, DEPTH, 2, 2], F32)
        gsA = sb("gsA", [128, DEPTH, nseq, 8], F32)
        shA = sb("shA", [128, DEPTH, nseq, 8], F32)
        gtA = sb("gtA", [128, DEPTH, nseq, 8], F32)
        mhalf = sb("mhalf", [128, 16], F32)

        mm = [E(nc.psum_tensor("mm%d" % i, [128, 512], F32)) for i in range(2)]
        tp = [E(nc.psum_tensor("tp%d" % i, [128, 1024], BF16)) for i in range(2)]
        st = [E(nc.psum_tensor("st%d" % i, [128, 512], F32)) for i in range(2)]
        ov = [E(nc.psum_tensor("ov%d" % i, [128, 512], F32)) for i in range(2)]
        R_mm = [Res("mm0"), Res("mm1")]
        R_tp = [Res("tp0"), Res("tp1")]
        R_st = [Res("st0"), Res("st1")]
        R_ov = [Res("ov0"), Res("ov1")]
        ctr = {"mm": 0, "tp": 0, "st": 0, "ov": 0, "w": 0, "fin": 0}

        def nxt(kind):
            i = ctr[kind] % 2
            ctr[kind] += 1
            return i

        R_const = Res("const")
        R_hT = [Res("hT%d" % g) for g in range(4)]
        R_yT = [Res("yT%d" % k) for k in range(8)]
        R_w = [Res("w%d" % i) for i in range(3)]
        R_kT1, R_kT2, R_VT1, R_VT2 = Res("kT1"), Res("kT2"), Res("VT1"), Res("VT2")
        R_tab, R_gate, R_mod = Res("tab"), Res("gate"), Res("mod")
        R_yd = [[Res("yd%d_%d" % (s, t)) for t in range(NT)] for s in range(nseq)]
        dump_list = []

        def dump(name, src_ap, reads):
            if name in dbg_d:
                dump_list.append(cx.dma("pool", "dbg", lambda e: e.dma_start(out=dbg_d[name], in_=src_ap), reads=reads))

        cx.dma("sp", "cst", lambda e: e.dma_start(out=rope_t[:], in_=crope_d), writes=[R_const])
        cx.dma("pool", "cstp", lambda e: e.dma_start(out=maskA[:], in_=cmask_d), writes=[R_const])
        cx.dma("sp", "cst", lambda e: e.dma_start(out=cret[:], in_=cret_d), writes=[R_const])
        cx.dma("sp", "cst", lambda e: e.dma_start(out=ctau[:], in_=ctau_d), writes=[R_const])
        cx.dma("pool", "cstp", lambda e: e.dma_start(out=ident[:], in_=cid_d), writes=[R_const])
        cx.dma("sp", "cst", lambda e: e.dma_start(out=identf[:], in_=cid_d), writes=[R_const])
        cx.op("pool", lambda e: e.memset(onesf[:], 1.0), writes=[R_const])
        cx.op("pool", lambda e: e.memset(mhalf[:], -0.5), writes=[R_const])
        cx.op("pool", lambda e: e.memset(kT1[:], 0.0), writes=[R_kT1])
        cx.op("pool", lambda e: e.memset(kT2[:], 0.0), writes=[R_kT2])
        cx.op("pool", lambda e: e.memset(VT1[:], 0.0), writes=[R_VT1])
        cx.op("pool", lambda e: e.memset(VT2[:], 0.0), writes=[R_VT2])

        with ExitStack() as ph:
            P = ph.enter_context
            wada = PT(ph, "wada", [128, 8, 3 * D], BF16)
            cT = PT(ph, "cT", [128, nseq, 8], F32)
            silc = PT(ph, "silc", [128, 8, nseq], BF16)
            badaT = PT(ph, "badaT", [128, 24], F32)
            gnT = PT(ph, "gnT", [128, 8], F32)
            modT = PT(ph, "modT", [128, 24, nseq], F32)
            R_wada, R_cT, R_silc, R_bada, R_gn, R_modT = (Res() for _ in range(6))
            for s in range(nseq):
                cx.dma("sp", "cst", lambda e, s=s: e.dma_start(
                    out=cT[:, s, :], in_=c_d[s].rearrange("(k p) -> p k", p=128), allow_slow_non_contiguous=True),
                    writes=[R_cT])
            cx.op("act", lambda e: e.activation(out=silc[:].rearrange("p k s -> p s k"), in_=cT[:], func=AF.Silu),
                  reads=[R_cT], writes=[R_silc])
            for l in range(depth):
                for i in range(6):
                    cx.dma("pool", "wada", lambda e, i=i, l=l: e.dma_start(
                        out=wada[:, :, i * 512:(i + 1) * 512],
                        in_=wada_d[l].rearrange("(k p) n -> p k n", p=128)[:, :, i * 512:(i + 1) * 512]),
                        writes=[R_wada])
                cx.dma("sp", "cst", lambda e, l=l: e.dma_start(
                    out=badaT[:], in_=bada_d[l].rearrange("(o p) -> p o", p=128), allow_slow_non_contiguous=True),
                    writes=[R_bada])
                cx.dma("sp", "cst", lambda e, l=l: e.dma_start(
                    out=gnT[:], in_=gn_d[l].rearrange("(k p) -> p k", p=128), allow_slow_non_contiguous=True),
                    writes=[R_gn])
                fns = []
                for oc in range(24):
                    for kc in range(8):
                        fns.append(lambda e, oc=oc, kc=kc: e.matmul(
                            mm[0][:, oc * nseq:(oc + 1) * nseq], lhsT=wada[:, kc, oc * 128:(oc + 1) * 128],
                            rhs=silc[:, kc, :], start=(kc == 0), stop=(kc == 7)))
                cx.op("pe", fns, reads=[R_wada, R_silc], writes=[R_mm[0]])
                mmv = mm[0][:, 0:24 * nseq].rearrange("p (o s) -> p o s", s=nseq)
                for s in range(nseq):
                    cx.op("dve", lambda e, s=s: e.tensor_tensor(out=modT[:, :, s], in0=mmv[:, :, s], in1=badaT[:], op=ALU.add),
                          reads=[R_mm[0], R_bada], writes=[R_modT])
                for s in range(nseq):
                    cx.op("dve", lambda e, s=s, l=l: e.scalar_tensor_tensor(
                        out=gsA[:, l, s, :], in0=modT[:, 8:16, s], scalar=1.0, in1=gnT[:], op0=ALU.add, op1=ALU.mult),
                        reads=[R_modT, R_gn], writes=[R_mod])
                    cx.op("dve", lambda e, s=s, l=l: e.tensor_copy(out=shA[:, l, s, :], in_=modT[:, 0:8, s]),
                          reads=[R_modT], writes=[R_mod])
                    cx.op("dve", lambda e, s=s, l=l: e.tensor_copy(out=gtA[:, l, s, :], in_=modT[:, 16:24, s]),
                          reads=[R_modT], writes=[R_mod])
            cx.barrier()

        for l in range(depth):
            DTt, TAB, Gfb = DTt_all[:, l], TAB_all[:, l], Gfb_all[:, l]
            with ExitStack() as ph:
                P = ph.enter_context
                dfb = PT(ph, "dfb", [128, 8], F32)
                lg = PT(ph, "lg", [128, 8], F32)
                dsc = PT(ph, "dsc", [128, 4, 4], F32)
                e1 = PT(ph, "e1", [128, 128], F32)
                e2 = PT(ph, "e2", [128, 128], F32)
                R_dfb, R_lg, R_dsc, R_e1, R_e2 = (Res() for _ in range(5))
                cx.dma("sp", "cst", lambda e: e.dma_start(out=dfb[:, 0:4], in_=dfw_d[l].partition_broadcast(128)), writes=[R_dfb])
                cx.dma("sp", "cst", lambda e: e.dma_start(out=dfb[:, 4:8], in_=dbw_d[l].partition_broadcast(128)), writes=[R_dfb])
                cx.op("act", lambda e: e.activation(out=lg[:], in_=dfb[:], func=AF.Exp, scale=-1.0), reads=[R_dfb], writes=[R_lg])
                cx.op("act", lambda e: e.activation(out=lg[:], in_=lg[:], func=AF.Ln, bias=1.0), reads=[R_lg], writes=[R_lg])
                cx.op("dve", lambda e: e.tensor_scalar(out=lg[:], in0=lg[:], scalar1=-1.0, scalar2=None, op0=ALU.mult),
                      reads=[R_lg], writes=[R_lg])
                for kind in range(4):
                    lo = 0 if kind in (0, 2) else 4
                    cx.op("act", lambda e, kind=kind, lo=lo: e.activation(
                        out=dsc[:, kind, :], in_=lg[:, lo:lo + 4], func=AF.Exp, scale=ctau[:, kind:kind + 1]),
                        reads=[R_lg, R_const], writes=[R_dsc])
                for kind in range(4):
                    for h in range(4):
                        cx.op("dve", lambda e, kind=kind, h=h: e.tensor_scalar(
                            out=TAB[:, kind, h, :], in0=onesf[:, 0:64], scalar1=dsc[:, kind, h:h + 1],
                            scalar2=(0.125 if kind < 2 else 1.0), op0=ALU.mult, op1=ALU.mult),
                            reads=[R_dsc, R_const], writes=[R_tab])
                for h in range(4):
                    cx.op("act", lambda e, h=h: e.activation(out=e1[:], in_=cret[:, 0, :], func=AF.Exp, scale=lg[:, h:h + 1]),
                          reads=[R_lg, R_const], writes=[R_e1])
                    cx.op("dve", lambda e: e.tensor_tensor(out=e1[:], in0=e1[:], in1=cret[:, 1, :], op=ALU.mult),
                          reads=[R_e1, R_const], writes=[R_e1])
                    cx.op("act", lambda e, h=h: e.activation(out=e2[:], in_=cret[:, 2, :], func=AF.Exp, scale=lg[:, 4 + h:5 + h]),
                          reads=[R_lg, R_const], writes=[R_e2])
                    cx.op("dve", lambda e: e.tensor_tensor(out=e2[:], in0=e2[:], in1=cret[:, 3, :], op=ALU.mult),
                          reads=[R_e2, R_const], writes=[R_e2])
                    cx.op("dve", lambda e, h=h: e.tensor_tensor(out=DTt[:, h, :], in0=e1[:], in1=e2[:], op=ALU.add),
                          reads=[R_e1, R_e2], writes=[R_tab])
                for d_ in range(2):
                    for p in range(2):
                        for hh in range(2):
                            rows = slice(hh * 64, hh * 64 + 64)
                            col = d_ * 4 + 2 * p + hh
                            cx.op("act", lambda e, d_=d_, p=p, rows=rows, col=col: e.activation(
                                out=Gfb[rows, d_, p:p + 1], in_=lg[rows, col:col + 1], func=AF.Exp, scale=128.0),
                                reads=[R_lg], writes=[R_tab])
                cx.barrier()

        R_wser = Res("wser")

        def load_w(slot, src_ap, ncols):
            if len(src_ap.shape) == 3:
                view = wsl[slot][:, :, 0:ncols]
                cx.dma("pool", "w%d" % slot, lambda e: e.dma_start(out=view, in_=src_ap), writes=[R_w[slot], R_wser])
            else:
                nr = src_ap.shape[2]
                w_ = src_ap.shape[3]
                for r in range(nr):
                    view = wsl[slot][:, :, r * w_:(r + 1) * w_]
                    cx.dma("pool", "w%d" % slot, lambda e, view=view, r=r: e.dma_start(out=view, in_=src_ap[:, :, r, :]),
                           writes=[R_w[slot], R_wser])

        def rope(src_psum, R_src, t, out_ap, R_out, tmp1, tmp2, R_t1, R_t2):
            xv = src_psum.rearrange("p (h d) -> p h d", d=64)
            cb = _ap(rope_t[:, 0, t, :], [[0, 4], [1, 64]])
            s_lo = _ap(rope_t[:, 1, t, 0:32], [[0, 4], [1, 32]])
            s_hi = _ap(rope_t[:, 1, t, 32:64], [[0, 4], [1, 32]])
            t1v = tmp1.rearrange("p (h d) -> p h d", d=64)
            t2v = tmp2.rearrange("p (h d) -> p h d", d=64)
            cx.op("dve", lambda e: e.tensor_tensor(out=t1v, in0=xv, in1=cb, op=ALU.mult),
                  reads=[R_src, R_const], writes=[R_t1])
            cx.op("dve", [lambda e: e.tensor_tensor(out=t2v[:, :, 0:32], in0=xv[:, :, 32:64], in1=s_lo, op=ALU.mult),
                          lambda e: e.tensor_tensor(out=t2v[:, :, 32:64], in0=xv[:, :, 0:32], in1=s_hi, op=ALU.mult)],
                  reads=[R_src, R_const], writes=[R_t2])
            cx.op("dve", lambda e: e.tensor_tensor(out=out_ap, in0=tmp1, in1=tmp2, op=ALU.add),
                  reads=[R_t1, R_t2], writes=[R_out])

        def proj_tok(t, wslot, c0, ncols):
            a = nxt("mm")
            fns = [lambda e, kc=kc: e.matmul(mm[a][:, 0:ncols], lhsT=hT[:, kc, t * 128:(t + 1) * 128],
                                              rhs=wsl[wslot][:, kc, c0:c0 + ncols], start=(kc == 0), stop=(kc == 7))
                   for kc in range(8)]
            cx.op("pe", fns, reads=[R_hT[t // 4], R_w[wslot]], writes=[R_mm[a]])
            return a

        def proj_feat(g4, wslot, c0):
            a = nxt("mm")
            fns = [lambda e, kc=kc: e.matmul(mm[a][:, :], lhsT=wsl[wslot][:, kc, c0:c0 + 128],
                                              rhs=hT[:, kc, g4 * 512:(g4 + 1) * 512], start=(kc == 0), stop=(kc == 7))
                   for kc in range(8)]
            cx.op("pe", fns, reads=[R_hT[g4], R_w[wslot]], writes=[R_mm[a]])
            return a

        win_v = [win_d[l].rearrange("(k p) n -> p k n", p=128) for l in range(DEPTH)]
        win_a = [win_d[l].rearrange("(k p) (r c) -> p k r c", p=128, r=7) for l in range(DEPTH)]
        wout_v = [wout_d[l].rearrange("(k p) n -> p k n", p=128) for l in range(DEPTH)]

        def chk(name):
            if stop == name:
                raise _Stop()

        try:
          chk("M")
          for s in range(nseq):
            for l in range(depth):
                xsrc = x_d if l == 0 else y_d
                DTt, TAB, Gfb = DTt_all[:, l], TAB_all[:, l], Gfb_all[:, l]
                last = (l == depth - 1)
                if not (s == 0 and l == 0):
                    cx.barrier()
                    cx.new_epoch()
                load_w(0, win_v[l][:, :, 2048:2560], 512)
                if s == 0 and l == 0:
                    load_w(1, win_v[l][:, :, 2560:3072], 512)
                load_w(2, win_v[l][:, :, 3072:3584], 512)

                chk("T")

                with ExitStack() as ph:
                  if l == 0:
                      P = ph.enter_context
                      xin = [PT(ph, "xin%d" % i, [128, D], F32) for i in range(4)]
                      xn = [PT(ph, "xn%d" % i, [128, 4, D], BF16) for i in range(2)]
                      junk = PT(ph, "junk", [128, D], BF16)
                      stat = PT(ph, "stat", [128, 3, NT], F32)
                      R_xin = [Res(), Res(), Res(), Res()]
                      R_xn = [Res(), Res()]
                      R_junk = Res()
                      R_stat = [Res() for _ in range(NT)]
                      def st_X(g4):
                          xb = xn[g4 % 2]
                          for j in range(4):
                              t = 4 * g4 + j
                              xi = xin[t % 4]
                              cx.dma("sp", "xin%d" % (t % 4), lambda e, t=t, xi=xi: e.dma_start(out=xi[:], in_=xsrc[s, t * 128:(t + 1) * 128, :]),
                                     reads=[R_yd[s][t]], writes=[R_xin[t % 4]])
                              cx.op("pool", lambda e, t=t: e.memset(stat[:, 0, t:t + 1], 0.0), writes=[R_stat[t]])
                              cx.op("act", lambda e, t=t, xi=xi: e.activation(out=junk[:], in_=xi[:], func=AF.Square,
                                                                             accum_out=stat[:, 0, t:t + 1]),
                                    reads=[R_xin[t % 4]], writes=[R_junk, R_stat[t]])
                              cx.op("dve", lambda e, t=t: e.tensor_scalar(out=stat[:, 1, t:t + 1], in0=stat[:, 0, t:t + 1], scalar1=1.0 / D,
                                                                          scalar2=EPS, op0=ALU.mult, op1=ALU.add),
                                    reads=[R_stat[t]], writes=[R_stat[t]])
                              cx.op("pool", lambda e, t=t: e.tensor_tensor(out=stat[:, 2, t:t + 1], in0=stat[:, 1, t:t + 1], in1=mhalf[:, 0:1], op=ALU.pow),
                                    reads=[R_stat[t], R_const], writes=[R_stat[t]])
                              cx.op("dve", lambda e, t=t, xi=xi, j=j, xb=xb: e.tensor_scalar(
                                  out=xb[:, j, :], in0=xi[:], scalar1=stat[:, 2, t:t + 1], scalar2=None, op0=ALU.mult),
                                  reads=[R_xin[t % 4], R_stat[t]], writes=[R_xn[g4 % 2]])

                      def st_H(g4):
                          xb = xn[g4 % 2]
                          for k in range(8):
                              b = nxt("tp")
                              fns = [lambda e, j=j, k=k, b=b, xb=xb: e.transpose(tp[b][:, j * 128:(j + 1) * 128],
                                                                                xb[:, j, k * 128:(k + 1) * 128], ident[:])
                                     for j in range(4)]
                              cx.op("pe", fns, reads=[R_xn[g4 % 2], R_const], writes=[R_tp[b]])
                              cx.op("act", lambda e, k=k, b=b, g4=g4: e.activation(
                                  out=hT[:, k, g4 * 512:(g4 + 1) * 512], in_=tp[b][:, 0:512], func=AF.Identity,
                                  scale=gsA[:, l, s, k:k + 1], bias=shA[:, l, s, k:k + 1]),
                                  reads=[R_tp[b], R_mod], writes=[R_hT[g4]])

                      _pipeline(4, [(0, st_X), (1, st_H)])
                      cx.barrier()
                if s == 0 and l == 0:
                    dump("hT", hT[:, 0, :], [R_hT[0], R_hT[1], R_hT[2], R_hT[3]])
                chk("N")

                with ExitStack() as ph:
                    P = ph.enter_context
                    kbT = PT(ph, "kbT", [128, 2, S], BF16)
                    vb = PT(ph, "vb", [128, NT, 512], BF16)
                    kdb = PT(ph, "kdb", [128, NT, 256], BF16)
                    SfB = PT(ph, "SfB", [128, NT, 2, 128], BF16)
                    SbB = PT(ph, "SbB", [128, NT, 2, 128], BF16)
                    Sst = PT(ph, "Sst", [128, 2, 2, 128], F32)
                    rt1 = PT(ph, "rt1", [128, 256], F32)
                    rt2 = PT(ph, "rt2", [128, 256], F32)
                    kr = [PT(ph, "kr%d" % i, [128, 256], F32) for i in range(2)]
                    kbf = [PT(ph, "kbf%d" % i, [128, 256], BF16) for i in range(2)]
                    kdf = [PT(ph, "kdf%d" % i, [128, 256], BF16) for i in range(2)]
                    q3 = [PT(ph, "q3_%d" % i, [128, 3, 256], BF16) for i in range(2)]
                    qT3 = [PT(ph, "qT3_%d" % i, [128, 6, 128], BF16) for i in range(2)]
                    gsl = [PT(ph, "gsl%d" % i, [128, 512], BF16) for i in range(2)]
                    inT = [PT(ph, "inT%d" % i, [128, 512], BF16) for i in range(2)]
                    yr = [PT(ph, "yr%d" % i, [128, 512], BF16) for i in range(2)]
                    junk2 = PT(ph, "junk2", [128, 128], BF16)
                    gst = [PT(ph, "gst%d" % i, [128, 3, 4], F32) for i in range(2)]
                    R_kbT = [Res() for _ in range(NT)]
                    R_vb = [Res() for _ in range(NT)]
                    R_kdb = [Res() for _ in range(NT)]
                    R_SfB = [Res() for _ in range(NT)]
                    R_SbB = [Res() for _ in range(NT)]
                    R_Sst = [Res(), Res()]
                    R_rt1, R_rt2, R_junk2 = Res(), Res(), Res()
                    R_kr, R_kbf, R_kdf, R_q3, R_qT3, R_gsl, R_inT, R_yr, R_gst = (
                        [Res(), Res()] for _ in range(9))
                    TABv = lambda kind: TAB[:, kind, :, :].rearrange("p h d -> p (h d)")
                    R_q3a, R_q3b, R_q3c = ([Res(), Res()] for _ in range(3))
                    cx.op("pool", lambda e: e.memset(Sst[:], 0.0), writes=R_Sst)

                    def scan_update(d_, a):
                        fns = []
                        for p in range(2):
                            for hh in range(2):
                                rows = slice(hh * 64, hh * 64 + 64)
                                fns.append(lambda e, p=p, hh=hh, rows=rows: e.scalar_tensor_tensor(
                                    out=Sst[rows, d_, p, :], in0=Sst[rows, d_, p, :], scalar=Gfb[rows, d_, p:p + 1],
                                    in1=ov[a][rows, p * 256 + hh * 128:p * 256 + hh * 128 + 128], op0=ALU.mult, op1=ALU.add))
                        return fns

                    def b1_P(n):
                        i2 = n % 2
                        a = proj_tok(n, 0, 256, 256)
                        rope(mm[a][:, 0:256], R_mm[a], n, kr[i2][:], R_kr[i2], rt1[:], rt2[:], R_rt1, R_rt2)
                        cx.op("act", lambda e, i2=i2: e.activation(out=kbf[i2][:], in_=kr[i2][:], func=AF.Copy, scale=0.125),
                              reads=[R_kr[i2]], writes=[R_kbf[i2]])
                        cx.op("pool", lambda e, i2=i2: e.tensor_tensor(out=kdf[i2][:], in0=kr[i2][:], in1=TABv(0), op=ALU.mult),
                              reads=[R_kr[i2], R_tab], writes=[R_kdf[i2]])
                        cx.op("dve", lambda e, i2=i2, n=n: e.tensor_tensor(out=kdb[:, n, :], in0=kr[i2][:], in1=TABv(1), op=ALU.mult),
                              reads=[R_kr[i2], R_tab], writes=[R_kdb[n]])
                        a2 = proj_tok(n, 1, 0, 512)
                        cx.op("act", lambda e, a2=a2, n=n: e.activation(out=vb[:, n, :], in_=mm[a2][:, :], func=AF.Copy),
                              reads=[R_mm[a2]], writes=[R_vb[n]])

                    def b1_T(n):
                        i2 = n % 2
                        b = nxt("tp")
                        cx.op("pe", [lambda e, p=p, b=b, i2=i2: e.transpose(tp[b][:, p * 128:(p + 1) * 128], kbf[i2][:, p * 128:(p + 1) * 128], ident[:])
                                     for p in range(2)], reads=[R_kbf[i2], R_const], writes=[R_tp[b]])
                        cx.op("act", lambda e, b=b, n=n: e.activation(
                            out=kbT[:, :, n * 128:(n + 1) * 128], in_=tp[b][:, 0:256].rearrange("p (a c) -> p a c", a=2), func=AF.Copy),
                            reads=[R_tp[b]], writes=[R_kbT[n]])
                        cx.op("dve", lambda e, n=n: e.tensor_copy(out=SfB[:, n, :, :], in_=Sst[:, 0, :, :]),
                              reads=[R_Sst[0]], writes=[R_SfB[n]])
                        if n < NT - 1:
                            o = nxt("ov")
                            cx.op("pe", [lambda e, p=p, o=o, i2=i2, n=n: e.matmul(
                                ov[o][:, p * 256:(p + 1) * 256], lhsT=kdf[i2][:, p * 128:(p + 1) * 128],
                                rhs=vb[:, n, p * 256:(p + 1) * 256], start=True, stop=True) for p in range(2)],
                                reads=[R_kdf[i2], R_vb[n]], writes=[R_ov[o]])
                            cx.op("dve", scan_update(0, o), reads=[R_ov[o], R_tab, R_Sst[0]], writes=[R_Sst[0]])

                    _pipeline(NT, [(0, b1_P), (1, b1_T)])
                    b_stage = {"B1": 1, "Bb": 2}.get(stop, 3)
                    for n in (range(NT - 1, -1, -1) if b_stage >= 2 else []):
                        cx.op("dve", lambda e, n=n: e.tensor_copy(out=SbB[:, n, :, :], in_=Sst[:, 1, :, :]),
                              reads=[R_Sst[1]], writes=[R_SbB[n]])
                        if n > 0:
                            o = nxt("ov")
                            cx.op("pe", [lambda e, p=p, o=o, n=n: e.matmul(
                                ov[o][:, p * 256:(p + 1) * 256], lhsT=kdb[:, n, p * 128:(p + 1) * 128],
                                rhs=vb[:, n, p * 256:(p + 1) * 256], start=True, stop=True) for p in range(2)],
                                reads=[R_kdb[n], R_vb[n]], writes=[R_ov[o]])
                            cx.op("dve", scan_update(1, o), reads=[R_ov[o], R_tab, R_Sst[1]], writes=[R_Sst[1]])
                    def b2_P(n):
                        i2 = n % 2
                        a = proj_tok(n, 0, 0, 256)
                        rope(mm[a][:, 0:256], R_mm[a], n, kr[i2][:], R_kr[i2], rt1[:], rt2[:], R_rt1, R_rt2)
                        cx.op("act", lambda e, i2=i2: e.activation(out=q3[i2][:, 0, :], in_=kr[i2][:], func=AF.Copy),
                              reads=[R_kr[i2]], writes=[R_q3a[i2]])
                        cx.op("pool", lambda e, i2=i2: e.tensor_tensor(out=q3[i2][:, 1, :], in0=kr[i2][:], in1=TABv(2), op=ALU.mult),
                              reads=[R_kr[i2], R_tab], writes=[R_q3b[i2]])
                        cx.op("dve", lambda e, i2=i2: e.tensor_tensor(out=q3[i2][:, 2, :], in0=kr[i2][:], in1=TABv(3), op=ALU.mult),
                              reads=[R_kr[i2], R_tab], writes=[R_q3c[i2]])

                    def b2_T(n):
                        i2 = n % 2
                        b = nxt("tp")
                        cx.op("pe", [lambda e, v=v, p=p, b=b, i2=i2: e.transpose(
                            tp[b][:, (v * 2 + p) * 128:(v * 2 + p + 1) * 128], q3[i2][:, v, p * 128:(p + 1) * 128], ident[:])
                            for v in range(3) for p in range(2)], reads=[R_q3a[i2], R_q3b[i2], R_q3c[i2], R_const], writes=[R_tp[b]])
                        cx.op("act", lambda e, b=b, i2=i2: e.activation(
                            out=qT3[i2][:], in_=tp[b][:, 0:768].rearrange("p (a c) -> p a c", a=6), func=AF.Copy),
                            reads=[R_tp[b]], writes=[R_qT3[i2]])

                    def b2_T2(n):
                        i2 = n % 2
                        fns = []
                        for h in range(4):
                            p, hh = h // 2, h % 2
                            rows = slice(hh * 64, hh * 64 + 64)
                            fns.append(lambda e, p=p, hh=hh, rows=rows, i2=i2, n=n: e.matmul(
                                st[hh][:, p * 128:(p + 1) * 128], lhsT=kbT[rows, p, n * 128:(n + 1) * 128],
                                rhs=qT3[i2][rows, p, :], start=True, stop=True))
                        cx.op("pe", fns, reads=[R_kbT[n], R_qT3[i2]], writes=[R_st[0], R_st[1]])
                        inv = inT[i2][:].rearrange("p (a b t) -> p a b t", a=2, b=2)
                        for hh in range(2):
                            cx.op("dve", lambda e, hh=hh, inv=inv: e.tensor_tensor(
                                out=inv[:, :, hh, :], in0=st[hh][:, 0:256].rearrange("p (a t) -> p a t", a=2),
                                in1=DTt[:].rearrange("p (a b) t -> p a b t", b=2)[:, :, hh, :], op=ALU.mult),
                                reads=[R_st[hh], R_tab], writes=[R_inT[i2]])
                        a2 = proj_tok(n, 2, 0, 512)
                        cx.op("act", lambda e, a2=a2, i2=i2: e.activation(out=gsl[i2][:], in_=mm[a2][:, :], func=AF.Silu),
                              reads=[R_mm[a2]], writes=[R_gsl[i2]])

                    def b2_Y(n):
                        i2 = n % 2
                        o = nxt("ov")
                        fns = []
                        for h in range(4):
                            p, hh = h // 2, h % 2
                            rows = slice(hh * 64, hh * 64 + 64)
                            oc = slice(h * 128, (h + 1) * 128)
                            fns.append(lambda e, oc=oc, o=o, i2=i2, n=n: e.matmul(
                                ov[o][:, oc], lhsT=inT[i2][:, oc], rhs=vb[:, n, oc], start=True, stop=False))
                            fns.append(lambda e, oc=oc, o=o, i2=i2, n=n, p=p, rows=rows: e.matmul(
                                ov[o][:, oc], lhsT=qT3[i2][rows, 2 + p, :], rhs=SfB[rows, n, p, :], start=False, stop=False))
                            fns.append(lambda e, oc=oc, o=o, i2=i2, n=n, p=p, rows=rows: e.matmul(
                                ov[o][:, oc], lhsT=qT3[i2][rows, 4 + p, :], rhs=SbB[rows, n, p, :], start=False, stop=True))
                        cx.op("pe", fns, reads=[R_inT[i2], R_vb[n], R_qT3[i2], R_SfB[n], R_SbB[n]], writes=[R_ov[o]])
                        cx.op("pool", lambda e, i2=i2: e.memset(gst[i2][:, 0, :], 0.0), writes=[R_gst[i2]])
                        cx.op("act", [lambda e, h=h, o=o, i2=i2: e.activation(
                            out=junk2[:], in_=ov[o][:, h * 128:(h + 1) * 128], func=AF.Square, accum_out=gst[i2][:, 0, h:h + 1])
                            for h in range(4)], reads=[R_ov[o]], writes=[R_junk2, R_gst[i2]])
                        cx.op("dve", lambda e, i2=i2: e.tensor_scalar(out=gst[i2][:, 1, :], in0=gst[i2][:, 0, :], scalar1=1.0 / 128,
                                                                      scalar2=EPS, op0=ALU.mult, op1=ALU.add),
                              reads=[R_gst[i2]], writes=[R_gst[i2]])
                        cx.op("pool", lambda e, i2=i2: e.tensor_tensor(out=gst[i2][:, 2, :], in0=gst[i2][:, 1, :], in1=mhalf[:, 0:4], op=ALU.pow),
                              reads=[R_gst[i2], R_const], writes=[R_gst[i2]])
                        cx.op("dve", [lambda e, h=h, o=o, i2=i2: e.scalar_tensor_tensor(
                            out=yr[i2][:, h * 128:(h + 1) * 128], in0=ov[o][:, h * 128:(h + 1) * 128],
                            scalar=gst[i2][:, 2, h:h + 1], in1=gsl[i2][:, h * 128:(h + 1) * 128], op0=ALU.mult, op1=ALU.mult)
                            for h in range(4)], reads=[R_ov[o], R_gst[i2], R_gsl[i2]], writes=[R_yr[i2]])

                    def b2_Z(n):
                        i2 = n % 2
                        b = nxt("tp")
                        cx.op("pe", [lambda e, h=h, b=b, i2=i2: e.transpose(tp[b][:, h * 128:(h + 1) * 128], yr[i2][:, h * 128:(h + 1) * 128], ident[:])
                                     for h in range(4)], reads=[R_yr[i2], R_const], writes=[R_tp[b]])
                        cx.op("act", lambda e, b=b, n=n: e.activation(
                            out=yT[:, 4:8, n * 128:(n + 1) * 128], in_=tp[b][:, 0:512].rearrange("p (a c) -> p a c", a=4), func=AF.Copy),
                            reads=[R_tp[b]], writes=R_yT[4:8])

                    if b_stage >= 3:
                        load_w(1, win_a[l][:, :, 0:4, 0:128], 512)
                        _pipeline(NT, [(0, b2_P), (1, b2_T), (2, b2_Y), (3, b2_Z), (1, b2_T2)])
                    cx.barrier()
                if s == 0 and l == 0:
                    dump("yrT", yT[:, 4, :], R_yT[4:8])
                if stop in ("B1", "Bb", "B2a", "B2b", "B2c", "B2d"):
                    raise _Stop()
                chk("B")

                with ExitStack() as ph:
                    P = ph.enter_context
                    qT = PT(ph, "qT", [128, S], BF16)
                    gaT2 = [PT(ph, "gaT%d" % i, [128, S], BF16) for i in range(2)]
                    Vp = [PT(ph, "Vp%d" % i, [128, 20, 2, 128], BF16) for i in range(2)]
                    ACC = [PT(ph, "ACC%d" % i, [128, S], F32) for i in range(2)]
                    rt1 = PT(ph, "art1", [128, 256], F32)
                    rt2 = PT(ph, "art2", [128, 256], F32)
                    qkr = [PT(ph, "qkr%d" % i, [128, 256], BF16) for i in range(2)]
                    pt = [PT(ph, "pt%d" % i, [128, 512], BF16) for i in range(4)]
                    Rr = [PT(ph, "Rr%d" % i, [128, 512], F32) for i in range(2)]
                    Tm = [PT(ph, "Tm%d" % i, [128, 512], F32) for i in range(2)]
                    R_qT = [Res() for _ in range(NT)]
                    R_gaT2 = [[Res() for _ in range(4)] for _ in range(2)]
                    R_Vp = [Res(), Res()]
                    R_ACC = [[Res() for _ in range(4)] for _ in range(2)]
                    R_rt1, R_rt2 = Res(), Res()
                    R_qkr, R_pt, R_Rr, R_Tm = ([Res(), Res(), Res(), Res()] for _ in range(4))
                    sbank = [st[0], st[1], mm[0], mm[1]]
                    R_sbank = [R_st[0], R_st[1], R_mm[0], R_mm[1]]
                    def vp_init():
                        for i in range(2):
                            vflat = Vp[i][:].rearrange("p a b c -> p (a b) c")
                            for q_ in range(5):
                                cx.op("act", lambda e, i=i, q_=q_, vflat=vflat: e.activation(
                                    out=vflat[:, q_ * 8:(q_ + 1) * 8, :], in_=_ap(onesf[:, 0:128], [[0, 8], [1, 128]]), func=AF.Copy),
                                    reads=[R_const], writes=[R_Vp[i]])
                    if os.environ.get("DUMMY_INIT"):
                        for q_ in range(10):
                            cx.op("act", lambda e, q_=q_: e.activation(out=ACC[0][:, q_ * 128:(q_ + 1) * 128], in_=onesf[:, 0:128], func=AF.Copy),
                                  reads=[R_const], writes=[R_ACC[0]])
                    elif not os.environ.get("VPM_LATE") and not os.environ.get("SKIP_VPM"):
                        vp_init()
                    vpc = [0]
                    fin_pend = []
                    for j in range(4):
                        wslot = [1, 0, 2, 1][j]
                        gaT, R_gaT = gaT2[j % 2], R_gaT2[j % 2]
                        def a1_P(t, wslot=wslot, gaT=gaT, R_gaT=R_gaT):
                            i2 = t % 2
                            a = proj_tok(t, wslot, 0, 256)
                            rope(mm[a][:, 0:256], R_mm[a], t, qkr[i2][:], R_qkr[i2], rt1[:], rt2[:], R_rt1, R_rt2)
                            if t % 4 == 3:
                                g4 = t // 4
                                a = proj_feat(g4, wslot, 256)
                                cx.op("act", lambda e, a=a, g4=g4: e.activation(out=VT1[:, 64 + g4 * 512:64 + (g4 + 1) * 512], in_=mm[a][:, :], func=AF.Copy),
                                      reads=[R_mm[a]], writes=[R_VT1])
                                cx.op("act", lambda e, a=a, g4=g4: e.activation(
                                    out=VT2[:, :, 64 + g4 * 128:64 + (g4 + 1) * 128],
                                    in_=mm[a][:, :].rearrange("p (l r) -> p r l", r=4), func=AF.Copy),
                                    reads=[R_mm[a]], writes=[R_VT2])
                                a = proj_feat(g4, wslot, 384)
                                cx.op("act", lambda e, a=a, g4=g4: e.activation(out=gaT[:, g4 * 512:(g4 + 1) * 512], in_=mm[a][:, :], func=AF.Silu),
                                      reads=[R_mm[a]], writes=[R_gaT[g4]])

                        def a1_T(t):
                            i2 = t % 2
                            b = nxt("tp")
                            cx.op("pe", [lambda e, p=p, b=b, i2=i2: e.transpose(tp[b][:, p * 128:(p + 1) * 128], qkr[i2][:, p * 128:(p + 1) * 128], ident[:])
                                         for p in range(2)], reads=[R_qkr[i2], R_const], writes=[R_tp[b]])
                            def cp(out_ap, in_ap, R_out, t=t, b=b):
                                if t % 2 == 0:
                                    cx.op("act", lambda e: e.activation(out=out_ap, in_=in_ap, func=AF.Copy), reads=[R_tp[b]], writes=[R_out])
                                else:
                                    cx.op("dve", lambda e: e.tensor_copy(out=out_ap, in_=in_ap), reads=[R_tp[b]], writes=[R_out])
                            cp(qT[:, t * 128:(t + 1) * 128], tp[b][:, 0:128], R_qT[t])
                            cp(kT1[:, 64 + t * 128:64 + (t + 1) * 128], tp[b][:, 128:256], R_kT1)
                            cp(kT2[:, :, 64 + t * 32:64 + (t + 1) * 32], tp[b][:, 128:256].rearrange("p (l r) -> p r l", r=4), R_kT2)

                        _pipeline(NT, [(0, a1_P), (1, a1_T)])
                        if j == 0:
                            load_w(0, win_a[l][:, :, 0:4, 128:256], 512)
                            load_w(2, win_a[l][:, :, 0:4, 256:384], 512)
                        elif j == 1:
                            load_w(1, win_a[l][:, :, 0:4, 384:512], 512)
                        elif j == 2:
                            load_w(0, wout_v[l][:, :, 0:512], 512)
                        else:
                            load_w(2, wout_v[l][:, :, 512:1024], 512)
                            nl, ns = (l + 1, s) if l + 1 < depth else (0, s + 1)
                            if ns < nseq:
                                load_w(1, win_v[nl][:, :, 2560:3072], 512)

                        if os.environ.get("VPM_LATE") and j == 0:
                            vp_init()
                        def build_vp(vi, srcs, R_src):
                            for g0 in range(0, len(srcs), 4):
                                grp = srcs[g0:g0 + 4]
                                b = nxt("tp")
                                cx.op("pe", [lambda e, ii=ii, sa_=sa_, b=b: e.transpose(tp[b][:, ii * 128:(ii + 1) * 128], sa_, ident[:])
                                             for ii, (_, sa_) in enumerate(grp)], reads=[R_src, R_const], writes=[R_tp[b]])
                                i0 = grp[0][0]
                                ng = len(grp)
                                for hh in range(2):
                                    eng = "act" if (g0 // 4) % 2 == 0 else "dve"
                                    src = tp[b][:, 0:ng * 128].rearrange("p (a c) -> p a c", c=128)[:, :, hh * 64:hh * 64 + 64]
                                    dst = Vp[vi][:, i0:i0 + ng, hh, hh * 64:hh * 64 + 64]
                                    if eng == "act":
                                        cx.op("act", lambda e, src=src, dst=dst: e.activation(out=dst, in_=src, func=AF.Copy),
                                              reads=[R_tp[b]], writes=[R_Vp[vi]])
                                    else:
                                        cx.op("dve", lambda e, src=src, dst=dst: e.tensor_copy(out=dst, in_=src),
                                              reads=[R_tp[b]], writes=[R_Vp[vi]])

                        pend = []

                        def flush_pend():
                            while pend:
                                pend.pop(0)()

                        tpf = [tp[0][:].bitcast(F32), tp[1][:].bitcast(F32)]
                        obank = [[ov[0][:, :], tpf[0]], [ov[1][:, :], tpf[1]]]
                        R_obank = [[R_ov[0], R_tp[0]], [R_ov[1], R_tp[1]]]

                        def attend2(vi, items2, mask_variant, R_k, acc2):
                            fc = ctr["ov"]
                            ctr["ov"] += 1
                            nk = len(items2[0][0][1])
                            per = 512 // (128 * nk)
                            ngrp = 4 // per
                            for gi, g0 in enumerate(range(0, 4, per)):
                                base = (ctr["w"] % 2) * 2
                                ctr["w"] += 1
                                mv = mask_variant(g0)
                                fns = [lambda e, sa=base + hh, mv=mv: e.matmul(sbank[sa][:, :], lhsT=ident[:], rhs=maskA[:, mv, :],
                                                                               start=True, stop=False) for hh in range(2)]
                                order = [(ii, kk, hh) for ii in range(per) for kk in range(nk) for hh in range(2)]
                                if os.environ.get("NO_ILV"):
                                    order = [(ii, kk, hh) for hh in range(2) for ii in range(per) for kk in range(nk)]
                                for (ii, kk, hh) in order:
                                    if True:
                                        c0 = (ii * nk + kk) * 128
                                        if True:
                                            q_ap, ks = items2[hh][g0 + ii]
                                            k_ap = ks[kk][0]
                                            fns.append(lambda e, c0=c0, sa=base + hh, k_ap=k_ap, q_ap=q_ap: e.matmul(
                                                sbank[sa][:, c0:c0 + 128], lhsT=k_ap, rhs=q_ap, start=False, stop=True))
                                cx.op("pe", fns, reads=[R_k, R_const] + [R_qT[t] for t in range(NT)], writes=[R_sbank[base], R_sbank[base + 1]])
                                for hh in range(2):
                                    pi = base + hh
                                    cx.op("act", lambda e, pi=pi: e.activation(out=pt[pi][:], in_=sbank[pi][:, :], func=AF.Exp, scale=0.125),
                                          reads=[R_sbank[pi]], writes=[R_pt[pi]])

                                def stage2(g0=g0, gi=gi, base=base):
                                    for hh in range(2):
                                        pi = base + hh
                                        fsel = 0
                                        o_ap, R_o = obank[hh][fsel], R_obank[hh][fsel]
                                        fns = []
                                        for ii in range(per):
                                            _, ks = items2[hh][g0 + ii]
                                            qi = g0 + ii
                                            for kk, (_, vidx) in enumerate(ks):
                                                c0 = (ii * nk + kk) * 128
                                                fns.append(lambda e, c0=c0, qi=qi, vidx=vidx, kk=kk, hh=hh, pi=pi, o_ap=o_ap: e.matmul(
                                                    o_ap[:, qi * 128:(qi + 1) * 128], lhsT=Vp[vi][:, vidx, hh, :], rhs=pt[pi][:, c0:c0 + 128],
                                                    start=(kk == 0), stop=(kk == nk - 1)))
                                        cx.op("pe", fns, reads=[R_pt[pi], R_Vp[vi]], writes=[R_o])
                                        if gi == ngrp - 1:
                                            acc2(hh, o_ap, R_o)

                                pend.append(stage2)
                                while len(pend) > 1:
                                    pend.pop(0)()

                        a_st = {"A1": 0, "Ap0": 1, "Ap1": 2, "Ap2": 3}.get(stop, 9)
                        if a_st < 9 and j > 0:
                            continue
                        hrows = [slice(0, 64), slice(64, 128)]
                        for pat in range(min(3, a_st)):
                            vi = vpc[0] % 2
                            vpc[0] += 1
                            if pat == 0:
                                build_vp(vi, [(jt, VT1[:, 128 * jt:128 * jt + 128]) for jt in range(17)], R_VT1)
                                for u in range(4):
                                    items2 = [[(qT[rows, 128 * i_:128 * i_ + 128],
                                                [(kT1[rows, 128 * i_:128 * i_ + 128], i_),
                                                 (kT1[rows, 128 * (i_ + 1):128 * (i_ + 1) + 128], i_ + 1)])
                                               for i_ in range(4 * u, 4 * u + 4)] for rows in hrows]
                                    mvf = lambda g0, u=u: (0 if (u == 0 and g0 == 0) else (2 if (u == 3 and g0 == 2) else 1))
                                    acc2 = lambda hh, o_ap, R_o, u=u: cx.op("dve", lambda e: e.tensor_copy(
                                        out=ACC[hh][:, 512 * u:512 * (u + 1)], in_=o_ap),
                                        reads=[R_o], writes=[R_ACC[hh][u]])
                                    for _ in range(2):
                                        if fin_pend:
                                            fin_pend.pop(0)()
                                    attend2(vi, items2, mvf, R_kT1, acc2)
                            elif pat == 1:
                                build_vp(vi, [(r * 5 + jt, VT2[:, r, 128 * jt:128 * jt + 128]) for r in range(4) for jt in range(5)], R_VT2)
                                for r in range(4):
                                    items2 = [[(qT[rows, 512 * i_ + r:512 * (i_ + 1):4],
                                                [(kT2[rows, r, 128 * i_:128 * i_ + 128], r * 5 + i_),
                                                 (kT2[rows, r, 128 * (i_ + 1):128 * (i_ + 1) + 128], r * 5 + i_ + 1)])
                                               for i_ in range(4)] for rows in hrows]
                                    mvf = lambda g0: (0 if g0 == 0 else 2)
                                    acc2 = lambda hh, o_ap, R_o, r=r: cx.op("dve", lambda e: e.tensor_tensor(
                                        out=ACC[hh][:, r:S:4], in0=o_ap, in1=ACC[hh][:, r:S:4], op=ALU.add),
                                        reads=[R_o] + R_ACC[hh], writes=R_ACC[hh])
                                    attend2(vi, items2, mvf, R_kT2, acc2)
                            else:
                                build_vp(vi, [(r, VT1[:, 64 + r:64 + S:16]) for r in range(16)], R_VT1)
                                for r0 in range(0, 16, 4):
                                    items2 = [[(qT[rows, r:S:16], [(kT1[rows, 64 + r:64 + S:16], r)])
                                               for r in range(r0, r0 + 4)] for rows in hrows]
                                    mvf = lambda g0: 3

                                    def acc2(hh, o_ap, R_o, r0=r0):
                                        accv = ACC[hh][:].rearrange("p (l r) -> p r l", r=16)[:, r0:r0 + 4, :]
                                        cx.op("dve", lambda e: e.tensor_tensor(
                                            out=accv, in0=o_ap.rearrange("p (r l) -> p r l", r=4), in1=accv, op=ALU.add),
                                            reads=[R_o] + R_ACC[hh], writes=R_ACC[hh])
                                    attend2(vi, items2, mvf, R_kT1, acc2)
                        flush_pend()

                        def fin_step(hh, u, j=j, gaT=gaT, R_gaT=R_gaT):
                            nr = slice(hh * 64, hh * 64 + 64)
                            dr = slice((1 - hh) * 64, (1 - hh) * 64 + 64)
                            cs = slice(512 * u, 512 * (u + 1))
                            i2 = ctr["fin"] % 2
                            ctr["fin"] += 1
                            cx.op("act", lambda e: e.activation(out=Rr[i2][nr, :], in_=ACC[hh][dr, cs], func=AF.Ln),
                                  reads=[R_ACC[hh][u]], writes=[R_Rr[i2]])
                            cx.op("act", lambda e: e.activation(out=Rr[i2][nr, :], in_=Rr[i2][nr, :], func=AF.Exp, scale=-1.0),
                                  reads=[R_Rr[i2]], writes=[R_Rr[i2]])
                            cx.op("dve", lambda e: e.tensor_tensor(out=Tm[i2][nr, :], in0=ACC[hh][nr, cs], in1=Rr[i2][nr, :], op=ALU.mult),
                                  reads=[R_ACC[hh][u], R_Rr[i2]], writes=[R_Tm[i2]])
                            cx.op("pool", lambda e: e.tensor_tensor(out=yT[nr, j, cs], in0=Tm[i2][nr, :], in1=gaT[nr, cs], op=ALU.mult),
                                  reads=[R_Tm[i2], R_gaT[u]], writes=[R_yT[j]])

                        if a_st >= 9:
                            for u in range(4):
                                for hh in range(2):
                                    fin_pend.append(lambda hh=hh, u=u, f=fin_step: f(hh, u))
                    while fin_pend:
                        fin_pend.pop(0)()
                    cx.barrier()
                if s == 0 and l == 0:
                    dump("yaT", yT[:, 0, :], R_yT[0:4])
                if stop in ("A1", "Ap0", "Ap1", "Ap2"):
                    raise _Stop()
                chk("A")

                with ExitStack() as ph:
                    P = ph.enter_context
                    gate_b = PT(ph, "gate_b", [128, D], F32)
                    gfin_b = PT(ph, "gfin_b", [128, D], F32)
                    R_gfin = Res()
                    if last:
                        cx.dma("sp", "cst", lambda e: e.dma_start(out=gfin_b[:], in_=gfin_d.partition_broadcast(128)), writes=[R_gfin])
                    xin = [PT(ph, "oxin%d" % i, [128, D], F32) for i in range(2)]
                    xo = [PT(ph, "xo%d" % i, [128, D], F32) for i in range(2)]
                    Dk = [PT(ph, "Dk%d" % i, [128, 128], F32) for i in range(2)]
                    junk = PT(ph, "ojunk", [128, D], BF16)
                    fst = [PT(ph, "fst%d" % i, [128, 3], F32) for i in range(2)]
                    R_xin, R_xo, R_Dk, R_fst = ([Res(), Res()] for _ in range(4))
                    R_xoa, R_xob = [Res(), Res()], [Res(), Res()]
                    R_junk = Res()
                    if not last:
                        oxn = [PT(ph, "oxn%d" % i, [128, 4, D], BF16) for i in range(2)]
                        ostat = PT(ph, "ostat", [128, 3, NT], F32)
                        R_oxn = [Res(), Res()]
                        R_ostat = [Res() for _ in range(NT)]
                    h_pend = []

                    def o_H(g4):
                        xb = oxn[g4 % 2]
                        for k in range(8):
                            b = nxt("tp")
                            cx.op("pe", [lambda e, jj=jj, k=k, b=b, xb=xb: e.transpose(tp[b][:, jj * 128:(jj + 1) * 128],
                                                                                   xb[:, jj, k * 128:(k + 1) * 128], ident[:])
                                         for jj in range(4)], reads=[R_oxn[g4 % 2], R_const], writes=[R_tp[b]])
                            cx.op("act", lambda e, k=k, b=b, g4=g4: e.activation(
                                out=hT[:, k, g4 * 512:(g4 + 1) * 512], in_=tp[b][:, 0:512], func=AF.Identity,
                                scale=gsA[:, l + 1, s, k:k + 1], bias=shA[:, l + 1, s, k:k + 1]),
                                reads=[R_tp[b], R_mod], writes=[R_hT[g4]])
                    for k in range(8):
                        i2 = k % 2
                        cx.op("dve", lambda e, k=k, i2=i2: e.tensor_scalar(out=Dk[i2][:], in0=identf[:], scalar1=gtA[:, l, s, k:k + 1],
                                                                         scalar2=None, op0=ALU.mult),
                              reads=[R_const, R_mod], writes=[R_Dk[i2]])
                        a = nxt("mm")
                        cx.op("pe", lambda e, a=a, i2=i2: e.matmul(mm[a][:, 0:128], lhsT=onesf[:], rhs=Dk[i2][:], start=True, stop=True),
                              reads=[R_Dk[i2], R_const], writes=[R_mm[a]])
                        cx.op("act", lambda e, a=a, k=k: e.activation(out=gate_b[:, k * 128:(k + 1) * 128], in_=mm[a][:, 0:128], func=AF.Copy),
                              reads=[R_mm[a]], writes=[R_gate])
                    for t in range(NT):
                        i2 = t % 2
                        cx.dma("sp", "xin%d" % i2, lambda e, t=t, i2=i2: e.dma_start(out=xin[i2][:], in_=xsrc[s, t * 128:(t + 1) * 128, :]),
                               reads=[R_yd[s][t]], writes=[R_xin[i2]])
                        for half in range(2):
                            a = nxt("mm")
                            wslot = 0 if half == 0 else 2
                            hs = slice(half * 512, (half + 1) * 512)
                            cx.op("pe", [lambda e, kc=kc, a=a, wslot=wslot, t=t: e.matmul(
                                mm[a][:, :], lhsT=yT[:, kc, t * 128:(t + 1) * 128], rhs=wsl[wslot][:, kc, :],
                                start=(kc == 0), stop=(kc == 7)) for kc in range(8)],
                                reads=R_yT + [R_w[wslot]], writes=[R_mm[a]])
                            cx.op("dve", lambda e, a=a, hs=hs, i2=i2: e.tensor_tensor(out=xo[i2][:, hs], in0=mm[a][:, :], in1=gate_b[:, hs], op=ALU.mult),
                                  reads=[R_mm[a], R_gate], writes=[(R_xo if half == 0 else R_xob)[i2]])
                        cx.op("dve", lambda e, i2=i2: e.tensor_tensor(out=xo[i2][:, 0:512], in0=xo[i2][:, 0:512], in1=xin[i2][:, 0:512], op=ALU.add),
                              reads=[R_xo[i2], R_xin[i2]], writes=[R_xo[i2]])
                        cx.op("pool", lambda e, i2=i2: e.tensor_tensor(out=xo[i2][:, 512:1024], in0=xo[i2][:, 512:1024], in1=xin[i2][:, 512:1024], op=ALU.add),
                              reads=[R_xob[i2], R_xin[i2]], writes=[R_xob[i2]])
                        if last:
                            cx.op("pool", lambda e, i2=i2: e.memset(fst[i2][:, 0:1], 0.0), writes=[R_fst[i2]])
                            cx.op("act", lambda e, i2=i2: e.activation(out=junk[:], in_=xo[i2][:], func=AF.Square, accum_out=fst[i2][:, 0:1]),
                                  reads=[R_xo[i2], R_xob[i2]], writes=[R_junk, R_fst[i2]])
                            cx.op("dve", lambda e, i2=i2: e.tensor_scalar(out=fst[i2][:, 1:2], in0=fst[i2][:, 0:1], scalar1=1.0 / D,
                                                                          scalar2=EPS, op0=ALU.mult, op1=ALU.add),
                                  reads=[R_fst[i2]], writes=[R_fst[i2]])
                            cx.op("pool", lambda e, i2=i2: e.tensor_tensor(out=fst[i2][:, 2:3], in0=fst[i2][:, 1:2], in1=mhalf[:, 0:1], op=ALU.pow),
                                  reads=[R_fst[i2], R_const], writes=[R_fst[i2]])
                            cx.op("dve", lambda e, i2=i2: e.scalar_tensor_tensor(
                                out=xo[i2][:], in0=xo[i2][:], scalar=fst[i2][:, 2:3], in1=gfin_b[:], op0=ALU.mult, op1=ALU.mult),
                                reads=[R_xo[i2], R_xob[i2], R_fst[i2], R_gfin], writes=[R_xo[i2], R_xob[i2]])
                        cx.dma("sp", "xout%d" % i2, lambda e, t=t, i2=i2: e.dma_start(out=y_d[s, t * 128:(t + 1) * 128, :], in_=xo[i2][:]),
                               reads=[R_xo[i2], R_xob[i2]], writes=[R_yd[s][t]])
                        if not last:
                            g4, jj = t // 4, t % 4
                            cx.op("pool", lambda e, t=t: e.memset(ostat[:, 0, t:t + 1], 0.0), writes=[R_ostat[t]])
                            cx.op("act", lambda e, t=t, i2=i2: e.activation(out=junk[:], in_=xo[i2][:], func=AF.Square,
                                                                           accum_out=ostat[:, 0, t:t + 1]),
                                  reads=[R_xo[i2], R_xob[i2]], writes=[R_junk, R_ostat[t]])
                            cx.op("dve", lambda e, t=t: e.tensor_scalar(out=ostat[:, 1, t:t + 1], in0=ostat[:, 0, t:t + 1], scalar1=1.0 / D,
                                                                        scalar2=EPS, op0=ALU.mult, op1=ALU.add),
                                  reads=[R_ostat[t]], writes=[R_ostat[t]])
                            cx.op("pool", lambda e, t=t: e.tensor_tensor(out=ostat[:, 2, t:t + 1], in0=ostat[:, 1, t:t + 1], in1=mhalf[:, 0:1], op=ALU.pow),
                                  reads=[R_ostat[t], R_const], writes=[R_ostat[t]])
                            cx.op("dve", lambda e, t=t, i2=i2, jj=jj, g4=g4: e.tensor_scalar(
                                out=oxn[g4 % 2][:, jj, :], in0=xo[i2][:], scalar1=ostat[:, 2, t:t + 1], scalar2=None, op0=ALU.mult),
                                reads=[R_xo[i2], R_xob[i2], R_ostat[t]], writes=[R_oxn[g4 % 2]])
                            if h_pend and h_pend[0][0] <= t:
                                o_H(h_pend.pop(0)[1])
                            if jj == 3:
                                h_pend.append((t + 2, g4))
                    while h_pend:
                        o_H(h_pend.pop(0)[1])
                    cx.barrier()
        except _Stop:
            pass
        cx.barrier()
    return nc


def _consts():
    f32 = np.float32
    pos = np.arange(S, dtype=np.float32)
    inv = (10000.0 ** (-np.arange(0, 64, 2, dtype=np.float32) / 64)).astype(f32)
    ang = (pos[:, None] * inv[None, :]).astype(f32)
    cos, sin = np.cos(ang).astype(f32), np.sin(ang).astype(f32)
    C2 = np.concatenate([cos, cos], axis=1)
    S2 = np.concatenate([-sin, sin], axis=1)
    rope = np.stack([C2.reshape(NT, 128, 64).transpose(1, 0, 2), S2.reshape(NT, 128, 64).transpose(1, 0, 2)], axis=1)
    p = np.arange(128)[:, None]
    c = np.arange(128)[None, :]
    A = (c <= p).astype(f32)
    B = (p <= c).astype(f32)
    A_first = A * (p >= 64)
    B_last = B * (p < 64)
    m_norm = np.concatenate([A, B], axis=1)
    m_first = np.concatenate([A_first, B], axis=1)
    m_last = np.concatenate([A, B_last], axis=1)
    band = (np.abs(p - c) <= 64).astype(f32)
    mask = np.stack([np.concatenate([m_first, m_norm], 1), np.concatenate([m_norm, m_norm], 1),
                     np.concatenate([m_norm, m_last], 1), np.concatenate([band] * 4, 1)], axis=1)
    diff = (c - p).astype(f32)
    ret = np.stack([np.maximum(diff, 0), (diff >= 0).astype(f32), np.maximum(-diff, 0), (diff < 0).astype(f32)], axis=1)
    tau = np.arange(128, dtype=f32)
    taus = np.stack([127 - tau, tau, tau + 1, 128 - tau], axis=1)
    mask = (mask - 1.0) * 30000.0
    return dict(cst_rope=np.ascontiguousarray(rope, f32), cst_mask=np.ascontiguousarray(mask, f32),
                cst_ret=np.ascontiguousarray(ret, f32), cst_tau=np.ascontiguousarray(taus, f32),
                cst_ident=np.eye(128, dtype=f32))


def kernel(x_prompt, x_sample, c_prompt, c_sample, g_norm, w_ada, b_ada, w_in, w_out,
           decay_fwd, decay_bwd, g_final):
    f = lambda a: np.ascontiguousarray(np.asarray(a), dtype=np.float32)
    xs = np.concatenate([f(x_prompt), f(x_sample)], axis=0)
    cs = np.concatenate([f(c_prompt), f(c_sample)], axis=0)
    shared = dict(g_norm=f(g_norm), w_ada=f(w_ada), b_ada=f(b_ada), w_in=f(w_in), w_out=f(w_out),
                  decay_fwd=f(decay_fwd), decay_bwd=f(decay_bwd), g_final=f(g_final))
    shared.update(_consts())
    nc = build_nc()
    in_maps = []
    for i in range(NCORES):
        m = dict(shared)
        m["x"] = np.ascontiguousarray(xs[i * NSEQ:(i + 1) * NSEQ])
        m["c"] = np.ascontiguousarray(cs[i * NSEQ:(i + 1) * NSEQ])
        in_maps.append(m)
    res = run_bass_kernel_spmd(nc, in_maps, core_ids=list(range(NCORES)))
    ys = np.concatenate([np.asarray(r["y"], dtype=np.float32) for r in res.results], axis=0)
    nb = np.asarray(x_prompt).shape[0]
    return (np.ascontiguousarray(ys[:nb]), np.ascontiguousarray(ys[nb:]))
```

```python
import math
import os
from contextlib import ExitStack

import numpy as np
import concourse.bass as bass
import concourse.mybir as mybir
from concourse.bass_utils import run_bass_kernel_spmd

F32 = mybir.dt.float32
BF16 = mybir.dt.bfloat16
AF = mybir.ActivationFunctionType
ALU = mybir.AluOpType

D = 1024
S = 2048
NT = 16
DEPTH = 2
NCORES = 8
NSEQ = 3
INW = 3584
EPS = 1e-6


class Tok:
    __slots__ = ("eng", "key", "val")

    def __init__(self, eng, key, val):
        self.eng, self.key, self.val = eng, key, val


class Res:
    __slots__ = ("name", "w", "r")

    def __init__(self, name=""):
        self.name, self.w, self.r = name, None, []


class Ctx:
    def __init__(self, nc, es):
        self.nc, self.es = nc, es
        self.engs = {"pe": nc.tensor, "act": nc.scalar, "dve": nc.vector, "pool": nc.gpsimd, "sp": nc.sync}
        self.sems, self.cnt = {}, {}
        self.seen = {e: {} for e in self.engs}
        self.epoch = 0
        self.dma_keys = set()
        self.new_epoch()

    def _mksem(self, key):
        if key not in self.sems:
            self.sems[key] = self.es.enter_context(self.nc.semaphore(key))
            self.cnt[key] = 0

    def new_epoch(self):
        self.epoch += 1
        self.ekey = {e: "%s%d" % (e, self.epoch) for e in self.engs if e != "sp"}
        for k in self.ekey.values():
            self._mksem(k)

    def _waits(self, eng, reads, writes, skipkey=None):
        deps = {}

        def need(tok, kind):
            if tok is None or tok.key == skipkey:
                return
            if tok.eng == eng and (eng == "pe" or kind != "RAW"):
                return
            if deps.get(tok.key, 0) < tok.val:
                deps[tok.key] = tok.val

        for r in reads:
            need(r.w, "RAW")
        for w in writes:
            need(w.w, "WAW")
            for t in w.r:
                need(t, "WAR")
        E, seen = self.engs[eng], self.seen[eng]
        for key, val in deps.items():
            if seen.get(key, 0) < val:
                E.wait_ge(self.sems[key], val)
                seen[key] = val

    def _commit(self, tok, reads, writes):
        for r in reads:
            r.r = [t for t in r.r if t.key != tok.key] + [tok]
        for w in writes:
            w.w, w.r = tok, []

    def op(self, eng, fns, reads=(), writes=()):
        self._waits(eng, reads, writes)
        if not isinstance(fns, (list, tuple)):
            fns = [fns]
        ins = None
        for f in fns:
            ins = f(self.engs[eng])
        key = self.ekey[eng]
        self.cnt[key] += 1
        ins.then_inc(self.sems[key], 1)
        tok = Tok(eng, key, self.cnt[key])
        self._commit(tok, reads, writes)
        return tok

    def dma(self, q, semname, fn, reads=(), writes=()):
        key = "d%d_%s" % (self.epoch, semname)
        self._mksem(key)
        self.dma_keys.add(key)
        self._waits(q, reads, writes, skipkey=key)
        ins = fn(self.engs[q])
        self.cnt[key] += 16
        ins.then_inc(self.sems[key], 16)
        tok = Tok("dma", key, self.cnt[key])
        self._commit(tok, reads, writes)
        return tok

    def barrier(self, with_dma=True):
        keys = list(self.ekey.values())
        if with_dma:
            keys += sorted(self.dma_keys)
        for e, E in self.engs.items():
            seen = self.seen[e]
            for key in keys:
                val = self.cnt[key]
                if val > 0 and seen.get(key, 0) < val:
                    E.wait_ge(self.sems[key], val)
                    seen[key] = val


def _pipeline(n_iter, stages):
    mx = max(sk for sk, _ in stages)
    for i in range(n_iter + mx):
        for sk, fn in stages:
            n = i - sk
            if 0 <= n < n_iter:
                fn(n)


def _ap(base, dims):
    return bass.AP(base.tensor, base.offset, [list(base.ap[0])] + [list(d) for d in dims])


class _Stop(Exception):
    pass


def build_nc(nseq=NSEQ, depth=DEPTH, dbg=(), stop=None):
    nc = bass.Bass("TRN2", target_bir_lowering=False)
    dt = nc.dram_tensor
    x_d = dt("x", [nseq, S, D], F32, kind="ExternalInput").ap()
    c_d = dt("c", [nseq, D], F32, kind="ExternalInput").ap()
    gn_d = dt("g_norm", [DEPTH, D], F32, kind="ExternalInput").ap()
    wada_d = dt("w_ada", [DEPTH, D, 3 * D], F32, kind="ExternalInput").ap()
    bada_d = dt("b_ada", [DEPTH, 3 * D], F32, kind="ExternalInput").ap()
    win_d = dt("w_in", [DEPTH, D, INW], F32, kind="ExternalInput").ap()
    wout_d = dt("w_out", [DEPTH, D, D], F32, kind="ExternalInput").ap()
    dfw_d = dt("decay_fwd", [DEPTH, 4], F32, kind="ExternalInput").ap()
    dbw_d = dt("decay_bwd", [DEPTH, 4], F32, kind="ExternalInput").ap()
    gfin_d = dt("g_final", [D], F32, kind="ExternalInput").ap()
    crope_d = dt("cst_rope", [128, 2, NT, 64], F32, kind="ExternalInput").ap()
    cmask_d = dt("cst_mask", [128, 4, 512], F32, kind="ExternalInput").ap()
    cret_d = dt("cst_ret", [128, 4, 128], F32, kind="ExternalInput").ap()
    ctau_d = dt("cst_tau", [128, 4], F32, kind="ExternalInput").ap()
    cid_d = dt("cst_ident", [128, 128], F32, kind="ExternalInput").ap()
    y_d = dt("y", [nseq, S, D], F32, kind="ExternalOutput").ap()
    dbg_d = {}
    for name, shape in dbg:
        dbg_d[name] = dt("dbg_" + name, list(shape), F32, kind="ExternalOutput").ap()

    with ExitStack() as es:
        E = es.enter_context
        cx = Ctx(nc, es)
        sb = lambda name, shape, dtype: E(nc.sbuf_tensor(name, list(shape), dtype))
        uid = [0]

        def PT(ph, name, shape, dtype):
            uid[0] += 1
            return ph.enter_context(nc.sbuf_tensor("%s_u%d" % (name, uid[0]), list(shape), dtype))

        hT = sb("hT", [128, 8, S], BF16)
        yT = sb("yT", [128, 8, S], BF16)
        wsl = [sb("wsl%d" % i, [128, 8, 512], BF16) for i in range(3)]
        kT1 = sb("kT1", [128, S + 128], BF16)
        kT2 = sb("kT2", [128, 4, 640], BF16)
        VT1 = sb("VT1", [128, S + 128], BF16)
        VT2 = sb("VT2", [128, 4, 640], BF16)
        ident = sb("ident", [128, 128], BF16)
        identf = sb("identf", [128, 128], F32)
        onesf = sb("onesf", [128, 128], F32)
        rope_t = sb("rope_t", [128, 2, NT, 64], F32)
        maskA = sb("maskA", [128, 4, 512], BF16)
        cret = sb("cret", [128, 4, 128], F32)
        ctau = sb("ctau", [128, 4], F32)
        DTt_all = sb("DTt", [128, DEPTH, 4, 128], F32)
        TAB_all = sb("TAB", [128, DEPTH, 4, 4, 64], F32)
        Gfb_all = sb("Gfb", [128, DEPTH, 2, 2], F32)
        gsA = sb("gsA", [128, DEPTH, nseq, 8], F32)
        shA = sb("shA", [128, DEPTH, nseq, 8], F32)
        gtA = sb("gtA", [128, DEPTH, nseq, 8], F32)
        mhalf = sb("mhalf", [128, 16], F32)

        mm = [E(nc.psum_tensor("mm%d" % i, [128, 512], F32)) for i in range(2)]
        tp = [E(nc.psum_tensor("tp%d" % i, [128, 1024], BF16)) for i in range(2)]
        st = [E(nc.psum_tensor("st%d" % i, [128, 512], F32)) for i in range(2)]
        ov = [E(nc.psum_tensor("ov%d" % i, [128, 512], F32)) for i in range(2)]
        R_mm = [Res("mm0"), Res("mm1")]
        R_tp = [Res("tp0"), Res("tp1")]
        R_st = [Res("st0"), Res("st1")]
        R_ov = [Res("ov0"), Res("ov1")]
        ctr = {"mm": 0, "tp": 0, "st": 0, "ov": 0, "w": 0, "fin": 0}

        def nxt(kind):
            i = ctr[kind] % 2
            ctr[kind] += 1
            return i

        R_const = Res("const")
        R_hT = [Res("hT%d" % g) for g in range(4)]
        R_yT = [Res("yT%d" % k) for k in range(8)]
        R_w = [Res("w%d" % i) for i in range(3)]
        R_kT1, R_kT2, R_VT1, R_VT2 = Res("kT1"), Res("kT2"), Res("VT1"), Res("VT2")
        R_tab, R_gate, R_mod = Res("tab"), Res("gate"), Res("mod")
        R_yd = [[Res("yd%d_%d" % (s, t)) for t in range(NT)] for s in range(nseq)]
        dump_list = []

        def dump(name, src_ap, reads):
            if name in dbg_d:
                dump_list.append(cx.dma("pool", "dbg", lambda e: e.dma_start(out=dbg_d[name], in_=src_ap), reads=reads))

        cx.dma("sp", "cst", lambda e: e.dma_start(out=rope_t[:], in_=crope_d), writes=[R_const])
        cx.dma("pool", "cstp", lambda e: e.dma_start(out=maskA[:], in_=cmask_d), writes=[R_const])
        cx.dma("sp", "cst", lambda e: e.dma_start(out=cret[:], in_=cret_d), writes=[R_const])
        cx.dma("sp", "cst", lambda e: e.dma_start(out=ctau[:], in_=ctau_d), writes=[R_const])
        cx.dma("pool", "cstp", lambda e: e.dma_start(out=ident[:], in_=cid_d), writes=[R_const])
        cx.dma("sp", "cst", lambda e: e.dma_start(out=identf[:], in_=cid_d), writes=[R_const])
        cx.op("pool", lambda e: e.memset(onesf[:], 1.0), writes=[R_const])
        cx.op("pool", lambda e: e.memset(mhalf[:], -0.5), writes=[R_const])
        cx.op("pool", lambda e: e.memset(kT1[:], 0.0), writes=[R_kT1])
        cx.op("pool", lambda e: e.memset(kT2[:], 0.0), writes=[R_kT2])
        cx.op("pool", lambda e: e.memset(VT1[:], 0.0), writes=[R_VT1])
        cx.op("pool", lambda e: e.memset(VT2[:], 0.0), writes=[R_VT2])

        with ExitStack() as ph:
            P = ph.enter_context
            wada = PT(ph, "wada", [128, 8, 3 * D], BF16)
            cT = PT(ph, "cT", [128, nseq, 8], F32)
            silc = PT(ph, "silc", [128, 8, nseq], BF16)
            badaT = PT(ph, "badaT", [128, 24], F32)
            gnT = PT(ph, "gnT", [128, 8], F32)
            modT = PT(ph, "modT", [128, 24, nseq], F32)
            R_wada, R_cT, R_silc, R_bada, R_gn, R_modT = (Res() for _ in range(6))
            for s in range(nseq):
                cx.dma("sp", "cst", lambda e, s=s: e.dma_start(
                    out=cT[:, s, :], in_=c_d[s].rearrange("(k p) -> p k", p=128), allow_slow_non_contiguous=True),
                    writes=[R_cT])
            cx.op("act", lambda e: e.activation(out=silc[:].rearrange("p k s -> p s k"), in_=cT[:], func=AF.Silu),
                  reads=[R_cT], writes=[R_silc])
            for l in range(depth):
                for i in range(6):
                    cx.dma("pool", "wada", lambda e, i=i, l=l: e.dma_start(
                        out=wada[:, :, i * 512:(i + 1) * 512],
                        in_=wada_d[l].rearrange("(k p) n -> p k n", p=128)[:, :, i * 512:(i + 1) * 512]),
                        writes=[R_wada])
                cx.dma("sp", "cst", lambda e, l=l: e.dma_start(
                    out=badaT[:], in_=bada_d[l].rearrange("(o p) -> p o", p=128), allow_slow_non_contiguous=True),
                    writes=[R_bada])
                cx.dma("sp", "cst", lambda e, l=l: e.dma_start(
                    out=gnT[:], in_=gn_d[l].rearrange("(k p) -> p k", p=128), allow_slow_non_contiguous=True),
                    writes=[R_gn])
                fns = []
                for oc in range(24):
                    for kc in range(8):
                        fns.append(lambda e, oc=oc, kc=kc: e.matmul(
                            mm[0][:, oc * nseq:(oc + 1) * nseq], lhsT=wada[:, kc, oc * 128:(oc + 1) * 128],
                            rhs=silc[:, kc, :], start=(kc == 0), stop=(kc == 7)))
                cx.op("pe", fns, reads=[R_wada, R_silc], writes=[R_mm[0]])
                mmv = mm[0][:, 0:24 * nseq].rearrange("p (o s) -> p o s", s=nseq)
                for s in range(nseq):
                    cx.op("dve", lambda e, s=s: e.tensor_tensor(out=modT[:, :, s], in0=mmv[:, :, s], in1=badaT[:], op=ALU.add),
                          reads=[R_mm[0], R_bada], writes=[R_modT])
                for s in range(nseq):
                    cx.op("dve", lambda e, s=s, l=l: e.scalar_tensor_tensor(
                        out=gsA[:, l, s, :], in0=modT[:, 8:16, s], scalar=1.0, in1=gnT[:], op0=ALU.add, op1=ALU.mult),
                        reads=[R_modT, R_gn], writes=[R_mod])
                    cx.op("dve", lambda e, s=s, l=l: e.tensor_copy(out=shA[:, l, s, :], in_=modT[:, 0:8, s]),
                          reads=[R_modT], writes=[R_mod])
                    cx.op("dve", lambda e, s=s, l=l: e.tensor_copy(out=gtA[:, l, s, :], in_=modT[:, 16:24, s]),
                          reads=[R_modT], writes=[R_mod])
            cx.barrier()

        for l in range(depth):
            DTt, TAB, Gfb = DTt_all[:, l], TAB_all[:, l], Gfb_all[:, l]
            with ExitStack() as ph:
                P = ph.enter_context
                dfb = PT(ph, "dfb", [128, 8], F32)
                lg = PT(ph, "lg", [128, 8], F32)
                dsc = PT(ph, "dsc", [128, 4, 4], F32)
                e1 = PT(ph, "e1", [128, 128], F32)
                e2 = PT(ph, "e2", [128, 128], F32)
                R_dfb, R_lg, R_dsc, R_e1, R_e2 = (Res() for _ in range(5))
                cx.dma("sp", "cst", lambda e: e.dma_start(out=dfb[:, 0:4], in_=dfw_d[l].partition_broadcast(128)), writes=[R_dfb])
                cx.dma("sp", "cst", lambda e: e.dma_start(out=dfb[:, 4:8], in_=dbw_d[l].partition_broadcast(128)), writes=[R_dfb])
                cx.op("act", lambda e: e.activation(out=lg[:], in_=dfb[:], func=AF.Exp, scale=-1.0), reads=[R_dfb], writes=[R_lg])
                cx.op("act", lambda e: e.activation(out=lg[:], in_=lg[:], func=AF.Ln, bias=1.0), reads=[R_lg], writes=[R_lg])
                cx.op("dve", lambda e: e.tensor_scalar(out=lg[:], in0=lg[:], scalar1=-1.0, scalar2=None, op0=ALU.mult),
                      reads=[R_lg], writes=[R_lg])
                for kind in range(4):
                    lo = 0 if kind in (0, 2) else 4
                    cx.op("act", lambda e, kind=kind, lo=lo: e.activation(
                        out=dsc[:, kind, :], in_=lg[:, lo:lo + 4], func=AF.Exp, scale=ctau[:, kind:kind + 1]),
                        reads=[R_lg, R_const], writes=[R_dsc])
                for kind in range(4):
                    for h in range(4):
                        cx.op("dve", lambda e, kind=kind, h=h: e.tensor_scalar(
                            out=TAB[:, kind, h, :], in0=onesf[:, 0:64], scalar1=dsc[:, kind, h:h + 1],
                            scalar2=(0.125 if kind < 2 else 1.0), op0=ALU.mult, op1=ALU.mult),
                            reads=[R_dsc, R_const], writes=[R_tab])
                for h in range(4):
                    cx.op("act", lambda e, h=h: e.activation(out=e1[:], in_=cret[:, 0, :], func=AF.Exp, scale=lg[:, h:h + 1]),
                          reads=[R_lg, R_const], writes=[R_e1])
                    cx.op("dve", lambda e: e.tensor_tensor(out=e1[:], in0=e1[:], in1=cret[:, 1, :], op=ALU.mult),
                          reads=[R_e1, R_const], writes=[R_e1])
                    cx.op("act", lambda e, h=h: e.activation(out=e2[:], in_=cret[:, 2, :], func=AF.Exp, scale=lg[:, 4 + h:5 + h]),
                          reads=[R_lg, R_const], writes=[R_e2])
                    cx.op("dve", lambda e: e.tensor_tensor(out=e2[:], in0=e2[:], in1=cret[:, 3, :], op=ALU.mult),
                          reads=[R_e2, R_const], writes=[R_e2])
                    cx.op("dve", lambda e, h=h: e.tensor_tensor(out=DTt[:, h, :], in0=e1[:], in1=e2[:], op=ALU.add),
                          reads=[R_e1, R_e2], writes=[R_tab])
                for d_ in range(2):
                    for p in range(2):
                        for hh in range(2):
                            rows = slice(hh * 64, hh * 64 + 64)
                            col = d_ * 4 + 2 * p + hh
                            cx.op("act", lambda e, d_=d_, p=p, rows=rows, col=col: e.activation(
                                out=Gfb[rows, d_, p:p + 1], in_=lg[rows, col:col + 1], func=AF.Exp, scale=128.0),
                                reads=[R_lg], writes=[R_tab])
                cx.barrier()

        R_wser = Res("wser")

        def load_w(slot, src_ap, ncols):
            if len(src_ap.shape) == 3:
                view = wsl[slot][:, :, 0:ncols]
                cx.dma("pool", "w%d" % slot, lambda e: e.dma_start(out=view, in_=src_ap), writes=[R_w[slot], R_wser])
            else:
                nr = src_ap.shape[2]
                w_ = src_ap.shape[3]
                for r in range(nr):
                    view = wsl[slot][:, :, r * w_:(r + 1) * w_]
                    cx.dma("pool", "w%d" % slot, lambda e, view=view, r=r: e.dma_start(out=view, in_=src_ap[:, :, r, :]),
                           writes=[R_w[slot], R_wser])

        def rope(src_psum, R_src, t, out_ap, R_out, tmp1, tmp2, R_t1, R_t2):
            xv = src_psum.rearrange("p (h d) -> p h d", d=64)
            cb = _ap(rope_t[:, 0, t, :], [[0, 4], [1, 64]])
            s_lo = _ap(rope_t[:, 1, t, 0:32], [[0, 4], [1, 32]])
            s_hi = _ap(rope_t[:, 1, t, 32:64], [[0, 4], [1, 32]])
            t1v = tmp1.rearrange("p (h d) -> p h d", d=64)
            t2v = tmp2.rearrange("p (h d) -> p h d", d=64)
            cx.op("dve", lambda e: e.tensor_tensor(out=t1v, in0=xv, in1=cb, op=ALU.mult),
                  reads=[R_src, R_const], writes=[R_t1])
            cx.op("dve", [lambda e: e.tensor_tensor(out=t2v[:, :, 0:32], in0=xv[:, :, 32:64], in1=s_lo, op=ALU.mult),
                          lambda e: e.tensor_tensor(out=t2v[:, :, 32:64], in0=xv[:, :, 0:32], in1=s_hi, op=ALU.mult)],
                  reads=[R_src, R_const], writes=[R_t2])
            cx.op("dve", lambda e: e.tensor_tensor(out=out_ap, in0=tmp1, in1=tmp2, op=ALU.add),
                  reads=[R_t1, R_t2], writes=[R_out])

        def proj_tok(t, wslot, c0, ncols):
            a = nxt("mm")
            fns = [lambda e, kc=kc: e.matmul(mm[a][:, 0:ncols], lhsT=hT[:, kc, t * 128:(t + 1) * 128],
                                              rhs=wsl[wslot][:, kc, c0:c0 + ncols], start=(kc == 0), stop=(kc == 7))
                   for kc in range(8)]
            cx.op("pe", fns, reads=[R_hT[t // 4], R_w[wslot]], writes=[R_mm[a]])
            return a

        def proj_feat(g4, wslot, c0):
            a = nxt("mm")
            fns = [lambda e, kc=kc: e.matmul(mm[a][:, :], lhsT=wsl[wslot][:, kc, c0:c0 + 128],
                                              rhs=hT[:, kc, g4 * 512:(g4 + 1) * 512], start=(kc == 0), stop=(kc == 7))
                   for kc in range(8)]
            cx.op("pe", fns, reads=[R_hT[g4], R_w[wslot]], writes=[R_mm[a]])
            return a

        win_v = [win_d[l].rearrange("(k p) n -> p k n", p=128) for l in range(DEPTH)]
        win_a = [win_d[l].rearrange("(k p) (r c) -> p k r c", p=128, r=7) for l in range(DEPTH)]
        wout_v = [wout_d[l].rearrange("(k p) n -> p k n", p=128) for l in range(DEPTH)]

        def chk(name):
            if stop == name:
                raise _Stop()

        try:
          chk("M")
          for s in range(nseq):
            for l in range(depth):
                xsrc = x_d if l == 0 else y_d
                DTt, TAB, Gfb = DTt_all[:, l], TAB_all[:, l], Gfb_all[:, l]
                last = (l == depth - 1)
                if not (s == 0 and l == 0):
                    cx.barrier()
                    cx.new_epoch()
                load_w(0, win_v[l][:, :, 2048:2560], 512)
                if s == 0 and l == 0:
                    load_w(1, win_v[l][:, :, 2560:3072], 512)
                load_w(2, win_v[l][:, :, 3072:3584], 512)

                chk("T")

                with ExitStack() as ph:
                  if l == 0:
                      P = ph.enter_context
                      xin = [PT(ph, "xin%d" % i, [128, D], F32) for i in range(4)]
                      xn = [PT(ph, "xn%d" % i, [128, 4, D], BF16) for i in range(2)]
                      junk = PT(ph, "junk", [128, D], BF16)
                      stat = PT(ph, "stat", [128, 3, NT], F32)
                      R_xin = [Res(), Res(), Res(), Res()]
                      R_xn = [Res(), Res()]
                      R_junk = Res()
                      R_stat = [Res() for _ in range(NT)]
                      def st_X(g4):
                          xb = xn[g4 % 2]
                          for j in range(4):
                              t = 4 * g4 + j
                              xi = xin[t % 4]
                              cx.dma("sp", "xin%d" % (t % 4), lambda e, t=t, xi=xi: e.dma_start(out=xi[:], in_=xsrc[s, t * 128:(t + 1) * 128, :]),
                                     reads=[R_yd[s][t]], writes=[R_xin[t % 4]])
                              cx.op("pool", lambda e, t=t: e.memset(stat[:, 0, t:t + 1], 0.0), writes=[R_stat[t]])
                              cx.op("act", lambda e, t=t, xi=xi: e.activation(out=junk[:], in_=xi[:], func=AF.Square,
                                                                             accum_out=stat[:, 0, t:t + 1]),
                                    reads=[R_xin[t % 4]], writes=[R_junk, R_stat[t]])
                              cx.op("dve", lambda e, t=t: e.tensor_scalar(out=stat[:, 1, t:t + 1], in0=stat[:, 0, t:t + 1], scalar1=1.0 / D,
                                                                          scalar2=EPS, op0=ALU.mult, op1=ALU.add),
                                    reads=[R_stat[t]], writes=[R_stat[t]])
                              cx.op("pool", lambda e, t=t: e.tensor_tensor(out=stat[:, 2, t:t + 1], in0=stat[:, 1, t:t + 1], in1=mhalf[:, 0:1], op=ALU.pow),
                                    reads=[R_stat[t], R_const], writes=[R_stat[t]])
                              cx.op("dve", lambda e, t=t, xi=xi, j=j, xb=xb: e.tensor_scalar(
                                  out=xb[:, j, :], in0=xi[:], scalar1=stat[:, 2, t:t + 1], scalar2=None, op0=ALU.mult),
                                  reads=[R_xin[t % 4], R_stat[t]], writes=[R_xn[g4 % 2]])

                      def st_H(g4):
                          xb = xn[g4 % 2]
                          for k in range(8):
                              b = nxt("tp")
                              fns = [lambda e, j=j, k=k, b=b, xb=xb: e.transpose(tp[b][:, j * 128:(j + 1) * 128],
                                                                                xb[:, j, k * 128:(k + 1) * 128], ident[:])
                                     for j in range(4)]
                              cx.op("pe", fns, reads=[R_xn[g4 % 2], R_const], writes=[R_tp[b]])
                              cx.op("act", lambda e, k=k, b=b, g4=g4: e.activation(
                                  out=hT[:, k, g4 * 512:(g4 + 1) * 512], in_=tp[b][:, 0:512], func=AF.Identity,
                                  scale=gsA[:, l, s, k:k + 1], bias=shA[:, l, s, k:k + 1]),
                                  reads=[R_tp[b], R_mod], writes=[R_hT[g4]])

                      _pipeline(4, [(0, st_X), (1, st_H)])
                      cx.barrier()
                if s == 0 and l == 0:
                    dump("hT", hT[:, 0, :], [R_hT[0], R_hT[1], R_hT[2], R_hT[3]])
                chk("N")

                with ExitStack() as ph:
                    P = ph.enter_context
                    kbT = PT(ph, "kbT", [128, 2, S], BF16)
                    vb = PT(ph, "vb", [128, NT, 512], BF16)
                    kdb = PT(ph, "kdb", [128, NT, 256], BF16)
                    SfB = PT(ph, "SfB", [128, NT, 2, 128], BF16)
                    SbB = PT(ph, "SbB", [128, NT, 2, 128], BF16)
                    Sst = PT(ph, "Sst", [128, 2, 2, 128], F32)
                    rt1 = PT(ph, "rt1", [128, 256], F32)
                    rt2 = PT(ph, "rt2", [128, 256], F32)
                    kr = [PT(ph, "kr%d" % i, [128, 256], F32) for i in range(2)]
                    kbf = [PT(ph, "kbf%d" % i, [128, 256], BF16) for i in range(2)]
                    kdf = [PT(ph, "kdf%d" % i, [128, 256], BF16) for i in range(2)]
                    q3 = [PT(ph, "q3_%d" % i, [128, 3, 256], BF16) for i in range(2)]
                    qT3 = [PT(ph, "qT3_%d" % i, [128, 6, 128], BF16) for i in range(2)]
                    gsl = [PT(ph, "gsl%d" % i, [128, 512], BF16) for i in range(2)]
                    inT = [PT(ph, "inT%d" % i, [128, 512], BF16) for i in range(2)]
                    yr = [PT(ph, "yr%d" % i, [128, 512], BF16) for i in range(2)]
                    junk2 = PT(ph, "junk2", [128, 128], BF16)
                    gst = [PT(ph, "gst%d" % i, [128, 3, 4], F32) for i in range(2)]
                    R_kbT = [Res() for _ in range(NT)]
                    R_vb = [Res() for _ in range(NT)]
                    R_kdb = [Res() for _ in range(NT)]
                    R_SfB = [Res() for _ in range(NT)]
                    R_SbB = [Res() for _ in range(NT)]
                    R_Sst = [Res(), Res()]
                    R_rt1, R_rt2, R_junk2 = Res(), Res(), Res()
                    R_kr, R_kbf, R_kdf, R_q3, R_qT3, R_gsl, R_inT, R_yr, R_gst = (
                        [Res(), Res()] for _ in range(9))
                    TABv = lambda kind: TAB[:, kind, :, :].rearrange("p h d -> p (h d)")
                    R_q3a, R_q3b, R_q3c = ([Res(), Res()] for _ in range(3))
                    cx.op("pool", lambda e: e.memset(Sst[:], 0.0), writes=R_Sst)

                    def scan_update(d_, a):
                        fns = []
                        for p in range(2):
                            for hh in range(2):
                                rows = slice(hh * 64, hh * 64 + 64)
                                fns.append(lambda e, p=p, hh=hh, rows=rows: e.scalar_tensor_tensor(
                                    out=Sst[rows, d_, p, :], in0=Sst[rows, d_, p, :], scalar=Gfb[rows, d_, p:p + 1],
                                    in1=ov[a][rows, p * 256 + hh * 128:p * 256 + hh * 128 + 128], op0=ALU.mult, op1=ALU.add))
                        return fns

                    def b1_P(n):
                        i2 = n % 2
                        a = proj_tok(n, 0, 256, 256)
                        rope(mm[a][:, 0:256], R_mm[a], n, kr[i2][:], R_kr[i2], rt1[:], rt2[:], R_rt1, R_rt2)
                        cx.op("act", lambda e, i2=i2: e.activation(out=kbf[i2][:], in_=kr[i2][:], func=AF.Copy, scale=0.125),
                              reads=[R_kr[i2]], writes=[R_kbf[i2]])
                        cx.op("pool", lambda e, i2=i2: e.tensor_tensor(out=kdf[i2][:], in0=kr[i2][:], in1=TABv(0), op=ALU.mult),
                              reads=[R_kr[i2], R_tab], writes=[R_kdf[i2]])
                        cx.op("dve", lambda e, i2=i2, n=n: e.tensor_tensor(out=kdb[:, n, :], in0=kr[i2][:], in1=TABv(1), op=ALU.mult),
                              reads=[R_kr[i2], R_tab], writes=[R_kdb[n]])
                        a2 = proj_tok(n, 1, 0, 512)
                        cx.op("act", lambda e, a2=a2, n=n: e.activation(out=vb[:, n, :], in_=mm[a2][:, :], func=AF.Copy),
                              reads=[R_mm[a2]], writes=[R_vb[n]])

                    def b1_T(n):
                        i2 = n % 2
                        b = nxt("tp")
                        cx.op("pe", [lambda e, p=p, b=b, i2=i2: e.transpose(tp[b][:, p * 128:(p + 1) * 128], kbf[i2][:, p * 128:(p + 1) * 128], ident[:])
                                     for p in range(2)], reads=[R_kbf[i2], R_const], writes=[R_tp[b]])
                        cx.op("act", lambda e, b=b, n=n: e.activation(
                            out=kbT[:, :, n * 128:(n + 1) * 128], in_=tp[b][:, 0:256].rearrange("p (a c) -> p a c", a=2), func=AF.Copy),
                            reads=[R_tp[b]], writes=[R_kbT[n]])
                        cx.op("dve", lambda e, n=n: e.tensor_copy(out=SfB[:, n, :, :], in_=Sst[:, 0, :, :]),
                              reads=[R_Sst[0]], writes=[R_SfB[n]])
                        if n < NT - 1:
                            o = nxt("ov")
                            cx.op("pe", [lambda e, p=p, o=o, i2=i2, n=n: e.matmul(
                                ov[o][:, p * 256:(p + 1) * 256], lhsT=kdf[i2][:, p * 128:(p + 1) * 128],
                                rhs=vb[:, n, p * 256:(p + 1) * 256], start=True, stop=True) for p in range(2)],
                                reads=[R_kdf[i2], R_vb[n]], writes=[R_ov[o]])
                            cx.op("dve", scan_update(0, o), reads=[R_ov[o], R_tab, R_Sst[0]], writes=[R_Sst[0]])

                    _pipeline(NT, [(0, b1_P), (1, b1_T)])
                    b_stage = {"B1": 1, "Bb": 2}.get(stop, 3)
                    for n in (range(NT - 1, -1, -1) if b_stage >= 2 else []):
                        cx.op("dve", lambda e, n=n: e.tensor_copy(out=SbB[:, n, :, :], in_=Sst[:, 1, :, :]),
                              reads=[R_Sst[1]], writes=[R_SbB[n]])
                        if n > 0:
                            o = nxt("ov")
                            cx.op("pe", [lambda e, p=p, o=o, n=n: e.matmul(
                                ov[o][:, p * 256:(p + 1) * 256], lhsT=kdb[:, n, p * 128:(p + 1) * 128],
                                rhs=vb[:, n, p * 256:(p + 1) * 256], start=True, stop=True) for p in range(2)],
                                reads=[R_kdb[n], R_vb[n]], writes=[R_ov[o]])
                            cx.op("dve", scan_update(1, o), reads=[R_ov[o], R_tab, R_Sst[1]], writes=[R_Sst[1]])
                    def b2_P(n):
                        i2 = n % 2
                        a = proj_tok(n, 0, 0, 256)
                        rope(mm[a][:, 0:256], R_mm[a], n, kr[i2][:], R_kr[i2], rt1[:], rt2[:], R_rt1, R_rt2)
                        cx.op("act", lambda e, i2=i2: e.activation(out=q3[i2][:, 0, :], in_=kr[i2][:], func=AF.Copy),
                              reads=[R_kr[i2]], writes=[R_q3a[i2]])
                        cx.op("pool", lambda e, i2=i2: e.tensor_tensor(out=q3[i2][:, 1, :], in0=kr[i2][:], in1=TABv(2), op=ALU.mult),
                              reads=[R_kr[i2], R_tab], writes=[R_q3b[i2]])
                        cx.op("dve", lambda e, i2=i2: e.tensor_tensor(out=q3[i2][:, 2, :], in0=kr[i2][:], in1=TABv(3), op=ALU.mult),
                              reads=[R_kr[i2], R_tab], writes=[R_q3c[i2]])

                    def b2_T(n):
                        i2 = n % 2
                        b = nxt("tp")
                        cx.op("pe", [lambda e, v=v, p=p, b=b, i2=i2: e.transpose(
                            tp[b][:, (v * 2 + p) * 128:(v * 2 + p + 1) * 128], q3[i2][:, v, p * 128:(p + 1) * 128], ident[:])
                            for v in range(3) for p in range(2)], reads=[R_q3a[i2], R_q3b[i2], R_q3c[i2], R_const], writes=[R_tp[b]])
                        cx.op("act", lambda e, b=b, i2=i2: e.activation(
                            out=qT3[i2][:], in_=tp[b][:, 0:768].rearrange("p (a c) -> p a c", a=6), func=AF.Copy),
                            reads=[R_tp[b]], writes=[R_qT3[i2]])

                    def b2_T2(n):
                        i2 = n % 2
                        fns = []
                        for h in range(4):
                            p, hh = h // 2, h % 2
                            rows = slice(hh * 64, hh * 64 + 64)
                            fns.append(lambda e, p=p, hh=hh, rows=rows, i2=i2, n=n: e.matmul(
                                st[hh][:, p * 128:(p + 1) * 128], lhsT=kbT[rows, p, n * 128:(n + 1) * 128],
                                rhs=qT3[i2][rows, p, :], start=True, stop=True))
                        cx.op("pe", fns, reads=[R_kbT[n], R_qT3[i2]], writes=[R_st[0], R_st[1]])
                        inv = inT[i2][:].rearrange("p (a b t) -> p a b t", a=2, b=2)
                        for hh in range(2):
                            cx.op("dve", lambda e, hh=hh, inv=inv: e.tensor_tensor(
                                out=inv[:, :, hh, :], in0=st[hh][:, 0:256].rearrange("p (a t) -> p a t", a=2),
                                in1=DTt[:].rearrange("p (a b) t -> p a b t", b=2)[:, :, hh, :], op=ALU.mult),
                                reads=[R_st[hh], R_tab], writes=[R_inT[i2]])
                        a2 = proj_tok(n, 2, 0, 512)
                        cx.op("act", lambda e, a2=a2, i2=i2: e.activation(out=gsl[i2][:], in_=mm[a2][:, :], func=AF.Silu),
                              reads=[R_mm[a2]], writes=[R_gsl[i2]])

                    def b2_Y(n):
                        i2 = n % 2
                        o = nxt("ov")
                        fns = []
                        for h in range(4):
                            p, hh = h // 2, h % 2
                            rows = slice(hh * 64, hh * 64 + 64)
                            oc = slice(h * 128, (h + 1) * 128)
                            fns.append(lambda e, oc=oc, o=o, i2=i2, n=n: e.matmul(
                                ov[o][:, oc], lhsT=inT[i2][:, oc], rhs=vb[:, n, oc], start=True, stop=False))
                            fns.append(lambda e, oc=oc, o=o, i2=i2, n=n, p=p, rows=rows: e.matmul(
                                ov[o][:, oc], lhsT=qT3[i2][rows, 2 + p, :], rhs=SfB[rows, n, p, :], start=False, stop=False))
                            fns.append(lambda e, oc=oc, o=o, i2=i2, n=n, p=p, rows=rows: e.matmul(
                                ov[o][:, oc], lhsT=qT3[i2][rows, 4 + p, :], rhs=SbB[rows, n, p, :], start=False, stop=True))
                        cx.op("pe", fns, reads=[R_inT[i2], R_vb[n], R_qT3[i2], R_SfB[n], R_SbB[n]], writes=[R_ov[o]])
                        cx.op("pool", lambda e, i2=i2: e.memset(gst[i2][:, 0, :], 0.0), writes=[R_gst[i2]])
                        cx.op("act", [lambda e, h=h, o=o, i2=i2: e.activation(
                            out=junk2[:], in_=ov[o][:, h * 128:(h + 1) * 128], func=AF.Square, accum_out=gst[i2][:, 0, h:h + 1])
                            for h in range(4)], reads=[R_ov[o]], writes=[R_junk2, R_gst[i2]])
                        cx.op("dve", lambda e, i2=i2: e.tensor_scalar(out=gst[i2][:, 1, :], in0=gst[i2][:, 0, :], scalar1=1.0 / 128,
                                                                      scalar2=EPS, op0=ALU.mult, op1=ALU.add),
                              reads=[R_gst[i2]], writes=[R_gst[i2]])
                        cx.op("pool", lambda e, i2=i2: e.tensor_tensor(out=gst[i2][:, 2, :], in0=gst[i2][:, 1, :], in1=mhalf[:, 0:4], op=ALU.pow),
                              reads=[R_gst[i2], R_const], writes=[R_gst[i2]])
                        cx.op("dve", [lambda e, h=h, o=o, i2=i2: e.scalar_tensor_tensor(
                            out=yr[i2][:, h * 128:(h + 1) * 128], in0=ov[o][:, h * 128:(h + 1) * 128],
                            scalar=gst[i2][:, 2, h:h + 1], in1=gsl[i2][:, h * 128:(h + 1) * 128], op0=ALU.mult, op1=ALU.mult)
                            for h in range(4)], reads=[R_ov[o], R_gst[i2], R_gsl[i2]], writes=[R_yr[i2]])

                    def b2_Z(n):
                        i2 = n % 2
                        b = nxt("tp")
                        cx.op("pe", [lambda e, h=h, b=b, i2=i2: e.transpose(tp[b][:, h * 128:(h + 1) * 128], yr[i2][:, h * 128:(h + 1) * 128], ident[:])
                                     for h in range(4)], reads=[R_yr[i2], R_const], writes=[R_tp[b]])
                        cx.op("act", lambda e, b=b, n=n: e.activation(
                            out=yT[:, 4:8, n * 128:(n + 1) * 128], in_=tp[b][:, 0:512].rearrange("p (a c) -> p a c", a=4), func=AF.Copy),
                            reads=[R_tp[b]], writes=R_yT[4:8])

                    if b_stage >= 3:
                        load_w(1, win_a[l][:, :, 0:4, 0:128], 512)
                        _pipeline(NT, [(0, b2_P), (1, b2_T), (2, b2_Y), (3, b2_Z), (1, b2_T2)])
                    cx.barrier()
                if s == 0 and l == 0:
                    dump("yrT", yT[:, 4, :], R_yT[4:8])
                if stop in ("B1", "Bb", "B2a", "B2b", "B2c", "B2d"):
                    raise _Stop()
                chk("B")

                with ExitStack() as ph:
                    P = ph.enter_context
                    qT = PT(ph, "qT", [128, S], BF16)
                    gaT2 = [PT(ph, "gaT%d" % i, [128, S], BF16) for i in range(2)]
                    Vp = [PT(ph, "Vp%d" % i, [128, 20, 2, 128], BF16) for i in range(2)]
                    ACC = [PT(ph, "ACC%d" % i, [128, S], F32) for i in range(2)]
                    rt1 = PT(ph, "art1", [128, 256], F32)
                    rt2 = PT(ph, "art2", [128, 256], F32)
                    qkr = [PT(ph, "qkr%d" % i, [128, 256], BF16) for i in range(2)]
                    pt = [PT(ph, "pt%d" % i, [128, 512], BF16) for i in range(4)]
                    Rr = [PT(ph, "Rr%d" % i, [128, 512], F32) for i in range(2)]
                    Tm = [PT(ph, "Tm%d" % i, [128, 512], F32) for i in range(2)]
                    R_qT = [Res() for _ in range(NT)]
                    R_gaT2 = [[Res() for _ in range(4)] for _ in range(2)]
                    R_Vp = [Res(), Res()]
                    R_ACC = [[Res() for _ in range(4)] for _ in range(2)]
                    R_rt1, R_rt2 = Res(), Res()
                    R_qkr, R_pt, R_Rr, R_Tm = ([Res(), Res(), Res(), Res()] for _ in range(4))
                    sbank = [st[0], st[1], mm[0], mm[1]]
                    R_sbank = [R_st[0], R_st[1], R_mm[0], R_mm[1]]
                    def vp_init():
                        for i in range(2):
                            vflat = Vp[i][:].rearrange("p a b c -> p (a b) c")
                            for q_ in range(5):
                                cx.op("act", lambda e, i=i, q_=q_, vflat=vflat: e.activation(
                                    out=vflat[:, q_ * 8:(q_ + 1) * 8, :], in_=_ap(onesf[:, 0:128], [[0, 8], [1, 128]]), func=AF.Copy),
                                    reads=[R_const], writes=[R_Vp[i]])
                    if os.environ.get("DUMMY_INIT"):
                        for q_ in range(10):
                            cx.op("act", lambda e, q_=q_: e.activation(out=ACC[0][:, q_ * 128:(q_ + 1) * 128], in_=onesf[:, 0:128], func=AF.Copy),
                                  reads=[R_const], writes=[R_ACC[0]])
                    elif not os.environ.get("VPM_LATE") and not os.environ.get("SKIP_VPM"):
                        vp_init()
                    vpc = [0]
                    fin_pend = []
                    for j in range(4):
                        wslot = [1, 0, 2, 1][j]
                        gaT, R_gaT = gaT2[j % 2], R_gaT2[j % 2]
                        def a1_P(t, wslot=wslot, gaT=gaT, R_gaT=R_gaT):
                            i2 = t % 2
                            a = proj_tok(t, wslot, 0, 256)
                            rope(mm[a][:, 0:256], R_mm[a], t, qkr[i2][:], R_qkr[i2], rt1[:], rt2[:], R_rt1, R_rt2)
                            if t % 4 == 3:
                                g4 = t // 4
                                a = proj_feat(g4, wslot, 256)
                                cx.op("act", lambda e, a=a, g4=g4: e.activation(out=VT1[:, 64 + g4 * 512:64 + (g4 + 1) * 512], in_=mm[a][:, :], func=AF.Copy),
                                      reads=[R_mm[a]], writes=[R_VT1])
                                cx.op("act", lambda e, a=a, g4=g4: e.activation(
                                    out=VT2[:, :, 64 + g4 * 128:64 + (g4 + 1) * 128],
                                    in_=mm[a][:, :].rearrange("p (l r) -> p r l", r=4), func=AF.Copy),
                                    reads=[R_mm[a]], writes=[R_VT2])
                                a = proj_feat(g4, wslot, 384)
                                cx.op("act", lambda e, a=a, g4=g4: e.activation(out=gaT[:, g4 * 512:(g4 + 1) * 512], in_=mm[a][:, :], func=AF.Silu),
                                      reads=[R_mm[a]], writes=[R_gaT[g4]])

                        def a1_T(t):
                            i2 = t % 2
                            b = nxt("tp")
                            cx.op("pe", [lambda e, p=p, b=b, i2=i2: e.transpose(tp[b][:, p * 128:(p + 1) * 128], qkr[i2][:, p * 128:(p + 1) * 128], ident[:])
                                         for p in range(2)], reads=[R_qkr[i2], R_const], writes=[R_tp[b]])
                            def cp(out_ap, in_ap, R_out, t=t, b=b):
                                if t % 2 == 0:
                                    cx.op("act", lambda e: e.activation(out=out_ap, in_=in_ap, func=AF.Copy), reads=[R_tp[b]], writes=[R_out])
                                else:
                                    cx.op("dve", lambda e: e.tensor_copy(out=out_ap, in_=in_ap), reads=[R_tp[b]], writes=[R_out])
                            cp(qT[:, t * 128:(t + 1) * 128], tp[b][:, 0:128], R_qT[t])
                            cp(kT1[:, 64 + t * 128:64 + (t + 1) * 128], tp[b][:, 128:256], R_kT1)
                            cp(kT2[:, :, 64 + t * 32:64 + (t + 1) * 32], tp[b][:, 128:256].rearrange("p (l r) -> p r l", r=4), R_kT2)

                        _pipeline(NT, [(0, a1_P), (1, a1_T)])
                        if j == 0:
                            load_w(0, win_a[l][:, :, 0:4, 128:256], 512)
                            load_w(2, win_a[l][:, :, 0:4, 256:384], 512)
                        elif j == 1:
                            load_w(1, win_a[l][:, :, 0:4, 384:512], 512)
                        elif j == 2:
                            load_w(0, wout_v[l][:, :, 0:512], 512)
                        else:
                            load_w(2, wout_v[l][:, :, 512:1024], 512)
                            nl, ns = (l + 1, s) if l + 1 < depth else (0, s + 1)
                            if ns < nseq:
                                load_w(1, win_v[nl][:, :, 2560:3072], 512)

                        if os.environ.get("VPM_LATE") and j == 0:
                            vp_init()
                        def build_vp(vi, srcs, R_src):
                            for g0 in range(0, len(srcs), 4):
                                grp = srcs[g0:g0 + 4]
                                b = nxt("tp")
                                cx.op("pe", [lambda e, ii=ii, sa_=sa_, b=b: e.transpose(tp[b][:, ii * 128:(ii + 1) * 128], sa_, ident[:])
                                             for ii, (_, sa_) in enumerate(grp)], reads=[R_src, R_const], writes=[R_tp[b]])
                                i0 = grp[0][0]
                                ng = len(grp)
                                for hh in range(2):
                                    eng = "act" if (g0 // 4) % 2 == 0 else "dve"
                                    src = tp[b][:, 0:ng * 128].rearrange("p (a c) -> p a c", c=128)[:, :, hh * 64:hh * 64 + 64]
                                    dst = Vp[vi][:, i0:i0 + ng, hh, hh * 64:hh * 64 + 64]
                                    if eng == "act":
                                        cx.op("act", lambda e, src=src, dst=dst: e.activation(out=dst, in_=src, func=AF.Copy),
                                              reads=[R_tp[b]], writes=[R_Vp[vi]])
                                    else:
                                        cx.op("dve", lambda e, src=src, dst=dst: e.tensor_copy(out=dst, in_=src),
                                              reads=[R_tp[b]], writes=[R_Vp[vi]])

                        pend = []

                        def flush_pend():
                            while pend:
                                pend.pop(0)()

                        tpf = [tp[0][:].bitcast(F32), tp[1][:].bitcast(F32)]
                        obank = [[ov[0][:, :], tpf[0]], [ov[1][:, :], tpf[1]]]
                        R_obank = [[R_ov[0], R_tp[0]], [R_ov[1], R_tp[1]]]

                        def attend2(vi, items2, mask_variant, R_k, acc2):
                            fc = ctr["ov"]
                            ctr["ov"] += 1
                            nk = len(items2[0][0][1])
                            per = 512 // (128 * nk)
                            ngrp = 4 // per
                            for gi, g0 in enumerate(range(0, 4, per)):
                                base = (ctr["w"] % 2) * 2
                                ctr["w"] += 1
                                mv = mask_variant(g0)
                                fns = [lambda e, sa=base + hh, mv=mv: e.matmul(sbank[sa][:, :], lhsT=ident[:], rhs=maskA[:, mv, :],
                                                                               start=True, stop=False) for hh in range(2)]
                                order = [(ii, kk, hh) for ii in range(per) for kk in range(nk) for hh in range(2)]
                                if os.environ.get("NO_ILV"):
                                    order = [(ii, kk, hh) for hh in range(2) for ii in range(per) for kk in range(nk)]
                                for (ii, kk, hh) in order:
                                    if True:
                                        c0 = (ii * nk + kk) * 128
                                        if True:
                                            q_ap, ks = items2[hh][g0 + ii]
                                            k_ap = ks[kk][0]
                                            fns.append(lambda e, c0=c0, sa=base + hh, k_ap=k_ap, q_ap=q_ap: e.matmul(
                                                sbank[sa][:, c0:c0 + 128], lhsT=k_ap, rhs=q_ap, start=False, stop=True))
                                cx.op("pe", fns, reads=[R_k, R_const] + [R_qT[t] for t in range(NT)], writes=[R_sbank[base], R_sbank[base + 1]])
                                for hh in range(2):
                                    pi = base + hh
                                    cx.op("act", lambda e, pi=pi: e.activation(out=pt[pi][:], in_=sbank[pi][:, :], func=AF.Exp, scale=0.125),
                                          reads=[R_sbank[pi]], writes=[R_pt[pi]])

                                def stage2(g0=g0, gi=gi, base=base):
                                    for hh in range(2):
                                        pi = base + hh
                                        fsel = 0
                                        o_ap, R_o = obank[hh][fsel], R_obank[hh][fsel]
                                        fns = []
                                        for ii in range(per):
                                            _, ks = items2[hh][g0 + ii]
                                            qi = g0 + ii
                                            for kk, (_, vidx) in enumerate(ks):
                                                c0 = (ii * nk + kk) * 128
                                                fns.append(lambda e, c0=c0, qi=qi, vidx=vidx, kk=kk, hh=hh, pi=pi, o_ap=o_ap: e.matmul(
                                                    o_ap[:, qi * 128:(qi + 1) * 128], lhsT=Vp[vi][:, vidx, hh, :], rhs=pt[pi][:, c0:c0 + 128],
                                                    start=(kk == 0), stop=(kk == nk - 1)))
                                        cx.op("pe", fns, reads=[R_pt[pi], R_Vp[vi]], writes=[R_o])
                                        if gi == ngrp - 1:
                                            acc2(hh, o_ap, R_o)

                                pend.append(stage2)
                                while len(pend) > 1:
                                    pend.pop(0)()

                        a_st = {"A1": 0, "Ap0": 1, "Ap1": 2, "Ap2": 3}.get(stop, 9)
                        if a_st < 9 and j > 0:
                            continue
                        hrows = [slice(0, 64), slice(64, 128)]
                        for pat in range(min(3, a_st)):
                            vi = vpc[0] % 2
                            vpc[0] += 1
                            if pat == 0:
                                build_vp(vi, [(jt, VT1[:, 128 * jt:128 * jt + 128]) for jt in range(17)], R_VT1)
                                for u in range(4):
                                    items2 = [[(qT[rows, 128 * i_:128 * i_ + 128],
                                                [(kT1[rows, 128 * i_:128 * i_ + 128], i_),
                                                 (kT1[rows, 128 * (i_ + 1):128 * (i_ + 1) + 128], i_ + 1)])
                                               for i_ in range(4 * u, 4 * u + 4)] for rows in hrows]
                                    mvf = lambda g0, u=u: (0 if (u == 0 and g0 == 0) else (2 if (u == 3 and g0 == 2) else 1))
                                    acc2 = lambda hh, o_ap, R_o, u=u: cx.op("dve", lambda e: e.tensor_copy(
                                        out=ACC[hh][:, 512 * u:512 * (u + 1)], in_=o_ap),
                                        reads=[R_o], writes=[R_ACC[hh][u]])
                                    for _ in range(2):
                                        if fin_pend:
                                            fin_pend.pop(0)()
                                    attend2(vi, items2, mvf, R_kT1, acc2)
                            elif pat == 1:
                                build_vp(vi, [(r * 5 + jt, VT2[:, r, 128 * jt:128 * jt + 128]) for r in range(4) for jt in range(5)], R_VT2)
                                for r in range(4):
                                    items2 = [[(qT[rows, 512 * i_ + r:512 * (i_ + 1):4],
                                                [(kT2[rows, r, 128 * i_:128 * i_ + 128], r * 5 + i_),
                                                 (kT2[rows, r, 128 * (i_ + 1):128 * (i_ + 1) + 128], r * 5 + i_ + 1)])
                                               for i_ in range(4)] for rows in hrows]
                                    mvf = lambda g0: (0 if g0 == 0 else 2)
                                    acc2 = lambda hh, o_ap, R_o, r=r: cx.op("dve", lambda e: e.tensor_tensor(
                                        out=ACC[hh][:, r:S:4], in0=o_ap, in1=ACC[hh][:, r:S:4], op=ALU.add),
                                        reads=[R_o] + R_ACC[hh], writes=R_ACC[hh])
                                    attend2(vi, items2, mvf, R_kT2, acc2)
                            else:
                                build_vp(vi, [(r, VT1[:, 64 + r:64 + S:16]) for r in range(16)], R_VT1)
                                for r0 in range(0, 16, 4):
                                    items2 = [[(qT[rows, r:S:16], [(kT1[rows, 64 + r:64 + S:16], r)])
                                               for r in range(r0, r0 + 4)] for rows in hrows]
                                    mvf = lambda g0: 3

                                    def acc2(hh, o_ap, R_o, r0=r0):
                                        accv = ACC[hh][:].rearrange("p (l r) -> p r l", r=16)[:, r0:r0 + 4, :]
                                        cx.op("dve", lambda e: e.tensor_tensor(
                                            out=accv, in0=o_ap.rearrange("p (r l) -> p r l", r=4), in1=accv, op=ALU.add),
                                            reads=[R_o] + R_ACC[hh], writes=R_ACC[hh])
                                    attend2(vi, items2, mvf, R_kT1, acc2)
                        flush_pend()

                        def fin_step(hh, u, j=j, gaT=gaT, R_gaT=R_gaT):
                            nr = slice(hh * 64, hh * 64 + 64)
                            dr = slice((1 - hh) * 64, (1 - hh) * 64 + 64)
                            cs = slice(512 * u, 512 * (u + 1))
                            i2 = ctr["fin"] % 2
                            ctr["fin"] += 1
                            cx.op("act", lambda e: e.activation(out=Rr[i2][nr, :], in_=ACC[hh][dr, cs], func=AF.Ln),
                                  reads=[R_ACC[hh][u]], writes=[R_Rr[i2]])
                            cx.op("act", lambda e: e.activation(out=Rr[i2][nr, :], in_=Rr[i2][nr, :], func=AF.Exp, scale=-1.0),
                                  reads=[R_Rr[i2]], writes=[R_Rr[i2]])
                            cx.op("dve", lambda e: e.tensor_tensor(out=Tm[i2][nr, :], in0=ACC[hh][nr, cs], in1=Rr[i2][nr, :], op=ALU.mult),
                                  reads=[R_ACC[hh][u], R_Rr[i2]], writes=[R_Tm[i2]])
                            cx.op("pool", lambda e: e.tensor_tensor(out=yT[nr, j, cs], in0=Tm[i2][nr, :], in1=gaT[nr, cs], op=ALU.mult),
                                  reads=[R_Tm[i2], R_gaT[u]], writes=[R_yT[j]])

                        if a_st >= 9:
                            for u in range(4):
                                for hh in range(2):
                                    fin_pend.append(lambda hh=hh, u=u, f=fin_step: f(hh, u))
                    while fin_pend:
                        fin_pend.pop(0)()
                    cx.barrier()
                if s == 0 and l == 0:
                    dump("yaT", yT[:, 0, :], R_yT[0:4])
                if stop in ("A1", "Ap0", "Ap1", "Ap2"):
                    raise _Stop()
                chk("A")

                with ExitStack() as ph:
                    P = ph.enter_context
                    gate_b = PT(ph, "gate_b", [128, D], F32)
                    gfin_b = PT(ph, "gfin_b", [128, D], F32)
                    R_gfin = Res()
                    if last:
                        cx.dma("sp", "cst", lambda e: e.dma_start(out=gfin_b[:], in_=gfin_d.partition_broadcast(128)), writes=[R_gfin])
                    xin = [PT(ph, "oxin%d" % i, [128, D], F32) for i in range(2)]
                    xo = [PT(ph, "xo%d" % i, [128, D], F32) for i in range(2)]
                    Dk = [PT(ph, "Dk%d" % i, [128, 128], F32) for i in range(2)]
                    junk = PT(ph, "ojunk", [128, D], BF16)
                    fst = [PT(ph, "fst%d" % i, [128, 3], F32) for i in range(2)]
                    R_xin, R_xo, R_Dk, R_fst = ([Res(), Res()] for _ in range(4))
                    R_xoa, R_xob = [Res(), Res()], [Res(), Res()]
                    R_junk = Res()
                    if not last:
                        oxn = [PT(ph, "oxn%d" % i, [128, 4, D], BF16) for i in range(2)]
                        ostat = PT(ph, "ostat", [128, 3, NT], F32)
                        R_oxn = [Res(), Res()]
                        R_ostat = [Res() for _ in range(NT)]
                    h_pend = []

                    def o_H(g4):
                        xb = oxn[g4 % 2]
                        for k in range(8):
                            b = nxt("tp")
                            cx.op("pe", [lambda e, jj=jj, k=k, b=b, xb=xb: e.transpose(tp[b][:, jj * 128:(jj + 1) * 128],
                                                                                   xb[:, jj, k * 128:(k + 1) * 128], ident[:])
                                         for jj in range(4)], reads=[R_oxn[g4 % 2], R_const], writes=[R_tp[b]])
                            cx.op("act", lambda e, k=k, b=b, g4=g4: e.activation(
                                out=hT[:, k, g4 * 512:(g4 + 1) * 512], in_=tp[b][:, 0:512], func=AF.Identity,
                                scale=gsA[:, l + 1, s, k:k + 1], bias=shA[:, l + 1, s, k:k + 1]),
                                reads=[R_tp[b], R_mod], writes=[R_hT[g4]])
                    for k in range(8):
                        i2 = k % 2
                        cx.op("dve", lambda e, k=k, i2=i2: e.tensor_scalar(out=Dk[i2][:], in0=identf[:], scalar1=gtA[:, l, s, k:k + 1],
                                                                         scalar2=None, op0=ALU.mult),
                              reads=[R_const, R_mod], writes=[R_Dk[i2]])
                        a = nxt("mm")
                        cx.op("pe", lambda e, a=a, i2=i2: e.matmul(mm[a][:, 0:128], lhsT=onesf[:], rhs=Dk[i2][:], start=True, stop=True),
                              reads=[R_Dk[i2], R_const], writes=[R_mm[a]])
                        cx.op("act", lambda e, a=a, k=k: e.activation(out=gate_b[:, k * 128:(k + 1) * 128], in_=mm[a][:, 0:128], func=AF.Copy),
                              reads=[R_mm[a]], writes=[R_gate])
                    def o_A(t):
                        i2 = t % 2
                        cx.dma("sp", "xin%d" % i2, lambda e, t=t, i2=i2: e.dma_start(out=xin[i2][:], in_=xsrc[s, t * 128:(t + 1) * 128, :]),
                               reads=[R_yd[s][t]], writes=[R_xin[i2]])
                        for half in range(2):
                            a = nxt("mm")
                            wslot = 0 if half == 0 else 2
                            hs = slice(half * 512, (half + 1) * 512)
                            cx.op("pe", [lambda e, kc=kc, a=a, wslot=wslot, t=t: e.matmul(
                                mm[a][:, :], lhsT=yT[:, kc, t * 128:(t + 1) * 128], rhs=wsl[wslot][:, kc, :],
                                start=(kc == 0), stop=(kc == 7)) for kc in range(8)],
                                reads=R_yT + [R_w[wslot]], writes=[R_mm[a]])
                            cx.op("dve", lambda e, a=a, hs=hs, i2=i2: e.tensor_tensor(out=xo[i2][:, hs], in0=mm[a][:, :], in1=gate_b[:, hs], op=ALU.mult),
                                  reads=[R_mm[a], R_gate], writes=[(R_xo if half == 0 else R_xob)[i2]])
                        cx.op("dve", lambda e, i2=i2: e.tensor_tensor(out=xo[i2][:, 0:512], in0=xo[i2][:, 0:512], in1=xin[i2][:, 0:512], op=ALU.add),
                              reads=[R_xo[i2], R_xin[i2]], writes=[R_xo[i2]])
                        cx.op("pool", lambda e, i2=i2: e.tensor_tensor(out=xo[i2][:, 512:1024], in0=xo[i2][:, 512:1024], in1=xin[i2][:, 512:1024], op=ALU.add),
                              reads=[R_xob[i2], R_xin[i2]], writes=[R_xob[i2]])

                    def o_B(t):
                        i2 = t % 2
                        if last:
                            cx.op("pool", lambda e, i2=i2: e.memset(fst[i2][:, 0:1], 0.0), writes=[R_fst[i2]])
                            cx.op("act", lambda e, i2=i2: e.activation(out=junk[:], in_=xo[i2][:], func=AF.Square, accum_out=fst[i2][:, 0:1]),
                                  reads=[R_xo[i2], R_xob[i2]], writes=[R_junk, R_fst[i2]])
                            cx.op("dve", lambda e, i2=i2: e.tensor_scalar(out=fst[i2][:, 1:2], in0=fst[i2][:, 0:1], scalar1=1.0 / D,
                                                                          scalar2=EPS, op0=ALU.mult, op1=ALU.add),
                                  reads=[R_fst[i2]], writes=[R_fst[i2]])
                            cx.op("pool", lambda e, i2=i2: e.tensor_tensor(out=fst[i2][:, 2:3], in0=fst[i2][:, 1:2], in1=mhalf[:, 0:1], op=ALU.pow),
                                  reads=[R_fst[i2], R_const], writes=[R_fst[i2]])
                            cx.op("dve", lambda e, i2=i2: e.scalar_tensor_tensor(
                                out=xo[i2][:], in0=xo[i2][:], scalar=fst[i2][:, 2:3], in1=gfin_b[:], op0=ALU.mult, op1=ALU.mult),
                                reads=[R_xo[i2], R_xob[i2], R_fst[i2], R_gfin], writes=[R_xo[i2], R_xob[i2]])
                        cx.dma("sp", "xout%d" % i2, lambda e, t=t, i2=i2: e.dma_start(out=y_d[s, t * 128:(t + 1) * 128, :], in_=xo[i2][:]),
                               reads=[R_xo[i2], R_xob[i2]], writes=[R_yd[s][t]])
                        if not last:
                            g4, jj = t // 4, t % 4
                            cx.op("pool", lambda e, t=t: e.memset(ostat[:, 0, t:t + 1], 0.0), writes=[R_ostat[t]])
                            cx.op("act", lambda e, t=t, i2=i2: e.activation(out=junk[:], in_=xo[i2][:], func=AF.Square,
                                                                           accum_out=ostat[:, 0, t:t + 1]),
                                  reads=[R_xo[i2], R_xob[i2]], writes=[R_junk, R_ostat[t]])
                            cx.op("dve", lambda e, t=t: e.tensor_scalar(out=ostat[:, 1, t:t + 1], in0=ostat[:, 0, t:t + 1], scalar1=1.0 / D,
                                                                        scalar2=EPS, op0=ALU.mult, op1=ALU.add),
                                  reads=[R_ostat[t]], writes=[R_ostat[t]])
                            cx.op("pool", lambda e, t=t: e.tensor_tensor(out=ostat[:, 2, t:t + 1], in0=ostat[:, 1, t:t + 1], in1=mhalf[:, 0:1], op=ALU.pow),
                                  reads=[R_ostat[t], R_const], writes=[R_ostat[t]])
                            cx.op("dve", lambda e, t=t, i2=i2, jj=jj, g4=g4: e.tensor_scalar(
                                out=oxn[g4 % 2][:, jj, :], in0=xo[i2][:], scalar1=ostat[:, 2, t:t + 1], scalar2=None, op0=ALU.mult),
                                reads=[R_xo[i2], R_xob[i2], R_ostat[t]], writes=[R_oxn[g4 % 2]])
                            if h_pend and h_pend[0][0] <= t:
                                o_H(h_pend.pop(0)[1])
                            if jj == 3:
                                h_pend.append((t + 2, g4))

                    _pipeline(NT, [(0, o_A), (1, o_B)])
                    while h_pend:
                        o_H(h_pend.pop(0)[1])
                    cx.barrier()
        except _Stop:
            pass
        cx.barrier()
    return nc


def _consts():
    f32 = np.float32
    pos = np.arange(S, dtype=np.float32)
    inv = (10000.0 ** (-np.arange(0, 64, 2, dtype=np.float32) / 64)).astype(f32)
    ang = (pos[:, None] * inv[None, :]).astype(f32)
    cos, sin = np.cos(ang).astype(f32), np.sin(ang).astype(f32)
    C2 = np.concatenate([cos, cos], axis=1)
    S2 = np.concatenate([-sin, sin], axis=1)
    rope = np.stack([C2.reshape(NT, 128, 64).transpose(1, 0, 2), S2.reshape(NT, 128, 64).transpose(1, 0, 2)], axis=1)
    p = np.arange(128)[:, None]
    c = np.arange(128)[None, :]
    A = (c <= p).astype(f32)
    B = (p <= c).astype(f32)
    A_first = A * (p >= 64)
    B_last = B * (p < 64)
    m_norm = np.concatenate([A, B], axis=1)
    m_first = np.concatenate([A_first, B], axis=1)
    m_last = np.concatenate([A, B_last], axis=1)
    band = (np.abs(p - c) <= 64).astype(f32)
    mask = np.stack([np.concatenate([m_first, m_norm], 1), np.concatenate([m_norm, m_norm], 1),
                     np.concatenate([m_norm, m_last], 1), np.concatenate([band] * 4, 1)], axis=1)
    diff = (c - p).astype(f32)
    ret = np.stack([np.maximum(diff, 0), (diff >= 0).astype(f32), np.maximum(-diff, 0), (diff < 0).astype(f32)], axis=1)
    tau = np.arange(128, dtype=f32)
    taus = np.stack([127 - tau, tau, tau + 1, 128 - tau], axis=1)
    mask = (mask - 1.0) * 30000.0
    return dict(cst_rope=np.ascontiguousarray(rope, f32), cst_mask=np.ascontiguousarray(mask, f32),
                cst_ret=np.ascontiguousarray(ret, f32), cst_tau=np.ascontiguousarray(taus, f32),
                cst_ident=np.eye(128, dtype=f32))


def kernel(x_prompt, x_sample, c_prompt, c_sample, g_norm, w_ada, b_ada, w_in, w_out,
           decay_fwd, decay_bwd, g_final):
    f = lambda a: np.ascontiguousarray(np.asarray(a), dtype=np.float32)
    xs = np.concatenate([f(x_prompt), f(x_sample)], axis=0)
    cs = np.concatenate([f(c_prompt), f(c_sample)], axis=0)
    shared = dict(g_norm=f(g_norm), w_ada=f(w_ada), b_ada=f(b_ada), w_in=f(w_in), w_out=f(w_out),
                  decay_fwd=f(decay_fwd), decay_bwd=f(decay_bwd), g_final=f(g_final))
    shared.update(_consts())
    nc = build_nc()
    in_maps = []
    for i in range(NCORES):
        m = dict(shared)
        m["x"] = np.ascontiguousarray(xs[i * NSEQ:(i + 1) * NSEQ])
        m["c"] = np.ascontiguousarray(cs[i * NSEQ:(i + 1) * NSEQ])
        in_maps.append(m)
    res = run_bass_kernel_spmd(nc, in_maps, core_ids=list(range(NCORES)))
    ys = np.concatenate([np.asarray(r["y"], dtype=np.float32) for r in res.results], axis=0)
    nb = np.asarray(x_prompt).shape[0]
    return (np.ascontiguousarray(ys[:nb]), np.ascontiguousarray(ys[nb:]))
```

```python
import math
import os
from contextlib import ExitStack

import numpy as np
import concourse.bass as bass
import concourse.mybir as mybir
from concourse.bass_utils import run_bass_kernel_spmd

F32 = mybir.dt.float32
BF16 = mybir.dt.bfloat16
AF = mybir.ActivationFunctionType
ALU = mybir.AluOpType

D = 1024
S = 2048
NT = 16
DEPTH = 2
NCORES = 8
NSEQ = 3
INW = 3584
EPS = 1e-6


class Tok:
    __slots__ = ("eng", "key", "val")

    def __init__(self, eng, key, val):
        self.eng, self.key, self.val = eng, key, val


class Res:
    __slots__ = ("name", "w", "r")

    def __init__(self, name=""):
        self.name, self.w, self.r = name, None, []


class Ctx:
    def __init__(self, nc, es):
        self.nc, self.es = nc, es
        self.engs = {"pe": nc.tensor, "act": nc.scalar, "dve": nc.vector, "pool": nc.gpsimd, "sp": nc.sync}
        self.sems, self.cnt = {}, {}
        self.seen = {e: {} for e in self.engs}
        self.epoch = 0
        self.dma_keys = set()
        self.new_epoch()

    def _mksem(self, key):
        if key not in self.sems:
            self.sems[key] = self.es.enter_context(self.nc.semaphore(key))
            self.cnt[key] = 0

    def new_epoch(self):
        self.epoch += 1
        self.ekey = {e: "%s%d" % (e, self.epoch) for e in self.engs if e != "sp"}
        for k in self.ekey.values():
            self._mksem(k)

    def _waits(self, eng, reads, writes, skipkey=None):
        deps = {}

        def need(tok, kind):
            if tok is None or tok.key == skipkey:
                return
            if tok.eng == eng and (eng == "pe" or kind != "RAW"):
                return
            if deps.get(tok.key, 0) < tok.val:
                deps[tok.key] = tok.val

        for r in reads:
            need(r.w, "RAW")
        for w in writes:
            need(w.w, "WAW")
            for t in w.r:
                need(t, "WAR")
        E, seen = self.engs[eng], self.seen[eng]
        for key, val in deps.items():
            if seen.get(key, 0) < val:
                E.wait_ge(self.sems[key], val)
                seen[key] = val

    def _commit(self, tok, reads, writes):
        for r in reads:
            r.r = [t for t in r.r if t.key != tok.key] + [tok]
        for w in writes:
            w.w, w.r = tok, []

    def op(self, eng, fns, reads=(), writes=()):
        self._waits(eng, reads, writes)
        if not isinstance(fns, (list, tuple)):
            fns = [fns]
        ins = None
        for f in fns:
            ins = f(self.engs[eng])
        key = self.ekey[eng]
        self.cnt[key] += 1
        ins.then_inc(self.sems[key], 1)
        tok = Tok(eng, key, self.cnt[key])
        self._commit(tok, reads, writes)
        return tok

    def dma(self, q, semname, fn, reads=(), writes=()):
        key = "d%d_%s" % (self.epoch, semname)
        self._mksem(key)
        self.dma_keys.add(key)
        self._waits(q, reads, writes, skipkey=key)
        ins = fn(self.engs[q])
        self.cnt[key] += 16
        ins.then_inc(self.sems[key], 16)
        tok = Tok("dma", key, self.cnt[key])
        self._commit(tok, reads, writes)
        return tok

    def barrier(self, with_dma=True):
        keys = list(self.ekey.values())
        if with_dma:
            keys += sorted(self.dma_keys)
        for e, E in self.engs.items():
            seen = self.seen[e]
            for key in keys:
                val = self.cnt[key]
                if val > 0 and seen.get(key, 0) < val:
                    E.wait_ge(self.sems[key], val)
                    seen[key] = val


def _pipeline(n_iter, stages):
    mx = max(sk for sk, _ in stages)
    for i in range(n_iter + mx):
        for sk, fn in stages:
            n = i - sk
            if 0 <= n < n_iter:
                fn(n)


def _ap(base, dims):
    return bass.AP(base.tensor, base.offset, [list(base.ap[0])] + [list(d) for d in dims])


class _Stop(Exception):
    pass


def build_nc(nseq=NSEQ, depth=DEPTH, dbg=(), stop=None):
    nc = bass.Bass("TRN2", target_bir_lowering=False)
    dt = nc.dram_tensor
    x_d = dt("x", [nseq, S, D], F32, kind="ExternalInput").ap()
    c_d = dt("c", [nseq, D], F32, kind="ExternalInput").ap()
    gn_d = dt("g_norm", [DEPTH, D], F32, kind="ExternalInput").ap()
    wada_d = dt("w_ada", [DEPTH, D, 3 * D], F32, kind="ExternalInput").ap()
    bada_d = dt("b_ada", [DEPTH, 3 * D], F32, kind="ExternalInput").ap()
    win_d = dt("w_in", [DEPTH, D, INW], F32, kind="ExternalInput").ap()
    wout_d = dt("w_out", [DEPTH, D, D], F32, kind="ExternalInput").ap()
    dfw_d = dt("decay_fwd", [DEPTH, 4], F32, kind="ExternalInput").ap()
    dbw_d = dt("decay_bwd", [DEPTH, 4], F32, kind="ExternalInput").ap()
    gfin_d = dt("g_final", [D], F32, kind="ExternalInput").ap()
    crope_d = dt("cst_rope", [128, 2, NT, 64], F32, kind="ExternalInput").ap()
    cmask_d = dt("cst_mask", [128, 4, 512], F32, kind="ExternalInput").ap()
    cret_d = dt("cst_ret", [128, 4, 128], F32, kind="ExternalInput").ap()
    ctau_d = dt("cst_tau", [128, 4], F32, kind="ExternalInput").ap()
    cid_d = dt("cst_ident", [128, 128], F32, kind="ExternalInput").ap()
    y_d = dt("y", [nseq, S, D], F32, kind="ExternalOutput").ap()
    dbg_d = {}
    for name, shape in dbg:
        dbg_d[name] = dt("dbg_" + name, list(shape), F32, kind="ExternalOutput").ap()

    with ExitStack() as es:
        E = es.enter_context
        cx = Ctx(nc, es)
        sb = lambda name, shape, dtype: E(nc.sbuf_tensor(name, list(shape), dtype))
        uid = [0]

        def PT(ph, name, shape, dtype):
            uid[0] += 1
            return ph.enter_context(nc.sbuf_tensor("%s_u%d" % (name, uid[0]), list(shape), dtype))

        hT = sb("hT", [128, 8, S], BF16)
        yT = sb("yT", [128, 8, S], BF16)
        wsl = [sb("wsl%d" % i, [128, 8, 512], BF16) for i in range(3)]
        kT1 = sb("kT1", [128, S + 128], BF16)
        kT2 = sb("kT2", [128, 4, 640], BF16)
        VT1 = sb("VT1", [128, S + 128], BF16)
        VT2 = sb("VT2", [128, 4, 640], BF16)
        ident = sb("ident", [128, 128], BF16)
        identf = sb("identf", [128, 128], F32)
        onesf = sb("onesf", [128, 128], F32)
        rope_t = sb("rope_t", [128, 2, NT, 64], F32)
        maskA = sb("maskA", [128, 4, 512], BF16)
        cret = sb("cret", [128, 4, 128], F32)
        ctau = sb("ctau", [128, 4], F32)
        DTt_all = sb("DTt", [128, DEPTH, 4, 128], F32)
        TAB_all = sb("TAB", [128, DEPTH, 4, 4, 64], F32)
        Gfb_all = sb("Gfb", [128, DEPTH, 2, 2], F32)
        gsA = sb("gsA", [128, DEPTH, nseq, 8], F32)
        shA = sb("shA", [128, DEPTH, nseq, 8], F32)
        gtA = sb("gtA", [128, DEPTH, nseq, 8], F32)
        mhalf = sb("mhalf", [128, 16], F32)

        mm = [E(nc.psum_tensor("mm%d" % i, [128, 512], F32)) for i in range(2)]
        tp = [E(nc.psum_tensor("tp%d" % i, [128, 1024], BF16)) for i in range(2)]
        st = [E(nc.psum_tensor("st%d" % i, [128, 512], F32)) for i in range(2)]
        ov = [E(nc.psum_tensor("ov%d" % i, [128, 512], F32)) for i in range(2)]
        R_mm = [Res("mm0"), Res("mm1")]
        R_tp = [Res("tp0"), Res("tp1")]
        R_st = [Res("st0"), Res("st1")]
        R_ov = [Res("ov0"), Res("ov1")]
        ctr = {"mm": 0, "tp": 0, "st": 0, "ov": 0, "w": 0, "fin": 0}

        def nxt(kind):
            i = ctr[kind] % 2
            ctr[kind] += 1
            return i

        R_const = Res("const")
        R_hT = [Res("hT%d" % g) for g in range(4)]
        R_yT = [Res("yT%d" % k) for k in range(8)]
        R_w = [Res("w%d" % i) for i in range(3)]
        R_kT1, R_kT2, R_VT1, R_VT2 = Res("kT1"), Res("kT2"), Res("VT1"), Res("VT2")
        R_tab, R_gate, R_mod = Res("tab"), Res("gate"), Res("mod")
        R_yd = [[Res("yd%d_%d" % (s, t)) for t in range(NT)] for s in range(nseq)]
        dump_list = []

        def dump(name, src_ap, reads):
            if name in dbg_d:
                dump_list.append(cx.dma("pool", "dbg", lambda e: e.dma_start(out=dbg_d[name], in_=src_ap), reads=reads))

        cx.dma("sp", "cst", lambda e: e.dma_start(out=rope_t[:], in_=crope_d), writes=[R_const])
        cx.dma("pool", "cstp", lambda e: e.dma_start(out=maskA[:], in_=cmask_d), writes=[R_const])
        cx.dma("sp", "cst", lambda e: e.dma_start(out=cret[:], in_=cret_d), writes=[R_const])
        cx.dma("sp", "cst", lambda e: e.dma_start(out=ctau[:], in_=ctau_d), writes=[R_const])
        cx.dma("pool", "cstp", lambda e: e.dma_start(out=ident[:], in_=cid_d), writes=[R_const])
        cx.dma("sp", "cst", lambda e: e.dma_start(out=identf[:], in_=cid_d), writes=[R_const])
        cx.op("pool", lambda e: e.memset(onesf[:], 1.0), writes=[R_const])
        cx.op("pool", lambda e: e.memset(mhalf[:], -0.5), writes=[R_const])
        cx.op("pool", lambda e: e.memset(kT1[:], 0.0), writes=[R_kT1])
        cx.op("pool", lambda e: e.memset(kT2[:], 0.0), writes=[R_kT2])
        cx.op("pool", lambda e: e.memset(VT1[:], 0.0), writes=[R_VT1])
        cx.op("pool", lambda e: e.memset(VT2[:], 0.0), writes=[R_VT2])

        with ExitStack() as ph:
            P = ph.enter_context
            wada = PT(ph, "wada", [128, 8, 3 * D], BF16)
            cT = PT(ph, "cT", [128, nseq, 8], F32)
            silc = PT(ph, "silc", [128, 8, nseq], BF16)
            badaT = PT(ph, "badaT", [128, 24], F32)
            gnT = PT(ph, "gnT", [128, 8], F32)
            modT = PT(ph, "modT", [128, 24, nseq], F32)
            R_wada, R_cT, R_silc, R_bada, R_gn, R_modT = (Res() for _ in range(6))
            for s in range(nseq):
                cx.dma("sp", "cst", lambda e, s=s: e.dma_start(
                    out=cT[:, s, :], in_=c_d[s].rearrange("(k p) -> p k", p=128), allow_slow_non_contiguous=True),
                    writes=[R_cT])
            cx.op("act", lambda e: e.activation(out=silc[:].rearrange("p k s -> p s k"), in_=cT[:], func=AF.Silu),
                  reads=[R_cT], writes=[R_silc])
            for l in range(depth):
                for i in range(6):
                    cx.dma("pool", "wada", lambda e, i=i, l=l: e.dma_start(
                        out=wada[:, :, i * 512:(i + 1) * 512],
                        in_=wada_d[l].rearrange("(k p) n -> p k n", p=128)[:, :, i * 512:(i + 1) * 512]),
                        writes=[R_wada])
                cx.dma("sp", "cst", lambda e, l=l: e.dma_start(
                    out=badaT[:], in_=bada_d[l].rearrange("(o p) -> p o", p=128), allow_slow_non_contiguous=True),
                    writes=[R_bada])
                cx.dma("sp", "cst", lambda e, l=l: e.dma_start(
                    out=gnT[:], in_=gn_d[l].rearrange("(k p) -> p k", p=128), allow_slow_non_contiguous=True),
                    writes=[R_gn])
                fns = []
                for oc in range(24):
                    for kc in range(8):
                        fns.append(lambda e, oc=oc, kc=kc: e.matmul(
                            mm[0][:, oc * nseq:(oc + 1) * nseq], lhsT=wada[:, kc, oc * 128:(oc + 1) * 128],
                            rhs=silc[:, kc, :], start=(kc == 0), stop=(kc == 7)))
                cx.op("pe", fns, reads=[R_wada, R_silc], writes=[R_mm[0]])
                mmv = mm[0][:, 0:24 * nseq].rearrange("p (o s) -> p o s", s=nseq)
                for s in range(nseq):
                    cx.op("dve", lambda e, s=s: e.tensor_tensor(out=modT[:, :, s], in0=mmv[:, :, s], in1=badaT[:], op=ALU.add),
                          reads=[R_mm[0], R_bada], writes=[R_modT])
                for s in range(nseq):
                    cx.op("dve", lambda e, s=s, l=l: e.scalar_tensor_tensor(
                        out=gsA[:, l, s, :], in0=modT[:, 8:16, s], scalar=1.0, in1=gnT[:], op0=ALU.add, op1=ALU.mult),
                        reads=[R_modT, R_gn], writes=[R_mod])
                    cx.op("dve", lambda e, s=s, l=l: e.tensor_copy(out=shA[:, l, s, :], in_=modT[:, 0:8, s]),
                          reads=[R_modT], writes=[R_mod])
                    cx.op("dve", lambda e, s=s, l=l: e.tensor_copy(out=gtA[:, l, s, :], in_=modT[:, 16:24, s]),
                          reads=[R_modT], writes=[R_mod])
            cx.barrier()

        for l in range(depth):
            DTt, TAB, Gfb = DTt_all[:, l], TAB_all[:, l], Gfb_all[:, l]
            with ExitStack() as ph:
                P = ph.enter_context
                dfb = PT(ph, "dfb", [128, 8], F32)
                lg = PT(ph, "lg", [128, 8], F32)
                dsc = PT(ph, "dsc", [128, 4, 4], F32)
                e1 = PT(ph, "e1", [128, 128], F32)
                e2 = PT(ph, "e2", [128, 128], F32)
                R_dfb, R_lg, R_dsc, R_e1, R_e2 = (Res() for _ in range(5))
                cx.dma("sp", "cst", lambda e: e.dma_start(out=dfb[:, 0:4], in_=dfw_d[l].partition_broadcast(128)), writes=[R_dfb])
                cx.dma("sp", "cst", lambda e: e.dma_start(out=dfb[:, 4:8], in_=dbw_d[l].partition_broadcast(128)), writes=[R_dfb])
                cx.op("act", lambda e: e.activation(out=lg[:], in_=dfb[:], func=AF.Exp, scale=-1.0), reads=[R_dfb], writes=[R_lg])
                cx.op("act", lambda e: e.activation(out=lg[:], in_=lg[:], func=AF.Ln, bias=1.0), reads=[R_lg], writes=[R_lg])
                cx.op("dve", lambda e: e.tensor_scalar(out=lg[:], in0=lg[:], scalar1=-1.0, scalar2=None, op0=ALU.mult),
                      reads=[R_lg], writes=[R_lg])
                for kind in range(4):
                    lo = 0 if kind in (0, 2) else 4
                    cx.op("act", lambda e, kind=kind, lo=lo: e.activation(
                        out=dsc[:, kind, :], in_=lg[:, lo:lo + 4], func=AF.Exp, scale=ctau[:, kind:kind + 1]),
                        reads=[R_lg, R_const], writes=[R_dsc])
                for kind in range(4):
                    for h in range(4):
                        cx.op("dve", lambda e, kind=kind, h=h: e.tensor_scalar(
                            out=TAB[:, kind, h, :], in0=onesf[:, 0:64], scalar1=dsc[:, kind, h:h + 1],
                            scalar2=(0.125 if kind < 2 else 1.0), op0=ALU.mult, op1=ALU.mult),
                            reads=[R_dsc, R_const], writes=[R_tab])
                for h in range(4):
                    cx.op("act", lambda e, h=h: e.activation(out=e1[:], in_=cret[:, 0, :], func=AF.Exp, scale=lg[:, h:h + 1]),
                          reads=[R_lg, R_const], writes=[R_e1])
                    cx.op("dve", lambda e: e.tensor_tensor(out=e1[:], in0=e1[:], in1=cret[:, 1, :], op=ALU.mult),
                          reads=[R_e1, R_const], writes=[R_e1])
                    cx.op("act", lambda e, h=h: e.activation(out=e2[:], in_=cret[:, 2, :], func=AF.Exp, scale=lg[:, 4 + h:5 + h]),
                          reads=[R_lg, R_const], writes=[R_e2])
                    cx.op("dve", lambda e: e.tensor_tensor(out=e2[:], in0=e2[:], in1=cret[:, 3, :], op=ALU.mult),
                          reads=[R_e2, R_const], writes=[R_e2])
                    cx.op("dve", lambda e, h=h: e.tensor_tensor(out=DTt[:, h, :], in0=e1[:], in1=e2[:], op=ALU.add),
                          reads=[R_e1, R_e2], writes=[R_tab])
                for d_ in range(2):
                    for p in range(2):
                        for hh in range(2):
                            rows = slice(hh * 64, hh * 64 + 64)
                            col = d_ * 4 + 2 * p + hh
                            cx.op("act", lambda e, d_=d_, p=p, rows=rows, col=col: e.activation(
                                out=Gfb[rows, d_, p:p + 1], in_=lg[rows, col:col + 1], func=AF.Exp, scale=128.0),
                                reads=[R_lg], writes=[R_tab])
                cx.barrier()

        R_wser = Res("wser")

        def load_w(slot, src_ap, ncols):
            if len(src_ap.shape) == 3:
                view = wsl[slot][:, :, 0:ncols]
                cx.dma("pool", "w%d" % slot, lambda e: e.dma_start(out=view, in_=src_ap), writes=[R_w[slot], R_wser])
            else:
                nr = src_ap.shape[2]
                w_ = src_ap.shape[3]
                for r in range(nr):
                    view = wsl[slot][:, :, r * w_:(r + 1) * w_]
                    cx.dma("pool", "w%d" % slot, lambda e, view=view, r=r: e.dma_start(out=view, in_=src_ap[:, :, r, :]),
                           writes=[R_w[slot], R_wser])

        def rope(src_psum, R_src, t, out_ap, R_out, tmp1, tmp2, R_t1, R_t2):
            xv = src_psum.rearrange("p (h d) -> p h d", d=64)
            cb = _ap(rope_t[:, 0, t, :], [[0, 4], [1, 64]])
            s_lo = _ap(rope_t[:, 1, t, 0:32], [[0, 4], [1, 32]])
            s_hi = _ap(rope_t[:, 1, t, 32:64], [[0, 4], [1, 32]])
            t1v = tmp1.rearrange("p (h d) -> p h d", d=64)
            t2v = tmp2.rearrange("p (h d) -> p h d", d=64)
            cx.op("dve", lambda e: e.tensor_tensor(out=t1v, in0=xv, in1=cb, op=ALU.mult),
                  reads=[R_src, R_const], writes=[R_t1])
            cx.op("dve", [lambda e: e.tensor_tensor(out=t2v[:, :, 0:32], in0=xv[:, :, 32:64], in1=s_lo, op=ALU.mult),
                          lambda e: e.tensor_tensor(out=t2v[:, :, 32:64], in0=xv[:, :, 0:32], in1=s_hi, op=ALU.mult)],
                  reads=[R_src, R_const], writes=[R_t2])
            cx.op("dve", lambda e: e.tensor_tensor(out=out_ap, in0=tmp1, in1=tmp2, op=ALU.add),
                  reads=[R_t1, R_t2], writes=[R_out])

        def proj_tok(t, wslot, c0, ncols):
            a = nxt("mm")
            fns = [lambda e, kc=kc: e.matmul(mm[a][:, 0:ncols], lhsT=hT[:, kc, t * 128:(t + 1) * 128],
                                              rhs=wsl[wslot][:, kc, c0:c0 + ncols], start=(kc == 0), stop=(kc == 7))
                   for kc in range(8)]
            cx.op("pe", fns, reads=[R_hT[t // 4], R_w[wslot]], writes=[R_mm[a]])
            return a

        def proj_feat(g4, wslot, c0):
            a = nxt("mm")
            fns = [lambda e, kc=kc: e.matmul(mm[a][:, :], lhsT=wsl[wslot][:, kc, c0:c0 + 128],
                                              rhs=hT[:, kc, g4 * 512:(g4 + 1) * 512], start=(kc == 0), stop=(kc == 7))
                   for kc in range(8)]
            cx.op("pe", fns, reads=[R_hT[g4], R_w[wslot]], writes=[R_mm[a]])
            return a

        win_v = [win_d[l].rearrange("(k p) n -> p k n", p=128) for l in range(DEPTH)]
        win_a = [win_d[l].rearrange("(k p) (r c) -> p k r c", p=128, r=7) for l in range(DEPTH)]
        wout_v = [wout_d[l].rearrange("(k p) n -> p k n", p=128) for l in range(DEPTH)]

        def chk(name):
            if stop == name:
                raise _Stop()

        try:
          chk("M")
          for s in range(nseq):
            for l in range(depth):
                xsrc = x_d if l == 0 else y_d
                DTt, TAB, Gfb = DTt_all[:, l], TAB_all[:, l], Gfb_all[:, l]
                last = (l == depth - 1)
                if not (s == 0 and l == 0):
                    cx.barrier()
                    cx.new_epoch()
                load_w(0, win_v[l][:, :, 2048:2560], 512)
                if s == 0 and l == 0:
                    load_w(1, win_v[l][:, :, 2560:3072], 512)
                load_w(2, win_v[l][:, :, 3072:3584], 512)

                chk("T")

                with ExitStack() as ph:
                  if l == 0:
                      P = ph.enter_context
                      xin = [PT(ph, "xin%d" % i, [128, D], F32) for i in range(4)]
                      xn = [PT(ph, "xn%d" % i, [128, 4, D], BF16) for i in range(2)]
                      junk = PT(ph, "junk", [128, D], BF16)
                      stat = PT(ph, "stat", [128, 3, NT], F32)
                      R_xin = [Res(), Res(), Res(), Res()]
                      R_xn = [Res(), Res()]
                      R_junk = Res()
                      R_stat = [Res() for _ in range(NT)]
                      def st_X(g4):
                          xb = xn[g4 % 2]
                          for j in range(4):
                              t = 4 * g4 + j
                              xi = xin[t % 4]
                              cx.dma("sp", "xin%d" % (t % 4), lambda e, t=t, xi=xi: e.dma_start(out=xi[:], in_=xsrc[s, t * 128:(t + 1) * 128, :]),
                                     reads=[R_yd[s][t]], writes=[R_xin[t % 4]])
                              cx.op("pool", lambda e, t=t: e.memset(stat[:, 0, t:t + 1], 0.0), writes=[R_stat[t]])
                              cx.op("act", lambda e, t=t, xi=xi: e.activation(out=junk[:], in_=xi[:], func=AF.Square,
                                                                             accum_out=stat[:, 0, t:t + 1]),
                                    reads=[R_xin[t % 4]], writes=[R_junk, R_stat[t]])
                              cx.op("dve", lambda e, t=t: e.tensor_scalar(out=stat[:, 1, t:t + 1], in0=stat[:, 0, t:t + 1], scalar1=1.0 / D,
                                                                          scalar2=EPS, op0=ALU.mult, op1=ALU.add),
                                    reads=[R_stat[t]], writes=[R_stat[t]])
                              cx.op("pool", lambda e, t=t: e.tensor_tensor(out=stat[:, 2, t:t + 1], in0=stat[:, 1, t:t + 1], in1=mhalf[:, 0:1], op=ALU.pow),
                                    reads=[R_stat[t], R_const], writes=[R_stat[t]])
                              cx.op("dve", lambda e, t=t, xi=xi, j=j, xb=xb: e.tensor_scalar(
                                  out=xb[:, j, :], in0=xi[:], scalar1=stat[:, 2, t:t + 1], scalar2=None, op0=ALU.mult),
                                  reads=[R_xin[t % 4], R_stat[t]], writes=[R_xn[g4 % 2]])

                      def st_H(g4):
                          xb = xn[g4 % 2]
                          for k in range(8):
                              b = nxt("tp")
                              fns = [lambda e, j=j, k=k, b=b, xb=xb: e.transpose(tp[b][:, j * 128:(j + 1) * 128],
                                                                                xb[:, j, k * 128:(k + 1) * 128], ident[:])
                                     for j in range(4)]
                              cx.op("pe", fns, reads=[R_xn[g4 % 2], R_const], writes=[R_tp[b]])
                              cx.op("act", lambda e, k=k, b=b, g4=g4: e.activation(
                                  out=hT[:, k, g4 * 512:(g4 + 1) * 512], in_=tp[b][:, 0:512], func=AF.Identity,
                                  scale=gsA[:, l, s, k:k + 1], bias=shA[:, l, s, k:k + 1]),
                                  reads=[R_tp[b], R_mod], writes=[R_hT[g4]])

                      _pipeline(4, [(0, st_X), (1, st_H)])
                      cx.barrier()
                if s == 0 and l == 0:
                    dump("hT", hT[:, 0, :], [R_hT[0], R_hT[1], R_hT[2], R_hT[3]])
                chk("N")

                with ExitStack() as ph:
                    P = ph.enter_context
                    kbT = PT(ph, "kbT", [128, 2, S], BF16)
                    vb = PT(ph, "vb", [128, NT, 512], BF16)
                    kdb = PT(ph, "kdb", [128, NT, 256], BF16)
                    SfB = PT(ph, "SfB", [128, NT, 2, 128], BF16)
                    SbB = PT(ph, "SbB", [128, NT, 2, 128], BF16)
                    Sst = PT(ph, "Sst", [128, 2, 2, 128], F32)
                    rt1 = PT(ph, "rt1", [128, 256], F32)
                    rt2 = PT(ph, "rt2", [128, 256], F32)
                    kr = [PT(ph, "kr%d" % i, [128, 256], F32) for i in range(2)]
                    kbf = [PT(ph, "kbf%d" % i, [128, 256], BF16) for i in range(2)]
                    kdf = [PT(ph, "kdf%d" % i, [128, 256], BF16) for i in range(2)]
                    q3 = [PT(ph, "q3_%d" % i, [128, 3, 256], BF16) for i in range(2)]
                    qT3 = [PT(ph, "qT3_%d" % i, [128, 6, 128], BF16) for i in range(2)]
                    gsl = [PT(ph, "gsl%d" % i, [128, 512], BF16) for i in range(2)]
                    inT = [PT(ph, "inT%d" % i, [128, 512], BF16) for i in range(2)]
                    yr = [PT(ph, "yr%d" % i, [128, 512], BF16) for i in range(2)]
                    junk2 = PT(ph, "junk2", [128, 128], BF16)
                    gst = [PT(ph, "gst%d" % i, [128, 3, 4], F32) for i in range(2)]
                    R_kbT = [Res() for _ in range(NT)]
                    R_vb = [Res() for _ in range(NT)]
                    R_kdb = [Res() for _ in range(NT)]
                    R_SfB = [Res() for _ in range(NT)]
                    R_SbB = [Res() for _ in range(NT)]
                    R_Sst = [Res(), Res()]
                    R_rt1, R_rt2, R_junk2 = Res(), Res(), Res()
                    R_kr, R_kbf, R_kdf, R_q3, R_qT3, R_gsl, R_inT, R_yr, R_gst = (
                        [Res(), Res()] for _ in range(9))
                    TABv = lambda kind: TAB[:, kind, :, :].rearrange("p h d -> p (h d)")
                    R_q3a, R_q3b, R_q3c = ([Res(), Res()] for _ in range(3))
                    cx.op("pool", lambda e: e.memset(Sst[:], 0.0), writes=R_Sst)

                    def scan_update(d_, a):
                        fns = []
                        for p in range(2):
                            for hh in range(2):
                                rows = slice(hh * 64, hh * 64 + 64)
                                fns.append(lambda e, p=p, hh=hh, rows=rows: e.scalar_tensor_tensor(
                                    out=Sst[rows, d_, p, :], in0=Sst[rows, d_, p, :], scalar=Gfb[rows, d_, p:p + 1],
                                    in1=ov[a][rows, p * 256 + hh * 128:p * 256 + hh * 128 + 128], op0=ALU.mult, op1=ALU.add))
                        return fns

                    def b1_P(n):
                        i2 = n % 2
                        a = proj_tok(n, 0, 256, 256)
                        rope(mm[a][:, 0:256], R_mm[a], n, kr[i2][:], R_kr[i2], rt1[:], rt2[:], R_rt1, R_rt2)
                        cx.op("act", lambda e, i2=i2: e.activation(out=kbf[i2][:], in_=kr[i2][:], func=AF.Copy, scale=0.125),
                              reads=[R_kr[i2]], writes=[R_kbf[i2]])
                        cx.op("pool", lambda e, i2=i2: e.tensor_tensor(out=kdf[i2][:], in0=kr[i2][:], in1=TABv(0), op=ALU.mult),
                              reads=[R_kr[i2], R_tab], writes=[R_kdf[i2]])
                        cx.op("dve", lambda e, i2=i2, n=n: e.tensor_tensor(out=kdb[:, n, :], in0=kr[i2][:], in1=TABv(1), op=ALU.mult),
                              reads=[R_kr[i2], R_tab], writes=[R_kdb[n]])
                        a2 = proj_tok(n, 1, 0, 512)
                        cx.op("act", lambda e, a2=a2, n=n: e.activation(out=vb[:, n, :], in_=mm[a2][:, :], func=AF.Copy),
                              reads=[R_mm[a2]], writes=[R_vb[n]])

                    def b1_T(n):
                        i2 = n % 2
                        b = nxt("tp")
                        cx.op("pe", [lambda e, p=p, b=b, i2=i2: e.transpose(tp[b][:, p * 128:(p + 1) * 128], kbf[i2][:, p * 128:(p + 1) * 128], ident[:])
                                     for p in range(2)], reads=[R_kbf[i2], R_const], writes=[R_tp[b]])
                        cx.op("act", lambda e, b=b, n=n: e.activation(
                            out=kbT[:, :, n * 128:(n + 1) * 128], in_=tp[b][:, 0:256].rearrange("p (a c) -> p a c", a=2), func=AF.Copy),
                            reads=[R_tp[b]], writes=[R_kbT[n]])
                        cx.op("dve", lambda e, n=n: e.tensor_copy(out=SfB[:, n, :, :], in_=Sst[:, 0, :, :]),
                              reads=[R_Sst[0]], writes=[R_SfB[n]])
                        if n < NT - 1:
                            o = nxt("ov")
                            cx.op("pe", [lambda e, p=p, o=o, i2=i2, n=n: e.matmul(
                                ov[o][:, p * 256:(p + 1) * 256], lhsT=kdf[i2][:, p * 128:(p + 1) * 128],
                                rhs=vb[:, n, p * 256:(p + 1) * 256], start=True, stop=True) for p in range(2)],
                                reads=[R_kdf[i2], R_vb[n]], writes=[R_ov[o]])
                            cx.op("dve", scan_update(0, o), reads=[R_ov[o], R_tab, R_Sst[0]], writes=[R_Sst[0]])

                    _pipeline(NT, [(0, b1_P), (1, b1_T)])
                    b_stage = {"B1": 1, "Bb": 2}.get(stop, 3)
                    for n in (range(NT - 1, -1, -1) if b_stage >= 2 else []):
                        cx.op("dve", lambda e, n=n: e.tensor_copy(out=SbB[:, n, :, :], in_=Sst[:, 1, :, :]),
                              reads=[R_Sst[1]], writes=[R_SbB[n]])
                        if n > 0:
                            o = nxt("ov")
                            cx.op("pe", [lambda e, p=p, o=o, n=n: e.matmul(
                                ov[o][:, p * 256:(p + 1) * 256], lhsT=kdb[:, n, p * 128:(p + 1) * 128],
                                rhs=vb[:, n, p * 256:(p + 1) * 256], start=True, stop=True) for p in range(2)],
                                reads=[R_kdb[n], R_vb[n]], writes=[R_ov[o]])
                            cx.op("dve", scan_update(1, o), reads=[R_ov[o], R_tab, R_Sst[1]], writes=[R_Sst[1]])
                    def b2_P(n):
                        i2 = n % 2
                        a = proj_tok(n, 0, 0, 256)
                        rope(mm[a][:, 0:256], R_mm[a], n, kr[i2][:], R_kr[i2], rt1[:], rt2[:], R_rt1, R_rt2)
                        cx.op("act", lambda e, i2=i2: e.activation(out=q3[i2][:, 0, :], in_=kr[i2][:], func=AF.Copy),
                              reads=[R_kr[i2]], writes=[R_q3a[i2]])
                        cx.op("pool", lambda e, i2=i2: e.tensor_tensor(out=q3[i2][:, 1, :], in0=kr[i2][:], in1=TABv(2), op=ALU.mult),
                              reads=[R_kr[i2], R_tab], writes=[R_q3b[i2]])
                        cx.op("dve", lambda e, i2=i2: e.tensor_tensor(out=q3[i2][:, 2, :], in0=kr[i2][:], in1=TABv(3), op=ALU.mult),
                              reads=[R_kr[i2], R_tab], writes=[R_q3c[i2]])

                    def b2_T(n):
                        i2 = n % 2
                        b = nxt("tp")
                        cx.op("pe", [lambda e, v=v, p=p, b=b, i2=i2: e.transpose(
                            tp[b][:, (v * 2 + p) * 128:(v * 2 + p + 1) * 128], q3[i2][:, v, p * 128:(p + 1) * 128], ident[:])
                            for v in range(3) for p in range(2)], reads=[R_q3a[i2], R_q3b[i2], R_q3c[i2], R_const], writes=[R_tp[b]])
                        cx.op("act", lambda e, b=b, i2=i2: e.activation(
                            out=qT3[i2][:], in_=tp[b][:, 0:768].rearrange("p (a c) -> p a c", a=6), func=AF.Copy),
                            reads=[R_tp[b]], writes=[R_qT3[i2]])

                    def b2_T2(n):
                        i2 = n % 2
                        fns = []
                        for h in range(4):
                            p, hh = h // 2, h % 2
                            rows = slice(hh * 64, hh * 64 + 64)
                            fns.append(lambda e, p=p, hh=hh, rows=rows, i2=i2, n=n: e.matmul(
                                st[hh][:, p * 128:(p + 1) * 128], lhsT=kbT[rows, p, n * 128:(n + 1) * 128],
                                rhs=qT3[i2][rows, p, :], start=True, stop=True))
                        cx.op("pe", fns, reads=[R_kbT[n], R_qT3[i2]], writes=[R_st[0], R_st[1]])
                        inv = inT[i2][:].rearrange("p (a b t) -> p a b t", a=2, b=2)
                        for hh in range(2):
                            cx.op("dve", lambda e, hh=hh, inv=inv: e.tensor_tensor(
                                out=inv[:, :, hh, :], in0=st[hh][:, 0:256].rearrange("p (a t) -> p a t", a=2),
                                in1=DTt[:].rearrange("p (a b) t -> p a b t", b=2)[:, :, hh, :], op=ALU.mult),
                                reads=[R_st[hh], R_tab], writes=[R_inT[i2]])
                        a2 = proj_tok(n, 2, 0, 512)
                        cx.op("act", lambda e, a2=a2, i2=i2: e.activation(out=gsl[i2][:], in_=mm[a2][:, :], func=AF.Silu),
                              reads=[R_mm[a2]], writes=[R_gsl[i2]])

                    def b2_Y(n):
                        i2 = n % 2
                        o = nxt("ov")
                        fns = []
                        for h in range(4):
                            p, hh = h // 2, h % 2
                            rows = slice(hh * 64, hh * 64 + 64)
                            oc = slice(h * 128, (h + 1) * 128)
                            fns.append(lambda e, oc=oc, o=o, i2=i2, n=n: e.matmul(
                                ov[o][:, oc], lhsT=inT[i2][:, oc], rhs=vb[:, n, oc], start=True, stop=False))
                            fns.append(lambda e, oc=oc, o=o, i2=i2, n=n, p=p, rows=rows: e.matmul(
                                ov[o][:, oc], lhsT=qT3[i2][rows, 2 + p, :], rhs=SfB[rows, n, p, :], start=False, stop=False))
                            fns.append(lambda e, oc=oc, o=o, i2=i2, n=n, p=p, rows=rows: e.matmul(
                                ov[o][:, oc], lhsT=qT3[i2][rows, 4 + p, :], rhs=SbB[rows, n, p, :], start=False, stop=True))
                        cx.op("pe", fns, reads=[R_inT[i2], R_vb[n], R_qT3[i2], R_SfB[n], R_SbB[n]], writes=[R_ov[o]])
                        cx.op("pool", lambda e, i2=i2: e.memset(gst[i2][:, 0, :], 0.0), writes=[R_gst[i2]])
                        cx.op("act", [lambda e, h=h, o=o, i2=i2: e.activation(
                            out=junk2[:], in_=ov[o][:, h * 128:(h + 1) * 128], func=AF.Square, accum_out=gst[i2][:, 0, h:h + 1])
                            for h in range(4)], reads=[R_ov[o]], writes=[R_junk2, R_gst[i2]])
                        cx.op("dve", lambda e, i2=i2: e.tensor_scalar(out=gst[i2][:, 1, :], in0=gst[i2][:, 0, :], scalar1=1.0 / 128,
                                                                      scalar2=EPS, op0=ALU.mult, op1=ALU.add),
                              reads=[R_gst[i2]], writes=[R_gst[i2]])
                        cx.op("pool", lambda e, i2=i2: e.tensor_tensor(out=gst[i2][:, 2, :], in0=gst[i2][:, 1, :], in1=mhalf[:, 0:4], op=ALU.pow),
                              reads=[R_gst[i2], R_const], writes=[R_gst[i2]])
                        cx.op("dve", [lambda e, h=h, o=o, i2=i2: e.scalar_tensor_tensor(
                            out=yr[i2][:, h * 128:(h + 1) * 128], in0=ov[o][:, h * 128:(h + 1) * 128],
                            scalar=gst[i2][:, 2, h:h + 1], in1=gsl[i2][:, h * 128:(h + 1) * 128], op0=ALU.mult, op1=ALU.mult)
                            for h in range(4)], reads=[R_ov[o], R_gst[i2], R_gsl[i2]], writes=[R_yr[i2]])

                    def b2_Z(n):
                        i2 = n % 2
                        b = nxt("tp")
                        cx.op("pe", [lambda e, h=h, b=b, i2=i2: e.transpose(tp[b][:, h * 128:(h + 1) * 128], yr[i2][:, h * 128:(h + 1) * 128], ident[:])
                                     for h in range(4)], reads=[R_yr[i2], R_const], writes=[R_tp[b]])
                        cx.op("act", lambda e, b=b, n=n: e.activation(
                            out=yT[:, 4:8, n * 128:(n + 1) * 128], in_=tp[b][:, 0:512].rearrange("p (a c) -> p a c", a=4), func=AF.Copy),
                            reads=[R_tp[b]], writes=R_yT[4:8])

                    if b_stage >= 3:
                        load_w(1, win_a[l][:, :, 0:4, 0:128], 512)
                        _pipeline(NT, [(0, b2_P), (1, b2_T), (2, b2_Y), (3, b2_Z), (1, b2_T2)])
                    cx.barrier()
                if s == 0 and l == 0:
                    dump("yrT", yT[:, 4, :], R_yT[4:8])
                if stop in ("B1", "Bb", "B2a", "B2b", "B2c", "B2d"):
                    raise _Stop()
                chk("B")

                with ExitStack() as ph:
                    P = ph.enter_context
                    qT = PT(ph, "qT", [128, S], BF16)
                    gaT2 = [PT(ph, "gaT%d" % i, [128, S], BF16) for i in range(2)]
                    Vp = [PT(ph, "Vp%d" % i, [128, 20, 2, 128], BF16) for i in range(2)]
                    ACC = [PT(ph, "ACC%d" % i, [128, S], F32) for i in range(2)]
                    rt1 = PT(ph, "art1", [128, 256], F32)
                    rt2 = PT(ph, "art2", [128, 256], F32)
                    qkr = [PT(ph, "qkr%d" % i, [128, 256], BF16) for i in range(2)]
                    pt = [PT(ph, "pt%d" % i, [128, 512], BF16) for i in range(4)]
                    Rr = [PT(ph, "Rr%d" % i, [128, 512], F32) for i in range(2)]
                    Tm = [PT(ph, "Tm%d" % i, [128, 512], F32) for i in range(2)]
                    R_qT = [Res() for _ in range(NT)]
                    R_gaT2 = [[Res() for _ in range(4)] for _ in range(2)]
                    R_Vp = [Res(), Res()]
                    R_ACC = [[Res() for _ in range(4)] for _ in range(2)]
                    R_rt1, R_rt2 = Res(), Res()
                    R_qkr, R_pt, R_Rr, R_Tm = ([Res(), Res(), Res(), Res()] for _ in range(4))
                    sbank = [st[0], st[1], mm[0], mm[1]]
                    R_sbank = [R_st[0], R_st[1], R_mm[0], R_mm[1]]
                    def vp_init():
                        for i in range(2):
                            vflat = Vp[i][:].rearrange("p a b c -> p (a b) c")
                            for q_ in range(5):
                                cx.op("act", lambda e, i=i, q_=q_, vflat=vflat: e.activation(
                                    out=vflat[:, q_ * 8:(q_ + 1) * 8, :], in_=_ap(onesf[:, 0:128], [[0, 8], [1, 128]]), func=AF.Copy),
                                    reads=[R_const], writes=[R_Vp[i]])
                    if os.environ.get("DUMMY_INIT"):
                        for q_ in range(10):
                            cx.op("act", lambda e, q_=q_: e.activation(out=ACC[0][:, q_ * 128:(q_ + 1) * 128], in_=onesf[:, 0:128], func=AF.Copy),
                                  reads=[R_const], writes=[R_ACC[0]])
                    elif not os.environ.get("VPM_LATE") and not os.environ.get("SKIP_VPM"):
                        vp_init()
                    vpc = [0]
                    fin_pend = []
                    for j in range(4):
                        wslot = [1, 0, 2, 1][j]
                        gaT, R_gaT = gaT2[j % 2], R_gaT2[j % 2]
                        def a1_P(t, wslot=wslot, gaT=gaT, R_gaT=R_gaT):
                            i2 = t % 2
                            a = proj_tok(t, wslot, 0, 256)
                            rope(mm[a][:, 0:256], R_mm[a], t, qkr[i2][:], R_qkr[i2], rt1[:], rt2[:], R_rt1, R_rt2)
                            if t % 4 == 3:
                                g4 = t // 4
                                a = proj_feat(g4, wslot, 256)
                                cx.op("act", lambda e, a=a, g4=g4: e.activation(out=VT1[:, 64 + g4 * 512:64 + (g4 + 1) * 512], in_=mm[a][:, :], func=AF.Copy),
                                      reads=[R_mm[a]], writes=[R_VT1])
                                cx.op("act", lambda e, a=a, g4=g4: e.activation(
                                    out=VT2[:, :, 64 + g4 * 128:64 + (g4 + 1) * 128],
                                    in_=mm[a][:, :].rearrange("p (l r) -> p r l", r=4), func=AF.Copy),
                                    reads=[R_mm[a]], writes=[R_VT2])
                                a = proj_feat(g4, wslot, 384)
                                cx.op("act", lambda e, a=a, g4=g4: e.activation(out=gaT[:, g4 * 512:(g4 + 1) * 512], in_=mm[a][:, :], func=AF.Silu),
                                      reads=[R_mm[a]], writes=[R_gaT[g4]])

                        def a1_T(t):
                            i2 = t % 2
                            b = nxt("tp")
                            cx.op("pe", [lambda e, p=p, b=b, i2=i2: e.transpose(tp[b][:, p * 128:(p + 1) * 128], qkr[i2][:, p * 128:(p + 1) * 128], ident[:])
                                         for p in range(2)], reads=[R_qkr[i2], R_const], writes=[R_tp[b]])
                            def cp(out_ap, in_ap, R_out, t=t, b=b):
                                if t % 2 == 0:
                                    cx.op("act", lambda e: e.activation(out=out_ap, in_=in_ap, func=AF.Copy), reads=[R_tp[b]], writes=[R_out])
                                else:
                                    cx.op("dve", lambda e: e.tensor_copy(out=out_ap, in_=in_ap), reads=[R_tp[b]], writes=[R_out])
                            cp(qT[:, t * 128:(t + 1) * 128], tp[b][:, 0:128], R_qT[t])
                            cp(kT1[:, 64 + t * 128:64 + (t + 1) * 128], tp[b][:, 128:256], R_kT1)
                            cp(kT2[:, :, 64 + t * 32:64 + (t + 1) * 32], tp[b][:, 128:256].rearrange("p (l r) -> p r l", r=4), R_kT2)

                        _pipeline(NT, [(0, a1_P), (1, a1_T)])
                        if j == 0:
                            load_w(0, win_a[l][:, :, 0:4, 128:256], 512)
                            load_w(2, win_a[l][:, :, 0:4, 256:384], 512)
                        elif j == 1:
                            load_w(1, win_a[l][:, :, 0:4, 384:512], 512)
                        elif j == 2:
                            load_w(0, wout_v[l][:, :, 0:512], 512)
                        else:
                            load_w(2, wout_v[l][:, :, 512:1024], 512)
                            nl, ns = (l + 1, s) if l + 1 < depth else (0, s + 1)
                            if ns < nseq:
                                load_w(1, win_v[nl][:, :, 2560:3072], 512)

                        if os.environ.get("VPM_LATE") and j == 0:
                            vp_init()
                        def build_vp(vi, srcs, R_src):
                            for g0 in range(0, len(srcs), 4):
                                grp = srcs[g0:g0 + 4]
                                b = nxt("tp")
                                cx.op("pe", [lambda e, ii=ii, sa_=sa_, b=b: e.transpose(tp[b][:, ii * 128:(ii + 1) * 128], sa_, ident[:])
                                             for ii, (_, sa_) in enumerate(grp)], reads=[R_src, R_const], writes=[R_tp[b]])
                                i0 = grp[0][0]
                                ng = len(grp)
                                for hh in range(2):
                                    eng = "act" if (g0 // 4) % 2 == 0 else "dve"
                                    src = tp[b][:, 0:ng * 128].rearrange("p (a c) -> p a c", c=128)[:, :, hh * 64:hh * 64 + 64]
                                    dst = Vp[vi][:, i0:i0 + ng, hh, hh * 64:hh * 64 + 64]
                                    if eng == "act":
                                        cx.op("act", lambda e, src=src, dst=dst: e.activation(out=dst, in_=src, func=AF.Copy),
                                              reads=[R_tp[b]], writes=[R_Vp[vi]])
                                    else:
                                        cx.op("dve", lambda e, src=src, dst=dst: e.tensor_copy(out=dst, in_=src),
                                              reads=[R_tp[b]], writes=[R_Vp[vi]])

                        pend = []

                        def flush_pend():
                            while pend:
                                pend.pop(0)()

                        tpf = [tp[0][:].bitcast(F32), tp[1][:].bitcast(F32)]
                        obank = [[ov[0][:, :], tpf[0]], [ov[1][:, :], tpf[1]]]
                        R_obank = [[R_ov[0], R_tp[0]], [R_ov[1], R_tp[1]]]

                        def attend2(vi, items2, mask_variant, R_k, acc2):
                            fc = ctr["ov"]
                            ctr["ov"] += 1
                            nk = len(items2[0][0][1])
                            per = 512 // (128 * nk)
                            ngrp = 4 // per
                            for gi, g0 in enumerate(range(0, 4, per)):
                                base = (ctr["w"] % 2) * 2
                                ctr["w"] += 1
                                mv = mask_variant(g0)
                                fns = [lambda e, sa=base + hh, mv=mv: e.matmul(sbank[sa][:, :], lhsT=ident[:], rhs=maskA[:, mv, :],
                                                                               start=True, stop=False) for hh in range(2)]
                                order = [(ii, kk, hh) for ii in range(per) for kk in range(nk) for hh in range(2)]
                                if os.environ.get("NO_ILV"):
                                    order = [(ii, kk, hh) for hh in range(2) for ii in range(per) for kk in range(nk)]
                                for (ii, kk, hh) in order:
                                    if True:
                                        c0 = (ii * nk + kk) * 128
                                        if True:
                                            q_ap, ks = items2[hh][g0 + ii]
                                            k_ap = ks[kk][0]
                                            fns.append(lambda e, c0=c0, sa=base + hh, k_ap=k_ap, q_ap=q_ap: e.matmul(
                                                sbank[sa][:, c0:c0 + 128], lhsT=k_ap, rhs=q_ap, start=False, stop=True))
                                cx.op("pe", fns, reads=[R_k, R_const] + [R_qT[t] for t in range(NT)], writes=[R_sbank[base], R_sbank[base + 1]])
                                for hh in range(2):
                                    pi = base + hh
                                    cx.op("act", lambda e, pi=pi: e.activation(out=pt[pi][:], in_=sbank[pi][:, :], func=AF.Exp, scale=0.125),
                                          reads=[R_sbank[pi]], writes=[R_pt[pi]])

                                def stage2(g0=g0, gi=gi, base=base):
                                    for hh in range(2):
                                        pi = base + hh
                                        fsel = 0
                                        o_ap, R_o = obank[hh][fsel], R_obank[hh][fsel]
                                        fns = []
                                        for ii in range(per):
                                            _, ks = items2[hh][g0 + ii]
                                            qi = g0 + ii
                                            for kk, (_, vidx) in enumerate(ks):
                                                c0 = (ii * nk + kk) * 128
                                                fns.append(lambda e, c0=c0, qi=qi, vidx=vidx, kk=kk, hh=hh, pi=pi, o_ap=o_ap: e.matmul(
                                                    o_ap[:, qi * 128:(qi + 1) * 128], lhsT=Vp[vi][:, vidx, hh, :], rhs=pt[pi][:, c0:c0 + 128],
                                                    start=(kk == 0), stop=(kk == nk - 1)))
                                        cx.op("pe", fns, reads=[R_pt[pi], R_Vp[vi]], writes=[R_o])
                                        if gi == ngrp - 1:
                                            acc2(hh, o_ap, R_o)

                                pend.append(stage2)
                                while len(pend) > 1:
                                    pend.pop(0)()

                        a_st = {"A1": 0, "Ap0": 1, "Ap1": 2, "Ap2": 3}.get(stop, 9)
                        if a_st < 9 and j > 0:
                            continue
                        hrows = [slice(0, 64), slice(64, 128)]
                        for pat in range(min(3, a_st)):
                            vi = vpc[0] % 2
                            vpc[0] += 1
                            if pat == 0:
                                build_vp(vi, [(jt, VT1[:, 128 * jt:128 * jt + 128]) for jt in range(17)], R_VT1)
                                for u in range(4):
                                    items2 = [[(qT[rows, 128 * i_:128 * i_ + 128],
                                                [(kT1[rows, 128 * i_:128 * i_ + 128], i_),
                                                 (kT1[rows, 128 * (i_ + 1):128 * (i_ + 1) + 128], i_ + 1)])
                                               for i_ in range(4 * u, 4 * u + 4)] for rows in hrows]
                                    mvf = lambda g0, u=u: (0 if (u == 0 and g0 == 0) else (2 if (u == 3 and g0 == 2) else 1))
                                    acc2 = lambda hh, o_ap, R_o, u=u: cx.op("dve", lambda e: e.tensor_copy(
                                        out=ACC[hh][:, 512 * u:512 * (u + 1)], in_=o_ap),
                                        reads=[R_o], writes=[R_ACC[hh][u]])
                                    for _ in range(2):
                                        if fin_pend:
                                            fin_pend.pop(0)()
                                    attend2(vi, items2, mvf, R_kT1, acc2)
                            elif pat == 1:
                                build_vp(vi, [(r * 5 + jt, VT2[:, r, 128 * jt:128 * jt + 128]) for r in range(4) for jt in range(5)], R_VT2)
                                for r in range(4):
                                    items2 = [[(qT[rows, 512 * i_ + r:512 * (i_ + 1):4],
                                                [(kT2[rows, r, 128 * i_:128 * i_ + 128], r * 5 + i_),
                                                 (kT2[rows, r, 128 * (i_ + 1):128 * (i_ + 1) + 128], r * 5 + i_ + 1)])
                                               for i_ in range(4)] for rows in hrows]
                                    mvf = lambda g0: (0 if g0 == 0 else 2)
                                    acc2 = lambda hh, o_ap, R_o, r=r: cx.op("dve", lambda e: e.tensor_tensor(
                                        out=ACC[hh][:, r:S:4], in0=o_ap, in1=ACC[hh][:, r:S:4], op=ALU.add),
                                        reads=[R_o] + R_ACC[hh], writes=R_ACC[hh])
                                    attend2(vi, items2, mvf, R_kT2, acc2)
                            else:
                                build_vp(vi, [(r, VT1[:, 64 + r:64 + S:16]) for r in range(16)], R_VT1)
                                for r0 in range(0, 16, 4):
                                    items2 = [[(qT[rows, r:S:16], [(kT1[rows, 64 + r:64 + S:16], r)])
                                               for r in range(r0, r0 + 4)] for rows in hrows]
                                    mvf = lambda g0: 3

                                    def acc2(hh, o_ap, R_o, r0=r0):
                                        accv = ACC[hh][:].rearrange("p (l r) -> p r l", r=16)[:, r0:r0 + 4, :]
                                        cx.op("dve", lambda e: e.tensor_tensor(
                                            out=accv, in0=o_ap.rearrange("p (r l) -> p r l", r=4), in1=accv, op=ALU.add),
                                            reads=[R_o] + R_ACC[hh], writes=R_ACC[hh])
                                    attend2(vi, items2, mvf, R_kT1, acc2)
                        flush_pend()

                        def fin_step(hh, u, j=j, gaT=gaT, R_gaT=R_gaT):
                            nr = slice(hh * 64, hh * 64 + 64)
                            dr = slice((1 - hh) * 64, (1 - hh) * 64 + 64)
                            cs = slice(512 * u, 512 * (u + 1))
                            i2 = ctr["fin"] % 2
                            ctr["fin"] += 1
                            cx.op("act", lambda e: e.activation(out=Rr[i2][nr, :], in_=ACC[hh][dr, cs], func=AF.Ln),
                                  reads=[R_ACC[hh][u]], writes=[R_Rr[i2]])
                            cx.op("act", lambda e: e.activation(out=Rr[i2][nr, :], in_=Rr[i2][nr, :], func=AF.Exp, scale=-1.0),
                                  reads=[R_Rr[i2]], writes=[R_Rr[i2]])
                            cx.op("dve", lambda e: e.tensor_tensor(out=Tm[i2][nr, :], in0=ACC[hh][nr, cs], in1=Rr[i2][nr, :], op=ALU.mult),
                                  reads=[R_ACC[hh][u], R_Rr[i2]], writes=[R_Tm[i2]])
                            cx.op("pool", lambda e: e.tensor_tensor(out=yT[nr, j, cs], in0=Tm[i2][nr, :], in1=gaT[nr, cs], op=ALU.mult),
                                  reads=[R_Tm[i2], R_gaT[u]], writes=[R_yT[j]])

                        if a_st >= 9:
                            for u in range(4):
                                for hh in range(2):
                                    fin_pend.append(lambda hh=hh, u=u, f=fin_step: f(hh, u))
                    while fin_pend:
                        fin_pend.pop(0)()
                    cx.barrier()
                if s == 0 and l == 0:
                    dump("yaT", yT[:, 0, :], R_yT[0:4])
                if stop in ("A1", "Ap0", "Ap1", "Ap2"):
                    raise _Stop()
                chk("A")

                with ExitStack() as ph:
                    P = ph.enter_context
                    gate_b = PT(ph, "gate_b", [128, D], F32)
                    gfin_b = PT(ph, "gfin_b", [128, D], F32)
                    R_gfin = Res()
                    if last:
                        cx.dma("sp", "cst", lambda e: e.dma_start(out=gfin_b[:], in_=gfin_d.partition_broadcast(128)), writes=[R_gfin])
                    xin = [PT(ph, "oxin%d" % i, [128, D], F32) for i in range(4)]
                    xo = [PT(ph, "xo%d" % i, [128, D], F32) for i in range(2)]
                    Dk = [PT(ph, "Dk%d" % i, [128, 128], F32) for i in range(2)]
                    junk = PT(ph, "ojunk", [128, D], BF16)
                    fst = [PT(ph, "fst%d" % i, [128, 3], F32) for i in range(2)]
                    R_xin, R_xo, R_Dk, R_fst = ([Res(), Res(), Res(), Res()] for _ in range(4))
                    R_xoa, R_xob = [Res(), Res()], [Res(), Res()]
                    R_junk = Res()
                    if not last:
                        oxn = [PT(ph, "oxn%d" % i, [128, 4, D], BF16) for i in range(2)]
                        ostat = PT(ph, "ostat", [128, 3, NT], F32)
                        R_oxn = [Res(), Res()]
                        R_ostat = [Res() for _ in range(NT)]
                    h_pend = []

                    def o_H(g4):
                        xb = oxn[g4 % 2]
                        for k in range(8):
                            b = nxt("tp")
                            cx.op("pe", [lambda e, jj=jj, k=k, b=b, xb=xb: e.transpose(tp[b][:, jj * 128:(jj + 1) * 128],
                                                                                   xb[:, jj, k * 128:(k + 1) * 128], ident[:])
                                         for jj in range(4)], reads=[R_oxn[g4 % 2], R_const], writes=[R_tp[b]])
                            cx.op("act", lambda e, k=k, b=b, g4=g4: e.activation(
                                out=hT[:, k, g4 * 512:(g4 + 1) * 512], in_=tp[b][:, 0:512], func=AF.Identity,
                                scale=gsA[:, l + 1, s, k:k + 1], bias=shA[:, l + 1, s, k:k + 1]),
                                reads=[R_tp[b], R_mod], writes=[R_hT[g4]])
                    for k in range(8):
                        i2 = k % 2
                        cx.op("dve", lambda e, k=k, i2=i2: e.tensor_scalar(out=Dk[i2][:], in0=identf[:], scalar1=gtA[:, l, s, k:k + 1],
                                                                         scalar2=None, op0=ALU.mult),
                              reads=[R_const, R_mod], writes=[R_Dk[i2]])
                        a = nxt("mm")
                        cx.op("pe", lambda e, a=a, i2=i2: e.matmul(mm[a][:, 0:128], lhsT=onesf[:], rhs=Dk[i2][:], start=True, stop=True),
                              reads=[R_Dk[i2], R_const], writes=[R_mm[a]])
                        cx.op("act", lambda e, a=a, k=k: e.activation(out=gate_b[:, k * 128:(k + 1) * 128], in_=mm[a][:, 0:128], func=AF.Copy),
                              reads=[R_mm[a]], writes=[R_gate])
                    def o_A(t):
                        i2 = t % 2
                        i4 = t % 4
                        cx.dma("sp", "xin%d" % i4, lambda e, t=t, i4=i4: e.dma_start(out=xin[i4][:], in_=xsrc[s, t * 128:(t + 1) * 128, :]),
                               reads=[R_yd[s][t]], writes=[R_xin[i4]])
                        for half in range(2):
                            a = nxt("mm")
                            wslot = 0 if half == 0 else 2
                            hs = slice(half * 512, (half + 1) * 512)
                            cx.op("pe", [lambda e, kc=kc, a=a, wslot=wslot, t=t: e.matmul(
                                mm[a][:, :], lhsT=yT[:, kc, t * 128:(t + 1) * 128], rhs=wsl[wslot][:, kc, :],
                                start=(kc == 0), stop=(kc == 7)) for kc in range(8)],
                                reads=R_yT + [R_w[wslot]], writes=[R_mm[a]])
                            cx.op("dve", lambda e, a=a, hs=hs, i2=i2: e.tensor_tensor(out=xo[i2][:, hs], in0=mm[a][:, :], in1=gate_b[:, hs], op=ALU.mult),
                                  reads=[R_mm[a], R_gate], writes=[(R_xo if half == 0 else R_xob)[i2]])
                        cx.op("dve", lambda e, i2=i2, i4=i4: e.tensor_tensor(out=xo[i2][:, 0:512], in0=xo[i2][:, 0:512], in1=xin[i4][:, 0:512], op=ALU.add),
                              reads=[R_xo[i2], R_xin[i4]], writes=[R_xo[i2]])
                        cx.op("pool", lambda e, i2=i2, i4=i4: e.tensor_tensor(out=xo[i2][:, 512:1024], in0=xo[i2][:, 512:1024], in1=xin[i4][:, 512:1024], op=ALU.add),
                              reads=[R_xob[i2], R_xin[i4]], writes=[R_xob[i2]])

                    def o_B(t):
                        i2 = t % 2
                        if last:
                            cx.op("pool", lambda e, i2=i2: e.memset(fst[i2][:, 0:1], 0.0), writes=[R_fst[i2]])
                            cx.op("act", lambda e, i2=i2: e.activation(out=junk[:], in_=xo[i2][:], func=AF.Square, accum_out=fst[i2][:, 0:1]),
                                  reads=[R_xo[i2], R_xob[i2]], writes=[R_junk, R_fst[i2]])
                            cx.op("dve", lambda e, i2=i2: e.tensor_scalar(out=fst[i2][:, 1:2], in0=fst[i2][:, 0:1], scalar1=1.0 / D,
                                                                          scalar2=EPS, op0=ALU.mult, op1=ALU.add),
                                  reads=[R_fst[i2]], writes=[R_fst[i2]])
                            cx.op("pool", lambda e, i2=i2: e.tensor_tensor(out=fst[i2][:, 2:3], in0=fst[i2][:, 1:2], in1=mhalf[:, 0:1], op=ALU.pow),
                                  reads=[R_fst[i2], R_const], writes=[R_fst[i2]])
                            cx.op("dve", lambda e, i2=i2: e.scalar_tensor_tensor(
                                out=xo[i2][:], in0=xo[i2][:], scalar=fst[i2][:, 2:3], in1=gfin_b[:], op0=ALU.mult, op1=ALU.mult),
                                reads=[R_xo[i2], R_xob[i2], R_fst[i2], R_gfin], writes=[R_xo[i2], R_xob[i2]])
                        cx.dma("sp", "xout%d" % i2, lambda e, t=t, i2=i2: e.dma_start(out=y_d[s, t * 128:(t + 1) * 128, :], in_=xo[i2][:]),
                               reads=[R_xo[i2], R_xob[i2]], writes=[R_yd[s][t]])
                        if not last:
                            g4, jj = t // 4, t % 4
                            cx.op("pool", lambda e, t=t: e.memset(ostat[:, 0, t:t + 1], 0.0), writes=[R_ostat[t]])
                            cx.op("act", lambda e, t=t, i2=i2: e.activation(out=junk[:], in_=xo[i2][:], func=AF.Square,
                                                                           accum_out=ostat[:, 0, t:t + 1]),
                                  reads=[R_xo[i2], R_xob[i2]], writes=[R_junk, R_ostat[t]])
                            cx.op("dve", lambda e, t=t: e.tensor_scalar(out=ostat[:, 1, t:t + 1], in0=ostat[:, 0, t:t + 1], scalar1=1.0 / D,
                                                                        scalar2=EPS, op0=ALU.mult, op1=ALU.add),
                                  reads=[R_ostat[t]], writes=[R_ostat[t]])
                            cx.op("pool", lambda e, t=t: e.tensor_tensor(out=ostat[:, 2, t:t + 1], in0=ostat[:, 1, t:t + 1], in1=mhalf[:, 0:1], op=ALU.pow),
                                  reads=[R_ostat[t], R_const], writes=[R_ostat[t]])
                            cx.op("dve", lambda e, t=t, i2=i2, jj=jj, g4=g4: e.tensor_scalar(
                                out=oxn[g4 % 2][:, jj, :], in0=xo[i2][:], scalar1=ostat[:, 2, t:t + 1], scalar2=None, op0=ALU.mult),
                                reads=[R_xo[i2], R_xob[i2], R_ostat[t]], writes=[R_oxn[g4 % 2]])
                            if h_pend and h_pend[0][0] <= t:
                                o_H(h_pend.pop(0)[1])
                            if jj == 3:
                                h_pend.append((t + 2, g4))

                    _pipeline(NT, [(0, o_A), (1, o_B)])
                    while h_pend:
                        o_H(h_pend.pop(0)[1])
                    cx.barrier()
        except _Stop:
            pass
        cx.barrier()
    return nc


def _consts():
    f32 = np.float32
    pos = np.arange(S, dtype=np.float32)
    inv = (10000.0 ** (-np.arange(0, 64, 2, dtype=np.float32) / 64)).astype(f32)
    ang = (pos[:, None] * inv[None, :]).astype(f32)
    cos, sin = np.cos(ang).astype(f32), np.sin(ang).astype(f32)
    C2 = np.concatenate([cos, cos], axis=1)
    S2 = np.concatenate([-sin, sin], axis=1)
    rope = np.stack([C2.reshape(NT, 128, 64).transpose(1, 0, 2), S2.reshape(NT, 128, 64).transpose(1, 0, 2)], axis=1)
    p = np.arange(128)[:, None]
    c = np.arange(128)[None, :]
    A = (c <= p).astype(f32)
    B = (p <= c).astype(f32)
    A_first = A * (p >= 64)
    B_last = B * (p < 64)
    m_norm = np.concatenate([A, B], axis=1)
    m_first = np.concatenate([A_first, B], axis=1)
    m_last = np.concatenate([A, B_last], axis=1)
    band = (np.abs(p - c) <= 64).astype(f32)
    mask = np.stack([np.concatenate([m_first, m_norm], 1), np.concatenate([m_norm, m_norm], 1),
                     np.concatenate([m_norm, m_last], 1), np.concatenate([band] * 4, 1)], axis=1)
    diff = (c - p).astype(f32)
    ret = np.stack([np.maximum(diff, 0), (diff >= 0).astype(f32), np.maximum(-diff, 0), (diff < 0).astype(f32)], axis=1)
    tau = np.arange(128, dtype=f32)
    taus = np.stack([127 - tau, tau, tau + 1, 128 - tau], axis=1)
    mask = (mask - 1.0) * 30000.0
    return dict(cst_rope=np.ascontiguousarray(rope, f32), cst_mask=np.ascontiguousarray(mask, f32),
                cst_ret=np.ascontiguousarray(ret, f32), cst_tau=np.ascontiguousarray(taus, f32),
                cst_ident=np.eye(128, dtype=f32))


def kernel(x_prompt, x_sample, c_prompt, c_sample, g_norm, w_ada, b_ada, w_in, w_out,
           decay_fwd, decay_bwd, g_final):
    f = lambda a: np.ascontiguousarray(np.asarray(a), dtype=np.float32)
    xs = np.concatenate([f(x_prompt), f(x_sample)], axis=0)
    cs = np.concatenate([f(c_prompt), f(c_sample)], axis=0)
    shared = dict(g_norm=f(g_norm), w_ada=f(w_ada), b_ada=f(b_ada), w_in=f(w_in), w_out=f(w_out),
                  decay_fwd=f(decay_fwd), decay_bwd=f(decay_bwd), g_final=f(g_final))
    shared.update(_consts())
    nc = build_nc()
    in_maps = []
    for i in range(NCORES):
        m = dict(shared)
        m["x"] = np.ascontiguousarray(xs[i * NSEQ:(i + 1) * NSEQ])
        m["c"] = np.ascontiguousarray(cs[i * NSEQ:(i + 1) * NSEQ])
        in_maps.append(m)
    res = run_bass_kernel_spmd(nc, in_maps, core_ids=list(range(NCORES)))
    ys = np.concatenate([np.asarray(r["y"], dtype=np.float32) for r in res.results], axis=0)
    nb = np.asarray(x_prompt).shape[0]
    return (np.ascontiguousarray(ys[:nb]), np.ascontiguousarray(ys[nb:]))
```

```python
import math
import os
from contextlib import ExitStack

import numpy as np
import concourse.bass as bass
import concourse.mybir as mybir
from concourse.bass_utils import run_bass_kernel_spmd

F32 = mybir.dt.float32
BF16 = mybir.dt.bfloat16
AF = mybir.ActivationFunctionType
ALU = mybir.AluOpType

D = 1024
S = 2048
NT = 16
DEPTH = 2
NCORES = 8
NSEQ = 3
INW = 3584
EPS = 1e-6


class Tok:
    __slots__ = ("eng", "key", "val")

    def __init__(self, eng, key, val):
        self.eng, self.key, self.val = eng, key, val


class Res:
    __slots__ = ("name", "w", "r")

    def __init__(self, name=""):
        self.name, self.w, self.r = name, None, []


class Ctx:
    def __init__(self, nc, es):
        self.nc, self.es = nc, es
        self.engs = {"pe": nc.tensor, "act": nc.scalar, "dve": nc.vector, "pool": nc.gpsimd, "sp": nc.sync}
        self.sems, self.cnt = {}, {}
        self.seen = {e: {} for e in self.engs}
        self.epoch = 0
        self.dma_keys = set()
        self.new_epoch()

    def _mksem(self, key):
        if key not in self.sems:
            self.sems[key] = self.es.enter_context(self.nc.semaphore(key))
            self.cnt[key] = 0

    def new_epoch(self):
        self.epoch += 1
        self.ekey = {e: "%s%d" % (e, self.epoch) for e in self.engs if e != "sp"}
        for k in self.ekey.values():
            self._mksem(k)

    def _waits(self, eng, reads, writes, skipkey=None):
        deps = {}

        def need(tok, kind):
            if tok is None or tok.key == skipkey:
                return
            if tok.eng == eng and (eng == "pe" or kind != "RAW"):
                return
            if deps.get(tok.key, 0) < tok.val:
                deps[tok.key] = tok.val

        for r in reads:
            need(r.w, "RAW")
        for w in writes:
            need(w.w, "WAW")
            for t in w.r:
                need(t, "WAR")
        E, seen = self.engs[eng], self.seen[eng]
        for key, val in deps.items():
            if seen.get(key, 0) < val:
                E.wait_ge(self.sems[key], val)
                seen[key] = val

    def _commit(self, tok, reads, writes):
        for r in reads:
            r.r = [t for t in r.r if t.key != tok.key] + [tok]
        for w in writes:
            w.w, w.r = tok, []

    def op(self, eng, fns, reads=(), writes=()):
        self._waits(eng, reads, writes)
        if not isinstance(fns, (list, tuple)):
            fns = [fns]
        ins = None
        for f in fns:
            ins = f(self.engs[eng])
        key = self.ekey[eng]
        self.cnt[key] += 1
        ins.then_inc(self.sems[key], 1)
        tok = Tok(eng, key, self.cnt[key])
        self._commit(tok, reads, writes)
        return tok

    def dma(self, q, semname, fn, reads=(), writes=()):
        key = "d%d_%s" % (self.epoch, semname)
        self._mksem(key)
        self.dma_keys.add(key)
        self._waits(q, reads, writes, skipkey=key)
        ins = fn(self.engs[q])
        self.cnt[key] += 16
        ins.then_inc(self.sems[key], 16)
        tok = Tok("dma", key, self.cnt[key])
        self._commit(tok, reads, writes)
        return tok

    def barrier(self, with_dma=True):
        keys = list(self.ekey.values())
        if with_dma:
            keys += sorted(self.dma_keys)
        for e, E in self.engs.items():
            seen = self.seen[e]
            for key in keys:
                val = self.cnt[key]
                if val > 0 and seen.get(key, 0) < val:
                    E.wait_ge(self.sems[key], val)
                    seen[key] = val


def _pipeline(n_iter, stages):
    mx = max(sk for sk, _ in stages)
    for i in range(n_iter + mx):
        for sk, fn in stages:
            n = i - sk
            if 0 <= n < n_iter:
                fn(n)


def _ap(base, dims):
    return bass.AP(base.tensor, base.offset, [list(base.ap[0])] + [list(d) for d in dims])


class _Stop(Exception):
    pass


def build_nc(nseq=NSEQ, depth=DEPTH, dbg=(), stop=None):
    nc = bass.Bass("TRN2", target_bir_lowering=False)
    dt = nc.dram_tensor
    x_d = dt("x", [nseq, S, D], F32, kind="ExternalInput").ap()
    c_d = dt("c", [nseq, D], F32, kind="ExternalInput").ap()
    gn_d = dt("g_norm", [DEPTH, D], F32, kind="ExternalInput").ap()
    wada_d = dt("w_ada", [DEPTH, D, 3 * D], F32, kind="ExternalInput").ap()
    bada_d = dt("b_ada", [DEPTH, 3 * D], F32, kind="ExternalInput").ap()
    win_d = dt("w_in", [DEPTH, D, INW], F32, kind="ExternalInput").ap()
    wout_d = dt("w_out", [DEPTH, D, D], F32, kind="ExternalInput").ap()
    dfw_d = dt("decay_fwd", [DEPTH, 4], F32, kind="ExternalInput").ap()
    dbw_d = dt("decay_bwd", [DEPTH, 4], F32, kind="ExternalInput").ap()
    gfin_d = dt("g_final", [D], F32, kind="ExternalInput").ap()
    crope_d = dt("cst_rope", [128, 2, NT, 64], F32, kind="ExternalInput").ap()
    cmask_d = dt("cst_mask", [128, 4, 512], F32, kind="ExternalInput").ap()
    cret_d = dt("cst_ret", [128, 4, 128], F32, kind="ExternalInput").ap()
    ctau_d = dt("cst_tau", [128, 4], F32, kind="ExternalInput").ap()
    cid_d = dt("cst_ident", [128, 128], F32, kind="ExternalInput").ap()
    y_d = dt("y", [nseq, S, D], F32, kind="ExternalOutput").ap()
    dbg_d = {}
    for name, shape in dbg:
        dbg_d[name] = dt("dbg_" + name, list(shape), F32, kind="ExternalOutput").ap()

    with ExitStack() as es:
        E = es.enter_context
        cx = Ctx(nc, es)
        sb = lambda name, shape, dtype: E(nc.sbuf_tensor(name, list(shape), dtype))
        uid = [0]

        def PT(ph, name, shape, dtype):
            uid[0] += 1
            return ph.enter_context(nc.sbuf_tensor("%s_u%d" % (name, uid[0]), list(shape), dtype))

        hT = sb("hT", [128, 8, S], BF16)
        yT = sb("yT", [128, 8, S], BF16)
        wsl = [sb("wsl%d" % i, [128, 8, 512], BF16) for i in range(3)]
        kT1 = sb("kT1", [128, S + 128], BF16)
        kT2 = sb("kT2", [128, 4, 640], BF16)
        VT1 = sb("VT1", [128, S + 128], BF16)
        VT2 = sb("VT2", [128, 4, 640], BF16)
        ident = sb("ident", [128, 128], BF16)
        identf = sb("identf", [128, 128], F32)
        onesf = sb("onesf", [128, 128], F32)
        rope_t = sb("rope_t", [128, 2, NT, 64], F32)
        maskA = sb("maskA", [128, 4, 512], BF16)
        cret = sb("cret", [128, 4, 128], F32)
        ctau = sb("ctau", [128, 4], F32)
        DTt_all = sb("DTt", [128, DEPTH, 4, 128], F32)
        TAB_all = sb("TAB", [128, DEPTH, 4, 4, 64], F32)
        Gfb_all = sb("Gfb", [128, DEPTH, 2, 2], F32)
        gsA = sb("gsA", [128, DEPTH, nseq, 8], F32)
        shA = sb("shA", [128, DEPTH, nseq, 8], F32)
        gtA = sb("gtA", [128, DEPTH, nseq, 8], F32)
        mhalf = sb("mhalf", [128, 16], F32)

        mm = [E(nc.psum_tensor("mm%d" % i, [128, 512], F32)) for i in range(2)]
        tp = [E(nc.psum_tensor("tp%d" % i, [128, 1024], BF16)) for i in range(2)]
        st = [E(nc.psum_tensor("st%d" % i, [128, 512], F32)) for i in range(2)]
        ov = [E(nc.psum_tensor("ov%d" % i, [128, 512], F32)) for i in range(2)]
        R_mm = [Res("mm0"), Res("mm1")]
        R_tp = [Res("tp0"), Res("tp1")]
        R_st = [Res("st0"), Res("st1")]
        R_ov = [Res("ov0"), Res("ov1")]
        ctr = {"mm": 0, "tp": 0, "st": 0, "ov": 0, "w": 0, "fin": 0}

        def nxt(kind):
            i = ctr[kind] % 2
            ctr[kind] += 1
            return i

        R_const = Res("const")
        R_hT = [Res("hT%d" % g) for g in range(4)]
        R_yT = [Res("yT%d" % k) for k in range(8)]
        R_w = [Res("w%d" % i) for i in range(3)]
        R_kT1, R_kT2, R_VT1, R_VT2 = Res("kT1"), Res("kT2"), Res("VT1"), Res("VT2")
        R_tab, R_gate, R_mod = Res("tab"), Res("gate"), Res("mod")
        R_yd = [[Res("yd%d_%d" % (s, t)) for t in range(NT)] for s in range(nseq)]
        dump_list = []

        def dump(name, src_ap, reads):
            if name in dbg_d:
                dump_list.append(cx.dma("pool", "dbg", lambda e: e.dma_start(out=dbg_d[name], in_=src_ap), reads=reads))

        cx.dma("sp", "cst", lambda e: e.dma_start(out=rope_t[:], in_=crope_d), writes=[R_const])
        cx.dma("pool", "cstp", lambda e: e.dma_start(out=maskA[:], in_=cmask_d), writes=[R_const])
        cx.dma("sp", "cst", lambda e: e.dma_start(out=cret[:], in_=cret_d), writes=[R_const])
        cx.dma("sp", "cst", lambda e: e.dma_start(out=ctau[:], in_=ctau_d), writes=[R_const])
        cx.dma("pool", "cstp", lambda e: e.dma_start(out=ident[:], in_=cid_d), writes=[R_const])
        cx.dma("sp", "cst", lambda e: e.dma_start(out=identf[:], in_=cid_d), writes=[R_const])
        cx.op("pool", lambda e: e.memset(onesf[:], 1.0), writes=[R_const])
        cx.op("pool", lambda e: e.memset(mhalf[:], -0.5), writes=[R_const])
        cx.op("pool", lambda e: e.memset(kT1[:], 0.0), writes=[R_kT1])
        cx.op("pool", lambda e: e.memset(kT2[:], 0.0), writes=[R_kT2])
        cx.op("pool", lambda e: e.memset(VT1[:], 0.0), writes=[R_VT1])
        cx.op("pool", lambda e: e.memset(VT2[:], 0.0), writes=[R_VT2])

        with ExitStack() as ph:
            P = ph.enter_context
            wada = PT(ph, "wada", [128, 8, 3 * D], BF16)
            cT = PT(ph, "cT", [128, nseq, 8], F32)
            silc = PT(ph, "silc", [128, 8, nseq], BF16)
            badaT = PT(ph, "badaT", [128, 24], F32)
            gnT = PT(ph, "gnT", [128, 8], F32)
            modT = PT(ph, "modT", [128, 24, nseq], F32)
            R_wada, R_cT, R_silc, R_bada, R_gn, R_modT = (Res() for _ in range(6))
            for s in range(nseq):
                cx.dma("sp", "cst", lambda e, s=s: e.dma_start(
                    out=cT[:, s, :], in_=c_d[s].rearrange("(k p) -> p k", p=128), allow_slow_non_contiguous=True),
                    writes=[R_cT])
            cx.op("act", lambda e: e.activation(out=silc[:].rearrange("p k s -> p s k"), in_=cT[:], func=AF.Silu),
                  reads=[R_cT], writes=[R_silc])
            for l in range(depth):
                for i in range(6):
                    cx.dma("pool", "wada", lambda e, i=i, l=l: e.dma_start(
                        out=wada[:, :, i * 512:(i + 1) * 512],
                        in_=wada_d[l].rearrange("(k p) n -> p k n", p=128)[:, :, i * 512:(i + 1) * 512]),
                        writes=[R_wada])
                cx.dma("sp", "cst", lambda e, l=l: e.dma_start(
                    out=badaT[:], in_=bada_d[l].rearrange("(o p) -> p o", p=128), allow_slow_non_contiguous=True),
                    writes=[R_bada])
                cx.dma("sp", "cst", lambda e, l=l: e.dma_start(
                    out=gnT[:], in_=gn_d[l].rearrange("(k p) -> p k", p=128), allow_slow_non_contiguous=True),
                    writes=[R_gn])
                fns = []
                for oc in range(24):
                    for kc in range(8):
                        fns.append(lambda e, oc=oc, kc=kc: e.matmul(
                            mm[0][:, oc * nseq:(oc + 1) * nseq], lhsT=wada[:, kc, oc * 128:(oc + 1) * 128],
                            rhs=silc[:, kc, :], start=(kc == 0), stop=(kc == 7)))
                cx.op("pe", fns, reads=[R_wada, R_silc], writes=[R_mm[0]])
                mmv = mm[0][:, 0:24 * nseq].rearrange("p (o s) -> p o s", s=nseq)
                for s in range(nseq):
                    cx.op("dve", lambda e, s=s: e.tensor_tensor(out=modT[:, :, s], in0=mmv[:, :, s], in1=badaT[:], op=ALU.add),
                          reads=[R_mm[0], R_bada], writes=[R_modT])
                for s in range(nseq):
                    cx.op("dve", lambda e, s=s, l=l: e.scalar_tensor_tensor(
                        out=gsA[:, l, s, :], in0=modT[:, 8:16, s], scalar=1.0, in1=gnT[:], op0=ALU.add, op1=ALU.mult),
                        reads=[R_modT, R_gn], writes=[R_mod])
                    cx.op("dve", lambda e, s=s, l=l: e.tensor_copy(out=shA[:, l, s, :], in_=modT[:, 0:8, s]),
                          reads=[R_modT], writes=[R_mod])
                    cx.op("dve", lambda e, s=s, l=l: e.tensor_copy(out=gtA[:, l, s, :], in_=modT[:, 16:24, s]),
                          reads=[R_modT], writes=[R_mod])
            cx.barrier()

        for l in range(depth):
            DTt, TAB, Gfb = DTt_all[:, l], TAB_all[:, l], Gfb_all[:, l]
            with ExitStack() as ph:
                P = ph.enter_context
                dfb = PT(ph, "dfb", [128, 8], F32)
                lg = PT(ph, "lg", [128, 8], F32)
                dsc = PT(ph, "dsc", [128, 4, 4], F32)
                e1 = PT(ph, "e1", [128, 128], F32)
                e2 = PT(ph, "e2", [128, 128], F32)
                R_dfb, R_lg, R_dsc, R_e1, R_e2 = (Res() for _ in range(5))
                cx.dma("sp", "cst", lambda e: e.dma_start(out=dfb[:, 0:4], in_=dfw_d[l].partition_broadcast(128)), writes=[R_dfb])
                cx.dma("sp", "cst", lambda e: e.dma_start(out=dfb[:, 4:8], in_=dbw_d[l].partition_broadcast(128)), writes=[R_dfb])
                cx.op("act", lambda e: e.activation(out=lg[:], in_=dfb[:], func=AF.Exp, scale=-1.0), reads=[R_dfb], writes=[R_lg])
                cx.op("act", lambda e: e.activation(out=lg[:], in_=lg[:], func=AF.Ln, bias=1.0), reads=[R_lg], writes=[R_lg])
                cx.op("dve", lambda e: e.tensor_scalar(out=lg[:], in0=lg[:], scalar1=-1.0, scalar2=None, op0=ALU.mult),
                      reads=[R_lg], writes=[R_lg])
                for kind in range(4):
                    lo = 0 if kind in (0, 2) else 4
                    cx.op("act", lambda e, kind=kind, lo=lo: e.activation(
                        out=dsc[:, kind, :], in_=lg[:, lo:lo + 4], func=AF.Exp, scale=ctau[:, kind:kind + 1]),
                        reads=[R_lg, R_const], writes=[R_dsc])
                for kind in range(4):
                    for h in range(4):
                        cx.op("dve", lambda e, kind=kind, h=h: e.tensor_scalar(
                            out=TAB[:, kind, h, :], in0=onesf[:, 0:64], scalar1=dsc[:, kind, h:h + 1],
                            scalar2=(0.125 if kind < 2 else 1.0), op0=ALU.mult, op1=ALU.mult),
                            reads=[R_dsc, R_const], writes=[R_tab])
                for h in range(4):
                    cx.op("act", lambda e, h=h: e.activation(out=e1[:], in_=cret[:, 0, :], func=AF.Exp, scale=lg[:, h:h + 1]),
                          reads=[R_lg, R_const], writes=[R_e1])
                    cx.op("dve", lambda e: e.tensor_tensor(out=e1[:], in0=e1[:], in1=cret[:, 1, :], op=ALU.mult),
                          reads=[R_e1, R_const], writes=[R_e1])
                    cx.op("act", lambda e, h=h: e.activation(out=e2[:], in_=cret[:, 2, :], func=AF.Exp, scale=lg[:, 4 + h:5 + h]),
                          reads=[R_lg, R_const], writes=[R_e2])
                    cx.op("dve", lambda e: e.tensor_tensor(out=e2[:], in0=e2[:], in1=cret[:, 3, :], op=ALU.mult),
                          reads=[R_e2, R_const], writes=[R_e2])
                    cx.op("dve", lambda e, h=h: e.tensor_tensor(out=DTt[:, h, :], in0=e1[:], in1=e2[:], op=ALU.add),
                          reads=[R_e1, R_e2], writes=[R_tab])
                for d_ in range(2):
                    for p in range(2):
                        for hh in range(2):
                            rows = slice(hh * 64, hh * 64 + 64)
                            col = d_ * 4 + 2 * p + hh
                            cx.op("act", lambda e, d_=d_, p=p, rows=rows, col=col: e.activation(
                                out=Gfb[rows, d_, p:p + 1], in_=lg[rows, col:col + 1], func=AF.Exp, scale=128.0),
                                reads=[R_lg], writes=[R_tab])
                cx.barrier()

        R_wser = Res("wser")

        def load_w(slot, src_ap, ncols):
            if len(src_ap.shape) == 3:
                view = wsl[slot][:, :, 0:ncols]
                cx.dma("pool", "w%d" % slot, lambda e: e.dma_start(out=view, in_=src_ap), writes=[R_w[slot], R_wser])
            else:
                nr = src_ap.shape[2]
                w_ = src_ap.shape[3]
                for r in range(nr):
                    view = wsl[slot][:, :, r * w_:(r + 1) * w_]
                    cx.dma("pool", "w%d" % slot, lambda e, view=view, r=r: e.dma_start(out=view, in_=src_ap[:, :, r, :]),
                           writes=[R_w[slot], R_wser])

        def rope(src_psum, R_src, t, out_ap, R_out, tmp1, tmp2, R_t1, R_t2):
            xv = src_psum.rearrange("p (h d) -> p h d", d=64)
            cb = _ap(rope_t[:, 0, t, :], [[0, 4], [1, 64]])
            s_lo = _ap(rope_t[:, 1, t, 0:32], [[0, 4], [1, 32]])
            s_hi = _ap(rope_t[:, 1, t, 32:64], [[0, 4], [1, 32]])
            t1v = tmp1.rearrange("p (h d) -> p h d", d=64)
            t2v = tmp2.rearrange("p (h d) -> p h d", d=64)
            cx.op("dve", lambda e: e.tensor_tensor(out=t1v, in0=xv, in1=cb, op=ALU.mult),
                  reads=[R_src, R_const], writes=[R_t1])
            cx.op("dve", [lambda e: e.tensor_tensor(out=t2v[:, :, 0:32], in0=xv[:, :, 32:64], in1=s_lo, op=ALU.mult),
                          lambda e: e.tensor_tensor(out=t2v[:, :, 32:64], in0=xv[:, :, 0:32], in1=s_hi, op=ALU.mult)],
                  reads=[R_src, R_const], writes=[R_t2])
            cx.op("dve", lambda e: e.tensor_tensor(out=out_ap, in0=tmp1, in1=tmp2, op=ALU.add),
                  reads=[R_t1, R_t2], writes=[R_out])

        def proj_tok(t, wslot, c0, ncols):
            a = nxt("mm")
            fns = [lambda e, kc=kc: e.matmul(mm[a][:, 0:ncols], lhsT=hT[:, kc, t * 128:(t + 1) * 128],
                                              rhs=wsl[wslot][:, kc, c0:c0 + ncols], start=(kc == 0), stop=(kc == 7))
                   for kc in range(8)]
            cx.op("pe", fns, reads=[R_hT[t // 4], R_w[wslot]], writes=[R_mm[a]])
            return a

        def proj_feat(g4, wslot, c0):
            a = nxt("mm")
            fns = [lambda e, kc=kc: e.matmul(mm[a][:, :], lhsT=wsl[wslot][:, kc, c0:c0 + 128],
                                              rhs=hT[:, kc, g4 * 512:(g4 + 1) * 512], start=(kc == 0), stop=(kc == 7))
                   for kc in range(8)]
            cx.op("pe", fns, reads=[R_hT[g4], R_w[wslot]], writes=[R_mm[a]])
            return a

        win_v = [win_d[l].rearrange("(k p) n -> p k n", p=128) for l in range(DEPTH)]
        win_a = [win_d[l].rearrange("(k p) (r c) -> p k r c", p=128, r=7) for l in range(DEPTH)]
        wout_v = [wout_d[l].rearrange("(k p) n -> p k n", p=128) for l in range(DEPTH)]

        def chk(name):
            if stop == name:
                raise _Stop()

        try:
          chk("M")
          for s in range(nseq):
            for l in range(depth):
                xsrc = x_d if l == 0 else y_d
                DTt, TAB, Gfb = DTt_all[:, l], TAB_all[:, l], Gfb_all[:, l]
                last = (l == depth - 1)
                if not (s == 0 and l == 0):
                    cx.barrier()
                    cx.new_epoch()
                load_w(0, win_v[l][:, :, 2048:2560], 512)
                if s == 0 and l == 0:
                    load_w(1, win_v[l][:, :, 2560:3072], 512)
                load_w(2, win_v[l][:, :, 3072:3584], 512)

                chk("T")

                with ExitStack() as ph:
                  if l == 0:
                      P = ph.enter_context
                      xin = [PT(ph, "xin%d" % i, [128, D], F32) for i in range(4)]
                      xn = [PT(ph, "xn%d" % i, [128, 4, D], BF16) for i in range(2)]
                      junk = PT(ph, "junk", [128, D], BF16)
                      stat = PT(ph, "stat", [128, 3, NT], F32)
                      R_xin = [Res(), Res(), Res(), Res()]
                      R_xn = [Res(), Res()]
                      R_junk = Res()
                      R_stat = [Res() for _ in range(NT)]
                      def st_X(g4):
                          xb = xn[g4 % 2]
                          for j in range(4):
                              t = 4 * g4 + j
                              xi = xin[t % 4]
                              cx.dma("sp", "xin%d" % (t % 4), lambda e, t=t, xi=xi: e.dma_start(out=xi[:], in_=xsrc[s, t * 128:(t + 1) * 128, :]),
                                     reads=[R_yd[s][t]], writes=[R_xin[t % 4]])
                              cx.op("pool", lambda e, t=t: e.memset(stat[:, 0, t:t + 1], 0.0), writes=[R_stat[t]])
                              cx.op("act", lambda e, t=t, xi=xi: e.activation(out=junk[:], in_=xi[:], func=AF.Square,
                                                                             accum_out=stat[:, 0, t:t + 1]),
                                    reads=[R_xin[t % 4]], writes=[R_junk, R_stat[t]])
                              cx.op("dve", lambda e, t=t: e.tensor_scalar(out=stat[:, 1, t:t + 1], in0=stat[:, 0, t:t + 1], scalar1=1.0 / D,
                                                                          scalar2=EPS, op0=ALU.mult, op1=ALU.add),
                                    reads=[R_stat[t]], writes=[R_stat[t]])
                              cx.op("pool", lambda e, t=t: e.tensor_tensor(out=stat[:, 2, t:t + 1], in0=stat[:, 1, t:t + 1], in1=mhalf[:, 0:1], op=ALU.pow),
                                    reads=[R_stat[t], R_const], writes=[R_stat[t]])
                              cx.op("dve", lambda e, t=t, xi=xi, j=j, xb=xb: e.tensor_scalar(
                                  out=xb[:, j, :], in0=xi[:], scalar1=stat[:, 2, t:t + 1], scalar2=None, op0=ALU.mult),
                                  reads=[R_xin[t % 4], R_stat[t]], writes=[R_xn[g4 % 2]])

                      def st_H(g4):
                          xb = xn[g4 % 2]
                          for k in range(8):
                              b = nxt("tp")
                              fns = [lambda e, j=j, k=k, b=b, xb=xb: e.transpose(tp[b][:, j * 128:(j + 1) * 128],
                                                                                xb[:, j, k * 128:(k + 1) * 128], ident[:])
                                     for j in range(4)]
                              cx.op("pe", fns, reads=[R_xn[g4 % 2], R_const], writes=[R_tp[b]])
                              cx.op("act", lambda e, k=k, b=b, g4=g4: e.activation(
                                  out=hT[:, k, g4 * 512:(g4 + 1) * 512], in_=tp[b][:, 0:512], func=AF.Identity,
                                  scale=gsA[:, l, s, k:k + 1], bias=shA[:, l, s, k:k + 1]),
                                  reads=[R_tp[b], R_mod], writes=[R_hT[g4]])

                      _pipeline(4, [(0, st_X), (1, st_H)])
                      cx.barrier()
                if s == 0 and l == 0:
                    dump("hT", hT[:, 0, :], [R_hT[0], R_hT[1], R_hT[2], R_hT[3]])
                chk("N")

                with ExitStack() as ph:
                    P = ph.enter_context
                    kbT = PT(ph, "kbT", [128, 2, S], BF16)
                    vb = PT(ph, "vb", [128, NT, 512], BF16)
                    kdb = PT(ph, "kdb", [128, NT, 256], BF16)
                    SfB = PT(ph, "SfB", [128, NT, 2, 128], BF16)
                    SbB = PT(ph, "SbB", [128, NT, 2, 128], BF16)
                    Sst = PT(ph, "Sst", [128, 2, 2, 128], F32)
                    rt1 = PT(ph, "rt1", [128, 256], F32)
                    rt2 = PT(ph, "rt2", [128, 256], F32)
                    kr = [PT(ph, "kr%d" % i, [128, 256], F32) for i in range(2)]
                    kbf = [PT(ph, "kbf%d" % i, [128, 256], BF16) for i in range(2)]
                    kdf = [PT(ph, "kdf%d" % i, [128, 256], BF16) for i in range(2)]
                    q3 = [PT(ph, "q3_%d" % i, [128, 3, 256], BF16) for i in range(2)]
                    qT3 = [PT(ph, "qT3_%d" % i, [128, 6, 128], BF16) for i in range(2)]
                    gsl = [PT(ph, "gsl%d" % i, [128, 512], BF16) for i in range(2)]
                    inT = [PT(ph, "inT%d" % i, [128, 512], BF16) for i in range(2)]
                    yr = [PT(ph, "yr%d" % i, [128, 512], BF16) for i in range(2)]
                    junk2 = PT(ph, "junk2", [128, 128], BF16)
                    gst = [PT(ph, "gst%d" % i, [128, 3, 4], F32) for i in range(2)]
                    R_kbT = [Res() for _ in range(NT)]
                    R_vb = [Res() for _ in range(NT)]
                    R_kdb = [Res() for _ in range(NT)]
                    R_SfB = [Res() for _ in range(NT)]
                    R_SbB = [Res() for _ in range(NT)]
                    R_Sst = [Res(), Res()]
                    R_rt1, R_rt2, R_junk2 = Res(), Res(), Res()
                    R_kr, R_kbf, R_kdf, R_q3, R_qT3, R_gsl, R_inT, R_yr, R_gst = (
                        [Res(), Res()] for _ in range(9))
                    TABv = lambda kind: TAB[:, kind, :, :].rearrange("p h d -> p (h d)")
                    R_q3a, R_q3b, R_q3c = ([Res(), Res()] for _ in range(3))
                    cx.op("pool", lambda e: e.memset(Sst[:], 0.0), writes=R_Sst)

                    def scan_update(d_, a):
                        fns = []
                        for p in range(2):
                            for hh in range(2):
                                rows = slice(hh * 64, hh * 64 + 64)
                                fns.append(lambda e, p=p, hh=hh, rows=rows: e.scalar_tensor_tensor(
                                    out=Sst[rows, d_, p, :], in0=Sst[rows, d_, p, :], scalar=Gfb[rows, d_, p:p + 1],
                                    in1=ov[a][rows, p * 256 + hh * 128:p * 256 + hh * 128 + 128], op0=ALU.mult, op1=ALU.add))
                        return fns

                    def b1_P(n):
                        i2 = n % 2
                        a = proj_tok(n, 0, 256, 256)
                        rope(mm[a][:, 0:256], R_mm[a], n, kr[i2][:], R_kr[i2], rt1[:], rt2[:], R_rt1, R_rt2)
                        cx.op("act", lambda e, i2=i2: e.activation(out=kbf[i2][:], in_=kr[i2][:], func=AF.Copy, scale=0.125),
                              reads=[R_kr[i2]], writes=[R_kbf[i2]])
                        cx.op("pool", lambda e, i2=i2: e.tensor_tensor(out=kdf[i2][:], in0=kr[i2][:], in1=TABv(0), op=ALU.mult),
                              reads=[R_kr[i2], R_tab], writes=[R_kdf[i2]])
                        cx.op("dve", lambda e, i2=i2, n=n: e.tensor_tensor(out=kdb[:, n, :], in0=kr[i2][:], in1=TABv(1), op=ALU.mult),
                              reads=[R_kr[i2], R_tab], writes=[R_kdb[n]])
                        a2 = proj_tok(n, 1, 0, 512)
                        cx.op("act", lambda e, a2=a2, n=n: e.activation(out=vb[:, n, :], in_=mm[a2][:, :], func=AF.Copy),
                              reads=[R_mm[a2]], writes=[R_vb[n]])

                    def b1_T(n):
                        i2 = n % 2
                        b = nxt("tp")
                        cx.op("pe", [lambda e, p=p, b=b, i2=i2: e.transpose(tp[b][:, p * 128:(p + 1) * 128], kbf[i2][:, p * 128:(p + 1) * 128], ident[:])
                                     for p in range(2)], reads=[R_kbf[i2], R_const], writes=[R_tp[b]])
                        cx.op("act", lambda e, b=b, n=n: e.activation(
                            out=kbT[:, :, n * 128:(n + 1) * 128], in_=tp[b][:, 0:256].rearrange("p (a c) -> p a c", a=2), func=AF.Copy),
                            reads=[R_tp[b]], writes=[R_kbT[n]])
                        cx.op("dve", lambda e, n=n: e.tensor_copy(out=SfB[:, n, :, :], in_=Sst[:, 0, :, :]),
                              reads=[R_Sst[0]], writes=[R_SfB[n]])
                        if n < NT - 1:
                            o = nxt("ov")
                            cx.op("pe", [lambda e, p=p, o=o, i2=i2, n=n: e.matmul(
                                ov[o][:, p * 256:(p + 1) * 256], lhsT=kdf[i2][:, p * 128:(p + 1) * 128],
                                rhs=vb[:, n, p * 256:(p + 1) * 256], start=True, stop=True) for p in range(2)],
                                reads=[R_kdf[i2], R_vb[n]], writes=[R_ov[o]])
                            cx.op("dve", scan_update(0, o), reads=[R_ov[o], R_tab, R_Sst[0]], writes=[R_Sst[0]])

                    _pipeline(NT, [(0, b1_P), (1, b1_T)])
                    b_stage = {"B1": 1, "Bb": 2}.get(stop, 3)
                    def bw_step(n):
                        cx.op("dve", lambda e, n=n: e.tensor_copy(out=SbB[:, n, :, :], in_=Sst[:, 1, :, :]),
                              reads=[R_Sst[1]], writes=[R_SbB[n]])
                        if n > 0:
                            o = nxt("ov")
                            cx.op("pe", [lambda e, p=p, o=o, n=n: e.matmul(
                                ov[o][:, p * 256:(p + 1) * 256], lhsT=kdb[:, n, p * 128:(p + 1) * 128],
                                rhs=vb[:, n, p * 256:(p + 1) * 256], start=True, stop=True) for p in range(2)],
                                reads=[R_kdb[n], R_vb[n]], writes=[R_ov[o]])
                            cx.op("dve", scan_update(1, o), reads=[R_ov[o], R_tab, R_Sst[1]], writes=[R_Sst[1]])

                    if b_stage == 2:
                        for n in range(NT - 1, -1, -1):
                            bw_step(n)
                    def b2_P(n):
                        i2 = n % 2
                        a = proj_tok(n, 0, 0, 256)
                        rope(mm[a][:, 0:256], R_mm[a], n, kr[i2][:], R_kr[i2], rt1[:], rt2[:], R_rt1, R_rt2)
                        cx.op("act", lambda e, i2=i2: e.activation(out=q3[i2][:, 0, :], in_=kr[i2][:], func=AF.Copy),
                              reads=[R_kr[i2]], writes=[R_q3a[i2]])
                        cx.op("pool", lambda e, i2=i2: e.tensor_tensor(out=q3[i2][:, 1, :], in0=kr[i2][:], in1=TABv(2), op=ALU.mult),
                              reads=[R_kr[i2], R_tab], writes=[R_q3b[i2]])
                        cx.op("dve", lambda e, i2=i2: e.tensor_tensor(out=q3[i2][:, 2, :], in0=kr[i2][:], in1=TABv(3), op=ALU.mult),
                              reads=[R_kr[i2], R_tab], writes=[R_q3c[i2]])

                    def b2_T(n):
                        i2 = n % 2
                        b = nxt("tp")
                        cx.op("pe", [lambda e, v=v, p=p, b=b, i2=i2: e.transpose(
                            tp[b][:, (v * 2 + p) * 128:(v * 2 + p + 1) * 128], q3[i2][:, v, p * 128:(p + 1) * 128], ident[:])
                            for v in range(3) for p in range(2)], reads=[R_q3a[i2], R_q3b[i2], R_q3c[i2], R_const], writes=[R_tp[b]])
                        cx.op("act", lambda e, b=b, i2=i2: e.activation(
                            out=qT3[i2][:], in_=tp[b][:, 0:768].rearrange("p (a c) -> p a c", a=6), func=AF.Copy),
                            reads=[R_tp[b]], writes=[R_qT3[i2]])

                    def b2_T2(n):
                        i2 = n % 2
                        fns = []
                        for h in range(4):
                            p, hh = h // 2, h % 2
                            rows = slice(hh * 64, hh * 64 + 64)
                            fns.append(lambda e, p=p, hh=hh, rows=rows, i2=i2, n=n: e.matmul(
                                st[hh][:, p * 128:(p + 1) * 128], lhsT=kbT[rows, p, n * 128:(n + 1) * 128],
                                rhs=qT3[i2][rows, p, :], start=True, stop=True))
                        cx.op("pe", fns, reads=[R_kbT[n], R_qT3[i2]], writes=[R_st[0], R_st[1]])
                        inv = inT[i2][:].rearrange("p (a b t) -> p a b t", a=2, b=2)
                        for hh in range(2):
                            cx.op("dve", lambda e, hh=hh, inv=inv: e.tensor_tensor(
                                out=inv[:, :, hh, :], in0=st[hh][:, 0:256].rearrange("p (a t) -> p a t", a=2),
                                in1=DTt[:].rearrange("p (a b) t -> p a b t", b=2)[:, :, hh, :], op=ALU.mult),
                                reads=[R_st[hh], R_tab], writes=[R_inT[i2]])
                        a2 = proj_tok(n, 2, 0, 512)
                        cx.op("act", lambda e, a2=a2, i2=i2: e.activation(out=gsl[i2][:], in_=mm[a2][:, :], func=AF.Silu),
                              reads=[R_mm[a2]], writes=[R_gsl[i2]])

                    def b2_Y(n):
                        i2 = n % 2
                        o = nxt("ov")
                        fns = []
                        for h in range(4):
                            p, hh = h // 2, h % 2
                            rows = slice(hh * 64, hh * 64 + 64)
                            oc = slice(h * 128, (h + 1) * 128)
                            fns.append(lambda e, oc=oc, o=o, i2=i2, n=n: e.matmul(
                                ov[o][:, oc], lhsT=inT[i2][:, oc], rhs=vb[:, n, oc], start=True, stop=False))
                            fns.append(lambda e, oc=oc, o=o, i2=i2, n=n, p=p, rows=rows: e.matmul(
                                ov[o][:, oc], lhsT=qT3[i2][rows, 2 + p, :], rhs=SfB[rows, n, p, :], start=False, stop=False))
                            fns.append(lambda e, oc=oc, o=o, i2=i2, n=n, p=p, rows=rows: e.matmul(
                                ov[o][:, oc], lhsT=qT3[i2][rows, 4 + p, :], rhs=SbB[rows, n, p, :], start=False, stop=True))
                        cx.op("pe", fns, reads=[R_inT[i2], R_vb[n], R_qT3[i2], R_SfB[n], R_SbB[n]], writes=[R_ov[o]])
                        cx.op("pool", lambda e, i2=i2: e.memset(gst[i2][:, 0, :], 0.0), writes=[R_gst[i2]])
                        cx.op("act", [lambda e, h=h, o=o, i2=i2: e.activation(
                            out=junk2[:], in_=ov[o][:, h * 128:(h + 1) * 128], func=AF.Square, accum_out=gst[i2][:, 0, h:h + 1])
                            for h in range(4)], reads=[R_ov[o]], writes=[R_junk2, R_gst[i2]])
                        cx.op("dve", lambda e, i2=i2: e.tensor_scalar(out=gst[i2][:, 1, :], in0=gst[i2][:, 0, :], scalar1=1.0 / 128,
                                                                      scalar2=EPS, op0=ALU.mult, op1=ALU.add),
                              reads=[R_gst[i2]], writes=[R_gst[i2]])
                        cx.op("pool", lambda e, i2=i2: e.tensor_tensor(out=gst[i2][:, 2, :], in0=gst[i2][:, 1, :], in1=mhalf[:, 0:4], op=ALU.pow),
                              reads=[R_gst[i2], R_const], writes=[R_gst[i2]])
                        cx.op("dve", [lambda e, h=h, o=o, i2=i2: e.scalar_tensor_tensor(
                            out=yr[i2][:, h * 128:(h + 1) * 128], in0=ov[o][:, h * 128:(h + 1) * 128],
                            scalar=gst[i2][:, 2, h:h + 1], in1=gsl[i2][:, h * 128:(h + 1) * 128], op0=ALU.mult, op1=ALU.mult)
                            for h in range(4)], reads=[R_ov[o], R_gst[i2], R_gsl[i2]], writes=[R_yr[i2]])

                    def b2_Z(n):
                        i2 = n % 2
                        b = nxt("tp")
                        cx.op("pe", [lambda e, h=h, b=b, i2=i2: e.transpose(tp[b][:, h * 128:(h + 1) * 128], yr[i2][:, h * 128:(h + 1) * 128], ident[:])
                                     for h in range(4)], reads=[R_yr[i2], R_const], writes=[R_tp[b]])
                        cx.op("act", lambda e, b=b, n=n: e.activation(
                            out=yT[:, 4:8, n * 128:(n + 1) * 128], in_=tp[b][:, 0:512].rearrange("p (a c) -> p a c", a=4), func=AF.Copy),
                            reads=[R_tp[b]], writes=R_yT[4:8])

                    if b_stage >= 3:
                        load_w(1, win_a[l][:, :, 0:4, 0:128], 512)
                        rev = lambda f: (lambda i: f(NT - 1 - i))
                        _pipeline(NT, [(0, rev(bw_step)), (0, rev(b2_P)), (1, rev(b2_T)), (2, rev(b2_Y)), (3, rev(b2_Z)), (1, rev(b2_T2))])
                    cx.barrier()
                if s == 0 and l == 0:
                    dump("yrT", yT[:, 4, :], R_yT[4:8])
                if stop in ("B1", "Bb", "B2a", "B2b", "B2c", "B2d"):
                    raise _Stop()
                chk("B")

                with ExitStack() as ph:
                    P = ph.enter_context
                    qT = PT(ph, "qT", [128, S], BF16)
                    gaT2 = [PT(ph, "gaT%d" % i, [128, S], BF16) for i in range(2)]
                    Vp = [PT(ph, "Vp%d" % i, [128, 20, 2, 128], BF16) for i in range(2)]
                    ACC = [PT(ph, "ACC%d" % i, [128, S], F32) for i in range(2)]
                    rt1 = PT(ph, "art1", [128, 256], F32)
                    rt2 = PT(ph, "art2", [128, 256], F32)
                    qkr = [PT(ph, "qkr%d" % i, [128, 256], BF16) for i in range(2)]
                    pt = [PT(ph, "pt%d" % i, [128, 512], BF16) for i in range(4)]
                    Rr = [PT(ph, "Rr%d" % i, [128, 512], F32) for i in range(2)]
                    Tm = [PT(ph, "Tm%d" % i, [128, 512], F32) for i in range(2)]
                    R_qT = [Res() for _ in range(NT)]
                    R_gaT2 = [[Res() for _ in range(4)] for _ in range(2)]
                    R_Vp = [Res(), Res()]
                    R_ACC = [[Res() for _ in range(4)] for _ in range(2)]
                    R_rt1, R_rt2 = Res(), Res()
                    R_qkr, R_pt, R_Rr, R_Tm = ([Res(), Res(), Res(), Res()] for _ in range(4))
                    sbank = [st[0], st[1], mm[0], mm[1]]
                    R_sbank = [R_st[0], R_st[1], R_mm[0], R_mm[1]]
                    def vp_init():
                        for i in range(2):
                            vflat = Vp[i][:].rearrange("p a b c -> p (a b) c")
                            for q_ in range(5):
                                cx.op("act", lambda e, i=i, q_=q_, vflat=vflat: e.activation(
                                    out=vflat[:, q_ * 8:(q_ + 1) * 8, :], in_=_ap(onesf[:, 0:128], [[0, 8], [1, 128]]), func=AF.Copy),
                                    reads=[R_const], writes=[R_Vp[i]])
                    if os.environ.get("DUMMY_INIT"):
                        for q_ in range(10):
                            cx.op("act", lambda e, q_=q_: e.activation(out=ACC[0][:, q_ * 128:(q_ + 1) * 128], in_=onesf[:, 0:128], func=AF.Copy),
                                  reads=[R_const], writes=[R_ACC[0]])
                    elif not os.environ.get("VPM_LATE") and not os.environ.get("SKIP_VPM"):
                        vp_init()
                    vpc = [0]
                    fin_pend = []
                    for j in range(4):
                        wslot = [1, 0, 2, 1][j]
                        gaT, R_gaT = gaT2[j % 2], R_gaT2[j % 2]
                        def a1_P(t, wslot=wslot, gaT=gaT, R_gaT=R_gaT):
                            i2 = t % 2
                            a = proj_tok(t, wslot, 0, 256)
                            rope(mm[a][:, 0:256], R_mm[a], t, qkr[i2][:], R_qkr[i2], rt1[:], rt2[:], R_rt1, R_rt2)
                            if t % 4 == 3:
                                g4 = t // 4
                                a = proj_feat(g4, wslot, 256)
                                cx.op("act", lambda e, a=a, g4=g4: e.activation(out=VT1[:, 64 + g4 * 512:64 + (g4 + 1) * 512], in_=mm[a][:, :], func=AF.Copy),
                                      reads=[R_mm[a]], writes=[R_VT1])
                                cx.op("act", lambda e, a=a, g4=g4: e.activation(
                                    out=VT2[:, :, 64 + g4 * 128:64 + (g4 + 1) * 128],
                                    in_=mm[a][:, :].rearrange("p (l r) -> p r l", r=4), func=AF.Copy),
                                    reads=[R_mm[a]], writes=[R_VT2])
                                a = proj_feat(g4, wslot, 384)
                                cx.op("act", lambda e, a=a, g4=g4: e.activation(out=gaT[:, g4 * 512:(g4 + 1) * 512], in_=mm[a][:, :], func=AF.Silu),
                                      reads=[R_mm[a]], writes=[R_gaT[g4]])

                        def a1_T(t):
                            i2 = t % 2
                            b = nxt("tp")
                            cx.op("pe", [lambda e, p=p, b=b, i2=i2: e.transpose(tp[b][:, p * 128:(p + 1) * 128], qkr[i2][:, p * 128:(p + 1) * 128], ident[:])
                                         for p in range(2)], reads=[R_qkr[i2], R_const], writes=[R_tp[b]])
                            def cp(out_ap, in_ap, R_out, t=t, b=b):
                                if t % 2 == 0:
                                    cx.op("act", lambda e: e.activation(out=out_ap, in_=in_ap, func=AF.Copy), reads=[R_tp[b]], writes=[R_out])
                                else:
                                    cx.op("dve", lambda e: e.tensor_copy(out=out_ap, in_=in_ap), reads=[R_tp[b]], writes=[R_out])
                            cp(qT[:, t * 128:(t + 1) * 128], tp[b][:, 0:128], R_qT[t])
                            cp(kT1[:, 64 + t * 128:64 + (t + 1) * 128], tp[b][:, 128:256], R_kT1)
                            cp(kT2[:, :, 64 + t * 32:64 + (t + 1) * 32], tp[b][:, 128:256].rearrange("p (l r) -> p r l", r=4), R_kT2)

                        _pipeline(NT, [(0, a1_P), (1, a1_T)])
                        if j == 0:
                            load_w(0, win_a[l][:, :, 0:4, 128:256], 512)
                            load_w(2, win_a[l][:, :, 0:4, 256:384], 512)
                        elif j == 1:
                            load_w(1, win_a[l][:, :, 0:4, 384:512], 512)
                        elif j == 2:
                            load_w(0, wout_v[l][:, :, 0:512], 512)
                        else:
                            load_w(2, wout_v[l][:, :, 512:1024], 512)
                            nl, ns = (l + 1, s) if l + 1 < depth else (0, s + 1)
                            if ns < nseq:
                                load_w(1, win_v[nl][:, :, 2560:3072], 512)

                        if os.environ.get("VPM_LATE") and j == 0:
                            vp_init()
                        def build_vp(vi, srcs, R_src):
                            for g0 in range(0, len(srcs), 4):
                                grp = srcs[g0:g0 + 4]
                                b = nxt("tp")
                                cx.op("pe", [lambda e, ii=ii, sa_=sa_, b=b: e.transpose(tp[b][:, ii * 128:(ii + 1) * 128], sa_, ident[:])
                                             for ii, (_, sa_) in enumerate(grp)], reads=[R_src, R_const], writes=[R_tp[b]])
                                i0 = grp[0][0]
                                ng = len(grp)
                                for hh in range(2):
                                    eng = "act" if (g0 // 4) % 2 == 0 else "dve"
                                    src = tp[b][:, 0:ng * 128].rearrange("p (a c) -> p a c", c=128)[:, :, hh * 64:hh * 64 + 64]
                                    dst = Vp[vi][:, i0:i0 + ng, hh, hh * 64:hh * 64 + 64]
                                    if eng == "act":
                                        cx.op("act", lambda e, src=src, dst=dst: e.activation(out=dst, in_=src, func=AF.Copy),
                                              reads=[R_tp[b]], writes=[R_Vp[vi]])
                                    else:
                                        cx.op("dve", lambda e, src=src, dst=dst: e.tensor_copy(out=dst, in_=src),
                                              reads=[R_tp[b]], writes=[R_Vp[vi]])

                        pend = []

                        def flush_pend():
                            while pend:
                                pend.pop(0)()

                        tpf = [tp[0][:].bitcast(F32), tp[1][:].bitcast(F32)]
                        obank = [[ov[0][:, :], tpf[0]], [ov[1][:, :], tpf[1]]]
                        R_obank = [[R_ov[0], R_tp[0]], [R_ov[1], R_tp[1]]]

                        def attend2(vi, items2, mask_variant, R_k, acc2):
                            fc = ctr["ov"]
                            ctr["ov"] += 1
                            nk = len(items2[0][0][1])
                            per = 512 // (128 * nk)
                            ngrp = 4 // per
                            for gi, g0 in enumerate(range(0, 4, per)):
                                base = (ctr["w"] % 2) * 2
                                ctr["w"] += 1
                                mv = mask_variant(g0)
                                fns = [lambda e, sa=base + hh, mv=mv: e.matmul(sbank[sa][:, :], lhsT=ident[:], rhs=maskA[:, mv, :],
                                                                               start=True, stop=False) for hh in range(2)]
                                order = [(ii, kk, hh) for ii in range(per) for kk in range(nk) for hh in range(2)]
                                if os.environ.get("NO_ILV"):
                                    order = [(ii, kk, hh) for hh in range(2) for ii in range(per) for kk in range(nk)]
                                for (ii, kk, hh) in order:
                                    if True:
                                        c0 = (ii * nk + kk) * 128
                                        if True:
                                            q_ap, ks = items2[hh][g0 + ii]
                                            k_ap = ks[kk][0]
                                            fns.append(lambda e, c0=c0, sa=base + hh, k_ap=k_ap, q_ap=q_ap: e.matmul(
                                                sbank[sa][:, c0:c0 + 128], lhsT=k_ap, rhs=q_ap, start=False, stop=True))
                                cx.op("pe", fns, reads=[R_k, R_const] + [R_qT[t] for t in range(NT)], writes=[R_sbank[base], R_sbank[base + 1]])
                                for hh in range(2):
                                    pi = base + hh
                                    cx.op("act", lambda e, pi=pi: e.activation(out=pt[pi][:], in_=sbank[pi][:, :], func=AF.Exp, scale=0.125),
                                          reads=[R_sbank[pi]], writes=[R_pt[pi]])

                                def stage2(g0=g0, gi=gi, base=base):
                                    for hh in range(2):
                                        pi = base + hh
                                        fsel = 0
                                        o_ap, R_o = obank[hh][fsel], R_obank[hh][fsel]
                                        fns = []
                                        for ii in range(per):
                                            _, ks = items2[hh][g0 + ii]
                                            qi = g0 + ii
                                            for kk, (_, vidx) in enumerate(ks):
                                                c0 = (ii * nk + kk) * 128
                                                fns.append(lambda e, c0=c0, qi=qi, vidx=vidx, kk=kk, hh=hh, pi=pi, o_ap=o_ap: e.matmul(
                                                    o_ap[:, qi * 128:(qi + 1) * 128], lhsT=Vp[vi][:, vidx, hh, :], rhs=pt[pi][:, c0:c0 + 128],
                                                    start=(kk == 0), stop=(kk == nk - 1)))
                                        cx.op("pe", fns, reads=[R_pt[pi], R_Vp[vi]], writes=[R_o])
                                        if gi == ngrp - 1:
                                            acc2(hh, o_ap, R_o)

                                pend.append(stage2)
                                while len(pend) > 1:
                                    pend.pop(0)()

                        a_st = {"A1": 0, "Ap0": 1, "Ap1": 2, "Ap2": 3}.get(stop, 9)
                        if a_st < 9 and j > 0:
                            continue
                        hrows = [slice(0, 64), slice(64, 128)]
                        for pat in range(min(3, a_st)):
                            vi = vpc[0] % 2
                            vpc[0] += 1
                            if pat == 0:
                                build_vp(vi, [(jt, VT1[:, 128 * jt:128 * jt + 128]) for jt in range(17)], R_VT1)
                                for u in range(4):
                                    items2 = [[(qT[rows, 128 * i_:128 * i_ + 128],
                                                [(kT1[rows, 128 * i_:128 * i_ + 128], i_),
                                                 (kT1[rows, 128 * (i_ + 1):128 * (i_ + 1) + 128], i_ + 1)])
                                               for i_ in range(4 * u, 4 * u + 4)] for rows in hrows]
                                    mvf = lambda g0, u=u: (0 if (u == 0 and g0 == 0) else (2 if (u == 3 and g0 == 2) else 1))
                                    acc2 = lambda hh, o_ap, R_o, u=u: cx.op("dve", lambda e: e.tensor_copy(
                                        out=ACC[hh][:, 512 * u:512 * (u + 1)], in_=o_ap),
                                        reads=[R_o], writes=[R_ACC[hh][u]])
                                    for _ in range(2):
                                        if fin_pend:
                                            fin_pend.pop(0)()
                                    attend2(vi, items2, mvf, R_kT1, acc2)
                            elif pat == 1:
                                build_vp(vi, [(r * 5 + jt, VT2[:, r, 128 * jt:128 * jt + 128]) for r in range(4) for jt in range(5)], R_VT2)
                                for r in range(4):
                                    items2 = [[(qT[rows, 512 * i_ + r:512 * (i_ + 1):4],
                                                [(kT2[rows, r, 128 * i_:128 * i_ + 128], r * 5 + i_),
                                                 (kT2[rows, r, 128 * (i_ + 1):128 * (i_ + 1) + 128], r * 5 + i_ + 1)])
                                               for i_ in range(4)] for rows in hrows]
                                    mvf = lambda g0: (0 if g0 == 0 else 2)
                                    acc2 = lambda hh, o_ap, R_o, r=r: cx.op("dve", lambda e: e.tensor_tensor(
                                        out=ACC[hh][:, r:S:4], in0=o_ap, in1=ACC[hh][:, r:S:4], op=ALU.add),
                                        reads=[R_o] + R_ACC[hh], writes=R_ACC[hh])
                                    attend2(vi, items2, mvf, R_kT2, acc2)
                            else:
                                build_vp(vi, [(r, VT1[:, 64 + r:64 + S:16]) for r in range(16)], R_VT1)
                                for r0 in range(0, 16, 4):
                                    items2 = [[(qT[rows, r:S:16], [(kT1[rows, 64 + r:64 + S:16], r)])
                                               for r in range(r0, r0 + 4)] for rows in hrows]
                                    mvf = lambda g0: 3

                                    def acc2(hh, o_ap, R_o, r0=r0):
                                        accv = ACC[hh][:].rearrange("p (l r) -> p r l", r=16)[:, r0:r0 + 4, :]
                                        cx.op("dve", lambda e: e.tensor_tensor(
                                            out=accv, in0=o_ap.rearrange("p (r l) -> p r l", r=4), in1=accv, op=ALU.add),
                                            reads=[R_o] + R_ACC[hh], writes=R_ACC[hh])
                                    attend2(vi, items2, mvf, R_kT1, acc2)
                        flush_pend()

                        def fin_step(hh, u, j=j, gaT=gaT, R_gaT=R_gaT):
                            nr = slice(hh * 64, hh * 64 + 64)
                            dr = slice((1 - hh) * 64, (1 - hh) * 64 + 64)
                            cs = slice(512 * u, 512 * (u + 1))
                            i2 = ctr["fin"] % 2
                            ctr["fin"] += 1
                            cx.op("act", lambda e: e.activation(out=Rr[i2][nr, :], in_=ACC[hh][dr, cs], func=AF.Ln),
                                  reads=[R_ACC[hh][u]], writes=[R_Rr[i2]])
                            cx.op("act", lambda e: e.activation(out=Rr[i2][nr, :], in_=Rr[i2][nr, :], func=AF.Exp, scale=-1.0),
                                  reads=[R_Rr[i2]], writes=[R_Rr[i2]])
                            cx.op("dve", lambda e: e.tensor_tensor(out=Tm[i2][nr, :], in0=ACC[hh][nr, cs], in1=Rr[i2][nr, :], op=ALU.mult),
                                  reads=[R_ACC[hh][u], R_Rr[i2]], writes=[R_Tm[i2]])
                            cx.op("pool", lambda e: e.tensor_tensor(out=yT[nr, j, cs], in0=Tm[i2][nr, :], in1=gaT[nr, cs], op=ALU.mult),
                                  reads=[R_Tm[i2], R_gaT[u]], writes=[R_yT[j]])

                        if a_st >= 9:
                            for u in range(4):
                                for hh in range(2):
                                    fin_pend.append(lambda hh=hh, u=u, f=fin_step: f(hh, u))
                    while fin_pend:
                        fin_pend.pop(0)()
                    cx.barrier()
                if s == 0 and l == 0:
                    dump("yaT", yT[:, 0, :], R_yT[0:4])
                if stop in ("A1", "Ap0", "Ap1", "Ap2"):
                    raise _Stop()
                chk("A")

                with ExitStack() as ph:
                    P = ph.enter_context
                    gate_b = PT(ph, "gate_b", [128, D], F32)
                    gfin_b = PT(ph, "gfin_b", [128, D], F32)
                    R_gfin = Res()
                    if last:
                        cx.dma("sp", "cst", lambda e: e.dma_start(out=gfin_b[:], in_=gfin_d.partition_broadcast(128)), writes=[R_gfin])
                    xin = [PT(ph, "oxin%d" % i, [128, D], F32) for i in range(4)]
                    xo = [PT(ph, "xo%d" % i, [128, D], F32) for i in range(2)]
                    Dk = [PT(ph, "Dk%d" % i, [128, 128], F32) for i in range(2)]
                    junk = PT(ph, "ojunk", [128, D], BF16)
                    fst = [PT(ph, "fst%d" % i, [128, 3], F32) for i in range(2)]
                    R_xin, R_xo, R_Dk, R_fst = ([Res(), Res(), Res(), Res()] for _ in range(4))
                    R_xoa, R_xob = [Res(), Res()], [Res(), Res()]
                    R_junk = Res()
                    if not last:
                        oxn = [PT(ph, "oxn%d" % i, [128, 4, D], BF16) for i in range(2)]
                        ostat = PT(ph, "ostat", [128, 3, NT], F32)
                        R_oxn = [Res(), Res()]
                        R_ostat = [Res() for _ in range(NT)]
                    h_pend = []

                    def o_H(g4):
                        xb = oxn[g4 % 2]
                        for k in range(8):
                            b = nxt("tp")
                            cx.op("pe", [lambda e, jj=jj, k=k, b=b, xb=xb: e.transpose(tp[b][:, jj * 128:(jj + 1) * 128],
                                                                                   xb[:, jj, k * 128:(k + 1) * 128], ident[:])
                                         for jj in range(4)], reads=[R_oxn[g4 % 2], R_const], writes=[R_tp[b]])
                            cx.op("act", lambda e, k=k, b=b, g4=g4: e.activation(
                                out=hT[:, k, g4 * 512:(g4 + 1) * 512], in_=tp[b][:, 0:512], func=AF.Identity,
                                scale=gsA[:, l + 1, s, k:k + 1], bias=shA[:, l + 1, s, k:k + 1]),
                                reads=[R_tp[b], R_mod], writes=[R_hT[g4]])
                    for k in range(8):
                        i2 = k % 2
                        cx.op("dve", lambda e, k=k, i2=i2: e.tensor_scalar(out=Dk[i2][:], in0=identf[:], scalar1=gtA[:, l, s, k:k + 1],
                                                                         scalar2=None, op0=ALU.mult),
                              reads=[R_const, R_mod], writes=[R_Dk[i2]])
                        a = nxt("mm")
                        cx.op("pe", lambda e, a=a, i2=i2: e.matmul(mm[a][:, 0:128], lhsT=onesf[:], rhs=Dk[i2][:], start=True, stop=True),
                              reads=[R_Dk[i2], R_const], writes=[R_mm[a]])
                        cx.op("act", lambda e, a=a, k=k: e.activation(out=gate_b[:, k * 128:(k + 1) * 128], in_=mm[a][:, 0:128], func=AF.Copy),
                              reads=[R_mm[a]], writes=[R_gate])
                    def o_A(t):
                        i2 = t % 2
                        i4 = t % 4
                        cx.dma("sp", "xin%d" % i4, lambda e, t=t, i4=i4: e.dma_start(out=xin[i4][:], in_=xsrc[s, t * 128:(t + 1) * 128, :]),
                               reads=[R_yd[s][t]], writes=[R_xin[i4]])
                        for half in range(2):
                            a = nxt("mm")
                            wslot = 0 if half == 0 else 2
                            hs = slice(half * 512, (half + 1) * 512)
                            cx.op("pe", [lambda e, kc=kc, a=a, wslot=wslot, t=t: e.matmul(
                                mm[a][:, :], lhsT=yT[:, kc, t * 128:(t + 1) * 128], rhs=wsl[wslot][:, kc, :],
                                start=(kc == 0), stop=(kc == 7)) for kc in range(8)],
                                reads=R_yT + [R_w[wslot]], writes=[R_mm[a]])
                            cx.op("dve", lambda e, a=a, hs=hs, i2=i2: e.tensor_tensor(out=xo[i2][:, hs], in0=mm[a][:, :], in1=gate_b[:, hs], op=ALU.mult),
                                  reads=[R_mm[a], R_gate], writes=[(R_xo if half == 0 else R_xob)[i2]])
                        cx.op("dve", lambda e, i2=i2, i4=i4: e.tensor_tensor(out=xo[i2][:, 0:512], in0=xo[i2][:, 0:512], in1=xin[i4][:, 0:512], op=ALU.add),
                              reads=[R_xo[i2], R_xin[i4]], writes=[R_xo[i2]])
                        cx.op("pool", lambda e, i2=i2, i4=i4: e.tensor_tensor(out=xo[i2][:, 512:1024], in0=xo[i2][:, 512:1024], in1=xin[i4][:, 512:1024], op=ALU.add),
                              reads=[R_xob[i2], R_xin[i4]], writes=[R_xob[i2]])

                    def o_B(t):
                        i2 = t % 2
                        if last:
                            cx.op("pool", lambda e, i2=i2: e.memset(fst[i2][:, 0:1], 0.0), writes=[R_fst[i2]])
                            cx.op("act", lambda e, i2=i2: e.activation(out=junk[:], in_=xo[i2][:], func=AF.Square, accum_out=fst[i2][:, 0:1]),
                                  reads=[R_xo[i2], R_xob[i2]], writes=[R_junk, R_fst[i2]])
                            cx.op("dve", lambda e, i2=i2: e.tensor_scalar(out=fst[i2][:, 1:2], in0=fst[i2][:, 0:1], scalar1=1.0 / D,
                                                                          scalar2=EPS, op0=ALU.mult, op1=ALU.add),
                                  reads=[R_fst[i2]], writes=[R_fst[i2]])
                            cx.op("pool", lambda e, i2=i2: e.tensor_tensor(out=fst[i2][:, 2:3], in0=fst[i2][:, 1:2], in1=mhalf[:, 0:1], op=ALU.pow),
                                  reads=[R_fst[i2], R_const], writes=[R_fst[i2]])
                            cx.op("dve", lambda e, i2=i2: e.scalar_tensor_tensor(
                                out=xo[i2][:], in0=xo[i2][:], scalar=fst[i2][:, 2:3], in1=gfin_b[:], op0=ALU.mult, op1=ALU.mult),
                                reads=[R_xo[i2], R_xob[i2], R_fst[i2], R_gfin], writes=[R_xo[i2], R_xob[i2]])
                        cx.dma("sp", "xout%d" % i2, lambda e, t=t, i2=i2: e.dma_start(out=y_d[s, t * 128:(t + 1) * 128, :], in_=xo[i2][:]),
                               reads=[R_xo[i2], R_xob[i2]], writes=[R_yd[s][t]])
                        if not last:
                            g4, jj = t // 4, t % 4
                            cx.op("pool", lambda e, t=t: e.memset(ostat[:, 0, t:t + 1], 0.0), writes=[R_ostat[t]])
                            cx.op("act", lambda e, t=t, i2=i2: e.activation(out=junk[:], in_=xo[i2][:], func=AF.Square,
                                                                           accum_out=ostat[:, 0, t:t + 1]),
                                  reads=[R_xo[i2], R_xob[i2]], writes=[R_junk, R_ostat[t]])
                            cx.op("dve", lambda e, t=t: e.tensor_scalar(out=ostat[:, 1, t:t + 1], in0=ostat[:, 0, t:t + 1], scalar1=1.0 / D,
                                                                        scalar2=EPS, op0=ALU.mult, op1=ALU.add),
                                  reads=[R_ostat[t]], writes=[R_ostat[t]])
                            cx.op("pool", lambda e, t=t: e.tensor_tensor(out=ostat[:, 2, t:t + 1], in0=ostat[:, 1, t:t + 1], in1=mhalf[:, 0:1], op=ALU.pow),
                                  reads=[R_ostat[t], R_const], writes=[R_ostat[t]])
                            cx.op("dve", lambda e, t=t, i2=i2, jj=jj, g4=g4: e.tensor_scalar(
                                out=oxn[g4 % 2][:, jj, :], in0=xo[i2][:], scalar1=ostat[:, 2, t:t + 1], scalar2=None, op0=ALU.mult),
                                reads=[R_xo[i2], R_xob[i2], R_ostat[t]], writes=[R_oxn[g4 % 2]])
                            if h_pend and h_pend[0][0] <= t:
                                o_H(h_pend.pop(0)[1])
                            if jj == 3:
                                h_pend.append((t + 2, g4))

                    _pipeline(NT, [(0, o_A), (1, o_B)])
                    while h_pend:
                        o_H(h_pend.pop(0)[1])
                    cx.barrier()
        except _Stop:
            pass
        cx.barrier()
    return nc


def _consts():
    f32 = np.float32
    pos = np.arange(S, dtype=np.float32)
    inv = (10000.0 ** (-np.arange(0, 64, 2, dtype=np.float32) / 64)).astype(f32)
    ang = (pos[:, None] * inv[None, :]).astype(f32)
    cos, sin = np.cos(ang).astype(f32), np.sin(ang).astype(f32)
    C2 = np.concatenate([cos, cos], axis=1)
    S2 = np.concatenate([-sin, sin], axis=1)
    rope = np.stack([C2.reshape(NT, 128, 64).transpose(1, 0, 2), S2.reshape(NT, 128, 64).transpose(1, 0, 2)], axis=1)
    p = np.arange(128)[:, None]
    c = np.arange(128)[None, :]
    A = (c <= p).astype(f32)
    B = (p <= c).astype(f32)
    A_first = A * (p >= 64)
    B_last = B * (p < 64)
    m_norm = np.concatenate([A, B], axis=1)
    m_first = np.concatenate([A_first, B], axis=1)
    m_last = np.concatenate([A, B_last], axis=1)
    band = (np.abs(p - c) <= 64).astype(f32)
    mask = np.stack([np.concatenate([m_first, m_norm], 1), np.concatenate([m_norm, m_norm], 1),
                     np.concatenate([m_norm, m_last], 1), np.concatenate([band] * 4, 1)], axis=1)
    diff = (c - p).astype(f32)
    ret = np.stack([np.maximum(diff, 0), (diff >= 0).astype(f32), np.maximum(-diff, 0), (diff < 0).astype(f32)], axis=1)
    tau = np.arange(128, dtype=f32)
    taus = np.stack([127 - tau, tau, tau + 1, 128 - tau], axis=1)
    mask = (mask - 1.0) * 30000.0
    return dict(cst_rope=np.ascontiguousarray(rope, f32), cst_mask=np.ascontiguousarray(mask, f32),
                cst_ret=np.ascontiguousarray(ret, f32), cst_tau=np.ascontiguousarray(taus, f32),
                cst_ident=np.eye(128, dtype=f32))


def kernel(x_prompt, x_sample, c_prompt, c_sample, g_norm, w_ada, b_ada, w_in, w_out,
           decay_fwd, decay_bwd, g_final):
    f = lambda a: np.ascontiguousarray(np.asarray(a), dtype=np.float32)
    xs = np.concatenate([f(x_prompt), f(x_sample)], axis=0)
    cs = np.concatenate([f(c_prompt), f(c_sample)], axis=0)
    shared = dict(g_norm=f(g_norm), w_ada=f(w_ada), b_ada=f(b_ada), w_in=f(w_in), w_out=f(w_out),
                  decay_fwd=f(decay_fwd), decay_bwd=f(decay_bwd), g_final=f(g_final))
    shared.update(_consts())
    nc = build_nc()
    in_maps = []
    for i in range(NCORES):
        m = dict(shared)
        m["x"] = np.ascontiguousarray(xs[i * NSEQ:(i + 1) * NSEQ])
        m["c"] = np.ascontiguousarray(cs[i * NSEQ:(i + 1) * NSEQ])
        in_maps.append(m)
    res = run_bass_kernel_spmd(nc, in_maps, core_ids=list(range(NCORES)))
    ys = np.concatenate([np.asarray(r["y"], dtype=np.float32) for r in res.results], axis=0)
    nb = np.asarray(x_prompt).shape[0]
    return (np.ascontiguousarray(ys[:nb]), np.ascontiguousarray(ys[nb:]))
```
